# Optimizing a Trainium2 kernel written in Bass

```python
import jax, jax.numpy as jnp
from jax import lax
import numpy as np

D_MODEL = 1024
BATCH = 4
SEQ = 4096
DEPTH = 4

GRID_W = 64
CTX_LEN = 256
HEAD_DIM = 64
A_GROUPS = 4
A_W = A_GROUPS * HEAD_DIM
CHUNK = 128
ATT_HEADS = 8
ATT_KV_HEADS = 2
ATT_Q = ATT_HEADS * HEAD_DIM
ATT_KV = ATT_KV_HEADS * HEAD_DIM
WINDOW = 128
QBLOCK = 128
ROPE_THETA = 10000.0
ROPE_FREQS = HEAD_DIM // 4
NEG_INF = -1e30
RW_HEADS = 4
RW_W = RW_HEADS * HEAD_DIM
W_LORA = 64
A_LORA = 64
G_LORA = 128
N_DIR = 2
GN_EPS = 64e-5
MIX_W = A_W + ATT_Q + RW_W
D_FF = -(-8 * D_MODEL // (3 * 256)) * 256
RMS_EPS = 1e-6
OFF_Q = 2 * A_W
OFF_R = OFF_Q + ATT_Q
OFF_CTX = OFF_R + RW_W + G_LORA
OFF_RWK = OFF_CTX + 2 * ATT_KV
IN_COLS = OFF_RWK + 2 * RW_W + N_DIR * (W_LORA + A_LORA)
MU_HEAD = RW_W + G_LORA
MU_W = MU_HEAD + 2 * RW_W + N_DIR * (W_LORA + A_LORA)

kernel_name = 'hybrid_flow_trunk'


def rms_norm(x, g):
    xf = x.astype(jnp.float32)
    y = xf * lax.rsqrt(jnp.mean(xf * xf, axis=-1, keepdims=True) + RMS_EPS)
    return (y * g.astype(jnp.float32)).astype(x.dtype)


def layer_norm(x, g, b):
    xf = x.astype(jnp.float32)
    mu = jnp.mean(xf, axis=-1, keepdims=True)
    var = jnp.mean(jnp.square(xf - mu), axis=-1, keepdims=True)
    return ((xf - mu) * lax.rsqrt(var + 1e-5) * g + b).astype(x.dtype)


def modulate(x, shift, scale):
    return x * (1.0 + scale) + shift


def swiglu(h, w_gu, w_down):
    gu = h @ w_gu
    return (jax.nn.silu(gu[..., :D_FF]) * gu[..., D_FF:]) @ w_down


def spatial_gating(z, ln_g, ln_b, w_s, b_s):
    z = jax.nn.gelu(z)
    u, v = z[..., :A_W], z[..., A_W:]
    B, T, _ = v.shape
    v = layer_norm(v, ln_g, ln_b).reshape(B, T // CHUNK, CHUNK, A_GROUPS, HEAD_DIM)
    mixed = jnp.einsum('gpq,bnqgd->bnpgd', w_s, v) + b_s.T[None, None, :, :, None]
    return u * mixed.reshape(B, T, A_W)


def rope_tables(T):
    t = jnp.arange(T)
    pos = jnp.stack([t // GRID_W, t % GRID_W], axis=-1).astype(jnp.float32)
    inv = ROPE_THETA ** (-jnp.arange(ROPE_FREQS, dtype=jnp.float32) / ROPE_FREQS)
    ang = pos[:, :, None] * inv
    return jnp.cos(ang), jnp.sin(ang)


def apply_rope(x, cos, sin):
    B, T, H, Dh = x.shape
    xr = x.astype(jnp.float32).reshape(B, T, H, 2, 2, ROPE_FREQS)
    x1, x2 = xr[..., 0, :], xr[..., 1, :]
    c, s = cos[None, :, None], sin[None, :, None]
    out = jnp.stack([x1 * c - x2 * s, x2 * c + x1 * s], axis=-2)
    return out.reshape(B, T, H, Dh).astype(x.dtype)


def window_attention(q, k, v, kc, vc, sink):
    B, T, H, Dh = q.shape
    G = H // ATT_KV_HEADS
    nb = T // QBLOCK
    C = kc.shape[1]
    scale = Dh ** -0.5
    qb = q.reshape(B, nb, QBLOCK, ATT_KV_HEADS, G, Dh)

    def band(t):
        tp = jnp.pad(t, ((0, 0), (QBLOCK, QBLOCK), (0, 0), (0, 0))).reshape(B, nb + 2, QBLOCK, ATT_KV_HEADS, Dh)
        return jnp.concatenate([tp[:, :-2], tp[:, 1:-1], tp[:, 2:]], axis=2)

    kw, vw = band(k), band(v)
    s_loc = jnp.einsum('bnqhgd,bnkhd->bnhgqk', qb, kw).astype(jnp.float32) * scale
    s_ctx = jnp.einsum('bnqhgd,bchd->bnhgqc', qb, kc).astype(jnp.float32) * scale
    qi = jnp.arange(QBLOCK)[:, None]
    kj = jnp.arange(3 * QBLOCK)[None, :]
    in_band = jnp.abs(kj - QBLOCK - qi) <= WINDOW
    kpos = jnp.arange(nb)[:, None] * QBLOCK - QBLOCK + jnp.arange(3 * QBLOCK)[None, :]
    in_seq = (kpos >= 0) & (kpos < T)
    mask = in_band[None] & in_seq[:, None, :]
    s_loc = jnp.where(mask[None, :, None, None], s_loc, NEG_INF)
    sink_l = jnp.broadcast_to(sink.astype(jnp.float32).reshape(ATT_KV_HEADS, G, 1, 1), s_ctx.shape[:-1] + (1,))
    p = jax.nn.softmax(jnp.concatenate([sink_l, s_ctx, s_loc], axis=-1), axis=-1)
    p_ctx = p[..., 1:1 + C].astype(v.dtype)
    p_loc = p[..., 1 + C:].astype(v.dtype)
    o = jnp.einsum('bnhgqc,bchd->bnqhgd', p_ctx, vc) + jnp.einsum('bnhgqk,bnkhd->bnqhgd', p_loc, vw)
    return o.reshape(B, T, H * Dh)


def context_attention(qc, kc, vc, sink):
    B, C, H, Dh = qc.shape
    G = H // ATT_KV_HEADS
    qg = qc.reshape(B, C, ATT_KV_HEADS, G, Dh)
    s = jnp.einsum('bqhgd,bkhd->bhgqk', qg, kc).astype(jnp.float32) * Dh ** -0.5
    sink_l = jnp.broadcast_to(sink.astype(jnp.float32).reshape(ATT_KV_HEADS, G, 1, 1), s.shape[:-1] + (1,))
    p = jax.nn.softmax(jnp.concatenate([sink_l, s], axis=-1), axis=-1)[..., 1:].astype(vc.dtype)
    return jnp.einsum('bhgqk,bkhd->bqhgd', p, vc).reshape(B, C, H * Dh)


def grid_shift(f):
    B, T, C = f.shape
    rows = T // GRID_W
    g = f.reshape(B, rows, GRID_W, C)
    q = C // 4
    left = jnp.pad(g[:, :, :-1, :q], ((0, 0), (0, 0), (1, 0), (0, 0)))
    right = jnp.pad(g[:, :, 1:, q:2 * q], ((0, 0), (0, 0), (0, 1), (0, 0)))
    up = jnp.pad(g[:, :-1, :, 2 * q:3 * q], ((0, 0), (1, 0), (0, 0), (0, 0)))
    down = jnp.pad(g[:, 1:, :, 3 * q:], ((0, 0), (0, 1), (0, 0), (0, 0)))
    return jnp.concatenate([left, right, up, down], axis=-1).reshape(B, T, C)


def seq_shift(f):
    h = f.shape[-1] // 2
    prev = jnp.pad(f[:, :-1, :h], ((0, 0), (1, 0), (0, 0)))
    nxt = jnp.pad(f[:, 1:, h:], ((0, 0), (0, 1), (0, 0)))
    return jnp.concatenate([prev, nxt], axis=-1)


def token_mix(f, shifted, mu):
    return f + (shifted - f) * mu


def heads(t):
    return t.reshape(t.shape[:-1] + (RW_HEADS, HEAD_DIM))


def rwkv_inputs(rwk, lp):
    f = rwk.astype(jnp.float32)
    base = 2 * RW_W
    k = f[..., :RW_W]
    v = f[..., RW_W:base]
    wd = f[..., base:base + N_DIR * W_LORA].reshape(f.shape[:-1] + (N_DIR, W_LORA))
    ad = f[..., base + N_DIR * W_LORA:].reshape(f.shape[:-1] + (N_DIR, A_LORA))
    kk = heads(k * lp['rw_kk'])
    kk = kk / jnp.maximum(jnp.sqrt(jnp.sum(kk * kk, axis=-1, keepdims=True)), 1e-12)
    z = lp['rw_w0'] + jnp.einsum('btdr,drc->btdc', jnp.tanh(wd), lp['rw_w2'])
    w = jnp.exp(-jnp.exp(-jax.nn.softplus(-z) - 0.5))
    a = jax.nn.sigmoid(lp['rw_a0'] + jnp.einsum('btdr,drc->btdc', ad, lp['rw_a2']))
    kd = k[..., None, :] * (1.0 + (a - 1.0) * lp['rw_ka'])
    return heads(v), kk, heads(w), heads(a), heads(kd)


def wkv_scan(w, k, v, kk, a, s0, reverse, r):
    tm = lambda t: jnp.moveaxis(t, 1, 0)

    def update(S, w_t, k_t, v_t, kk_t, a_t):
        sa = jnp.einsum('bhvk,bhk->bhv', S, kk_t)
        return (S * w_t[:, :, None, :] - sa[..., None] * (kk_t * a_t)[:, :, None, :]
                + v_t[..., None] * k_t[:, :, None, :])

    xs = (tm(w), tm(k), tm(v), tm(kk), tm(a))
    if r is None:
        S, _ = lax.scan(lambda S, x: (update(S, *x), None), s0, xs, reverse=reverse)
        return S, None

    def step(S, x):
        S = update(S, *x[1:])
        return S, jnp.einsum('bhvk,bhk->bhv', S, x[0])

    S, y = lax.scan(step, s0, (tm(r),) + xs, reverse=reverse)
    return S, jnp.moveaxis(y, 0, 1)


def rwkv_output(y, r, kd, v, gd, lp):
    B, T = y.shape[:2]
    mu = jnp.mean(y, axis=-1, keepdims=True)
    var = jnp.mean(jnp.square(y - mu), axis=-1, keepdims=True)
    yn = ((y - mu) * lax.rsqrt(var + GN_EPS)).reshape(B, T, RW_W) * lp['rw_lnx_g'] + lp['rw_lnx_b']
    bonus = jnp.sum(r[:, :, None] * kd * lp['rw_rk'], axis=-1, keepdims=True).sum(axis=2) * v
    g = jax.nn.sigmoid(gd) @ lp['rw_g2']
    return (yn + bonus.reshape(B, T, RW_W)) * g


def mix_tokens(h, hc, lp, last):
    B, T, _ = h.shape
    C = hc.shape[1]
    w_in = lp['w_in']
    mu = lp['rw_mu']
    px = h @ w_in
    pc_tail = hc @ w_in[:, OFF_CTX:]
    kc = pc_tail[..., :ATT_KV].reshape(B, C, ATT_KV_HEADS, HEAD_DIM)
    vc = pc_tail[..., ATT_KV:2 * ATT_KV].reshape(B, C, ATT_KV_HEADS, HEAD_DIM)
    rwk_c = pc_tail[..., 2 * ATT_KV:]
    vC, kkC, wC, aC, kdC = rwkv_inputs(token_mix(rwk_c, seq_shift(rwk_c), mu[MU_HEAD:]), lp)
    rC = None
    if not last:
        pc_head = hc @ w_in[:, :OFF_CTX]
        rg_c = pc_head[..., OFF_R:]
        rg_c = token_mix(rg_c, seq_shift(rg_c), mu[:MU_HEAD])
        rC = heads(rg_c[..., :RW_W].astype(jnp.float32))
    s0 = jnp.zeros((B, RW_HEADS, HEAD_DIM, HEAD_DIM), jnp.float32)
    Sc_f, yc_f = wkv_scan(wC[:, :, 0], kdC[:, :, 0], vC, kkC, aC[:, :, 0], s0, False, rC)
    Sc_b, yc_b = wkv_scan(wC[:, :, 1], kdC[:, :, 1], vC, kkC, aC[:, :, 1], s0, True, rC)

    cos, sin = rope_tables(T)
    q = apply_rope(px[..., OFF_Q:OFF_R].reshape(B, T, ATT_HEADS, HEAD_DIM), cos, sin)
    k = apply_rope(px[..., OFF_CTX:OFF_CTX + ATT_KV].reshape(B, T, ATT_KV_HEADS, HEAD_DIM), cos, sin)
    v = px[..., OFF_CTX + ATT_KV:OFF_RWK].reshape(B, T, ATT_KV_HEADS, HEAD_DIM)
    o_att = window_attention(q, k, v, kc, vc, lp['attn_sink'])
    o_sg = spatial_gating(px[..., :OFF_Q], lp['sg_ln_g'], lp['sg_ln_b'], lp['sg_w'], lp['sg_b'])
    rg = px[..., OFF_R:OFF_CTX]
    rg = token_mix(rg, grid_shift(rg), mu[:MU_HEAD])
    rwk = px[..., OFF_RWK:]
    vX, kkX, wX, aX, kdX = rwkv_inputs(token_mix(rwk, grid_shift(rwk), mu[MU_HEAD:]), lp)
    rX = heads(rg[..., :RW_W].astype(jnp.float32))
    _, y_f = wkv_scan(wX[:, :, 0], kdX[:, :, 0], vX, kkX, aX[:, :, 0], Sc_f, False, rX)
    _, y_b = wkv_scan(wX[:, :, 1], kdX[:, :, 1], vX, kkX, aX[:, :, 1], Sc_b, True, rX)
    o_rw = rwkv_output(y_f + y_b, rX, kdX, vX, rg[..., RW_W:], lp).astype(h.dtype)
    y = jnp.concatenate([o_sg, o_att, o_rw], axis=-1) @ lp['w_out']
    if last:
        return y, None

    qc = pc_head[..., OFF_Q:OFF_R].reshape(B, C, ATT_HEADS, HEAD_DIM)
    oc_att = context_attention(qc, kc, vc, lp['attn_sink'])
    oc_sg = spatial_gating(pc_head[..., :OFF_Q], lp['sg_ln_g'], lp['sg_ln_b'], lp['sg_w'], lp['sg_b'])
    oc_rw = rwkv_output(yc_f + yc_b, rC, kdC, vC, rg_c[..., RW_W:], lp).astype(hc.dtype)
    yc = jnp.concatenate([oc_sg, oc_att, oc_rw], axis=-1) @ lp['w_out']
    return y, yc


def trunk_layer(x, xc, c_act, cc_act, lp, last):
    D = D_MODEL
    g = lp['norm_g']
    sh1, sc1, gt1, sh2, sc2, gt2 = [m[:, None, :] for m in jnp.split(c_act @ lp['w_mod'] + lp['b_mod'], 6, axis=-1)]
    n_ctx = 2 if last else 6
    mc = jnp.split(cc_act @ lp['w_mod'][:, :n_ctx * D] + lp['b_mod'][:n_ctx * D], n_ctx)
    h = modulate(rms_norm(x, g[0]), sh1, sc1)
    hc = modulate(rms_norm(xc, g[0]), mc[0], mc[1])
    y, yc = mix_tokens(h, hc, lp, last)
    x = x + gt1 * rms_norm(y, g[1])
    h = modulate(rms_norm(x, g[2]), sh2, sc2)
    x = x + gt2 * rms_norm(swiglu(h, lp['w_gu'], lp['w_down']), g[3])
    if last:
        return x, xc
    xc = xc + mc[2] * rms_norm(yc, g[1])
    hc = modulate(rms_norm(xc, g[2]), mc[3], mc[4])
    xc = xc + mc[5] * rms_norm(swiglu(hc, lp['w_gu'], lp['w_down']), g[3])
    return x, xc


def setup_inputs(seed: int = 0) -> dict:
    key = jax.random.key(seed)
    k = jax.random.split(key, 27)
    D, L = D_MODEL, DEPTH
    f32 = jnp.float32
    nrm = lambda i, shape, s: jax.random.normal(k[i], shape, f32) * s
    return {
        'x': nrm(0, (BATCH, SEQ, D), 1.0),
        'c': nrm(1, (BATCH, D), 1.0),
        'ctx': nrm(2, (BATCH, CTX_LEN, D), 1.0),
        'c_ctx': nrm(3, (D,), 1.0),
        'w_mod': nrm(4, (L, D, 6 * D), 0.5 * D ** -0.5),
        'b_mod': nrm(5, (L, 6 * D), 0.01),
        'norm_g': 1.0 + nrm(6, (L, 4, D), 0.05),
        'w_in': nrm(7, (L, D, IN_COLS), D ** -0.5),
        'w_out': nrm(8, (L, MIX_W, D), MIX_W ** -0.5),
        'sg_ln_g': 1.0 + nrm(9, (L, A_W), 0.05),
        'sg_ln_b': nrm(10, (L, A_W), 0.01),
        'sg_w': nrm(11, (L, A_GROUPS, CHUNK, CHUNK), CHUNK ** -0.5),
        'sg_b': 1.0 + nrm(12, (L, A_GROUPS, CHUNK), 0.05),
        'attn_sink': nrm(13, (L, ATT_HEADS), 0.5),
        'rw_mu': jax.random.uniform(k[14], (L, MU_W), f32),
        'rw_w0': -1.0 + nrm(15, (L, N_DIR, RW_W), 0.5),
        'rw_w2': nrm(16, (L, N_DIR, W_LORA, RW_W), 0.5 * W_LORA ** -0.5),
        'rw_a0': nrm(17, (L, N_DIR, RW_W), 0.5),
        'rw_a2': nrm(18, (L, N_DIR, A_LORA, RW_W), 0.5 * A_LORA ** -0.5),
        'rw_kk': 0.85 + nrm(19, (L, RW_W), 0.05),
        'rw_ka': 1.0 + nrm(20, (L, RW_W), 0.05),
        'rw_rk': nrm(21, (L, RW_HEADS, HEAD_DIM), 0.1),
        'rw_g2': nrm(22, (L, G_LORA, RW_W), G_LORA ** -0.5),
        'rw_lnx_g': 1.0 + nrm(23, (L, RW_W), 0.05),
        'rw_lnx_b': nrm(24, (L, RW_W), 0.01),
        'w_gu': nrm(25, (L, D, 2 * D_FF), D ** -0.5),
        'w_down': nrm(26, (L, D_FF, D), D_FF ** -0.5),
    }


def reference(x, c, ctx, c_ctx, w_mod, b_mod, norm_g, w_in, w_out, sg_ln_g, sg_ln_b, sg_w, sg_b,
              attn_sink, rw_mu, rw_w0, rw_w2, rw_a0, rw_a2, rw_kk, rw_ka, rw_rk, rw_g2,
              rw_lnx_g, rw_lnx_b, w_gu, w_down):
    c_act = jax.nn.silu(c)
    cc_act = jax.nn.silu(c_ctx)
    xc = ctx
    for l in range(DEPTH):
        lp = {
            'w_mod': w_mod[l], 'b_mod': b_mod[l], 'norm_g': norm_g[l],
            'w_in': w_in[l], 'w_out': w_out[l],
            'sg_ln_g': sg_ln_g[l], 'sg_ln_b': sg_ln_b[l], 'sg_w': sg_w[l], 'sg_b': sg_b[l],
            'attn_sink': attn_sink[l], 'rw_mu': rw_mu[l],
            'rw_w0': rw_w0[l], 'rw_w2': rw_w2[l], 'rw_a0': rw_a0[l], 'rw_a2': rw_a2[l],
            'rw_kk': rw_kk[l], 'rw_ka': rw_ka[l], 'rw_rk': rw_rk[l], 'rw_g2': rw_g2[l],
            'rw_lnx_g': rw_lnx_g[l], 'rw_lnx_b': rw_lnx_b[l],
            'w_gu': w_gu[l], 'w_down': w_down[l],
        }
        x, xc = trunk_layer(x, xc, c_act, cc_act, lp, l == DEPTH - 1)
    return x
```

```python
import numpy as np
import concourse.bass as bass
import concourse.mybir as mybir
from concourse.bass_utils import run_bass_kernel_spmd
from contextlib import ExitStack

F32 = mybir.dt.float32
AF = mybir.ActivationFunctionType
ALU = mybir.AluOpType

D = 1024; T = 4096; CT = 256; TT = T + CT; L = 4
NTOKB = TT // 128
NCHK = TT // 64
C0 = float(np.exp(-0.5))
TILES = [(0, 256, 1)] + [(256 + 512 * i, 512, 0) for i in range(8)]
NG = 0; BM = 32; MU = 80; KK = 89; KA = 91; RK = 93; LXG = 95; LXB = 97; W0 = 99; A0 = 103; SLG = 107; SLB = 109; NP = 111
SRC_CH = [8, 9, 10, 13, 14, 15, 16, 17, 18]

SAME_ENGINE_SYNC = True
EPOCH = 12000
DMA_EPOCH = 1500
NDMASEM = 6


class Buf:
    __slots__ = ("name", "w", "r")

    def __init__(self, name):
        self.name = name
        self.w = {}
        self.r = {}


class KB:
    def __init__(self, nc):
        self.nc = nc
        self.es = ExitStack()
        self.E = {"pe": nc.tensor, "act": nc.scalar, "dve": nc.vector, "pool": nc.gpsimd, "sp": nc.sync}
        self.sem = {}
        self.cnt = {}
        self.seen = {e: {} for e in self.E}
        self.nsem = 0
        for e in self.E:
            self._new_sem(e)
        self.dsem = {}
        self.dcnt = {}
        self.drr = {}
        for q in ("sp", "pool"):
            self.dsem[q] = [self._alloc_sem(f"d{q}{i}") for i in range(NDMASEM)]
            self.dcnt[q] = [0] * NDMASEM
            self.drr[q] = 0
        self.ninstr = 0
        self.banks = []
        self.bi = 0

    def _alloc_sem(self, name):
        self.nsem += 1
        return self.nc.alloc_semaphore(name=f"{name}_{self.nsem}")

    def _new_sem(self, e):
        self.sem[e] = self._alloc_sem(f"s{e}")
        self.cnt[e] = 0

    def sb(self, name, shape, dtype=F32, stack=None):
        self.nsb = getattr(self, "nsb", 0) + 1
        return (stack or self.es).enter_context(self.nc.sbuf_tensor(f"sb{self.nsb}_{name}", list(shape), dtype))

    def mkbanks(self):
        for i in range(8):
            t = self.es.enter_context(self.nc.psum_tensor(f"bank{i}", [128, 512], F32))
            self.banks.append((t, Buf(f"bank{i}")))

    def bank(self):
        b = self.banks[self.bi]
        self.bi = (self.bi + 1) % 8
        return b

    def _wait(self, e, ev):
        sem, val, src = ev
        if src == e and (not SAME_ENGINE_SYNC or e == "pe"):
            return
        key = id(sem)
        if self.seen[e].get(key, 0) >= val:
            return
        self.E[e].wait_ge(sem, val)
        self.seen[e][key] = val

    def _deps(self, e, reads, writes, adds, isdma=False):
        for b in reads:
            for ev in b.w.values():
                self._wait(e, ev)
        for b in writes:
            for ev in b.w.values():
                self._wait(e, ev)
            for ev in b.r.values():
                self._wait(e, ev)
        for b in adds:
            for ev in b.r.values():
                self._wait(e, ev)
            for ev in b.w.values():
                if not (isdma and ev[2] == "dma"):
                    self._wait(e, ev)

    def _commit(self, key, ev, reads, writes, adds):
        for b in reads:
            b.r[key] = ev
        for b in writes:
            b.w = {key: ev}
            b.r = {}
        for b in adds:
            b.w[key] = ev

    def op(self, e, fn, reads=(), writes=(), adds=()):
        self._deps(e, reads, writes, adds)
        ins = fn(self.E[e])
        if self.cnt[e] >= EPOCH:
            self._new_sem(e)
        self.cnt[e] += 1
        ins.then_inc(self.sem[e], 1)
        ev = (self.sem[e], self.cnt[e], e)
        self._commit(e, ev, reads, writes, adds)
        self.ninstr += 1
        return ev

    def dma(self, out, in_, reads=(), writes=(), adds=(), q="sp"):
        self._deps(q, reads, writes, adds, True)
        j = self.drr[q]
        self.drr[q] = (j + 1) % NDMASEM
        if self.dcnt[q][j] > 0:
            self._wait(q, (self.dsem[q][j], self.dcnt[q][j] * 16, "dma"))
        if self.dcnt[q][j] >= DMA_EPOCH:
            self.dsem[q][j] = self._alloc_sem(f"d{q}{j}")
            self.dcnt[q][j] = 0
        sem = self.dsem[q][j]
        self.E[q].dma_start(out=out, in_=in_).then_inc(sem, 16)
        self.dcnt[q][j] += 1
        ev = (sem, self.dcnt[q][j] * 16, "dma")
        self._commit(id(sem), ev, reads, writes, adds)
        self.ninstr += 1
        return ev

    def barrier(self):
        evs = []
        for e in ("pe", "act", "dve", "pool", "sp"):
            if self.cnt[e] > 0:
                evs.append((self.sem[e], self.cnt[e], e))
        for q in self.dsem:
            for j in range(NDMASEM):
                if self.dcnt[q][j] > 0:
                    evs.append((self.dsem[q][j], self.dcnt[q][j] * 16, "dma"))
        for e in self.E:
            for ev in evs:
                if ev[2] != e:
                    self._wait(e, ev)


PHASES = None


def i_phase_on(i):
    return PHASES is None or i in PHASES


def build(nlayers=L, debug=False):
    nc = bass.Bass("TRN2", target_bir_lowering=False)
    k = KB(nc)
    k.mkbanks()

    def din(name, shape):
        return nc.dram_tensor(name, list(shape), F32, kind="ExternalInput")

    def dscr(name, shape, out=False):
        return nc.dram_tensor(name, list(shape), F32, kind="ExternalOutput" if (out and debug) else "Internal")

    xin = din("xin", [8, 128, TT])
    cin = din("cin", [128, 8, 2])
    w_mod = din("w_mod", [L, D, 6 * D]); w_in = din("w_in", [L, D, 2432]); w_out = din("w_out", [L, D, D])
    w_gu = din("w_gu", [L, D, 5632]); w_down = din("w_down", [L, 2816, D])
    ppd = din("pp", [L, 128, NP])
    sgwT = din("sgwT", [L, 128, 4, 128]); sgb = din("sgb", [L, 1, 512]); sink = din("sink", [L, 1, 8])
    w2d = din("w2", [L, 128, 256]); a2d = din("a2", [L, 128, 256]); g2d = din("g2", [L, 128, 256])
    cc = din("cc", [128, 384]); smL = din("smL", [128, 9, 4]); smC = din("smC", [128, 9, 2])
    cmd = din("cm", [64, 4, 512]); amd = din("am", [128, 2, 512]); rsd = din("rs", [128, 512])
    cosd = din("cosT", [128, T]); sind = din("sinT", [128, T])
    outT = nc.dram_tensor("outT", [8, 128, T], F32, kind="ExternalOutput")

    xs = dscr("xs", [8, 128, TT], True); bxs = Buf("xs")
    pxT = dscr("pxT", [19, 128, TT], True); bpx = Buf("pxT")
    mixT = dscr("mixT", [8, 128, TT], True); bmix = Buf("mixT")
    rmT = dscr("rmT", [9, 128, TT], True); brm = Buf("rmT")
    scin = dscr("scin", [4, 128, 4, TT], True); bscin = Buf("scin")
    gcd = dscr("gcd", [4, 128, NCHK], True); bgcd = Buf("gcd")
    rksT = dscr("rksT", [2, 128, TT], True); brks = Buf("rksT")
    yT = dscr("yT", [2, 256, TT], True); byT = Buf("yT")
    bout = Buf("outT")

    cst = k.sb("cst", [128, 384]); bc = Buf("cst")
    k.dma(cst[:], cc.ap(), writes=[bc])
    ident = cst[:, 0:128]; bones = cst[:, 128:256]; Rm = cst[:, 256:384]
    ones = k.sb("ones", [128, 128]); bo = Buf("ones")
    k.op("dve", lambda e: e.memset(ones[:], 1.0), writes=[bo])
    cact = k.sb("cact", [128, 8, 2]); bca = Buf("cact")
    k.dma(cact[:], cin.ap(), writes=[bca])
    k.op("act", lambda e: e.activation(out=cact[:], in_=cact[:], func=AF.Silu), writes=[bca])
    pp = k.sb("ppt", [128, NP]); bpp = Buf("pp")
    modt = k.sb("modt", [128, 48, 2]); bmod = Buf("modt")
    coef = k.sb("coef", [128, 6, 8, 2]); bcoef = Buf("coef")
    cfs = k.sb("cfs", [128, 9, 7]); bcfs = Buf("cfs")
    smLs = k.sb("smLs", [128, 9, 4]); smCs = k.sb("smCs", [128, 9, 2]); bsm = Buf("sm")
    k.dma(smLs[:], smL.ap(), adds=[bsm]); k.dma(smCs[:], smC.ap(), adds=[bsm])
    omka = k.sb("omka", [128, 2]); bomka = Buf("omka")
    k.barrier()

    def xsrc(l):
        return xin if l == 0 else xs

    def rmsn(st, X, bX, W, SQ, bSQ, rs, brs):
        k.op("act", lambda e: e.activation(out=SQ[:, :, :W], in_=X[:, :, :W], func=AF.Square), reads=[bX], writes=[bSQ])
        bk, bb = k.bank()
        for c in range(8):
            k.op("pe", lambda e, c=c: e.matmul(bk[:, :W], lhsT=ones[:], rhs=SQ[:, c, :W], start=(c == 0), stop=(c == 7)),
                 reads=[bSQ, bo], writes=[bb])
        k.op("dve", lambda e: e.tensor_scalar(out=rs[:, :W], in0=bk[:, :W], scalar1=1.0 / D, scalar2=1e-6, op0=ALU.mult, op1=ALU.add),
             reads=[bb], writes=[brs])
        k.op("act", lambda e: e.activation(out=rs[:, :W], in_=rs[:, :W], func=AF.Sqrt), writes=[brs])
        k.op("dve", lambda e: e.reciprocal(out=rs[:, :W], in_=rs[:, :W]), writes=[brs])

    def modulate(H, bH, X, bX, rs, brs, W, ia, ib, who):
        for c in range(8):
            k.op("dve", lambda e, c=c: e.scalar_tensor_tensor(out=H[:, c, :W], in0=X[:, c, :W], scalar=coef[:, ia, c, who:who + 1],
                                                             in1=rs[:, :W], op0=ALU.mult, op1=ALU.mult),
                 reads=[bX, brs, bcoef], adds=[bH])
            k.op("pool", lambda e, c=c: e.tensor_scalar(out=H[:, c, :W], in0=H[:, c, :W], scalar1=coef[:, ib, c, who:who + 1], scalar2=None,
                                                        op0=ALU.add), reads=[bcoef], writes=[bH])

    def resid_update(X, bX, Y, bY, rs, brs, W, ig, who):
        for j in range(8):
            k.op("dve", lambda e, j=j: e.scalar_tensor_tensor(out=Y[:, j, :W], in0=Y[:, j, :W], scalar=coef[:, ig, j, who:who + 1],
                                                             in1=rs[:, :W], op0=ALU.mult, op1=ALU.mult),
                 reads=[brs, bcoef], writes=[bY])
            k.op("pool", lambda e, j=j: e.tensor_tensor(out=X[:, j, :W], in0=X[:, j, :W], in1=Y[:, j, :W], op=ALU.add),
                 reads=[bY], writes=[bX])

    evflip = [0]

    def evac(out, in_, reads, writes=(), adds=()):
        evflip[0] ^= 1
        if evflip[0]:
            k.op("act", lambda e: e.copy(out=out, in_=in_), reads=reads, writes=writes, adds=adds)
        else:
            k.op("dve", lambda e: e.tensor_copy(out=out, in_=in_), reads=reads, writes=writes, adds=adds)

    for l in range(nlayers):
        last = (l == L - 1)
        with ExitStack() as st:
          if i_phase_on(0):
            k.dma(pp[:], ppd.ap()[l], writes=[bpp])
            wm = [k.sb(f"wm{i}", [128, 8, 1024], stack=st) for i in range(2)]; bwm = [Buf("wm0"), Buf("wm1")]
            for m in range(6):
                k.dma(wm[m % 2][:], w_mod.ap()[l][:, m * 1024:(m + 1) * 1024].rearrange("(kc p) n -> p kc n", p=128), writes=[bwm[m % 2]],
                      q="sp" if m % 2 == 0 else "pool")
                for c in range(8):
                    bk, bb = k.bank()
                    for kc in range(8):
                        k.op("pe", lambda e, kc=kc, c=c, m=m: e.matmul(bk[:, 0:2], lhsT=wm[m % 2][:, kc, c * 128:(c + 1) * 128], rhs=cact[:, kc, :],
                                                                      start=(kc == 0), stop=(kc == 7)), reads=[bwm[m % 2], bca], writes=[bb])
                    j = m * 8 + c
                    k.op("dve", lambda e, j=j: e.tensor_scalar(out=modt[:, j, :], in0=bk[:, 0:2], scalar1=pp[:, BM + j:BM + j + 1], scalar2=None, op0=ALU.add),
                         reads=[bb, bpp], adds=[bmod])
            for who in range(2):
                for (ic, isc, ig) in ((0, 1, 0), (3, 4, 2)):
                    k.op("dve", lambda e, who=who, ic=ic, isc=isc, ig=ig: e.tensor_tensor(out=coef[:, ic, :, who], in0=modt[:, isc * 8:isc * 8 + 8, who],
                                                                                        in1=pp[:, NG + ig * 8:NG + ig * 8 + 8], op=ALU.mult),
                         reads=[bmod, bpp], writes=[bcoef])
                    k.op("dve", lambda e, who=who, ic=ic, ig=ig: e.tensor_tensor(out=coef[:, ic, :, who], in0=coef[:, ic, :, who],
                                                                               in1=pp[:, NG + ig * 8:NG + ig * 8 + 8], op=ALU.add), writes=[bcoef])
                for (ic, ish) in ((1, 0), (4, 3)):
                    k.op("dve", lambda e, who=who, ic=ic, ish=ish: e.tensor_copy(out=coef[:, ic, :, who], in_=modt[:, ish * 8:ish * 8 + 8, who]),
                         reads=[bmod], writes=[bcoef])
                for (ic, igt, ig) in ((2, 2, 1), (5, 5, 3)):
                    k.op("dve", lambda e, who=who, ic=ic, igt=igt, ig=ig: e.tensor_tensor(out=coef[:, ic, :, who], in0=modt[:, igt * 8:igt * 8 + 8, who],
                                                                                        in1=pp[:, NG + ig * 8:NG + ig * 8 + 8], op=ALU.mult),
                         reads=[bmod, bpp], writes=[bcoef])
            k.op("dve", lambda e: e.tensor_scalar(out=cfs[:, :, 0], in0=pp[:, MU:MU + 9], scalar1=-1.0, scalar2=1.0, op0=ALU.mult, op1=ALU.add),
                 reads=[bpp], writes=[bcfs])
            for dd in range(4):
                k.op("dve", lambda e, dd=dd: e.tensor_tensor(out=cfs[:, :, 1 + dd], in0=pp[:, MU:MU + 9], in1=smLs[:, :, dd], op=ALU.mult),
                     reads=[bpp, bsm], writes=[bcfs])
            for dd in range(2):
                k.op("dve", lambda e, dd=dd: e.tensor_tensor(out=cfs[:, :, 5 + dd], in0=pp[:, MU:MU + 9], in1=smCs[:, :, dd], op=ALU.mult),
                     reads=[bpp, bsm], writes=[bcfs])
            k.op("dve", lambda e: e.tensor_scalar(out=omka[:], in0=pp[:, KA:KA + 2], scalar1=-1.0, scalar2=1.0, op0=ALU.mult, op1=ALU.add),
                 reads=[bpp], writes=[bomka])
            k.barrier()

        with ExitStack() as st:
          if i_phase_on(1):
            win = k.sb("win", [128, 8, 2432], stack=st); bwin = Buf("win")
            wv = w_in.ap()[l].rearrange("(kc p) n -> p kc n", p=128)
            for kc in range(8):
                k.dma(win[:, kc, :], wv[:, kc, :], adds=[bwin], q="sp" if kc % 2 == 0 else "pool")
            X = k.sb("X", [128, 8, 512], stack=st); bX = Buf("X")
            SQ = k.sb("SQ", [128, 8, 512], stack=st); bSQ = Buf("SQ")
            H = k.sb("H", [128, 8, 512], stack=st); bH = Buf("H")
            rs = k.sb("rs", [128, 512], stack=st); brs = Buf("rs")
            stg = [k.sb(f"stg{i}", [128, 512], stack=st) for i in range(2)]; bstg = [Buf("s0"), Buf("s1")]
            xv = xsrc(l).ap().rearrange("c p t -> p c t")
            for (t0, W, who) in TILES:
                k.dma(X[:, :, :W], xv[:, :, t0:t0 + W], reads=[bxs], writes=[bX])
                rmsn(st, X, bX, W, SQ, bSQ, rs, brs)
                modulate(H, bH, X, bX, rs, brs, W, 0, 1, who)
                for j in range(19):
                    bk, bb = k.bank()
                    for kc in range(8):
                        k.op("pe", lambda e, kc=kc, j=j: e.matmul(bk[:, :W], lhsT=win[:, kc, j * 128:(j + 1) * 128], rhs=H[:, kc, :W],
                                                                 start=(kc == 0), stop=(kc == 7)), reads=[bwin, bH], writes=[bb])
                    s = stg[j % 2]
                    evac(s[:, :W], bk[:, :W], [bb], writes=[bstg[j % 2]])
                    k.dma(pxT.ap()[j][:, t0:t0 + W], s[:, :W], reads=[bstg[j % 2]], adds=[bpx])
            k.barrier()

        with ExitStack() as st:
          if i_phase_on(2):
            fb = [k.sb(f"fb{i}", [128, TT], stack=st) for i in range(2)]; bfb = [Buf("fb0"), Buf("fb1")]
            ob = [k.sb(f"ob{i}", [128, TT], stack=st) for i in range(2)]; bob = [Buf("ob0"), Buf("ob1")]
            for ci in range(9):
                f = fb[ci % 2]; o = ob[ci % 2]; bf = bfb[ci % 2]; bo_ = bob[ci % 2]
                k.dma(f[:], pxT.ap()[SRC_CH[ci]], reads=[bpx], writes=[bf], q="sp" if ci % 2 == 0 else "pool")
                k.op("act", lambda e, f=f, o=o, ci=ci: e.activation(out=o[:], in_=f[:], func=AF.Identity, scale=cfs[:, ci, 0:1]), reads=[bf, bcfs], writes=[bo_])
                fl = f[:, CT:].rearrange("p (r w) -> p r w", w=64)
                ol = o[:, CT:].rearrange("p (r w) -> p r w", w=64)
                views = [(ol[:, :, 1:64], fl[:, :, 0:63]), (ol[:, :, 0:63], fl[:, :, 1:64]), (ol[:, 1:64, :], fl[:, 0:63, :]), (ol[:, 0:63, :], fl[:, 1:64, :]),
                         (o[:, 1:CT], f[:, 0:CT - 1]), (o[:, 0:CT - 1], f[:, 1:CT])]
                for di, (ov, fv) in enumerate(views):
                    eng = "dve"
                    k.op(eng, lambda e, ov=ov, fv=fv, ci=ci, di=di: e.scalar_tensor_tensor(out=ov, in0=fv, scalar=cfs[:, ci, 1 + di:2 + di], in1=ov,
                                                                                          op0=ALU.mult, op1=ALU.add), reads=[bf, bcfs], writes=[bo_])
                k.dma(rmT.ap()[ci], o[:], reads=[bo_], adds=[brm])
            k.barrier()

        with ExitStack() as st:
          if i_phase_on(3):
            w2 = k.sb("w2s", [128, 256], stack=st); a2 = k.sb("a2s", [128, 256], stack=st); bw2 = Buf("w2")
            k.dma(w2[:], w2d.ap()[l], adds=[bw2]); k.dma(a2[:], a2d.ap()[l], adds=[bw2])
            rsm = k.sb("rsm", [128, 512], stack=st); brsm = Buf("rsm")
            k.dma(rsm[:], rsd.ap(), writes=[brsm])
            rm = k.sb("rm", [128, 9, 512], stack=st); brmt = Buf("rm")
            tw = k.sb("tw", [128, 512], stack=st); btw = Buf("tw")
            kkr = k.sb("kkr", [128, 2, 512], stack=st); bkkr = Buf("kkr")
            sqk = k.sb("sqk", [128, 2, 512], stack=st); bsqk = Buf("sqk")
            kk = k.sb("kk", [128, 2, 512], stack=st); bkk = Buf("kk")
            nr = k.sb("nr", [128, 512], stack=st); bnr = Buf("nr")
            sg = k.sb("sg", [128, 512], stack=st); bsg = Buf("sg")
            aa = k.sb("aa", [128, 512], stack=st); baa = Buf("aa")
            cs = k.sb("cs", [128, 512], stack=st); bcs = Buf("cs")
            Ei = k.sb("Ei", [128, 512], stack=st); bEi = Buf("Ei")
            Ex = k.sb("Ex", [128, 512], stack=st); bEx = Buf("Ex")
            gi = k.sb("gi", [128, 512], stack=st); bgi = Buf("gi")
            gx = k.sb("gx", [128, 512], stack=st); bgx = Buf("gx")
            gv = k.sb("gv", [128, 512], stack=st); bgv = Buf("gv")
            gc = k.sb("gc", [128, 8], stack=st); bgc = Buf("gc")
            kd = k.sb("kd", [128, 512], stack=st); bkd = Buf("kd")
            bt = k.sb("bt", [128, 512], stack=st); bbt = Buf("bt")
            kds = k.sb("kds", [128, 2, 512], stack=st); bkds = Buf("kds")
            so = [k.sb(f"so{i}", [128, 4, 512], stack=st) for i in range(2)]; bso = [Buf("so0"), Buf("so1")]
            it = 0
            for (t0, W, who) in TILES:
                nck = W // 64
                k.dma(rm[:, :, :W], rmT.ap().rearrange("c p t -> p c t")[:, :, t0:t0 + W], reads=[brm], writes=[brmt])
                k.op("act", lambda e: e.activation(out=tw[:, :W], in_=rm[:, 7, :W], func=AF.Tanh), reads=[brmt], writes=[btw])
                for hp in range(2):
                    k.op("dve", lambda e, hp=hp: e.tensor_scalar(out=kkr[:, hp, :W], in0=rm[:, 3 + hp, :W], scalar1=pp[:, KK + hp:KK + hp + 1], scalar2=None, op0=ALU.mult),
                         reads=[brmt, bpp], adds=[bkkr])
                k.op("act", lambda e: e.activation(out=sqk[:, :, :W], in_=kkr[:, :, :W], func=AF.Square), reads=[bkkr], writes=[bsqk])
                for hp in range(2):
                    bk, bb = k.bank()
                    k.op("pe", lambda e, hp=hp: e.matmul(bk[:, :W], lhsT=bones, rhs=sqk[:, hp, :W], start=True, stop=True), reads=[bsqk, bc], writes=[bb])
                    k.op("act", lambda e: e.activation(out=nr[:, :W], in_=bk[:, :W], func=AF.Sqrt), reads=[bb], writes=[bnr])
                    k.op("dve", lambda e: e.tensor_scalar(out=nr[:, :W], in0=nr[:, :W], scalar1=1e-12, scalar2=None, op0=ALU.max), writes=[bnr])
                    k.op("dve", lambda e: e.reciprocal(out=nr[:, :W], in_=nr[:, :W]), writes=[bnr])
                    k.op("dve", lambda e, hp=hp: e.tensor_tensor(out=kk[:, hp, :W], in0=kkr[:, hp, :W], in1=nr[:, :W], op=ALU.mult), reads=[bkkr, bnr], adds=[bkk])
                for hp in range(2):
                    for d in range(2):
                        bk, bb = k.bank()
                        k.op("pe", lambda e, d=d, hp=hp: e.matmul(bk[:, :W], lhsT=w2[d * 64:(d + 1) * 64, hp * 128:(hp + 1) * 128], rhs=tw[d * 64:(d + 1) * 64, :W],
                                                                 start=True, stop=True), reads=[bw2, btw], writes=[bb])
                        k.op("act", lambda e, d=d, hp=hp: e.activation(out=sg[:, :W], in_=bk[:, :W], func=AF.Sigmoid, bias=pp[:, W0 + d * 2 + hp:W0 + d * 2 + hp + 1]),
                             reads=[bb, bpp], writes=[bsg])
                        bk2, bb2 = k.bank()
                        k.op("pe", lambda e, d=d, hp=hp: e.matmul(bk2[:, :W], lhsT=a2[d * 64:(d + 1) * 64, hp * 128:(hp + 1) * 128], rhs=rm[d * 64:(d + 1) * 64, 8, :W],
                                                                 start=True, stop=True), reads=[bw2, brmt], writes=[bb2])
                        k.op("act", lambda e, d=d, hp=hp: e.activation(out=aa[:, :W], in_=bk2[:, :W], func=AF.Sigmoid, bias=pp[:, A0 + d * 2 + hp:A0 + d * 2 + hp + 1]),
                             reads=[bb2, bpp], writes=[baa])
                        k.op("dve", lambda e: e.tensor_tensor_scan(out=cs[:, :W], data0=rsm[:, :W], data1=sg[:, :W], initial=0.0, op0=ALU.mult, op1=ALU.add),
                             reads=[brsm, bsg], writes=[bcs])
                        cs3 = cs[:, :W].rearrange("p (c t) -> p c t", t=64)
                        if d == 0:
                            k.op("pool", lambda e: e.tensor_tensor(out=Ex[:, :W], in0=cs[:, :W], in1=sg[:, :W], op=ALU.subtract), reads=[bcs, bsg], writes=[bEx])
                            Eiv = cs; bEiv = bcs
                        else:
                            Ex3 = Ex[:, :W].rearrange("p (c t) -> p c t", t=64)
                            k.op("pool", lambda e, cs3=cs3, Ex3=Ex3: e.tensor_tensor(out=Ex3, in0=cs3[:, :, 63:64].broadcast_to([128, nck, 64]), in1=cs3, op=ALU.subtract),
                                 reads=[bcs], writes=[bEx])
                            k.op("pool", lambda e: e.tensor_tensor(out=Ei[:, :W], in0=Ex[:, :W], in1=sg[:, :W], op=ALU.add), reads=[bEx, bsg], writes=[bEi])
                            Eiv = Ei; bEiv = bEi
                        k.op("act", lambda e, cs3=cs3: e.activation(out=gc[:, :nck], in_=cs3[:, :, 63], func=AF.Exp, scale=-C0), reads=[bcs], writes=[bgc])
                        k.op("act", lambda e, Eiv=Eiv: e.activation(out=gi[:, :W], in_=Eiv[:, :W], func=AF.Exp, scale=-C0), reads=[bEiv], writes=[bgi])
                        k.op("act", lambda e: e.activation(out=gx[:, :W], in_=Ex[:, :W], func=AF.Exp, scale=-C0), reads=[bEx], writes=[bgx])
                        k.op("act", lambda e, Eiv=Eiv: e.activation(out=gv[:, :W], in_=Eiv[:, :W], func=AF.Exp, scale=C0), reads=[bEiv], writes=[bgv])
                        s_ = so[it % 2]; bs_ = bso[it % 2]; it += 1
                        k.op("dve", lambda e, s_=s_, hp=hp: e.tensor_tensor(out=s_[:, 0, :W], in0=kk[:, hp, :W], in1=gx[:, :W], op=ALU.mult), reads=[bkk, bgx], writes=[bs_])
                        k.op("pool", lambda e, s_=s_, hp=hp: e.tensor_tensor(out=s_[:, 1, :W], in0=rm[:, hp, :W], in1=gi[:, :W], op=ALU.mult), reads=[brmt, bgi], adds=[bs_])
                        k.op("dve", lambda e, hp=hp: e.tensor_scalar(out=kd[:, :W], in0=aa[:, :W], scalar1=pp[:, KA + hp:KA + hp + 1], scalar2=omka[:, hp:hp + 1],
                                                                    op0=ALU.mult, op1=ALU.add), reads=[baa, bpp, bomka], writes=[bkd])
                        k.op("dve", lambda e, hp=hp: e.tensor_tensor(out=kd[:, :W], in0=kd[:, :W], in1=rm[:, 3 + hp, :W], op=ALU.mult), reads=[brmt], writes=[bkd])
                        k.op("dve", lambda e, s_=s_: e.tensor_tensor(out=s_[:, 2, :W], in0=kd[:, :W], in1=gv[:, :W], op=ALU.mult), reads=[bkd, bgv], adds=[bs_])
                        k.op("pool", lambda e, hp=hp: e.tensor_tensor(out=bt[:, :W], in0=aa[:, :W], in1=kk[:, hp, :W], op=ALU.mult), reads=[baa, bkk], writes=[bbt])
                        k.op("pool", lambda e, s_=s_: e.tensor_tensor(out=s_[:, 3, :W], in0=bt[:, :W], in1=gv[:, :W], op=ALU.mult), reads=[bbt, bgv], adds=[bs_])
                        if d == 0:
                            k.op("dve", lambda e, hp=hp: e.tensor_copy(out=kds[:, hp, :W], in_=kd[:, :W]), reads=[bkd], writes=[bkds])
                        else:
                            k.op("dve", lambda e, hp=hp: e.tensor_tensor(out=kds[:, hp, :W], in0=kds[:, hp, :W], in1=kd[:, :W], op=ALU.add), reads=[bkd], writes=[bkds])
                        k.dma(scin.ap()[d * 2 + hp][:, :, t0:t0 + W], s_[:, :, :W], reads=[bs_], adds=[bscin])
                        k.dma(gcd.ap()[d * 2 + hp][:, t0 // 64:t0 // 64 + nck], gc[:, :nck], reads=[bgc], adds=[bgcd])
                    k.op("dve", lambda e, hp=hp: e.scalar_tensor_tensor(out=kds[:, hp, :W], in0=kds[:, hp, :W], scalar=pp[:, RK + hp:RK + hp + 1], in1=rm[:, hp, :W],
                                                                       op0=ALU.mult, op1=ALU.mult), reads=[brmt, bpp], writes=[bkds])
                k.dma(rksT.ap().rearrange("c p t -> p c t")[:, :, t0:t0 + W], kds[:, :, :W], reads=[bkds], adds=[brks])
            k.barrier()

        with ExitStack() as st:
          if i_phase_on(4):
            cm = k.sb("cm", [64, 4, 512], stack=st); bcm = Buf("cm")
            k.dma(cm[:], cmd.ap(), writes=[bcm])
            inT = [k.sb(f"inT{d}", [64, 4, 4, 512], stack=st) for d in range(2)]; binT = [Buf("inT0"), Buf("inT1")]
            gct = [k.sb(f"gct{d}", [64, 4, 8], stack=st) for d in range(2)]; bgct = [Buf("g0"), Buf("g1")]
            vt = [k.sb(f"vt{d}", [64, 4, 512], stack=st) for d in range(2)]; bvt = [Buf("vt0"), Buf("vt1")]
            yst = [k.sb(f"yst{d}", [64, 4, 512], stack=st) for d in range(2)]; byst = [Buf("y0"), Buf("y1")]
            Hs = k.sb("Hs", [64, 8, 64], stack=st); bHs = Buf("Hs")
            k.op("dve", lambda e: e.memset(Hs[:], 0.0), writes=[bHs])
            Ktm = k.sb("Ktm", [64, 512], stack=st); bKtm = Buf("Ktm")
            NBtm = k.sb("NBtm", [64, 512], stack=st); bNB = Buf("NBtm")
            Vtm = k.sb("Vtm", [64, 512], stack=st); bVtm = Buf("Vtm")
            Pm = [k.sb(f"Pm{i}", [64, 512], stack=st) for i in range(2)]; bPm = [Buf("P0"), Buf("P1")]
            Nm = [k.sb(f"Nm{i}", [64, 512], stack=st) for i in range(2)]; bNm = [Buf("N0"), Buf("N1")]
            Am = [k.sb(f"Am{i}", [64, 512], stack=st) for i in range(2)]; bAm = [Buf("A0"), Buf("A1")]
            AKT = k.sb("AKT", [64, 512], stack=st); bAKT = Buf("AKT")
            RKT = k.sb("RKT", [64, 512], stack=st); bRKT = Buf("RKT")
            RBT = k.sb("RBT", [64, 512], stack=st); bRBT = Buf("RBT")
            RHSs = k.sb("RHSs", [64, 512], stack=st); bRHS = Buf("RHS")
            Us = k.sb("Us", [64, 512], stack=st); bUs = Buf("Us")
            Ht = k.sb("Ht", [64, 8, 64], stack=st); bHt = Buf("Ht")
            Fc = [(0, ci) for ci in range(4)] + [(ti, ci) for ti in range(1, 9) for ci in range(8)]
            Bc = [(0, ci) for ci in range(3, -1, -1)] + [(ti, ci) for ti in range(8, 0, -1) for ci in range(7, -1, -1)]
            cur = [None, None]

            def flush(d):
                ti = cur[d]
                t0, W, _ = TILES[ti]
                yv = yT.ap()[d].rearrange("(h v) t -> v h t", v=64)
                k.dma(yv[:, :, t0:t0 + W], yst[d][:, :, :W], reads=[byst[d]], adds=[byT])

            def load(d, ti):
                t0, W, _ = TILES[ti]
                for h in range(4):
                    hp = h // 2; e_ = h % 2
                    k.dma(inT[d][:, h, :, :W], scin.ap()[d * 2 + hp][e_ * 64:(e_ + 1) * 64, :, t0:t0 + W], reads=[bscin],
                          writes=[binT[d]] if h == 0 else (), adds=[binT[d]] if h else ())
                    k.dma(gct[d][:, h, :W // 64], gcd.ap()[d * 2 + hp][e_ * 64:(e_ + 1) * 64, t0 // 64:t0 // 64 + W // 64], reads=[bgcd],
                          writes=[bgct[d]] if h == 0 else (), adds=[bgct[d]] if h else ())
                    k.dma(vt[d][:, h, :W], rmT.ap()[5 + hp][e_ * 64:(e_ + 1) * 64, t0:t0 + W], reads=[brm],
                          writes=[bvt[d]] if h == 0 else (), adds=[bvt[d]] if h else ())

            def opd(d, h, kind, ci):
                return inT[d][:, h, kind, ci * 64:(ci + 1) * 64]

            def blk(t, q):
                return t[:, q * 64:(q + 1) * 64]

            import os as _os
            R3S = int(_os.environ.get('R3S', '99')); R3N = int(_os.environ.get('R3N', str(NCHK)))
            for step in range(R3N):
                tcs = [Fc[step], Bc[step]]
                for d in range(2):
                    if cur[d] != tcs[d][0]:
                        if cur[d] is not None:
                            flush(d)
                        load(d, tcs[d][0])
                        cur[d] = tcs[d][0]
                cis = [tcs[0][1], tcs[1][1]]
                if R3S < 1:
                    continue
                for kind, dst, bdst, neg in ((2, Ktm, bKtm, False), (3, NBtm, bNB, True)):
                    bk, bb = k.bank()
                    for d in range(2):
                        for h in range(4):
                            q = d * 4 + h
                            k.op("pe", lambda e, d=d, h=h, q=q, kind=kind: e.transpose(bk[0:64, q * 64:(q + 1) * 64], opd(d, h, kind, cis[d]), ident[0:64, 0:64]),
                                 reads=[binT[d], bc], writes=[bb])
                    if neg:
                        k.op("act", lambda e, dst=dst: e.activation(out=dst[:], in_=bk[0:64, :], func=AF.Copy, scale=-1.0), reads=[bb], writes=[bdst])
                    else:
                        k.op("dve", lambda e, dst=dst: e.tensor_copy(out=dst[:], in_=bk[0:64, :]), reads=[bb], writes=[bdst])
                bk, bb = k.bank()
                for d in range(2):
                    for h in range(4):
                        q = d * 4 + h
                        k.op("pe", lambda e, d=d, h=h, q=q: e.transpose(bk[0:64, q * 64:(q + 1) * 64], vt[d][:, h, cis[d] * 64:(cis[d] + 1) * 64], ident[0:64, 0:64]),
                             reads=[bvt[d], bc], writes=[bb])
                k.op("dve", lambda e: e.tensor_copy(out=Vtm[:], in_=bk[0:64, :]), reads=[bb], writes=[bVtm])

                if R3S < 2:
                    continue
                def prod(la, ra, dst, bdst, mki, neg, eng):
                    bk, bb = k.bank()
                    for d in range(2):
                        for h in range(4):
                            q = d * 4 + h
                            k.op("pe", lambda e, d=d, h=h, q=q: e.matmul(bk[0:64, q * 64:(q + 1) * 64], lhsT=opd(d, h, la, cis[d]), rhs=opd(d, h, ra, cis[d]),
                                                                        start=True, stop=True), reads=[binT[d]], writes=[bb])
                    k.op(eng, lambda e: e.scalar_tensor_tensor(out=dst[:], in0=bk[0:64, :], scalar=(-1.0 if neg else 1.0), in1=cm[:, mki, :], op0=ALU.mult, op1=ALU.mult),
                         reads=[bb, bcm], writes=[bdst])
                prod(3, 0, Pm[0], bPm[0], 0, True, "dve")
                prod(0, 3, Nm[0], bNm[0], 1, True, "dve")
                prod(2, 0, AKT, bAKT, 0, False, "dve")
                prod(2, 1, RKT, bRKT, 2, False, "dve")
                prod(3, 1, RBT, bRBT, 2, True, "dve")
                if R3S < 3:
                    continue
                k.op("pool", lambda e: e.tensor_tensor(out=Am[0][:], in0=Pm[0][:], in1=cm[:, 3, :], op=ALU.add), reads=[bPm[0], bcm], writes=[bAm[0]])
                pc = 0; ac = 0
                for j in range(1, 6):
                    nn = 1 - pc
                    bk, bb = k.bank()
                    for q in range(8):
                        k.op("pe", lambda e, q=q, pc=pc: e.matmul(blk(bk[0:64, :], q), lhsT=blk(Pm[pc], q), rhs=blk(Nm[pc], q), start=True, stop=True),
                             reads=[bPm[pc], bNm[pc]], writes=[bb])
                    if j < 5:
                        bk2, bb2 = k.bank()
                        for q in range(8):
                            k.op("pe", lambda e, q=q, pc=pc: e.matmul(blk(bk2[0:64, :], q), lhsT=blk(Nm[pc], q), rhs=blk(Pm[pc], q), start=True, stop=True),
                                 reads=[bPm[pc], bNm[pc]], writes=[bb2])
                    k.op("act", lambda e, nn=nn, bk=bk: e.copy(out=Nm[nn][:], in_=bk[0:64, :]), reads=[bb], writes=[bNm[nn]])
                    if j < 5:
                        k.op("dve", lambda e, nn=nn, bk2=bk2: e.tensor_copy(out=Pm[nn][:], in_=bk2[0:64, :]), reads=[bb2], writes=[bPm[nn]])
                    bk3, bb3 = k.bank()
                    for q in range(8):
                        k.op("pe", lambda e, q=q, nn=nn, ac=ac: e.matmul(blk(bk3[0:64, :], q), lhsT=blk(Nm[nn], q), rhs=blk(Am[ac], q), start=True, stop=True),
                             reads=[bNm[nn], bAm[ac]], writes=[bb3])
                    k.op("dve", lambda e, ac=ac, bk3=bk3: e.tensor_tensor(out=Am[1 - ac][:], in0=bk3[0:64, :], in1=Am[ac][:], op=ALU.add), reads=[bb3, bAm[ac]], writes=[bAm[1 - ac]])
                    pc = nn; ac = 1 - ac
                A = Am[ac]; bA = bAm[ac]
                if R3S < 4:
                    continue
                bk, bb = k.bank()
                for d in range(2):
                    for h in range(4):
                        q = d * 4 + h
                        k.op("pe", lambda e, d=d, h=h, q=q: e.matmul(blk(bk[0:64, :], q), lhsT=opd(d, h, 0, cis[d]), rhs=Hs[:, q, :], start=True, stop=False),
                             reads=[binT[d], bHs], writes=[bb])
                        k.op("pe", lambda e, q=q: e.matmul(blk(bk[0:64, :], q), lhsT=blk(AKT, q), rhs=blk(Vtm, q), start=False, stop=True),
                             reads=[bAKT, bVtm], writes=[bb])
                k.op("act", lambda e, bk=bk: e.copy(out=RHSs[:], in_=bk[0:64, :]), reads=[bb], writes=[bRHS])
                bk, bb = k.bank()
                for q in range(8):
                    k.op("pe", lambda e, q=q: e.matmul(blk(bk[0:64, :], q), lhsT=blk(A, q), rhs=blk(RHSs, q), start=True, stop=True), reads=[bA, bRHS], writes=[bb])
                k.op("dve", lambda e, bk=bk: e.tensor_copy(out=Us[:], in_=bk[0:64, :]), reads=[bb], writes=[bUs])
                if R3S < 5:
                    continue
                bk, bb = k.bank()
                for d in range(2):
                    for h in range(4):
                        q = d * 4 + h
                        k.op("pe", lambda e, d=d, h=h, q=q: e.matmul(blk(bk[0:64, :], q), lhsT=Hs[:, q, :], rhs=opd(d, h, 1, cis[d]), start=True, stop=False),
                             reads=[binT[d], bHs], writes=[bb])
                        k.op("pe", lambda e, q=q: e.matmul(blk(bk[0:64, :], q), lhsT=blk(Vtm, q), rhs=blk(RKT, q), start=False, stop=False), reads=[bVtm, bRKT], writes=[bb])
                        k.op("pe", lambda e, q=q: e.matmul(blk(bk[0:64, :], q), lhsT=blk(Us, q), rhs=blk(RBT, q), start=False, stop=True), reads=[bUs, bRBT], writes=[bb])
                for d in range(2):
                    for h in range(4):
                        q = d * 4 + h
                        evac(yst[d][:, h, cis[d] * 64:(cis[d] + 1) * 64], bk[0:64, q * 64:(q + 1) * 64], [bb], adds=[byst[d]])
                bk, bb = k.bank()
                for q in range(8):
                    k.op("pe", lambda e, q=q: e.matmul(blk(bk[0:64, :], q), lhsT=blk(Ktm, q), rhs=blk(Vtm, q), start=True, stop=False), reads=[bKtm, bVtm], writes=[bb])
                    k.op("pe", lambda e, q=q: e.matmul(blk(bk[0:64, :], q), lhsT=blk(NBtm, q), rhs=blk(Us, q), start=False, stop=True), reads=[bNB, bUs], writes=[bb])
                k.op("dve", lambda e, bk=bk: e.tensor_tensor(out=Ht[:].rearrange("p a b -> p (a b)"), in0=bk[0:64, :], in1=Hs[:].rearrange("p a b -> p (a b)"), op=ALU.add),
                     reads=[bb, bHs], writes=[bHt])
                for d in range(2):
                    for h in range(4):
                        q = d * 4 + h
                        k.op("dve", lambda e, d=d, h=h, q=q: e.tensor_scalar(out=Hs[:, q, :], in0=Ht[:, q, :], scalar1=gct[d][:, h, cis[d]:cis[d] + 1], scalar2=None, op0=ALU.mult),
                             reads=[bHt, bgct[d]], writes=[bHs] if q == 0 else (), adds=[bHs] if q else ())
            flush(0); flush(1)
            k.barrier()

        with ExitStack() as st:
          if i_phase_on(5):
            g2 = k.sb("g2s", [128, 256], stack=st); bg2 = Buf("g2")
            k.dma(g2[:], g2d.ap()[l], writes=[bg2])
            yf = k.sb("yf", [128, 2, 512], stack=st); byf = Buf("yf")
            yb = k.sb("yb", [128, 2, 512], stack=st); byb = Buf("yb")
            rmo = k.sb("rmo", [128, 7, 512], stack=st); brmo = Buf("rmo")
            rk_ = k.sb("rks", [128, 2, 512], stack=st); brk_ = Buf("rks")
            sq = k.sb("sq4", [128, 512], stack=st); bsq = Buf("sq4")
            mu = k.sb("mu4", [128, 512], stack=st); bmu = Buf("mu4")
            va = k.sb("va4", [128, 512], stack=st); bva = Buf("va4")
            sgl = k.sb("sgl", [128, 512], stack=st); bsgl = Buf("sgl")
            orw = k.sb("orw", [128, 2, 512], stack=st); borw = Buf("orw")
            bn = k.sb("bn4", [128, 512], stack=st); bbn = Buf("bn4")
            for (t0, W, who) in TILES:
                if last and who == 1:
                    continue
                yv = yT.ap().rearrange("d (hp p) t -> d p hp t", p=128)
                k.dma(yf[:, :, :W], yv[0][:, :, t0:t0 + W], reads=[byT], writes=[byf])
                k.dma(yb[:, :, :W], yv[1][:, :, t0:t0 + W], reads=[byT], writes=[byb])
                k.dma(rmo[:, :, :W], rmT.ap().rearrange("c p t -> p c t")[:, 0:7, t0:t0 + W], reads=[brm], writes=[brmo])
                k.dma(rk_[:, :, :W], rksT.ap().rearrange("c p t -> p c t")[:, :, t0:t0 + W], reads=[brks], writes=[brk_])
                k.op("dve", lambda e: e.tensor_tensor(out=yf[:, :, :W], in0=yf[:, :, :W], in1=yb[:, :, :W], op=ALU.add), reads=[byb], writes=[byf])
                k.op("act", lambda e: e.activation(out=sgl[:, :W], in_=rmo[:, 2, :W], func=AF.Sigmoid), reads=[brmo], writes=[bsgl])
                for hp in range(2):
                    bk1, bb1 = k.bank()
                    k.op("pe", lambda e, hp=hp: e.matmul(bk1[:, :W], lhsT=bones, rhs=yf[:, hp, :W], start=True, stop=True), reads=[byf, bc], writes=[bb1])
                    k.op("act", lambda e, hp=hp: e.activation(out=sq[:, :W], in_=yf[:, hp, :W], func=AF.Square), reads=[byf], writes=[bsq])
                    bk2, bb2 = k.bank()
                    k.op("pe", lambda e: e.matmul(bk2[:, :W], lhsT=bones, rhs=sq[:, :W], start=True, stop=True), reads=[bsq, bc], writes=[bb2])
                    k.op("dve", lambda e: e.tensor_scalar(out=mu[:, :W], in0=bk1[:, :W], scalar1=1.0 / 64, scalar2=None, op0=ALU.mult), reads=[bb1], writes=[bmu])
                    k.op("pool", lambda e: e.tensor_tensor(out=va[:, :W], in0=mu[:, :W], in1=mu[:, :W], op=ALU.mult), reads=[bmu], writes=[bva])
                    k.op("dve", lambda e: e.scalar_tensor_tensor(out=va[:, :W], in0=bk2[:, :W], scalar=1.0 / 64, in1=va[:, :W], op0=ALU.mult, op1=ALU.subtract),
                         reads=[bb2], writes=[bva])
                    k.op("dve", lambda e: e.tensor_scalar(out=va[:, :W], in0=va[:, :W], scalar1=64e-5, scalar2=None, op0=ALU.add), writes=[bva])
                    k.op("act", lambda e: e.activation(out=va[:, :W], in_=va[:, :W], func=AF.Sqrt), writes=[bva])
                    k.op("dve", lambda e: e.reciprocal(out=va[:, :W], in_=va[:, :W]), writes=[bva])
                    k.op("dve", lambda e, hp=hp: e.tensor_tensor(out=yf[:, hp, :W], in0=yf[:, hp, :W], in1=mu[:, :W], op=ALU.subtract), reads=[bmu], writes=[byf])
                    k.op("dve", lambda e, hp=hp: e.tensor_tensor(out=yf[:, hp, :W], in0=yf[:, hp, :W], in1=va[:, :W], op=ALU.mult), reads=[bva], writes=[byf])
                    k.op("dve", lambda e, hp=hp: e.tensor_scalar(out=yf[:, hp, :W], in0=yf[:, hp, :W], scalar1=pp[:, LXG + hp:LXG + hp + 1], scalar2=pp[:, LXB + hp:LXB + hp + 1],
                                                                op0=ALU.mult, op1=ALU.add), reads=[bpp], writes=[byf])
                    bk3, bb3 = k.bank()
                    k.op("pe", lambda e, hp=hp: e.matmul(bk3[:, :W], lhsT=bones, rhs=rk_[:, hp, :W], start=True, stop=True), reads=[brk_, bc], writes=[bb3])
                    k.op("dve", lambda e, hp=hp: e.tensor_tensor(out=bn[:, :W], in0=bk3[:, :W], in1=rmo[:, 5 + hp, :W], op=ALU.mult), reads=[bb3, brmo], writes=[bbn])
                    k.op("dve", lambda e, hp=hp: e.tensor_tensor(out=bn[:, :W], in0=bn[:, :W], in1=yf[:, hp, :W], op=ALU.add), reads=[byf], writes=[bbn])
                    bk4, bb4 = k.bank()
                    k.op("pe", lambda e, hp=hp: e.matmul(bk4[:, :W], lhsT=g2[:, hp * 128:(hp + 1) * 128], rhs=sgl[:, :W], start=True, stop=True), reads=[bg2, bsgl], writes=[bb4])
                    k.op("dve", lambda e, hp=hp: e.tensor_tensor(out=orw[:, hp, :W], in0=bk4[:, :W], in1=bn[:, :W], op=ALU.mult), reads=[bb4, bbn], writes=[borw])
                k.dma(mixT.ap().rearrange("c p t -> p c t")[:, 6:8, t0:t0 + W], orw[:, :, :W], reads=[borw], adds=[bmix])
            k.barrier()

        with ExitStack() as st:
          if i_phase_on(6):
            wsT = k.sb("wsT", [128, 4, 128], stack=st); bws = Buf("wsT")
            k.dma(wsT[:], sgwT.ap()[l], writes=[bws])
            bsb = k.sb("bsb", [128, 512], stack=st); bbsb = Buf("bsb")
            k.dma(bsb[:], sgb.ap()[l].partition_broadcast(128), writes=[bbsb])
            uv = k.sb("uv", [128, 4, 512], stack=st); buv = Buf("uv")
            sq = k.sb("sqg", [128, 2, 512], stack=st); bsq = Buf("sqg")
            mn = k.sb("mng", [128, 512], stack=st); bmn = Buf("mng")
            va = k.sb("vag", [128, 512], stack=st); bva = Buf("vag")
            vtm = k.sb("vtmg", [128, 256], stack=st); bvtm = Buf("vtmg")
            tmp = k.sb("tmpg", [128, 128], stack=st); btmp = Buf("tmpg")
            osg = k.sb("osg", [128, 2, 512], stack=st); bosg = Buf("osg")
            for (t0, W, who) in TILES:
                if last and who == 1:
                    continue
                k.dma(uv[:, :, :W], pxT.ap().rearrange("c p t -> p c t")[:, 0:4, t0:t0 + W], reads=[bpx], writes=[buv])
                k.op("act", lambda e: e.activation(out=uv[:, :, :W], in_=uv[:, :, :W], func=AF.Gelu), writes=[buv])
                k.op("act", lambda e: e.activation(out=sq[:, :, :W], in_=uv[:, 2:4, :W], func=AF.Square), reads=[buv], writes=[bsq])
                bk1, bb1 = k.bank(); bk2, bb2 = k.bank()
                for c in range(2):
                    k.op("pe", lambda e, c=c: e.matmul(bk1[:, :W], lhsT=ones[:], rhs=uv[:, 2 + c, :W], start=(c == 0), stop=(c == 1)), reads=[buv, bo], writes=[bb1])
                for c in range(2):
                    k.op("pe", lambda e, c=c: e.matmul(bk2[:, :W], lhsT=ones[:], rhs=sq[:, c, :W], start=(c == 0), stop=(c == 1)), reads=[bsq, bo], writes=[bb2])
                k.op("dve", lambda e: e.tensor_scalar(out=mn[:, :W], in0=bk1[:, :W], scalar1=1.0 / 256, scalar2=None, op0=ALU.mult), reads=[bb1], writes=[bmn])
                k.op("pool", lambda e: e.tensor_tensor(out=va[:, :W], in0=mn[:, :W], in1=mn[:, :W], op=ALU.mult), reads=[bmn], writes=[bva])
                k.op("dve", lambda e: e.scalar_tensor_tensor(out=va[:, :W], in0=bk2[:, :W], scalar=1.0 / 256, in1=va[:, :W], op0=ALU.mult, op1=ALU.subtract), reads=[bb2], writes=[bva])
                k.op("dve", lambda e: e.tensor_scalar(out=va[:, :W], in0=va[:, :W], scalar1=1e-5, scalar2=None, op0=ALU.add), writes=[bva])
                k.op("act", lambda e: e.activation(out=va[:, :W], in_=va[:, :W], func=AF.Sqrt), writes=[bva])
                k.op("dve", lambda e: e.reciprocal(out=va[:, :W], in_=va[:, :W]), writes=[bva])
                for c in range(2):
                    k.op("dve", lambda e, c=c: e.tensor_tensor(out=uv[:, 2 + c, :W], in0=uv[:, 2 + c, :W], in1=mn[:, :W], op=ALU.subtract), reads=[bmn], writes=[buv])
                    k.op("dve", lambda e, c=c: e.tensor_tensor(out=uv[:, 2 + c, :W], in0=uv[:, 2 + c, :W], in1=va[:, :W], op=ALU.mult), reads=[bva], writes=[buv])
                    k.op("dve", lambda e, c=c: e.tensor_scalar(out=uv[:, 2 + c, :W], in0=uv[:, 2 + c, :W], scalar1=pp[:, SLG + c:SLG + c + 1], scalar2=pp[:, SLB + c:SLB + c + 1],
                                                              op0=ALU.mult, op1=ALU.add), reads=[bpp], writes=[buv])
                for nb in range(W // 128):
                    cols = slice(nb * 128, (nb + 1) * 128)
                    bkT, bbT = k.bank()
                    for c in range(2):
                        k.op("pe", lambda e, c=c, cols=cols: e.transpose(bkT[:, c * 128:(c + 1) * 128], uv[:, 2 + c, cols], ident), reads=[buv, bc], writes=[bbT])
                    k.op("act", lambda e: e.copy(out=vtm[:], in_=bkT[:, 0:256]), reads=[bbT], writes=[bvtm])
                    bkM, bbM = k.bank()
                    for g in range(4):
                        k.op("pe", lambda e, g=g: e.matmul(bkM[:, g * 128:(g + 1) * 128], lhsT=vtm[:, (g // 2) * 128:(g // 2 + 1) * 128], rhs=wsT[:, g, :], start=True, stop=True),
                             reads=[bvtm, bws], writes=[bbM])
                    for g in range(4):
                        ps_ = slice((g % 2) * 64, (g % 2 + 1) * 64)
                        k.op("dve", lambda e, g=g, ps_=ps_: e.tensor_tensor(out=tmp[ps_, :], in0=bkM[ps_, g * 128:(g + 1) * 128], in1=bsb[ps_, g * 128:(g + 1) * 128], op=ALU.add),
                             reads=[bbM, bbsb], writes=[btmp])
                        k.op("dve", lambda e, g=g, ps_=ps_, cols=cols: e.tensor_tensor(out=osg[ps_, g // 2, cols], in0=tmp[ps_, :], in1=uv[ps_, g // 2, cols], op=ALU.mult),
                             reads=[btmp, buv], writes=[bosg])
                k.dma(mixT.ap().rearrange("c p t -> p c t")[:, 0:2, t0:t0 + W], osg[:, :, :W], reads=[bosg], adds=[bmix])
            k.barrier()

        with ExitStack() as st:
          if i_phase_on(7):
            cosT = k.sb("cosT", [128, T], stack=st); sinT = k.sb("sinT", [128, T], stack=st); btab = Buf("tab")
            k.dma(cosT[:], cosd.ap(), adds=[btab]); k.dma(sinT[:], sind.ap(), adds=[btab], q="pool")
            am = k.sb("am", [128, 2, 512], stack=st); bam = Buf("am")
            k.dma(am[:], amd.ap(), writes=[bam])
            esb = k.sb("esb", [64, 8], stack=st); besb = Buf("esb")
            k.dma(esb[:], sink.ap()[l].partition_broadcast(64), writes=[besb])
            k.op("act", lambda e: e.activation(out=esb[:], in_=esb[:], func=AF.Exp), writes=[besb])
            KT2 = k.sb("KT2", [128, 2, TT], stack=st); bKT = Buf("KT2")
            Vt = k.sb("Vt", [128, NTOKB, 128], stack=st); bVt = Buf("Vt")
            kraw = k.sb("kraw", [128, TT], stack=st); bkr = Buf("kraw")
            t1 = k.sb("t1", [128, 512], stack=st); bt1 = Buf("t1")
            k.dma(kraw[:], pxT.ap()[11], reads=[bpx], writes=[bkr])

            def rope(dst, src, bsrc, tl, W, wr_reads, bdst):
                bk, bb = k.bank()
                k.op("pe", lambda e: e.matmul(bk[:, :W], lhsT=Rm, rhs=src, start=True, stop=True), reads=[bsrc, bc], writes=[bb])
                k.op("dve", lambda e: e.tensor_tensor(out=t1[:, :W], in0=bk[:, :W], in1=sinT[:, tl:tl + W], op=ALU.mult), reads=[bb, btab], writes=[bt1])
                k.op("pool", lambda e: e.tensor_tensor(out=dst, in0=src, in1=cosT[:, tl:tl + W], op=ALU.mult), reads=[bsrc, btab], writes=[bdst])
                k.op("dve", lambda e: e.tensor_tensor(out=dst, in0=dst, in1=t1[:, :W], op=ALU.add), reads=[bt1], writes=[bdst])

            for i in range(8):
                sl = slice(CT + i * 512, CT + (i + 1) * 512)
                rope(kraw[:, sl], kraw[:, sl], bkr, i * 512, 512, [], bkr)
            for hk in range(2):
                ps_ = slice(hk * 64, (hk + 1) * 64)
                for e_ in range(2):
                    po = slice(e_ * 64, (e_ + 1) * 64)
                    k.op("dve" if e_ == 0 else "pool", lambda e, hk=hk, ps_=ps_, po=po: e.tensor_copy(out=KT2[po, hk, :], in_=kraw[ps_, :]), reads=[bkr], adds=[bKT])
            vraw = kraw
            k.dma(vraw[:], pxT.ap()[12], reads=[bpx], writes=[bkr])
            for b4 in range(0, NTOKB, 4):
                nb_ = min(4, NTOKB - b4)
                bk, bb = k.bank()
                for i in range(nb_):
                    k.op("pe", lambda e, i=i, b4=b4: e.transpose(bk[:, i * 128:(i + 1) * 128], vraw[:, (b4 + i) * 128:(b4 + i + 1) * 128], ident), reads=[bkr, bc], writes=[bb])
                evac(Vt[:, b4:b4 + nb_, :].rearrange("p a b -> p (a b)"), bk[:, :nb_ * 128], [bb], adds=[bVt])
            qraw = k.sb("qraw", [128, 4, 512], stack=st); bq = Buf("qraw")
            qr = k.sb("qr", [128, 4, 512], stack=st); bqr = Buf("qr")
            PT = [k.sb(f"PT{i}", [128, 512], stack=st) for i in range(2)]; bPT = [Buf("PT0"), Buf("PT1")]
            rc = k.sb("rc", [64, 512], stack=st); brc = Buf("rc")
            obt = k.sb("obt", [64, 2, 4, 512], stack=st); bobt = Buf("obt")
            pti = 0; sbi = [0]
            for (t0, W, who) in TILES:
                if last and who == 1:
                    continue
                k.dma(qraw[:, :, :W], pxT.ap().rearrange("c p t -> p c t")[:, 4:8, t0:t0 + W], reads=[bpx], writes=[bq])
                if who == 0:
                    for c in range(4):
                        rope(qr[:, c, :W], qraw[:, c, :W], bq, t0 - CT, W, [], bqr)
                    Q = qr; bQ = bqr
                else:
                    Q = qraw; bQ = bq
                for nb in range(W // 128):
                    n = t0 // 128 + nb
                    cols = slice(nb * 128, (nb + 1) * 128)
                    if who == 1:
                        kbs = [(0, None), (1, None)]
                    else:
                        kbs = [(0, None), (1, None)]
                        if n - 1 >= 2:
                            kbs.append((n - 1, 0))
                        kbs.append((n, None))
                        if n + 1 < NTOKB:
                            kbs.append((n + 1, 1))
                    GO = [0, 2, 1, 3]
                    for hk in range(2):
                        bkO, bbO = k.banks[(hk % 2) * 2]; bkD, bbD = k.banks[(hk % 2) * 2 + 1]
                        for idx, (kb, mk) in enumerate(kbs):
                            bkS = [None, None]; bbS = [None, None]
                            bkS[0], bbS[0] = k.banks[4 + (sbi[0] % 2) * 2]; bkS[1], bbS[1] = k.banks[5 + (sbi[0] % 2) * 2]; sbi[0] += 1
                            for pos in range(4):
                                g = GO[pos]; h = hk * 4 + g; e_ = h % 2; chn = h // 2
                                assert e_ == pos // 2
                                k.op("pe", lambda e, pos=pos, e_=e_, chn=chn, kb=kb, hk=hk: e.matmul(bkS[e_][:, (pos % 2) * 128:(pos % 2 + 1) * 128],
                                                                                                    lhsT=KT2[e_ * 64:(e_ + 1) * 64, hk, kb * 128:(kb + 1) * 128],
                                                                                                    rhs=Q[e_ * 64:(e_ + 1) * 64, chn, cols], start=True, stop=True),
                                     reads=[bKT, bQ], writes=[bbS[e_]])
                            P_ = PT[pti % 2]; bP_ = bPT[pti % 2]; pti += 1
                            k.op("act", lambda e, P_=P_, b0=bkS[0]: e.activation(out=P_[:, 0:256], in_=b0[:, 0:256], func=AF.Exp, scale=0.125), reads=[bbS[0]], writes=[bP_])
                            k.op("act", lambda e, P_=P_, b1=bkS[1]: e.activation(out=P_[:, 256:512], in_=b1[:, 0:256], func=AF.Exp, scale=0.125), reads=[bbS[1]], adds=[bP_])
                            if mk is not None:
                                k.op("pool", lambda e, P_=P_, mk=mk: e.tensor_tensor(out=P_[:], in0=P_[:], in1=am[:, mk, :], op=ALU.mult), reads=[bam], writes=[bP_])
                            k.op("pe", lambda e, P_=P_, kb=kb, hk=hk, idx=idx: e.matmul(bkO[0:64, :], lhsT=Vt[:, kb, hk * 64:(hk + 1) * 64], rhs=P_[:], start=(idx == 0), stop=(idx == len(kbs) - 1)),
                                 reads=[bVt, bP_], writes=[bbO])
                            k.op("pe", lambda e, P_=P_, idx=idx: e.matmul(bkD[0:64, :], lhsT=ones[:, 0:64], rhs=P_[:], start=(idx == 0), stop=(idx == len(kbs) - 1)), reads=[bo, bP_], writes=[bbD])
                        for pos in range(4):
                            h = hk * 4 + GO[pos]
                            k.op("dve", lambda e, pos=pos, h=h, bkD=bkD: e.tensor_scalar(out=rc[:, pos * 128:(pos + 1) * 128], in0=bkD[0:64, pos * 128:(pos + 1) * 128],
                                                                                       scalar1=esb[:, h:h + 1], scalar2=None, op0=ALU.add),
                                 reads=[bbD, besb], writes=[brc] if pos == 0 else (), adds=[brc] if pos else ())
                        k.op("dve", lambda e: e.reciprocal(out=rc[:], in_=rc[:]), writes=[brc])
                        for pos in range(4):
                            g = GO[pos]
                            k.op("dve", lambda e, pos=pos, g=g, bkO=bkO, hk=hk, cols=cols: e.tensor_tensor(out=obt[:, hk, g, cols], in0=bkO[0:64, pos * 128:(pos + 1) * 128],
                                                                                                      in1=rc[:, pos * 128:(pos + 1) * 128], op=ALU.mult), reads=[bbO, brc], adds=[bobt])
                for hk in range(2):
                    for g in range(4):
                        h = hk * 4 + g
                        k.dma(mixT.ap()[2 + h // 2][(h % 2) * 64:(h % 2 + 1) * 64, t0:t0 + W], obt[:, hk, g, :W], reads=[bobt], adds=[bmix], q="sp" if g % 2 == 0 else "pool")
                k.op("dve", lambda e: e.memset(rc[:, 0:1], 0.0), reads=[], writes=[bobt, brc])
            k.barrier()

        with ExitStack() as st:
          if i_phase_on(8):
            wo = k.sb("wo", [128, 8, 1024], stack=st); bwo = Buf("wo")
            wv = w_out.ap()[l].rearrange("(kc p) n -> p kc n", p=128)
            for kc in range(8):
                k.dma(wo[:, kc, :], wv[:, kc, :], adds=[bwo], q="sp" if kc % 2 == 0 else "pool")
            X = k.sb("X5", [128, 8, 512], stack=st); bX = Buf("X5")
            M = k.sb("M5", [128, 8, 512], stack=st); bM = Buf("M5")
            Y = k.sb("Y5", [128, 8, 512], stack=st); bY = Buf("Y5")
            SQ = k.sb("SQ5", [128, 8, 512], stack=st); bSQ = Buf("SQ5")
            rs = k.sb("rs5", [128, 512], stack=st); brs = Buf("rs5")
            xv = xsrc(l).ap().rearrange("c p t -> p c t")
            xo = xs.ap().rearrange("c p t -> p c t")
            for (t0, W, who) in TILES:
                if last and who == 1:
                    continue
                k.dma(X[:, :, :W], xv[:, :, t0:t0 + W], reads=[bxs], writes=[bX])
                k.dma(M[:, :, :W], mixT.ap().rearrange("c p t -> p c t")[:, :, t0:t0 + W], reads=[bmix], writes=[bM], q="pool")
                for j in range(8):
                    bk, bb = k.bank()
                    for kc in range(8):
                        k.op("pe", lambda e, kc=kc, j=j: e.matmul(bk[:, :W], lhsT=wo[:, kc, j * 128:(j + 1) * 128], rhs=M[:, kc, :W], start=(kc == 0), stop=(kc == 7)),
                             reads=[bwo, bM], writes=[bb])
                    evac(Y[:, j, :W], bk[:, :W], [bb], adds=[bY])
                rmsn(st, Y, bY, W, SQ, bSQ, rs, brs)
                resid_update(X, bX, Y, bY, rs, brs, W, 2, who)
                k.dma(xo[:, :, t0:t0 + W], X[:, :, :W], reads=[bX], adds=[bxs])
            k.barrier()

        with ExitStack() as st:
          if i_phase_on(9):
            X = k.sb("X6", [128, 8, 512], stack=st); bX = Buf("X6")
            H = k.sb("H6", [128, 8, 512], stack=st); bH = Buf("H6")
            SQ = k.sb("SQ6", [128, 8, 512], stack=st); bSQ = Buf("SQ6")
            rs = k.sb("rs6", [128, 512], stack=st); brs = Buf("rs6")
            actb = k.sb("actb", [128, 22, 512], stack=st); bact = Buf("actb")
            wg = [k.sb(f"wg{i}", [128, 8, 512], stack=st) for i in range(2)]; bwg = [Buf("wg0"), Buf("wg1")]
            wd = [k.sb(f"wd{i}", [128, 22, 128], stack=st) for i in range(2)]; bwd = [Buf("wd0"), Buf("wd1")]
            sl = k.sb("sl6", [128, 512], stack=st); bsl = Buf("sl6")
            xo = xs.ap().rearrange("c p t -> p c t")
            gv_ = w_gu.ap()[l].rearrange("(kc p) n -> p kc n", p=128)
            dv_ = w_down.ap()[l].rearrange("(j p) n -> p j n", p=128)
            wi = 0; di = 0
            for (t0, W, who) in TILES:
                if last and who == 1:
                    continue
                k.dma(X[:, :, :W], xo[:, :, t0:t0 + W], reads=[bxs], writes=[bX])
                rmsn(st, X, bX, W, SQ, bSQ, rs, brs)
                modulate(H, bH, X, bX, rs, brs, W, 3, 4, who)
                for bi in range(11):
                    w_ = wg[wi % 2]; bw_ = bwg[wi % 2]; wi += 1
                    k.dma(w_[:, :, 0:256], gv_[:, :, bi * 256:(bi + 1) * 256], adds=[bw_], q="sp")
                    k.dma(w_[:, :, 256:512], gv_[:, :, 2816 + bi * 256:2816 + (bi + 1) * 256], adds=[bw_], q="pool")
                    for s in range(2):
                        j = bi * 2 + s
                        bkg, bbg = k.bank(); bku, bbu = k.bank()
                        for kc in range(8):
                            k.op("pe", lambda e, kc=kc, s=s, w_=w_: e.matmul(bkg[:, :W], lhsT=w_[:, kc, s * 128:(s + 1) * 128], rhs=H[:, kc, :W], start=(kc == 0), stop=(kc == 7)),
                                 reads=[bw_, bH], writes=[bbg])
                        for kc in range(8):
                            k.op("pe", lambda e, kc=kc, s=s, w_=w_: e.matmul(bku[:, :W], lhsT=w_[:, kc, 256 + s * 128:256 + (s + 1) * 128], rhs=H[:, kc, :W], start=(kc == 0), stop=(kc == 7)),
                                 reads=[bw_, bH], writes=[bbu])
                        k.op("act", lambda e, bkg=bkg: e.activation(out=sl[:, :W], in_=bkg[:, :W], func=AF.Silu), reads=[bbg], writes=[bsl])
                        k.op("dve", lambda e, j=j, bku=bku: e.tensor_tensor(out=actb[:, j, :W], in0=bku[:, :W], in1=sl[:, :W], op=ALU.mult), reads=[bbu, bsl], adds=[bact])
                Y = SQ; bY = bSQ
                for jo in range(8):
                    d_ = wd[di % 2]; bd_ = bwd[di % 2]; di += 1
                    k.dma(d_[:], dv_[:, :, jo * 128:(jo + 1) * 128], writes=[bd_], q="sp" if jo % 2 == 0 else "pool")
                    bk, bb = k.bank()
                    for j in range(22):
                        k.op("pe", lambda e, j=j, d_=d_: e.matmul(bk[:, :W], lhsT=d_[:, j, :], rhs=actb[:, j, :W], start=(j == 0), stop=(j == 21)), reads=[bd_, bact], writes=[bb])
                    evac(Y[:, jo, :W], bk[:, :W], [bb], writes=[bY] if jo == 0 else (), adds=[bY] if jo else ())
                rmsn(st, Y, bY, W, H, bH, rs, brs)
                resid_update(X, bX, Y, bY, rs, brs, W, 5, who)
                if l == L - 1:
                    k.dma(outT.ap().rearrange("c p t -> p c t")[:, :, t0 - CT:t0 - CT + W], X[:, :, :W], reads=[bX], adds=[bout])
                else:
                    k.dma(xo[:, :, t0:t0 + W], X[:, :, :W], reads=[bX], adds=[bxs])
                k.op("dve", lambda e: e.memset(sl[:, 0:1], 0.0), writes=[bact, bsl])
            k.barrier()
    k.barrier()
    return nc, k


def host_consts():
    ident = np.eye(128, dtype=np.float32)
    bones = np.zeros((128, 128), np.float32); bones[:64, :64] = 1; bones[64:, 64:] = 1
    Rm = np.zeros((128, 128), np.float32)
    for m in range(128):
        i = m % 32
        if i < 16:
            Rm[m + 16, m] = -1.0
        else:
            Rm[m - 16, m] = 1.0
    cc = np.concatenate([ident, bones, Rm], axis=1)
    smL = np.zeros((128, 9, 4), np.float32); smC = np.zeros((128, 9, 2), np.float32)
    for ci in range(9):
        for p in range(128):
            if ci < 3:
                ch = ci * 128 + p; qd = ch // 96; half = 0 if ch < 192 else 1
            else:
                ch = (ci - 3) * 128 + p; qd = ch // 192; half = 0 if ch < 384 else 1
            smL[p, ci, qd] = 1; smC[p, ci, half] = 1
    s = np.arange(64)[:, None]; t = np.arange(64)[None, :]
    mS = [(s < t), (s > t)]; mST = [(s > t), (s < t)]; mI = [(s <= t), (s >= t)]
    cm = np.zeros((64, 4, 512), np.float32)
    for d in range(2):
        for h in range(4):
            q = d * 4 + h
            cm[:, 0, q * 64:(q + 1) * 64] = mS[d]
            cm[:, 1, q * 64:(q + 1) * 64] = mST[d]
            cm[:, 2, q * 64:(q + 1) * 64] = mI[d]
            cm[:, 3, q * 64:(q + 1) * 64] = np.eye(64)
    j = np.arange(128)[:, None]; i = np.arange(128)[None, :]
    am = np.zeros((128, 2, 512), np.float32)
    for g in range(4):
        am[:, 0, g * 128:(g + 1) * 128] = (j >= i)
        am[:, 1, g * 128:(g + 1) * 128] = (j <= i)
    rs = np.ones((128, 512), np.float32); rs[:, ::64] = 0
    tt = np.arange(T)
    pos = np.stack([tt // 64, tt % 64], -1).astype(np.float32)
    inv = (10000.0 ** (-np.arange(16, dtype=np.float32) / 16)).astype(np.float32)
    ang = pos[:, :, None] * inv
    cosT = np.zeros((128, T), np.float32); sinT = np.zeros((128, T), np.float32)
    for p in range(128):
        dd = p % 64; ax = dd // 32; fr = dd % 16
        cosT[p] = np.cos(ang[:, ax, fr]); sinT[p] = np.sin(ang[:, ax, fr])
    return dict(cc=cc, smL=smL, smC=smC, cm=cm, am=am, rs=rs, cosT=cosT, sinT=sinT)


def cmaj(v, n):
    return np.ascontiguousarray(np.asarray(v, np.float32).reshape(n, 128).T)


def host_inputs(inp, b):
    f = lambda a: np.ascontiguousarray(np.asarray(a, np.float32))
    xall = np.concatenate([inp["ctx"][b], inp["x"][b]], axis=0)
    xin = np.ascontiguousarray(xall.T.reshape(8, 128, TT))
    cin = np.stack([cmaj(inp["c"][b], 8), cmaj(inp["c_ctx"], 8)], axis=-1)
    pp = np.zeros((L, 128, NP), np.float32)
    for l in range(L):
        for i in range(4):
            pp[l, :, NG + i * 8:NG + i * 8 + 8] = cmaj(inp["norm_g"][l, i], 8)
        pp[l, :, BM:BM + 48] = cmaj(inp["b_mod"][l], 48)
        pp[l, :, MU:MU + 9] = cmaj(inp["rw_mu"][l], 9)
        pp[l, :, KK:KK + 2] = cmaj(inp["rw_kk"][l], 2)
        pp[l, :, KA:KA + 2] = cmaj(inp["rw_ka"][l], 2)
        pp[l, :, RK:RK + 2] = cmaj(inp["rw_rk"][l].reshape(-1), 2)
        pp[l, :, LXG:LXG + 2] = cmaj(inp["rw_lnx_g"][l], 2)
        pp[l, :, LXB:LXB + 2] = cmaj(inp["rw_lnx_b"][l], 2)
        pp[l, :, W0:W0 + 4] = cmaj(inp["rw_w0"][l].reshape(-1), 4)
        pp[l, :, A0:A0 + 4] = cmaj(inp["rw_a0"][l].reshape(-1), 4)
        pp[l, :, SLG:SLG + 2] = cmaj(inp["sg_ln_g"][l], 2)
        pp[l, :, SLB:SLB + 2] = cmaj(inp["sg_ln_b"][l], 2)
    m = dict(xin=xin, cin=f(cin), w_mod=f(inp["w_mod"]), w_in=f(inp["w_in"]), w_out=f(inp["w_out"]), w_gu=f(inp["w_gu"]), w_down=f(inp["w_down"]),
             pp=pp, sgwT=f(np.transpose(inp["sg_w"], (0, 3, 1, 2))), sgb=f(inp["sg_b"]).reshape(L, 1, 512), sink=f(inp["attn_sink"]).reshape(L, 1, 8),
             w2=f(inp["rw_w2"]).reshape(L, 128, 256), a2=f(inp["rw_a2"]).reshape(L, 128, 256), g2=f(inp["rw_g2"]))
    m.update(host_consts())
    return m


def kernel(**inp):
    inp = {k_: np.asarray(v) for k_, v in inp.items()}
    nc, _ = build()
    maps = [host_inputs(inp, b % 4) for b in range(4)]
    in_maps = [maps[i % 4] for i in range(8)]
    res = run_bass_kernel_spmd(nc, in_maps, core_ids=list(range(8)))
    out = np.zeros((4, T, D), np.float32)
    for b in range(4):
        o = res.results[b]["outT"]
        out[b] = o.reshape(D, T).T
    return out
```

```python
import numpy as np
import concourse.bass as bass
import concourse.mybir as mybir
from concourse.bass_utils import run_bass_kernel_spmd
from contextlib import ExitStack

F32 = mybir.dt.float32
BF16 = mybir.dt.bfloat16
AF = mybir.ActivationFunctionType
ALU = mybir.AluOpType

D = 1024; T = 4096; CT = 256; TT = T + CT; L = 4
NTOKB = TT // 128
NCHK = TT // 64
C0 = float(np.exp(-0.5))
TILES = [(0, 256, 1)] + [(256 + 512 * i, 512, 0) for i in range(8)]
TILES5 = [(0, 256, 1)] + [(256 + 256 * i, 256, 0) for i in range(16)]
NG = 0; BM = 32; MU = 80; KK = 89; KA = 91; RK = 93; LXG = 95; LXB = 97; W0 = 99; A0 = 103; SLG = 107; SLB = 109; NP = 111
SRC_CH = [8, 9, 10, 13, 14, 15, 16, 17, 18]

SAME_ENGINE_SYNC = True
EPOCH = 12000
DMA_EPOCH = 1500
NDMASEM = 6


class Buf:
    __slots__ = ("name", "w", "r")

    def __init__(self, name):
        self.name = name
        self.w = {}
        self.r = {}


class KB:
    def __init__(self, nc):
        self.nc = nc
        self.es = ExitStack()
        self.E = {"pe": nc.tensor, "act": nc.scalar, "dve": nc.vector, "pool": nc.gpsimd, "sp": nc.sync}
        self.sem = {}
        self.cnt = {}
        self.seen = {e: {} for e in self.E}
        self.nsem = 0
        for e in self.E:
            self._new_sem(e)
        self.dsem = {}
        self.dcnt = {}
        self.drr = {}
        for q in ("sp", "pool"):
            self.dsem[q] = [self._alloc_sem(f"d{q}{i}") for i in range(NDMASEM)]
            self.dcnt[q] = [0] * NDMASEM
            self.drr[q] = 0
        self.ninstr = 0
        self.banks = []
        self.bi = 0

    def _alloc_sem(self, name):
        self.nsem += 1
        return self.nc.alloc_semaphore(name=f"{name}_{self.nsem}")

    def _new_sem(self, e):
        self.sem[e] = self._alloc_sem(f"s{e}")
        self.cnt[e] = 0

    def sb(self, name, shape, dtype=F32, stack=None):
        self.nsb = getattr(self, "nsb", 0) + 1
        return (stack or self.es).enter_context(self.nc.sbuf_tensor(f"sb{self.nsb}_{name}", list(shape), dtype))

    def mkbanks(self):
        for i in range(8):
            t = self.es.enter_context(self.nc.psum_tensor(f"bank{i}", [128, 512], F32))
            self.banks.append((t, Buf(f"bank{i}")))

    def bank(self):
        b = self.banks[self.bi]
        self.bi = (self.bi + 1) % 8
        return b

    def _wait(self, e, ev):
        sem, val, src = ev
        if src == e and (not SAME_ENGINE_SYNC or e == "pe"):
            return
        key = id(sem)
        if self.seen[e].get(key, 0) >= val:
            return
        self.E[e].wait_ge(sem, val)
        self.seen[e][key] = val

    def _deps(self, e, reads, writes, adds, isdma=False):
        for b in reads:
            for ev in b.w.values():
                self._wait(e, ev)
        for b in writes:
            for ev in b.w.values():
                self._wait(e, ev)
            for ev in b.r.values():
                self._wait(e, ev)
        for b in adds:
            for ev in b.r.values():
                self._wait(e, ev)
            for ev in b.w.values():
                if not (isdma and ev[2] == "dma"):
                    self._wait(e, ev)

    def _commit(self, key, ev, reads, writes, adds):
        for b in reads:
            b.r[key] = ev
        for b in writes:
            b.w = {key: ev}
            b.r = {}
        for b in adds:
            b.w[key] = ev

    def op(self, e, fn, reads=(), writes=(), adds=()):
        self._deps(e, reads, writes, adds)
        ins = fn(self.E[e])
        if self.cnt[e] >= EPOCH:
            self._new_sem(e)
        self.cnt[e] += 1
        ins.then_inc(self.sem[e], 1)
        ev = (self.sem[e], self.cnt[e], e)
        self._commit(e, ev, reads, writes, adds)
        self.ninstr += 1
        return ev

    def dma(self, out, in_, reads=(), writes=(), adds=(), q="sp"):
        self._deps(q, reads, writes, adds, True)
        j = self.drr[q]
        self.drr[q] = (j + 1) % NDMASEM
        if self.dcnt[q][j] > 0:
            self._wait(q, (self.dsem[q][j], self.dcnt[q][j] * 16, "dma"))
        if self.dcnt[q][j] >= DMA_EPOCH:
            self.dsem[q][j] = self._alloc_sem(f"d{q}{j}")
            self.dcnt[q][j] = 0
        sem = self.dsem[q][j]
        self.E[q].dma_start(out=out, in_=in_).then_inc(sem, 16)
        self.dcnt[q][j] += 1
        ev = (sem, self.dcnt[q][j] * 16, "dma")
        self._commit(id(sem), ev, reads, writes, adds)
        self.ninstr += 1
        return ev

    def barrier(self):
        evs = []
        for e in ("pe", "act", "dve", "pool", "sp"):
            if self.cnt[e] > 0:
                evs.append((self.sem[e], self.cnt[e], e))
        for q in self.dsem:
            for j in range(NDMASEM):
                if self.dcnt[q][j] > 0:
                    evs.append((self.dsem[q][j], self.dcnt[q][j] * 16, "dma"))
        for e in self.E:
            for ev in evs:
                if ev[2] != e:
                    self._wait(e, ev)


PHASES = None


def i_phase_on(i):
    return PHASES is None or i in PHASES


def build(nlayers=L, debug=False):
    nc = bass.Bass("TRN2", target_bir_lowering=False)
    k = KB(nc)
    k.mkbanks()

    def din(name, shape):
        return nc.dram_tensor(name, list(shape), F32, kind="ExternalInput")

    def dscr(name, shape, out=False):
        return nc.dram_tensor(name, list(shape), F32, kind="ExternalOutput" if (out and debug) else "Internal")

    xin = din("xin", [8, 128, TT])
    cin = din("cin", [128, 8, 2])
    w_mod = din("w_mod", [L, D, 6 * D]); w_in = din("w_in", [L, D, 2432]); w_out = din("w_out", [L, D, D])
    w_gu = din("w_gu", [L, D, 5632]); w_down = din("w_down", [L, 2816, D])
    ppd = din("pp", [L, 128, NP])
    sgwT = din("sgwT", [L, 128, 4, 128]); sgb = din("sgb", [L, 1, 512]); sink = din("sink", [L, 1, 8])
    w2d = din("w2", [L, 128, 256]); a2d = din("a2", [L, 128, 256]); g2d = din("g2", [L, 128, 256])
    cc = din("cc", [128, 384]); smL = din("smL", [128, 9, 4]); smC = din("smC", [128, 9, 2])
    cmd = din("cm", [64, 4, 512]); amd = din("am", [128, 2, 512]); rsd = din("rs", [128, 512])
    cosd = din("cosT", [128, T]); sind = din("sinT", [128, T])
    outT = nc.dram_tensor("outT", [8, 128, T], F32, kind="ExternalOutput")

    xs = dscr("xs", [8, 128, TT], True); bxs = Buf("xs")
    pxT = dscr("pxT", [19, 128, TT], True); bpx = Buf("pxT")
    mixT = dscr("mixT", [8, 128, TT], True); bmix = Buf("mixT")
    rmT = dscr("rmT", [9, 128, TT], True); brm = Buf("rmT")
    scin = dscr("scin", [4, 128, 4, TT], True); bscin = Buf("scin")
    gcd = dscr("gcd", [4, 128, NCHK], True); bgcd = Buf("gcd")
    rksT = dscr("rksT", [2, 128, TT], True); brks = Buf("rksT")
    yT = dscr("yT", [2, 256, TT], True); byT = Buf("yT")
    bout = Buf("outT")

    cst = k.sb("cst", [128, 384]); bc = Buf("cst")
    k.dma(cst[:], cc.ap(), writes=[bc])
    ident = cst[:, 0:128]; bones = cst[:, 128:256]; Rm = cst[:, 256:384]
    ones = k.sb("ones", [128, 128]); bo = Buf("ones")
    k.op("dve", lambda e: e.memset(ones[:], 1.0), writes=[bo])
    cact = k.sb("cact", [128, 8, 2]); bca = Buf("cact")
    k.dma(cact[:], cin.ap(), writes=[bca])
    k.op("act", lambda e: e.activation(out=cact[:], in_=cact[:], func=AF.Silu), writes=[bca])
    pp = k.sb("ppt", [128, NP]); bpp = Buf("pp")
    modt = k.sb("modt", [128, 48, 2]); bmod = Buf("modt")
    coef = k.sb("coef", [128, 6, 8, 2]); bcoef = Buf("coef")
    cfs = k.sb("cfs", [128, 9, 7]); bcfs = Buf("cfs")
    smLs = k.sb("smLs", [128, 9, 4]); smCs = k.sb("smCs", [128, 9, 2]); bsm = Buf("sm")
    k.dma(smLs[:], smL.ap(), adds=[bsm]); k.dma(smCs[:], smC.ap(), adds=[bsm])
    omka = k.sb("omka", [128, 2]); bomka = Buf("omka")
    k.barrier()

    def xsrc(l):
        return xin if l == 0 else xs

    def rmsn(st, X, bX, W, SQ, bSQ, rs, brs):
        k.op("act", lambda e: e.activation(out=SQ[:, :, :W], in_=X[:, :, :W], func=AF.Square), reads=[bX], writes=[bSQ])
        bk, bb = k.bank()
        for c in range(8):
            k.op("pe", lambda e, c=c: e.matmul(bk[:, :W], lhsT=ones[:], rhs=SQ[:, c, :W], start=(c == 0), stop=(c == 7)),
                 reads=[bSQ, bo], writes=[bb])
        k.op("dve", lambda e: e.tensor_scalar(out=rs[:, :W], in0=bk[:, :W], scalar1=1.0 / D, scalar2=1e-6, op0=ALU.mult, op1=ALU.add),
             reads=[bb], writes=[brs])
        k.op("act", lambda e: e.activation(out=rs[:, :W], in_=rs[:, :W], func=AF.Sqrt), writes=[brs])
        k.op("dve", lambda e: e.reciprocal(out=rs[:, :W], in_=rs[:, :W]), writes=[brs])

    t32 = [k.sb(f"t32_{i}", [128, 512]) for i in range(2)]; bt32 = [Buf("t32a"), Buf("t32b")]

    def modulate(H, bH, X, bX, rs, brs, W, ia, ib, who):
        for c in range(8):
            t_ = t32[c % 2]; bt_ = bt32[c % 2]
            k.op("dve", lambda e, c=c, t_=t_: e.scalar_tensor_tensor(out=t_[:, :W], in0=X[:, c, :W], scalar=coef[:, ia, c, who:who + 1],
                                                                   in1=rs[:, :W], op0=ALU.mult, op1=ALU.mult),
                 reads=[bX, brs, bcoef], writes=[bt_])
            k.op("pool", lambda e, c=c, t_=t_: e.tensor_scalar(out=H[:, c, :W], in0=t_[:, :W], scalar1=coef[:, ib, c, who:who + 1], scalar2=None,
                                                              op0=ALU.add), reads=[bcoef, bt_], writes=[bH] if c == 0 else (), adds=[bH] if c else ())

    castrr = [0]

    def cast(out, in_, reads, writes=(), adds=()):
        castrr[0] = (castrr[0] + 1) % 3
        if castrr[0] == 0:
            k.op("act", lambda e: e.copy(out=out, in_=in_), reads=reads, writes=writes, adds=adds)
        elif castrr[0] == 1:
            k.op("dve", lambda e: e.tensor_copy(out=out, in_=in_), reads=reads, writes=writes, adds=adds)
        else:
            k.op("pool", lambda e: e.tensor_copy(out=out, in_=in_), reads=reads, writes=writes, adds=adds)

    def resid_update(X, bX, Y, bY, rs, brs, W, ig, who):
        for j in range(8):
            k.op("dve", lambda e, j=j: e.scalar_tensor_tensor(out=Y[:, j, :W], in0=Y[:, j, :W], scalar=coef[:, ig, j, who:who + 1],
                                                             in1=rs[:, :W], op0=ALU.mult, op1=ALU.mult),
                 reads=[brs, bcoef], writes=[bY])
            k.op("pool", lambda e, j=j: e.tensor_tensor(out=X[:, j, :W], in0=X[:, j, :W], in1=Y[:, j, :W], op=ALU.add),
                 reads=[bY], writes=[bX])

    evflip = [0]

    def evac(out, in_, reads, writes=(), adds=()):
        evflip[0] ^= 1
        if evflip[0]:
            k.op("act", lambda e: e.copy(out=out, in_=in_), reads=reads, writes=writes, adds=adds)
        else:
            k.op("dve", lambda e: e.tensor_copy(out=out, in_=in_), reads=reads, writes=writes, adds=adds)

    for l in range(nlayers):
        last = (l == L - 1)
        with ExitStack() as st:
          if i_phase_on(0):
            k.dma(pp[:], ppd.ap()[l], writes=[bpp])
            wm = [k.sb(f"wm{i}", [128, 8, 1024], stack=st) for i in range(2)]; bwm = [Buf("wm0"), Buf("wm1")]
            for m in range(6):
                k.dma(wm[m % 2][:], w_mod.ap()[l][:, m * 1024:(m + 1) * 1024].rearrange("(kc p) n -> p kc n", p=128), writes=[bwm[m % 2]],
                      q="sp" if m % 2 == 0 else "pool")
                for c in range(8):
                    bk, bb = k.bank()
                    for kc in range(8):
                        k.op("pe", lambda e, kc=kc, c=c, m=m: e.matmul(bk[:, 0:2], lhsT=wm[m % 2][:, kc, c * 128:(c + 1) * 128], rhs=cact[:, kc, :],
                                                                      start=(kc == 0), stop=(kc == 7)), reads=[bwm[m % 2], bca], writes=[bb])
                    j = m * 8 + c
                    k.op("dve", lambda e, j=j: e.tensor_scalar(out=modt[:, j, :], in0=bk[:, 0:2], scalar1=pp[:, BM + j:BM + j + 1], scalar2=None, op0=ALU.add),
                         reads=[bb, bpp], adds=[bmod])
            for who in range(2):
                for (ic, isc, ig) in ((0, 1, 0), (3, 4, 2)):
                    k.op("dve", lambda e, who=who, ic=ic, isc=isc, ig=ig: e.tensor_tensor(out=coef[:, ic, :, who], in0=modt[:, isc * 8:isc * 8 + 8, who],
                                                                                        in1=pp[:, NG + ig * 8:NG + ig * 8 + 8], op=ALU.mult),
                         reads=[bmod, bpp], writes=[bcoef])
                    k.op("dve", lambda e, who=who, ic=ic, ig=ig: e.tensor_tensor(out=coef[:, ic, :, who], in0=coef[:, ic, :, who],
                                                                               in1=pp[:, NG + ig * 8:NG + ig * 8 + 8], op=ALU.add), writes=[bcoef])
                for (ic, ish) in ((1, 0), (4, 3)):
                    k.op("dve", lambda e, who=who, ic=ic, ish=ish: e.tensor_copy(out=coef[:, ic, :, who], in_=modt[:, ish * 8:ish * 8 + 8, who]),
                         reads=[bmod], writes=[bcoef])
                for (ic, igt, ig) in ((2, 2, 1), (5, 5, 3)):
                    k.op("dve", lambda e, who=who, ic=ic, igt=igt, ig=ig: e.tensor_tensor(out=coef[:, ic, :, who], in0=modt[:, igt * 8:igt * 8 + 8, who],
                                                                                        in1=pp[:, NG + ig * 8:NG + ig * 8 + 8], op=ALU.mult),
                         reads=[bmod, bpp], writes=[bcoef])
            k.op("dve", lambda e: e.tensor_scalar(out=cfs[:, :, 0], in0=pp[:, MU:MU + 9], scalar1=-1.0, scalar2=1.0, op0=ALU.mult, op1=ALU.add),
                 reads=[bpp], writes=[bcfs])
            for dd in range(4):
                k.op("dve", lambda e, dd=dd: e.tensor_tensor(out=cfs[:, :, 1 + dd], in0=pp[:, MU:MU + 9], in1=smLs[:, :, dd], op=ALU.mult),
                     reads=[bpp, bsm], writes=[bcfs])
            for dd in range(2):
                k.op("dve", lambda e, dd=dd: e.tensor_tensor(out=cfs[:, :, 5 + dd], in0=pp[:, MU:MU + 9], in1=smCs[:, :, dd], op=ALU.mult),
                     reads=[bpp, bsm], writes=[bcfs])
            k.op("dve", lambda e: e.tensor_scalar(out=omka[:], in0=pp[:, KA:KA + 2], scalar1=-1.0, scalar2=1.0, op0=ALU.mult, op1=ALU.add),
                 reads=[bpp], writes=[bomka])
            k.barrier()

        with ExitStack() as st:
          if i_phase_on(1):
            win = k.sb("win", [128, 8, 2432], BF16, stack=st); bwin = Buf("win")
            wst = [k.sb(f"wst{i}", [128, 2432], stack=st) for i in range(2)]; bwst = [Buf("wst0"), Buf("wst1")]
            wv = w_in.ap()[l].rearrange("(kc p) n -> p kc n", p=128)
            for kc in range(8):
                k.dma(wst[kc % 2][:], wv[:, kc, :], writes=[bwst[kc % 2]], q="sp" if kc % 2 == 0 else "pool")
                cast(win[:, kc, :], wst[kc % 2][:], [bwst[kc % 2]], adds=[bwin])
            X = k.sb("X", [128, 8, 512], stack=st); bX = Buf("X")
            SQ = k.sb("SQ", [128, 8, 512], stack=st); bSQ = Buf("SQ")
            H = k.sb("H", [128, 8, 512], BF16, stack=st); bH = Buf("H")
            rs = k.sb("rs", [128, 512], stack=st); brs = Buf("rs")
            stg = [k.sb(f"stg{i}", [128, 512], stack=st) for i in range(2)]; bstg = [Buf("s0"), Buf("s1")]
            xv = xsrc(l).ap().rearrange("c p t -> p c t")
            for (t0, W, who) in TILES:
                k.dma(X[:, :, :W], xv[:, :, t0:t0 + W], reads=[bxs], writes=[bX])
                rmsn(st, X, bX, W, SQ, bSQ, rs, brs)
                modulate(H, bH, X, bX, rs, brs, W, 0, 1, who)
                for j in range(19):
                    bk, bb = k.bank()
                    for kc in range(8):
                        k.op("pe", lambda e, kc=kc, j=j: e.matmul(bk[:, :W], lhsT=win[:, kc, j * 128:(j + 1) * 128], rhs=H[:, kc, :W],
                                                                 start=(kc == 0), stop=(kc == 7)), reads=[bwin, bH], writes=[bb])
                    s = stg[j % 2]
                    evac(s[:, :W], bk[:, :W], [bb], writes=[bstg[j % 2]])
                    k.dma(pxT.ap()[j][:, t0:t0 + W], s[:, :W], reads=[bstg[j % 2]], adds=[bpx])
            k.barrier()

        with ExitStack() as st:
          if i_phase_on(2):
            fb = [k.sb(f"fb{i}", [128, TT], stack=st) for i in range(2)]; bfb = [Buf("fb0"), Buf("fb1")]
            ob = [k.sb(f"ob{i}", [128, TT], stack=st) for i in range(2)]; bob = [Buf("ob0"), Buf("ob1")]
            for ci in range(9):
                f = fb[ci % 2]; o = ob[ci % 2]; bf = bfb[ci % 2]; bo_ = bob[ci % 2]
                k.dma(f[:], pxT.ap()[SRC_CH[ci]], reads=[bpx], writes=[bf], q="sp" if ci % 2 == 0 else "pool")
                k.op("act", lambda e, f=f, o=o, ci=ci: e.activation(out=o[:], in_=f[:], func=AF.Identity, scale=cfs[:, ci, 0:1]), reads=[bf, bcfs], writes=[bo_])
                fl = f[:, CT:].rearrange("p (r w) -> p r w", w=64)
                ol = o[:, CT:].rearrange("p (r w) -> p r w", w=64)
                views = [(ol[:, :, 1:64], fl[:, :, 0:63]), (ol[:, :, 0:63], fl[:, :, 1:64]), (ol[:, 1:64, :], fl[:, 0:63, :]), (ol[:, 0:63, :], fl[:, 1:64, :]),
                         (o[:, 1:CT], f[:, 0:CT - 1]), (o[:, 0:CT - 1], f[:, 1:CT])]
                for di, (ov, fv) in enumerate(views):
                    eng = "dve"
                    k.op(eng, lambda e, ov=ov, fv=fv, ci=ci, di=di: e.scalar_tensor_tensor(out=ov, in0=fv, scalar=cfs[:, ci, 1 + di:2 + di], in1=ov,
                                                                                          op0=ALU.mult, op1=ALU.add), reads=[bf, bcfs], writes=[bo_])
                k.dma(rmT.ap()[ci], o[:], reads=[bo_], adds=[brm])
            k.barrier()

        with ExitStack() as st:
          if i_phase_on(3):
            w2 = k.sb("w2s", [128, 256], stack=st); a2 = k.sb("a2s", [128, 256], stack=st); bw2 = Buf("w2")
            k.dma(w2[:], w2d.ap()[l], adds=[bw2]); k.dma(a2[:], a2d.ap()[l], adds=[bw2])
            rsm = k.sb("rsm", [128, 512], stack=st); brsm = Buf("rsm")
            k.dma(rsm[:], rsd.ap(), writes=[brsm])
            rm = k.sb("rm", [128, 9, 512], stack=st); brmt = Buf("rm")
            tw = k.sb("tw", [128, 512], stack=st); btw = Buf("tw")
            kkr = k.sb("kkr", [128, 2, 512], stack=st); bkkr = Buf("kkr")
            sqk = k.sb("sqk", [128, 2, 512], stack=st); bsqk = Buf("sqk")
            kk = k.sb("kk", [128, 2, 512], stack=st); bkk = Buf("kk")
            nr = k.sb("nr", [128, 512], stack=st); bnr = Buf("nr")
            sg = k.sb("sg", [128, 512], stack=st); bsg = Buf("sg")
            aa = k.sb("aa", [128, 512], stack=st); baa = Buf("aa")
            cs = k.sb("cs", [128, 512], stack=st); bcs = Buf("cs")
            Ei = k.sb("Ei", [128, 512], stack=st); bEi = Buf("Ei")
            Ex = k.sb("Ex", [128, 512], stack=st); bEx = Buf("Ex")
            gi = k.sb("gi", [128, 512], stack=st); bgi = Buf("gi")
            gx = k.sb("gx", [128, 512], stack=st); bgx = Buf("gx")
            gv = k.sb("gv", [128, 512], stack=st); bgv = Buf("gv")
            gc = k.sb("gc", [128, 8], stack=st); bgc = Buf("gc")
            kd = k.sb("kd", [128, 512], stack=st); bkd = Buf("kd")
            bt = k.sb("bt", [128, 512], stack=st); bbt = Buf("bt")
            kds = k.sb("kds", [128, 2, 512], stack=st); bkds = Buf("kds")
            so = [k.sb(f"so{i}", [128, 4, 512], stack=st) for i in range(2)]; bso = [Buf("so0"), Buf("so1")]
            it = 0
            for (t0, W, who) in TILES:
                nck = W // 64
                k.dma(rm[:, :, :W], rmT.ap().rearrange("c p t -> p c t")[:, :, t0:t0 + W], reads=[brm], writes=[brmt])
                k.op("act", lambda e: e.activation(out=tw[:, :W], in_=rm[:, 7, :W], func=AF.Tanh), reads=[brmt], writes=[btw])
                for hp in range(2):
                    k.op("dve", lambda e, hp=hp: e.tensor_scalar(out=kkr[:, hp, :W], in0=rm[:, 3 + hp, :W], scalar1=pp[:, KK + hp:KK + hp + 1], scalar2=None, op0=ALU.mult),
                         reads=[brmt, bpp], adds=[bkkr])
                k.op("act", lambda e: e.activation(out=sqk[:, :, :W], in_=kkr[:, :, :W], func=AF.Square), reads=[bkkr], writes=[bsqk])
                for hp in range(2):
                    bk, bb = k.bank()
                    k.op("pe", lambda e, hp=hp: e.matmul(bk[:, :W], lhsT=bones, rhs=sqk[:, hp, :W], start=True, stop=True), reads=[bsqk, bc], writes=[bb])
                    k.op("act", lambda e: e.activation(out=nr[:, :W], in_=bk[:, :W], func=AF.Sqrt), reads=[bb], writes=[bnr])
                    k.op("dve", lambda e: e.tensor_scalar(out=nr[:, :W], in0=nr[:, :W], scalar1=1e-12, scalar2=None, op0=ALU.max), writes=[bnr])
                    k.op("dve", lambda e: e.reciprocal(out=nr[:, :W], in_=nr[:, :W]), writes=[bnr])
                    k.op("dve", lambda e, hp=hp: e.tensor_tensor(out=kk[:, hp, :W], in0=kkr[:, hp, :W], in1=nr[:, :W], op=ALU.mult), reads=[bkkr, bnr], adds=[bkk])
                for hp in range(2):
                    for d in range(2):
                        bk, bb = k.bank()
                        k.op("pe", lambda e, d=d, hp=hp: e.matmul(bk[:, :W], lhsT=w2[d * 64:(d + 1) * 64, hp * 128:(hp + 1) * 128], rhs=tw[d * 64:(d + 1) * 64, :W],
                                                                 start=True, stop=True), reads=[bw2, btw], writes=[bb])
                        k.op("act", lambda e, d=d, hp=hp: e.activation(out=sg[:, :W], in_=bk[:, :W], func=AF.Sigmoid, bias=pp[:, W0 + d * 2 + hp:W0 + d * 2 + hp + 1]),
                             reads=[bb, bpp], writes=[bsg])
                        bk2, bb2 = k.bank()
                        k.op("pe", lambda e, d=d, hp=hp: e.matmul(bk2[:, :W], lhsT=a2[d * 64:(d + 1) * 64, hp * 128:(hp + 1) * 128], rhs=rm[d * 64:(d + 1) * 64, 8, :W],
                                                                 start=True, stop=True), reads=[bw2, brmt], writes=[bb2])
                        k.op("act", lambda e, d=d, hp=hp: e.activation(out=aa[:, :W], in_=bk2[:, :W], func=AF.Sigmoid, bias=pp[:, A0 + d * 2 + hp:A0 + d * 2 + hp + 1]),
                             reads=[bb2, bpp], writes=[baa])
                        k.op("dve", lambda e: e.tensor_tensor_scan(out=cs[:, :W], data0=rsm[:, :W], data1=sg[:, :W], initial=0.0, op0=ALU.mult, op1=ALU.add),
                             reads=[brsm, bsg], writes=[bcs])
                        cs3 = cs[:, :W].rearrange("p (c t) -> p c t", t=64)
                        if d == 0:
                            k.op("pool", lambda e: e.tensor_tensor(out=Ex[:, :W], in0=cs[:, :W], in1=sg[:, :W], op=ALU.subtract), reads=[bcs, bsg], writes=[bEx])
                            Eiv = cs; bEiv = bcs
                        else:
                            Ex3 = Ex[:, :W].rearrange("p (c t) -> p c t", t=64)
                            k.op("pool", lambda e, cs3=cs3, Ex3=Ex3: e.tensor_tensor(out=Ex3, in0=cs3[:, :, 63:64].broadcast_to([128, nck, 64]), in1=cs3, op=ALU.subtract),
                                 reads=[bcs], writes=[bEx])
                            k.op("pool", lambda e: e.tensor_tensor(out=Ei[:, :W], in0=Ex[:, :W], in1=sg[:, :W], op=ALU.add), reads=[bEx, bsg], writes=[bEi])
                            Eiv = Ei; bEiv = bEi
                        k.op("act", lambda e, cs3=cs3: e.activation(out=gc[:, :nck], in_=cs3[:, :, 63], func=AF.Exp, scale=-C0), reads=[bcs], writes=[bgc])
                        k.op("act", lambda e, Eiv=Eiv: e.activation(out=gi[:, :W], in_=Eiv[:, :W], func=AF.Exp, scale=-C0), reads=[bEiv], writes=[bgi])
                        k.op("act", lambda e: e.activation(out=gx[:, :W], in_=Ex[:, :W], func=AF.Exp, scale=-C0), reads=[bEx], writes=[bgx])
                        k.op("act", lambda e, Eiv=Eiv: e.activation(out=gv[:, :W], in_=Eiv[:, :W], func=AF.Exp, scale=C0), reads=[bEiv], writes=[bgv])
                        s_ = so[it % 2]; bs_ = bso[it % 2]; it += 1
                        k.op("dve", lambda e, s_=s_, hp=hp: e.tensor_tensor(out=s_[:, 0, :W], in0=kk[:, hp, :W], in1=gx[:, :W], op=ALU.mult), reads=[bkk, bgx], writes=[bs_])
                        k.op("pool", lambda e, s_=s_, hp=hp: e.tensor_tensor(out=s_[:, 1, :W], in0=rm[:, hp, :W], in1=gi[:, :W], op=ALU.mult), reads=[brmt, bgi], adds=[bs_])
                        k.op("dve", lambda e, hp=hp: e.tensor_scalar(out=kd[:, :W], in0=aa[:, :W], scalar1=pp[:, KA + hp:KA + hp + 1], scalar2=omka[:, hp:hp + 1],
                                                                    op0=ALU.mult, op1=ALU.add), reads=[baa, bpp, bomka], writes=[bkd])
                        k.op("dve", lambda e, hp=hp: e.tensor_tensor(out=kd[:, :W], in0=kd[:, :W], in1=rm[:, 3 + hp, :W], op=ALU.mult), reads=[brmt], writes=[bkd])
                        k.op("dve", lambda e, s_=s_: e.tensor_tensor(out=s_[:, 2, :W], in0=kd[:, :W], in1=gv[:, :W], op=ALU.mult), reads=[bkd, bgv], adds=[bs_])
                        k.op("pool", lambda e, hp=hp: e.tensor_tensor(out=bt[:, :W], in0=aa[:, :W], in1=kk[:, hp, :W], op=ALU.mult), reads=[baa, bkk], writes=[bbt])
                        k.op("pool", lambda e, s_=s_: e.tensor_tensor(out=s_[:, 3, :W], in0=bt[:, :W], in1=gv[:, :W], op=ALU.mult), reads=[bbt, bgv], adds=[bs_])
                        if d == 0:
                            k.op("dve", lambda e, hp=hp: e.tensor_copy(out=kds[:, hp, :W], in_=kd[:, :W]), reads=[bkd], writes=[bkds])
                        else:
                            k.op("dve", lambda e, hp=hp: e.tensor_tensor(out=kds[:, hp, :W], in0=kds[:, hp, :W], in1=kd[:, :W], op=ALU.add), reads=[bkd], writes=[bkds])
                        k.dma(scin.ap()[d * 2 + hp][:, :, t0:t0 + W], s_[:, :, :W], reads=[bs_], adds=[bscin])
                        k.dma(gcd.ap()[d * 2 + hp][:, t0 // 64:t0 // 64 + nck], gc[:, :nck], reads=[bgc], adds=[bgcd])
                    k.op("dve", lambda e, hp=hp: e.scalar_tensor_tensor(out=kds[:, hp, :W], in0=kds[:, hp, :W], scalar=pp[:, RK + hp:RK + hp + 1], in1=rm[:, hp, :W],
                                                                       op0=ALU.mult, op1=ALU.mult), reads=[brmt, bpp], writes=[bkds])
                k.dma(rksT.ap().rearrange("c p t -> p c t")[:, :, t0:t0 + W], kds[:, :, :W], reads=[bkds], adds=[brks])
            k.barrier()

        with ExitStack() as st:
          if i_phase_on(4):
            cm = k.sb("cm", [64, 4, 512], stack=st); bcm = Buf("cm")
            k.dma(cm[:], cmd.ap(), writes=[bcm])
            inT = [k.sb(f"inT{d}", [64, 4, 4, 512], stack=st) for d in range(2)]; binT = [Buf("inT0"), Buf("inT1")]
            gct = [k.sb(f"gct{d}", [64, 4, 8], stack=st) for d in range(2)]; bgct = [Buf("g0"), Buf("g1")]
            vt = [k.sb(f"vt{d}", [64, 4, 512], stack=st) for d in range(2)]; bvt = [Buf("vt0"), Buf("vt1")]
            yst = [k.sb(f"yst{d}", [64, 4, 512], stack=st) for d in range(2)]; byst = [Buf("y0"), Buf("y1")]
            Hs = k.sb("Hs", [64, 8, 64], stack=st); bHs = Buf("Hs")
            k.op("dve", lambda e: e.memset(Hs[:], 0.0), writes=[bHs])
            Ktm = k.sb("Ktm", [64, 512], stack=st); bKtm = Buf("Ktm")
            NBtm = k.sb("NBtm", [64, 512], stack=st); bNB = Buf("NBtm")
            Vtm = k.sb("Vtm", [64, 512], stack=st); bVtm = Buf("Vtm")
            Pm = [k.sb(f"Pm{i}", [64, 512], stack=st) for i in range(2)]; bPm = [Buf("P0"), Buf("P1")]
            Nm = [k.sb(f"Nm{i}", [64, 512], stack=st) for i in range(2)]; bNm = [Buf("N0"), Buf("N1")]
            Am = [k.sb(f"Am{i}", [64, 512], stack=st) for i in range(2)]; bAm = [Buf("A0"), Buf("A1")]
            AKT = k.sb("AKT", [64, 512], stack=st); bAKT = Buf("AKT")
            RKT = k.sb("RKT", [64, 512], stack=st); bRKT = Buf("RKT")
            RBT = k.sb("RBT", [64, 512], stack=st); bRBT = Buf("RBT")
            RHSs = k.sb("RHSs", [64, 512], stack=st); bRHS = Buf("RHS")
            Us = k.sb("Us", [64, 512], stack=st); bUs = Buf("Us")
            Ht = k.sb("Ht", [64, 8, 64], stack=st); bHt = Buf("Ht")
            Fc = [(0, ci) for ci in range(4)] + [(ti, ci) for ti in range(1, 9) for ci in range(8)]
            Bc = [(0, ci) for ci in range(3, -1, -1)] + [(ti, ci) for ti in range(8, 0, -1) for ci in range(7, -1, -1)]
            cur = [None, None]

            def flush(d):
                ti = cur[d]
                t0, W, _ = TILES[ti]
                yv = yT.ap()[d].rearrange("(h v) t -> v h t", v=64)
                k.dma(yv[:, :, t0:t0 + W], yst[d][:, :, :W], reads=[byst[d]], adds=[byT])

            def load(d, ti):
                t0, W, _ = TILES[ti]
                for h in range(4):
                    hp = h // 2; e_ = h % 2
                    k.dma(inT[d][:, h, :, :W], scin.ap()[d * 2 + hp][e_ * 64:(e_ + 1) * 64, :, t0:t0 + W], reads=[bscin],
                          writes=[binT[d]] if h == 0 else (), adds=[binT[d]] if h else ())
                    k.dma(gct[d][:, h, :W // 64], gcd.ap()[d * 2 + hp][e_ * 64:(e_ + 1) * 64, t0 // 64:t0 // 64 + W // 64], reads=[bgcd],
                          writes=[bgct[d]] if h == 0 else (), adds=[bgct[d]] if h else ())
                    k.dma(vt[d][:, h, :W], rmT.ap()[5 + hp][e_ * 64:(e_ + 1) * 64, t0:t0 + W], reads=[brm],
                          writes=[bvt[d]] if h == 0 else (), adds=[bvt[d]] if h else ())

            def opd(d, h, kind, ci):
                return inT[d][:, h, kind, ci * 64:(ci + 1) * 64]

            def blk(t, q):
                return t[:, q * 64:(q + 1) * 64]

            import os as _os
            R3S = int(_os.environ.get('R3S', '99')); R3N = int(_os.environ.get('R3N', str(NCHK)))
            for step in range(R3N):
                tcs = [Fc[step], Bc[step]]
                for d in range(2):
                    if cur[d] != tcs[d][0]:
                        if cur[d] is not None:
                            flush(d)
                        load(d, tcs[d][0])
                        cur[d] = tcs[d][0]
                cis = [tcs[0][1], tcs[1][1]]
                if R3S < 1:
                    continue
                for kind, dst, bdst, neg in ((2, Ktm, bKtm, False), (3, NBtm, bNB, True)):
                    bk, bb = k.bank()
                    for d in range(2):
                        for h in range(4):
                            q = d * 4 + h
                            k.op("pe", lambda e, d=d, h=h, q=q, kind=kind: e.transpose(bk[0:64, q * 64:(q + 1) * 64], opd(d, h, kind, cis[d]), ident[0:64, 0:64]),
                                 reads=[binT[d], bc], writes=[bb])
                    if neg:
                        k.op("act", lambda e, dst=dst: e.activation(out=dst[:], in_=bk[0:64, :], func=AF.Copy, scale=-1.0), reads=[bb], writes=[bdst])
                    else:
                        k.op("dve", lambda e, dst=dst: e.tensor_copy(out=dst[:], in_=bk[0:64, :]), reads=[bb], writes=[bdst])
                bk, bb = k.bank()
                for d in range(2):
                    for h in range(4):
                        q = d * 4 + h
                        k.op("pe", lambda e, d=d, h=h, q=q: e.transpose(bk[0:64, q * 64:(q + 1) * 64], vt[d][:, h, cis[d] * 64:(cis[d] + 1) * 64], ident[0:64, 0:64]),
                             reads=[bvt[d], bc], writes=[bb])
                k.op("dve", lambda e: e.tensor_copy(out=Vtm[:], in_=bk[0:64, :]), reads=[bb], writes=[bVtm])

                if R3S < 2:
                    continue
                def prod(la, ra, dst, bdst, mki, neg, eng):
                    bk, bb = k.bank()
                    for d in range(2):
                        for h in range(4):
                            q = d * 4 + h
                            k.op("pe", lambda e, d=d, h=h, q=q: e.matmul(bk[0:64, q * 64:(q + 1) * 64], lhsT=opd(d, h, la, cis[d]), rhs=opd(d, h, ra, cis[d]),
                                                                        start=True, stop=True), reads=[binT[d]], writes=[bb])
                    k.op(eng, lambda e: e.scalar_tensor_tensor(out=dst[:], in0=bk[0:64, :], scalar=(-1.0 if neg else 1.0), in1=cm[:, mki, :], op0=ALU.mult, op1=ALU.mult),
                         reads=[bb, bcm], writes=[bdst])
                prod(3, 0, Pm[0], bPm[0], 0, True, "dve")
                prod(0, 3, Nm[0], bNm[0], 1, True, "dve")
                prod(2, 0, AKT, bAKT, 0, False, "dve")
                prod(2, 1, RKT, bRKT, 2, False, "dve")
                prod(3, 1, RBT, bRBT, 2, True, "dve")
                if R3S < 3:
                    continue
                k.op("pool", lambda e: e.tensor_tensor(out=Am[0][:], in0=Pm[0][:], in1=cm[:, 3, :], op=ALU.add), reads=[bPm[0], bcm], writes=[bAm[0]])
                pc = 0; ac = 0
                for j in range(1, 6):
                    nn = 1 - pc
                    bk, bb = k.bank()
                    for q in range(8):
                        k.op("pe", lambda e, q=q, pc=pc: e.matmul(blk(bk[0:64, :], q), lhsT=blk(Pm[pc], q), rhs=blk(Nm[pc], q), start=True, stop=True),
                             reads=[bPm[pc], bNm[pc]], writes=[bb])
                    if j < 5:
                        bk2, bb2 = k.bank()
                        for q in range(8):
                            k.op("pe", lambda e, q=q, pc=pc: e.matmul(blk(bk2[0:64, :], q), lhsT=blk(Nm[pc], q), rhs=blk(Pm[pc], q), start=True, stop=True),
                                 reads=[bPm[pc], bNm[pc]], writes=[bb2])
                    k.op("act", lambda e, nn=nn, bk=bk: e.copy(out=Nm[nn][:], in_=bk[0:64, :]), reads=[bb], writes=[bNm[nn]])
                    if j < 5:
                        k.op("dve", lambda e, nn=nn, bk2=bk2: e.tensor_copy(out=Pm[nn][:], in_=bk2[0:64, :]), reads=[bb2], writes=[bPm[nn]])
                    bk3, bb3 = k.bank()
                    for q in range(8):
                        k.op("pe", lambda e, q=q, nn=nn, ac=ac: e.matmul(blk(bk3[0:64, :], q), lhsT=blk(Nm[nn], q), rhs=blk(Am[ac], q), start=True, stop=True),
                             reads=[bNm[nn], bAm[ac]], writes=[bb3])
                    k.op("dve", lambda e, ac=ac, bk3=bk3: e.tensor_tensor(out=Am[1 - ac][:], in0=bk3[0:64, :], in1=Am[ac][:], op=ALU.add), reads=[bb3, bAm[ac]], writes=[bAm[1 - ac]])
                    pc = nn; ac = 1 - ac
                A = Am[ac]; bA = bAm[ac]
                if R3S < 4:
                    continue
                bk, bb = k.bank()
                for d in range(2):
                    for h in range(4):
                        q = d * 4 + h
                        k.op("pe", lambda e, d=d, h=h, q=q: e.matmul(blk(bk[0:64, :], q), lhsT=opd(d, h, 0, cis[d]), rhs=Hs[:, q, :], start=True, stop=False),
                             reads=[binT[d], bHs], writes=[bb])
                        k.op("pe", lambda e, q=q: e.matmul(blk(bk[0:64, :], q), lhsT=blk(AKT, q), rhs=blk(Vtm, q), start=False, stop=True),
                             reads=[bAKT, bVtm], writes=[bb])
                k.op("act", lambda e, bk=bk: e.copy(out=RHSs[:], in_=bk[0:64, :]), reads=[bb], writes=[bRHS])
                bk, bb = k.bank()
                for q in range(8):
                    k.op("pe", lambda e, q=q: e.matmul(blk(bk[0:64, :], q), lhsT=blk(A, q), rhs=blk(RHSs, q), start=True, stop=True), reads=[bA, bRHS], writes=[bb])
                k.op("dve", lambda e, bk=bk: e.tensor_copy(out=Us[:], in_=bk[0:64, :]), reads=[bb], writes=[bUs])
                if R3S < 5:
                    continue
                bk, bb = k.bank()
                for d in range(2):
                    for h in range(4):
                        q = d * 4 + h
                        k.op("pe", lambda e, d=d, h=h, q=q: e.matmul(blk(bk[0:64, :], q), lhsT=Hs[:, q, :], rhs=opd(d, h, 1, cis[d]), start=True, stop=False),
                             reads=[binT[d], bHs], writes=[bb])
                        k.op("pe", lambda e, q=q: e.matmul(blk(bk[0:64, :], q), lhsT=blk(Vtm, q), rhs=blk(RKT, q), start=False, stop=False), reads=[bVtm, bRKT], writes=[bb])
                        k.op("pe", lambda e, q=q: e.matmul(blk(bk[0:64, :], q), lhsT=blk(Us, q), rhs=blk(RBT, q), start=False, stop=True), reads=[bUs, bRBT], writes=[bb])
                for d in range(2):
                    for h in range(4):
                        q = d * 4 + h
                        evac(yst[d][:, h, cis[d] * 64:(cis[d] + 1) * 64], bk[0:64, q * 64:(q + 1) * 64], [bb], adds=[byst[d]])
                bk, bb = k.bank()
                for q in range(8):
                    k.op("pe", lambda e, q=q: e.matmul(blk(bk[0:64, :], q), lhsT=blk(Ktm, q), rhs=blk(Vtm, q), start=True, stop=False), reads=[bKtm, bVtm], writes=[bb])
                    k.op("pe", lambda e, q=q: e.matmul(blk(bk[0:64, :], q), lhsT=blk(NBtm, q), rhs=blk(Us, q), start=False, stop=True), reads=[bNB, bUs], writes=[bb])
                k.op("dve", lambda e, bk=bk: e.tensor_tensor(out=Ht[:].rearrange("p a b -> p (a b)"), in0=bk[0:64, :], in1=Hs[:].rearrange("p a b -> p (a b)"), op=ALU.add),
                     reads=[bb, bHs], writes=[bHt])
                for d in range(2):
                    for h in range(4):
                        q = d * 4 + h
                        k.op("dve", lambda e, d=d, h=h, q=q: e.tensor_scalar(out=Hs[:, q, :], in0=Ht[:, q, :], scalar1=gct[d][:, h, cis[d]:cis[d] + 1], scalar2=None, op0=ALU.mult),
                             reads=[bHt, bgct[d]], writes=[bHs] if q == 0 else (), adds=[bHs] if q else ())
            flush(0); flush(1)
            k.barrier()

        with ExitStack() as st:
          if i_phase_on(5):
            g2 = k.sb("g2s", [128, 256], stack=st); bg2 = Buf("g2")
            k.dma(g2[:], g2d.ap()[l], writes=[bg2])
            yf = k.sb("yf", [128, 2, 512], stack=st); byf = Buf("yf")
            yb = k.sb("yb", [128, 2, 512], stack=st); byb = Buf("yb")
            rmo = k.sb("rmo", [128, 7, 512], stack=st); brmo = Buf("rmo")
            rk_ = k.sb("rks", [128, 2, 512], stack=st); brk_ = Buf("rks")
            sq = k.sb("sq4", [128, 512], stack=st); bsq = Buf("sq4")
            mu = k.sb("mu4", [128, 512], stack=st); bmu = Buf("mu4")
            va = k.sb("va4", [128, 512], stack=st); bva = Buf("va4")
            sgl = k.sb("sgl", [128, 512], stack=st); bsgl = Buf("sgl")
            orw = k.sb("orw", [128, 2, 512], stack=st); borw = Buf("orw")
            bn = k.sb("bn4", [128, 512], stack=st); bbn = Buf("bn4")
            for (t0, W, who) in TILES:
                if last and who == 1:
                    continue
                yv = yT.ap().rearrange("d (hp p) t -> d p hp t", p=128)
                k.dma(yf[:, :, :W], yv[0][:, :, t0:t0 + W], reads=[byT], writes=[byf])
                k.dma(yb[:, :, :W], yv[1][:, :, t0:t0 + W], reads=[byT], writes=[byb])
                k.dma(rmo[:, :, :W], rmT.ap().rearrange("c p t -> p c t")[:, 0:7, t0:t0 + W], reads=[brm], writes=[brmo])
                k.dma(rk_[:, :, :W], rksT.ap().rearrange("c p t -> p c t")[:, :, t0:t0 + W], reads=[brks], writes=[brk_])
                k.op("dve", lambda e: e.tensor_tensor(out=yf[:, :, :W], in0=yf[:, :, :W], in1=yb[:, :, :W], op=ALU.add), reads=[byb], writes=[byf])
                k.op("act", lambda e: e.activation(out=sgl[:, :W], in_=rmo[:, 2, :W], func=AF.Sigmoid), reads=[brmo], writes=[bsgl])
                for hp in range(2):
                    bk1, bb1 = k.bank()
                    k.op("pe", lambda e, hp=hp: e.matmul(bk1[:, :W], lhsT=bones, rhs=yf[:, hp, :W], start=True, stop=True), reads=[byf, bc], writes=[bb1])
                    k.op("act", lambda e, hp=hp: e.activation(out=sq[:, :W], in_=yf[:, hp, :W], func=AF.Square), reads=[byf], writes=[bsq])
                    bk2, bb2 = k.bank()
                    k.op("pe", lambda e: e.matmul(bk2[:, :W], lhsT=bones, rhs=sq[:, :W], start=True, stop=True), reads=[bsq, bc], writes=[bb2])
                    k.op("dve", lambda e: e.tensor_scalar(out=mu[:, :W], in0=bk1[:, :W], scalar1=1.0 / 64, scalar2=None, op0=ALU.mult), reads=[bb1], writes=[bmu])
                    k.op("pool", lambda e: e.tensor_tensor(out=va[:, :W], in0=mu[:, :W], in1=mu[:, :W], op=ALU.mult), reads=[bmu], writes=[bva])
                    k.op("dve", lambda e: e.scalar_tensor_tensor(out=va[:, :W], in0=bk2[:, :W], scalar=1.0 / 64, in1=va[:, :W], op0=ALU.mult, op1=ALU.subtract),
                         reads=[bb2], writes=[bva])
                    k.op("dve", lambda e: e.tensor_scalar(out=va[:, :W], in0=va[:, :W], scalar1=64e-5, scalar2=None, op0=ALU.add), writes=[bva])
                    k.op("act", lambda e: e.activation(out=va[:, :W], in_=va[:, :W], func=AF.Sqrt), writes=[bva])
                    k.op("dve", lambda e: e.reciprocal(out=va[:, :W], in_=va[:, :W]), writes=[bva])
                    k.op("dve", lambda e, hp=hp: e.tensor_tensor(out=yf[:, hp, :W], in0=yf[:, hp, :W], in1=mu[:, :W], op=ALU.subtract), reads=[bmu], writes=[byf])
                    k.op("dve", lambda e, hp=hp: e.tensor_tensor(out=yf[:, hp, :W], in0=yf[:, hp, :W], in1=va[:, :W], op=ALU.mult), reads=[bva], writes=[byf])
                    k.op("dve", lambda e, hp=hp: e.tensor_scalar(out=yf[:, hp, :W], in0=yf[:, hp, :W], scalar1=pp[:, LXG + hp:LXG + hp + 1], scalar2=pp[:, LXB + hp:LXB + hp + 1],
                                                                op0=ALU.mult, op1=ALU.add), reads=[bpp], writes=[byf])
                    bk3, bb3 = k.bank()
                    k.op("pe", lambda e, hp=hp: e.matmul(bk3[:, :W], lhsT=bones, rhs=rk_[:, hp, :W], start=True, stop=True), reads=[brk_, bc], writes=[bb3])
                    k.op("dve", lambda e, hp=hp: e.tensor_tensor(out=bn[:, :W], in0=bk3[:, :W], in1=rmo[:, 5 + hp, :W], op=ALU.mult), reads=[bb3, brmo], writes=[bbn])
                    k.op("dve", lambda e, hp=hp: e.tensor_tensor(out=bn[:, :W], in0=bn[:, :W], in1=yf[:, hp, :W], op=ALU.add), reads=[byf], writes=[bbn])
                    bk4, bb4 = k.bank()
                    k.op("pe", lambda e, hp=hp: e.matmul(bk4[:, :W], lhsT=g2[:, hp * 128:(hp + 1) * 128], rhs=sgl[:, :W], start=True, stop=True), reads=[bg2, bsgl], writes=[bb4])
                    k.op("dve", lambda e, hp=hp: e.tensor_tensor(out=orw[:, hp, :W], in0=bk4[:, :W], in1=bn[:, :W], op=ALU.mult), reads=[bb4, bbn], writes=[borw])
                k.dma(mixT.ap().rearrange("c p t -> p c t")[:, 6:8, t0:t0 + W], orw[:, :, :W], reads=[borw], adds=[bmix])
            k.barrier()

        with ExitStack() as st:
          if i_phase_on(6):
            wsT = k.sb("wsT", [128, 4, 128], stack=st); bws = Buf("wsT")
            k.dma(wsT[:], sgwT.ap()[l], writes=[bws])
            bsb = k.sb("bsb", [128, 512], stack=st); bbsb = Buf("bsb")
            k.dma(bsb[:], sgb.ap()[l].partition_broadcast(128), writes=[bbsb])
            uv = k.sb("uv", [128, 4, 512], stack=st); buv = Buf("uv")
            sq = k.sb("sqg", [128, 2, 512], stack=st); bsq = Buf("sqg")
            mn = k.sb("mng", [128, 512], stack=st); bmn = Buf("mng")
            va = k.sb("vag", [128, 512], stack=st); bva = Buf("vag")
            vtm = k.sb("vtmg", [128, 256], stack=st); bvtm = Buf("vtmg")
            tmp = k.sb("tmpg", [128, 128], stack=st); btmp = Buf("tmpg")
            osg = k.sb("osg", [128, 2, 512], stack=st); bosg = Buf("osg")
            for (t0, W, who) in TILES:
                if last and who == 1:
                    continue
                k.dma(uv[:, :, :W], pxT.ap().rearrange("c p t -> p c t")[:, 0:4, t0:t0 + W], reads=[bpx], writes=[buv])
                k.op("act", lambda e: e.activation(out=uv[:, :, :W], in_=uv[:, :, :W], func=AF.Gelu), writes=[buv])
                k.op("act", lambda e: e.activation(out=sq[:, :, :W], in_=uv[:, 2:4, :W], func=AF.Square), reads=[buv], writes=[bsq])
                bk1, bb1 = k.bank(); bk2, bb2 = k.bank()
                for c in range(2):
                    k.op("pe", lambda e, c=c: e.matmul(bk1[:, :W], lhsT=ones[:], rhs=uv[:, 2 + c, :W], start=(c == 0), stop=(c == 1)), reads=[buv, bo], writes=[bb1])
                for c in range(2):
                    k.op("pe", lambda e, c=c: e.matmul(bk2[:, :W], lhsT=ones[:], rhs=sq[:, c, :W], start=(c == 0), stop=(c == 1)), reads=[bsq, bo], writes=[bb2])
                k.op("dve", lambda e: e.tensor_scalar(out=mn[:, :W], in0=bk1[:, :W], scalar1=1.0 / 256, scalar2=None, op0=ALU.mult), reads=[bb1], writes=[bmn])
                k.op("pool", lambda e: e.tensor_tensor(out=va[:, :W], in0=mn[:, :W], in1=mn[:, :W], op=ALU.mult), reads=[bmn], writes=[bva])
                k.op("dve", lambda e: e.scalar_tensor_tensor(out=va[:, :W], in0=bk2[:, :W], scalar=1.0 / 256, in1=va[:, :W], op0=ALU.mult, op1=ALU.subtract), reads=[bb2], writes=[bva])
                k.op("dve", lambda e: e.tensor_scalar(out=va[:, :W], in0=va[:, :W], scalar1=1e-5, scalar2=None, op0=ALU.add), writes=[bva])
                k.op("act", lambda e: e.activation(out=va[:, :W], in_=va[:, :W], func=AF.Sqrt), writes=[bva])
                k.op("dve", lambda e: e.reciprocal(out=va[:, :W], in_=va[:, :W]), writes=[bva])
                for c in range(2):
                    k.op("dve", lambda e, c=c: e.tensor_tensor(out=uv[:, 2 + c, :W], in0=uv[:, 2 + c, :W], in1=mn[:, :W], op=ALU.subtract), reads=[bmn], writes=[buv])
                    k.op("dve", lambda e, c=c: e.tensor_tensor(out=uv[:, 2 + c, :W], in0=uv[:, 2 + c, :W], in1=va[:, :W], op=ALU.mult), reads=[bva], writes=[buv])
                    k.op("dve", lambda e, c=c: e.tensor_scalar(out=uv[:, 2 + c, :W], in0=uv[:, 2 + c, :W], scalar1=pp[:, SLG + c:SLG + c + 1], scalar2=pp[:, SLB + c:SLB + c + 1],
                                                              op0=ALU.mult, op1=ALU.add), reads=[bpp], writes=[buv])
                for nb in range(W // 128):
                    cols = slice(nb * 128, (nb + 1) * 128)
                    bkT, bbT = k.bank()
                    for c in range(2):
                        k.op("pe", lambda e, c=c, cols=cols: e.transpose(bkT[:, c * 128:(c + 1) * 128], uv[:, 2 + c, cols], ident), reads=[buv, bc], writes=[bbT])
                    k.op("act", lambda e: e.copy(out=vtm[:], in_=bkT[:, 0:256]), reads=[bbT], writes=[bvtm])
                    bkM, bbM = k.bank()
                    for g in range(4):
                        k.op("pe", lambda e, g=g: e.matmul(bkM[:, g * 128:(g + 1) * 128], lhsT=vtm[:, (g // 2) * 128:(g // 2 + 1) * 128], rhs=wsT[:, g, :], start=True, stop=True),
                             reads=[bvtm, bws], writes=[bbM])
                    for g in range(4):
                        ps_ = slice((g % 2) * 64, (g % 2 + 1) * 64)
                        k.op("dve", lambda e, g=g, ps_=ps_: e.tensor_tensor(out=tmp[ps_, :], in0=bkM[ps_, g * 128:(g + 1) * 128], in1=bsb[ps_, g * 128:(g + 1) * 128], op=ALU.add),
                             reads=[bbM, bbsb], writes=[btmp])
                        k.op("dve", lambda e, g=g, ps_=ps_, cols=cols: e.tensor_tensor(out=osg[ps_, g // 2, cols], in0=tmp[ps_, :], in1=uv[ps_, g // 2, cols], op=ALU.mult),
                             reads=[btmp, buv], writes=[bosg])
                k.dma(mixT.ap().rearrange("c p t -> p c t")[:, 0:2, t0:t0 + W], osg[:, :, :W], reads=[bosg], adds=[bmix])
            k.barrier()

        with ExitStack() as st:
          if i_phase_on(7):
            cosT = k.sb("cosT", [128, T], stack=st); sinT = k.sb("sinT", [128, T], stack=st); btab = Buf("tab")
            k.dma(cosT[:], cosd.ap(), adds=[btab]); k.dma(sinT[:], sind.ap(), adds=[btab], q="pool")
            am = k.sb("am", [128, 2, 512], stack=st); bam = Buf("am")
            k.dma(am[:], amd.ap(), writes=[bam])
            esb = k.sb("esb", [64, 8], stack=st); besb = Buf("esb")
            k.dma(esb[:], sink.ap()[l].partition_broadcast(64), writes=[besb])
            k.op("act", lambda e: e.activation(out=esb[:], in_=esb[:], func=AF.Exp), writes=[besb])
            KT2 = k.sb("KT2", [128, 2, TT], stack=st); bKT = Buf("KT2")
            Vt = k.sb("Vt", [128, NTOKB, 128], stack=st); bVt = Buf("Vt")
            kraw = k.sb("kraw", [128, TT], stack=st); bkr = Buf("kraw")
            t1 = k.sb("t1", [128, 512], stack=st); bt1 = Buf("t1")
            k.dma(kraw[:], pxT.ap()[11], reads=[bpx], writes=[bkr])

            def rope(dst, src, bsrc, tl, W, wr_reads, bdst):
                bk, bb = k.bank()
                k.op("pe", lambda e: e.matmul(bk[:, :W], lhsT=Rm, rhs=src, start=True, stop=True), reads=[bsrc, bc], writes=[bb])
                k.op("dve", lambda e: e.tensor_tensor(out=t1[:, :W], in0=bk[:, :W], in1=sinT[:, tl:tl + W], op=ALU.mult), reads=[bb, btab], writes=[bt1])
                k.op("pool", lambda e: e.tensor_tensor(out=dst, in0=src, in1=cosT[:, tl:tl + W], op=ALU.mult), reads=[bsrc, btab], writes=[bdst])
                k.op("dve", lambda e: e.tensor_tensor(out=dst, in0=dst, in1=t1[:, :W], op=ALU.add), reads=[bt1], writes=[bdst])

            for i in range(8):
                sl = slice(CT + i * 512, CT + (i + 1) * 512)
                rope(kraw[:, sl], kraw[:, sl], bkr, i * 512, 512, [], bkr)
            for hk in range(2):
                ps_ = slice(hk * 64, (hk + 1) * 64)
                for e_ in range(2):
                    po = slice(e_ * 64, (e_ + 1) * 64)
                    k.op("dve" if e_ == 0 else "pool", lambda e, hk=hk, ps_=ps_, po=po: e.tensor_copy(out=KT2[po, hk, :], in_=kraw[ps_, :]), reads=[bkr], adds=[bKT])
            vraw = kraw
            k.dma(vraw[:], pxT.ap()[12], reads=[bpx], writes=[bkr])
            for b4 in range(0, NTOKB, 4):
                nb_ = min(4, NTOKB - b4)
                bk, bb = k.bank()
                for i in range(nb_):
                    k.op("pe", lambda e, i=i, b4=b4: e.transpose(bk[:, i * 128:(i + 1) * 128], vraw[:, (b4 + i) * 128:(b4 + i + 1) * 128], ident), reads=[bkr, bc], writes=[bb])
                evac(Vt[:, b4:b4 + nb_, :].rearrange("p a b -> p (a b)"), bk[:, :nb_ * 128], [bb], adds=[bVt])
            qraw = k.sb("qraw", [128, 4, 512], stack=st); bq = Buf("qraw")
            qr = k.sb("qr", [128, 4, 512], stack=st); bqr = Buf("qr")
            PT = [k.sb(f"PT{i}", [128, 512], stack=st) for i in range(2)]; bPT = [Buf("PT0"), Buf("PT1")]
            rc = k.sb("rc", [64, 512], stack=st); brc = Buf("rc")
            obt = k.sb("obt", [64, 2, 4, 512], stack=st); bobt = Buf("obt")
            pti = 0; sbi = [0]
            for (t0, W, who) in TILES:
                if last and who == 1:
                    continue
                k.dma(qraw[:, :, :W], pxT.ap().rearrange("c p t -> p c t")[:, 4:8, t0:t0 + W], reads=[bpx], writes=[bq])
                if who == 0:
                    for c in range(4):
                        rope(qr[:, c, :W], qraw[:, c, :W], bq, t0 - CT, W, [], bqr)
                    Q = qr; bQ = bqr
                else:
                    Q = qraw; bQ = bq
                for nb in range(W // 128):
                    n = t0 // 128 + nb
                    cols = slice(nb * 128, (nb + 1) * 128)
                    if who == 1:
                        kbs = [(0, None), (1, None)]
                    else:
                        kbs = [(0, None), (1, None)]
                        if n - 1 >= 2:
                            kbs.append((n - 1, 0))
                        kbs.append((n, None))
                        if n + 1 < NTOKB:
                            kbs.append((n + 1, 1))
                    GO = [0, 2, 1, 3]
                    for hk in range(2):
                        bkO, bbO = k.banks[(hk % 2) * 2]; bkD, bbD = k.banks[(hk % 2) * 2 + 1]
                        for idx, (kb, mk) in enumerate(kbs):
                            bkS = [None, None]; bbS = [None, None]
                            bkS[0], bbS[0] = k.banks[4 + (sbi[0] % 2) * 2]; bkS[1], bbS[1] = k.banks[5 + (sbi[0] % 2) * 2]; sbi[0] += 1
                            for pos in range(4):
                                g = GO[pos]; h = hk * 4 + g; e_ = h % 2; chn = h // 2
                                assert e_ == pos // 2
                                k.op("pe", lambda e, pos=pos, e_=e_, chn=chn, kb=kb, hk=hk: e.matmul(bkS[e_][:, (pos % 2) * 128:(pos % 2 + 1) * 128],
                                                                                                    lhsT=KT2[e_ * 64:(e_ + 1) * 64, hk, kb * 128:(kb + 1) * 128],
                                                                                                    rhs=Q[e_ * 64:(e_ + 1) * 64, chn, cols], start=True, stop=True),
                                     reads=[bKT, bQ], writes=[bbS[e_]])
                            P_ = PT[pti % 2]; bP_ = bPT[pti % 2]; pti += 1
                            k.op("act", lambda e, P_=P_, b0=bkS[0]: e.activation(out=P_[:, 0:256], in_=b0[:, 0:256], func=AF.Exp, scale=0.125), reads=[bbS[0]], writes=[bP_])
                            k.op("act", lambda e, P_=P_, b1=bkS[1]: e.activation(out=P_[:, 256:512], in_=b1[:, 0:256], func=AF.Exp, scale=0.125), reads=[bbS[1]], adds=[bP_])
                            if mk is not None:
                                k.op("pool", lambda e, P_=P_, mk=mk: e.tensor_tensor(out=P_[:], in0=P_[:], in1=am[:, mk, :], op=ALU.mult), reads=[bam], writes=[bP_])
                            k.op("pe", lambda e, P_=P_, kb=kb, hk=hk, idx=idx: e.matmul(bkO[0:64, :], lhsT=Vt[:, kb, hk * 64:(hk + 1) * 64], rhs=P_[:], start=(idx == 0), stop=(idx == len(kbs) - 1)),
                                 reads=[bVt, bP_], writes=[bbO])
                            k.op("pe", lambda e, P_=P_, idx=idx: e.matmul(bkD[0:64, :], lhsT=ones[:, 0:64], rhs=P_[:], start=(idx == 0), stop=(idx == len(kbs) - 1)), reads=[bo, bP_], writes=[bbD])
                        for pos in range(4):
                            h = hk * 4 + GO[pos]
                            k.op("dve", lambda e, pos=pos, h=h, bkD=bkD: e.tensor_scalar(out=rc[:, pos * 128:(pos + 1) * 128], in0=bkD[0:64, pos * 128:(pos + 1) * 128],
                                                                                       scalar1=esb[:, h:h + 1], scalar2=None, op0=ALU.add),
                                 reads=[bbD, besb], writes=[brc] if pos == 0 else (), adds=[brc] if pos else ())
                        k.op("dve", lambda e: e.reciprocal(out=rc[:], in_=rc[:]), writes=[brc])
                        for pos in range(4):
                            g = GO[pos]
                            k.op("dve", lambda e, pos=pos, g=g, bkO=bkO, hk=hk, cols=cols: e.tensor_tensor(out=obt[:, hk, g, cols], in0=bkO[0:64, pos * 128:(pos + 1) * 128],
                                                                                                      in1=rc[:, pos * 128:(pos + 1) * 128], op=ALU.mult), reads=[bbO, brc], adds=[bobt])
                for hk in range(2):
                    for g in range(4):
                        h = hk * 4 + g
                        k.dma(mixT.ap()[2 + h // 2][(h % 2) * 64:(h % 2 + 1) * 64, t0:t0 + W], obt[:, hk, g, :W], reads=[bobt], adds=[bmix], q="sp" if g % 2 == 0 else "pool")
                k.op("dve", lambda e: e.memset(rc[:, 0:1], 0.0), reads=[], writes=[bobt, brc])
            k.barrier()

        with ExitStack() as st:
          if i_phase_on(8):
            wo = k.sb("wo", [128, 8, 1024], BF16, stack=st); bwo = Buf("wo")
            wst = [k.sb(f"wst5{i}", [128, 1024], stack=st) for i in range(2)]; bwst = [Buf("wst0"), Buf("wst1")]
            wv = w_out.ap()[l].rearrange("(kc p) n -> p kc n", p=128)
            for kc in range(8):
                k.dma(wst[kc % 2][:], wv[:, kc, :], writes=[bwst[kc % 2]], q="sp" if kc % 2 == 0 else "pool")
                cast(wo[:, kc, :], wst[kc % 2][:], [bwst[kc % 2]], adds=[bwo])
            Mb = k.sb("Mb5", [128, 8, 512], BF16, stack=st); bMb = Buf("Mb5")
            X = k.sb("X5", [128, 8, 512], stack=st); bX = Buf("X5")
            M = k.sb("M5", [128, 8, 512], stack=st); bM = Buf("M5")
            Y = k.sb("Y5", [128, 8, 512], stack=st); bY = Buf("Y5")
            SQ = k.sb("SQ5", [128, 8, 512], stack=st); bSQ = Buf("SQ5")
            rs = k.sb("rs5", [128, 512], stack=st); brs = Buf("rs5")
            xv = xsrc(l).ap().rearrange("c p t -> p c t")
            xo = xs.ap().rearrange("c p t -> p c t")
            for (t0, W, who) in TILES:
                if last and who == 1:
                    continue
                k.dma(X[:, :, :W], xv[:, :, t0:t0 + W], reads=[bxs], writes=[bX])
                k.dma(M[:, :, :W], mixT.ap().rearrange("c p t -> p c t")[:, :, t0:t0 + W], reads=[bmix], writes=[bM], q="pool")
                for kc in range(8):
                    cast(Mb[:, kc, :W], M[:, kc, :W], [bM], writes=[bMb] if kc == 0 else (), adds=[bMb] if kc else ())
                for j in range(8):
                    bk, bb = k.bank()
                    for kc in range(8):
                        k.op("pe", lambda e, kc=kc, j=j: e.matmul(bk[:, :W], lhsT=wo[:, kc, j * 128:(j + 1) * 128], rhs=Mb[:, kc, :W], start=(kc == 0), stop=(kc == 7)),
                             reads=[bwo, bMb], writes=[bb])
                    evac(Y[:, j, :W], bk[:, :W], [bb], adds=[bY])
                rmsn(st, Y, bY, W, SQ, bSQ, rs, brs)
                resid_update(X, bX, Y, bY, rs, brs, W, 2, who)
                k.dma(xo[:, :, t0:t0 + W], X[:, :, :W], reads=[bX], adds=[bxs])
            k.barrier()

        with ExitStack() as st:
          if i_phase_on(9):
            wgb = k.sb("wgb", [128, 8, 5632], BF16, stack=st); bwgb = Buf("wgb")
            wdb = k.sb("wdb", [128, 22, 1024], BF16, stack=st); bwdb = Buf("wdb")
            wst = [k.sb(f"wst6{i}", [128, 1024], stack=st) for i in range(2)]; bwst = [Buf("wst0"), Buf("wst1")]
            gv_ = w_gu.ap()[l].rearrange("(kc p) n -> p kc n", p=128)
            dv_ = w_down.ap()[l].rearrange("(j p) n -> p j n", p=128)
            wi = 0
            for kc in range(8):
                for cb in range(6):
                    c0_ = cb * 1024; cw = min(1024, 5632 - c0_)
                    k.dma(wst[wi % 2][:, :cw], gv_[:, kc, c0_:c0_ + cw], writes=[bwst[wi % 2]], q="sp" if wi % 2 == 0 else "pool")
                    cast(wgb[:, kc, c0_:c0_ + cw], wst[wi % 2][:, :cw], [bwst[wi % 2]], adds=[bwgb])
                    wi += 1
            for j in range(22):
                k.dma(wst[wi % 2][:], dv_[:, j, :], writes=[bwst[wi % 2]], q="sp" if wi % 2 == 0 else "pool")
                cast(wdb[:, j, :], wst[wi % 2][:], [bwst[wi % 2]], adds=[bwdb])
                wi += 1
            X = k.sb("X6", [128, 8, 256], stack=st); bX = Buf("X6")
            H = k.sb("H6", [128, 8, 256], BF16, stack=st); bH = Buf("H6")
            SQ = k.sb("SQ6", [128, 8, 256], stack=st); bSQ = Buf("SQ6")
            Y = k.sb("Y6", [128, 8, 256], stack=st); bY = Buf("Y6")
            rs = k.sb("rs6", [128, 256], stack=st); brs = Buf("rs6")
            actb = k.sb("actb", [128, 22, 256], BF16, stack=st); bact = Buf("actb")
            sl = [k.sb(f"sl6{i}", [128, 256], stack=st) for i in range(2)]; bsl = [Buf("sl0"), Buf("sl1")]
            xo = xs.ap().rearrange("c p t -> p c t")
            for (t0, W, who) in TILES5:
                if last and who == 1:
                    continue
                k.dma(X[:, :, :W], xo[:, :, t0:t0 + W], reads=[bxs], writes=[bX])
                rmsn(st, X, bX, W, SQ, bSQ, rs, brs)
                modulate(H, bH, X, bX, rs, brs, W, 3, 4, who)
                for j in range(22):
                    bkg, bbg = k.bank(); bku, bbu = k.bank()
                    for kc in range(8):
                        k.op("pe", lambda e, kc=kc, j=j: e.matmul(bkg[:, :W], lhsT=wgb[:, kc, j * 128:(j + 1) * 128], rhs=H[:, kc, :W], start=(kc == 0), stop=(kc == 7)),
                             reads=[bwgb, bH], writes=[bbg])
                    for kc in range(8):
                        k.op("pe", lambda e, kc=kc, j=j: e.matmul(bku[:, :W], lhsT=wgb[:, kc, 2816 + j * 128:2816 + (j + 1) * 128], rhs=H[:, kc, :W], start=(kc == 0), stop=(kc == 7)),
                             reads=[bwgb, bH], writes=[bbu])
                    s_ = sl[j % 2]; bs_ = bsl[j % 2]
                    k.op("act", lambda e, bkg=bkg, s_=s_: e.activation(out=s_[:, :W], in_=bkg[:, :W], func=AF.Silu), reads=[bbg], writes=[bs_])
                    k.op("dve", lambda e, j=j, bku=bku, s_=s_: e.tensor_tensor(out=actb[:, j, :W], in0=bku[:, :W], in1=s_[:, :W], op=ALU.mult), reads=[bbu, bs_],
                         writes=[bact] if j == 0 else (), adds=[bact] if j else ())
                for jo in range(8):
                    bk, bb = k.bank()
                    for j in range(22):
                        k.op("pe", lambda e, j=j, jo=jo: e.matmul(bk[:, :W], lhsT=wdb[:, j, jo * 128:(jo + 1) * 128], rhs=actb[:, j, :W], start=(j == 0), stop=(j == 21)),
                             reads=[bwdb, bact], writes=[bb])
                    evac(Y[:, jo, :W], bk[:, :W], [bb], writes=[bY] if jo == 0 else (), adds=[bY] if jo else ())
                rmsn(st, Y, bY, W, SQ, bSQ, rs, brs)
                resid_update(X, bX, Y, bY, rs, brs, W, 5, who)
                if l == L - 1:
                    k.dma(outT.ap().rearrange("c p t -> p c t")[:, :, t0 - CT:t0 - CT + W], X[:, :, :W], reads=[bX], adds=[bout])
                else:
                    k.dma(xo[:, :, t0:t0 + W], X[:, :, :W], reads=[bX], adds=[bxs])
            k.barrier()
    k.barrier()
    return nc, k


def host_consts():
    ident = np.eye(128, dtype=np.float32)
    bones = np.zeros((128, 128), np.float32); bones[:64, :64] = 1; bones[64:, 64:] = 1
    Rm = np.zeros((128, 128), np.float32)
    for m in range(128):
        i = m % 32
        if i < 16:
            Rm[m + 16, m] = -1.0
        else:
            Rm[m - 16, m] = 1.0
    cc = np.concatenate([ident, bones, Rm], axis=1)
    smL = np.zeros((128, 9, 4), np.float32); smC = np.zeros((128, 9, 2), np.float32)
    for ci in range(9):
        for p in range(128):
            if ci < 3:
                ch = ci * 128 + p; qd = ch // 96; half = 0 if ch < 192 else 1
            else:
                ch = (ci - 3) * 128 + p; qd = ch // 192; half = 0 if ch < 384 else 1
            smL[p, ci, qd] = 1; smC[p, ci, half] = 1
    s = np.arange(64)[:, None]; t = np.arange(64)[None, :]
    mS = [(s < t), (s > t)]; mST = [(s > t), (s < t)]; mI = [(s <= t), (s >= t)]
    cm = np.zeros((64, 4, 512), np.float32)
    for d in range(2):
        for h in range(4):
            q = d * 4 + h
            cm[:, 0, q * 64:(q + 1) * 64] = mS[d]
            cm[:, 1, q * 64:(q + 1) * 64] = mST[d]
            cm[:, 2, q * 64:(q + 1) * 64] = mI[d]
            cm[:, 3, q * 64:(q + 1) * 64] = np.eye(64)
    j = np.arange(128)[:, None]; i = np.arange(128)[None, :]
    am = np.zeros((128, 2, 512), np.float32)
    for g in range(4):
        am[:, 0, g * 128:(g + 1) * 128] = (j >= i)
        am[:, 1, g * 128:(g + 1) * 128] = (j <= i)
    rs = np.ones((128, 512), np.float32); rs[:, ::64] = 0
    tt = np.arange(T)
    pos = np.stack([tt // 64, tt % 64], -1).astype(np.float32)
    inv = (10000.0 ** (-np.arange(16, dtype=np.float32) / 16)).astype(np.float32)
    ang = pos[:, :, None] * inv
    cosT = np.zeros((128, T), np.float32); sinT = np.zeros((128, T), np.float32)
    for p in range(128):
        dd = p % 64; ax = dd // 32; fr = dd % 16
        cosT[p] = np.cos(ang[:, ax, fr]); sinT[p] = np.sin(ang[:, ax, fr])
    return dict(cc=cc, smL=smL, smC=smC, cm=cm, am=am, rs=rs, cosT=cosT, sinT=sinT)


def cmaj(v, n):
    return np.ascontiguousarray(np.asarray(v, np.float32).reshape(n, 128).T)


def host_inputs(inp, b):
    f = lambda a: np.ascontiguousarray(np.asarray(a, np.float32))
    xall = np.concatenate([inp["ctx"][b], inp["x"][b]], axis=0)
    xin = np.ascontiguousarray(xall.T.reshape(8, 128, TT))
    cin = np.stack([cmaj(inp["c"][b], 8), cmaj(inp["c_ctx"], 8)], axis=-1)
    pp = np.zeros((L, 128, NP), np.float32)
    for l in range(L):
        for i in range(4):
            pp[l, :, NG + i * 8:NG + i * 8 + 8] = cmaj(inp["norm_g"][l, i], 8)
        pp[l, :, BM:BM + 48] = cmaj(inp["b_mod"][l], 48)
        pp[l, :, MU:MU + 9] = cmaj(inp["rw_mu"][l], 9)
        pp[l, :, KK:KK + 2] = cmaj(inp["rw_kk"][l], 2)
        pp[l, :, KA:KA + 2] = cmaj(inp["rw_ka"][l], 2)
        pp[l, :, RK:RK + 2] = cmaj(inp["rw_rk"][l].reshape(-1), 2)
        pp[l, :, LXG:LXG + 2] = cmaj(inp["rw_lnx_g"][l], 2)
        pp[l, :, LXB:LXB + 2] = cmaj(inp["rw_lnx_b"][l], 2)
        pp[l, :, W0:W0 + 4] = cmaj(inp["rw_w0"][l].reshape(-1), 4)
        pp[l, :, A0:A0 + 4] = cmaj(inp["rw_a0"][l].reshape(-1), 4)
        pp[l, :, SLG:SLG + 2] = cmaj(inp["sg_ln_g"][l], 2)
        pp[l, :, SLB:SLB + 2] = cmaj(inp["sg_ln_b"][l], 2)
    m = dict(xin=xin, cin=f(cin), w_mod=f(inp["w_mod"]), w_in=f(inp["w_in"]), w_out=f(inp["w_out"]), w_gu=f(inp["w_gu"]), w_down=f(inp["w_down"]),
             pp=pp, sgwT=f(np.transpose(inp["sg_w"], (0, 3, 1, 2))), sgb=f(inp["sg_b"]).reshape(L, 1, 512), sink=f(inp["attn_sink"]).reshape(L, 1, 8),
             w2=f(inp["rw_w2"]).reshape(L, 128, 256), a2=f(inp["rw_a2"]).reshape(L, 128, 256), g2=f(inp["rw_g2"]))
    m.update(host_consts())
    return m


def kernel(**inp):
    inp = {k_: np.asarray(v) for k_, v in inp.items()}
    nc, _ = build()
    maps = [host_inputs(inp, b % 4) for b in range(4)]
    in_maps = [maps[i % 4] for i in range(8)]
    res = run_bass_kernel_spmd(nc, in_maps, core_ids=list(range(8)))
    out = np.zeros((4, T, D), np.float32)
    for b in range(4):
        o = res.results[b]["outT"]
        out[b] = o.reshape(D, T).T
    return out
```

```python
import numpy as np
import concourse.bass as bass
import concourse.mybir as mybir
from concourse.bass_utils import run_bass_kernel_spmd
from contextlib import ExitStack

F32 = mybir.dt.float32
BF16 = mybir.dt.bfloat16
AF = mybir.ActivationFunctionType
ALU = mybir.AluOpType

D = 1024; T = 4096; CT = 256; TT = T + CT; L = 4
NTOKB = TT // 128
NCHK = TT // 64
C0 = float(np.exp(-0.5))
TILES = [(0, 256, 1)] + [(256 + 512 * i, 512, 0) for i in range(8)]
TILES5 = [(0, 256, 1)] + [(256 + 256 * i, 256, 0) for i in range(16)]
NG = 0; BM = 32; MU = 80; KK = 89; KA = 91; RK = 93; LXG = 95; LXB = 97; W0 = 99; A0 = 103; SLG = 107; SLB = 109; NP = 111
SRC_CH = [8, 9, 10, 13, 14, 15, 16, 17, 18]

SAME_ENGINE_SYNC = True
EPOCH = 12000
DMA_EPOCH = 1500
NDMASEM = 6


class Buf:
    __slots__ = ("name", "w", "r")

    def __init__(self, name):
        self.name = name
        self.w = {}
        self.r = {}


class KB:
    def __init__(self, nc):
        self.nc = nc
        self.es = ExitStack()
        self.E = {"pe": nc.tensor, "act": nc.scalar, "dve": nc.vector, "pool": nc.gpsimd, "sp": nc.sync}
        self.sem = {}
        self.cnt = {}
        self.seen = {e: {} for e in self.E}
        self.nsem = 0
        for e in self.E:
            self._new_sem(e)
        self.dsem = {}
        self.dcnt = {}
        self.drr = {}
        for q in ("sp", "pool"):
            self.dsem[q] = [self._alloc_sem(f"d{q}{i}") for i in range(NDMASEM)]
            self.dcnt[q] = [0] * NDMASEM
            self.drr[q] = 0
        self.ninstr = 0
        self.banks = []
        self.bi = 0

    def _alloc_sem(self, name):
        self.nsem += 1
        h = self.nc.alloc_semaphore(name=f"{name}_{self.nsem}")
        self._keep = getattr(self, "_keep", [])
        self._keep.append(h)
        return h

    def _new_sem(self, e):
        self.sem[e] = self._alloc_sem(f"s{e}")
        self.cnt[e] = 0

    def sb(self, name, shape, dtype=F32, stack=None):
        self.nsb = getattr(self, "nsb", 0) + 1
        return (stack or self.es).enter_context(self.nc.sbuf_tensor(f"sb{self.nsb}_{name}", list(shape), dtype))

    def mkbanks(self):
        for i in range(8):
            t = self.es.enter_context(self.nc.psum_tensor(f"bank{i}", [128, 512], F32))
            self.banks.append((t, Buf(f"bank{i}")))

    def bank(self):
        b = self.banks[self.bi]
        self.bi = (self.bi + 1) % 8
        return b

    def _wait(self, e, ev):
        sem, val, src = ev
        if src == e and (not SAME_ENGINE_SYNC or e == "pe"):
            return
        key = id(sem)
        if self.seen[e].get(key, 0) >= val:
            return
        self.E[e].wait_ge(sem, val)
        self.seen[e][key] = val

    def _deps(self, e, reads, writes, adds, isdma=False):
        for b in reads:
            for ev in b.w.values():
                self._wait(e, ev)
        for b in writes:
            for ev in b.w.values():
                self._wait(e, ev)
            for ev in b.r.values():
                self._wait(e, ev)
        for b in adds:
            for ev in b.r.values():
                self._wait(e, ev)
            for ev in b.w.values():
                if not (isdma and ev[2] == "dma"):
                    self._wait(e, ev)

    def _commit(self, key, ev, reads, writes, adds):
        for b in reads:
            b.r[key] = ev
        for b in writes:
            b.w = {key: ev}
            b.r = {}
        for b in adds:
            b.w[key] = ev

    def op(self, e, fn, reads=(), writes=(), adds=()):
        self._deps(e, reads, writes, adds)
        ins = fn(self.E[e])
        if self.cnt[e] >= EPOCH:
            self._new_sem(e)
        self.cnt[e] += 1
        ins.then_inc(self.sem[e], 1)
        ev = (self.sem[e], self.cnt[e], e)
        self._commit(e, ev, reads, writes, adds)
        self.ninstr += 1
        return ev

    def dma(self, out, in_, reads=(), writes=(), adds=(), q="sp"):
        self._deps(q, reads, writes, adds, True)
        j = self.drr[q]
        self.drr[q] = (j + 1) % NDMASEM
        if self.dcnt[q][j] > 0:
            self._wait(q, (self.dsem[q][j], self.dcnt[q][j] * 16, "dma"))
        if self.dcnt[q][j] >= DMA_EPOCH:
            self.dsem[q][j] = self._alloc_sem(f"d{q}{j}")
            self.dcnt[q][j] = 0
        sem = self.dsem[q][j]
        self.E[q].dma_start(out=out, in_=in_).then_inc(sem, 16)
        self.dcnt[q][j] += 1
        ev = (sem, self.dcnt[q][j] * 16, "dma")
        self._commit(id(sem), ev, reads, writes, adds)
        self.ninstr += 1
        return ev

    def barrier(self):
        if getattr(self, "ptick", None) is not None:
            self.op("pool", lambda e: e.memset(self.ptick[:], 0.0), writes=[self.ptickb])
        evs = []
        for e in ("pe", "act", "dve", "pool", "sp"):
            if self.cnt[e] > 0:
                evs.append((self.sem[e], self.cnt[e], e))
        for q in self.dsem:
            for j in range(NDMASEM):
                if self.dcnt[q][j] > 0:
                    evs.append((self.dsem[q][j], self.dcnt[q][j] * 16, "dma"))
        for e in self.E:
            for ev in evs:
                if ev[2] != e:
                    self._wait(e, ev)


PHASES = None


CURL = [0]
PHASES_L = None


def i_phase_on(i):
    if PHASES_L is not None:
        return i in PHASES_L[CURL[0]]
    return PHASES is None or i in PHASES


def build(nlayers=L, debug=False):
    nc = bass.Bass("TRN2", target_bir_lowering=False)
    k = KB(nc)
    k.mkbanks()

    def din(name, shape):
        return nc.dram_tensor(name, list(shape), F32, kind="ExternalInput")

    def dscr(name, shape, out=False):
        return nc.dram_tensor(name, list(shape), F32, kind="ExternalOutput" if (out and debug) else "Internal")

    xin = din("xin", [8, 128, TT])
    cin = din("cin", [128, 8, 2])
    w_mod = din("w_mod", [L, D, 6 * D]); w_in = din("w_in", [L, D, 2432]); w_out = din("w_out", [L, D, D])
    w_gu = din("w_gu", [L, D, 5632]); w_down = din("w_down", [L, 2816, D])
    ppd = din("pp", [L, 128, NP])
    sgwT = din("sgwT", [L, 128, 4, 128]); sgb = din("sgb", [L, 1, 512]); sink = din("sink", [L, 1, 8])
    w2d = din("w2", [L, 128, 256]); a2d = din("a2", [L, 128, 256]); g2d = din("g2", [L, 128, 256])
    cc = din("cc", [128, 384]); smL = din("smL", [128, 9, 4]); smC = din("smC", [128, 9, 2])
    cmd = din("cm", [64, 4, 512]); amd = din("am", [128, 2, 512]); rsd = din("rs", [128, 512])
    cosd = din("cosT", [128, T]); sind = din("sinT", [128, T])
    outT = nc.dram_tensor("outT", [8, 128, T], F32, kind="ExternalOutput")

    xs = dscr("xs", [8, 128, TT], True); bxs = Buf("xs")
    pxT = dscr("pxT", [19, 128, TT], True); bpx = Buf("pxT")
    mixT = dscr("mixT", [8, 128, TT], True); bmix = Buf("mixT")
    rmT = dscr("rmT", [9, 128, TT], True); brm = Buf("rmT")
    scin = dscr("scin", [4, 128, 4, TT], True); bscin = Buf("scin")
    gcd = dscr("gcd", [4, 128, NCHK], True); bgcd = Buf("gcd")
    rksT = dscr("rksT", [2, 128, TT], True); brks = Buf("rksT")
    yT = dscr("yT", [2, 256, TT], True); byT = Buf("yT")
    bout = Buf("outT")

    cst = k.sb("cst", [128, 384]); bc = Buf("cst")
    k.dma(cst[:], cc.ap(), writes=[bc])
    ident = cst[:, 0:128]; bones = cst[:, 128:256]; Rm = cst[:, 256:384]
    k.ptick = k.sb("ptick", [128, 8]); k.ptickb = Buf("ptick")
    ones = k.sb("ones", [128, 128]); bo = Buf("ones")
    k.op("dve", lambda e: e.memset(ones[:], 1.0), writes=[bo])
    cact = k.sb("cact", [128, 8, 2]); bca = Buf("cact")
    k.dma(cact[:], cin.ap(), writes=[bca])
    k.op("act", lambda e: e.activation(out=cact[:], in_=cact[:], func=AF.Silu), writes=[bca])
    pp = k.sb("ppt", [128, NP]); bpp = Buf("pp")
    modt = k.sb("modt", [128, 48, 2]); bmod = Buf("modt")
    coef = k.sb("coef", [128, 6, 8, 2]); bcoef = Buf("coef")
    cfs = k.sb("cfs", [128, 9, 7]); bcfs = Buf("cfs")
    smLs = k.sb("smLs", [128, 9, 4]); smCs = k.sb("smCs", [128, 9, 2]); bsm = Buf("sm")
    k.dma(smLs[:], smL.ap(), adds=[bsm]); k.dma(smCs[:], smC.ap(), adds=[bsm])
    omka = k.sb("omka", [128, 2]); bomka = Buf("omka")
    k.barrier()

    def xsrc(l):
        return xin if l == 0 else xs

    def rmsn(st, X, bX, W, SQ, bSQ, rs, brs):
        k.op("act", lambda e: e.activation(out=SQ[:, :, :W], in_=X[:, :, :W], func=AF.Square), reads=[bX], writes=[bSQ])
        bk, bb = k.bank()
        for c in range(8):
            k.op("pe", lambda e, c=c: e.matmul(bk[:, :W], lhsT=ones[:], rhs=SQ[:, c, :W], start=(c == 0), stop=(c == 7)),
                 reads=[bSQ, bo], writes=[bb])
        k.op("dve", lambda e: e.tensor_scalar(out=rs[:, :W], in0=bk[:, :W], scalar1=1.0 / D, scalar2=1e-6, op0=ALU.mult, op1=ALU.add),
             reads=[bb], writes=[brs])
        k.op("act", lambda e: e.activation(out=rs[:, :W], in_=rs[:, :W], func=AF.Sqrt), writes=[brs])
        k.op("dve", lambda e: e.reciprocal(out=rs[:, :W], in_=rs[:, :W]), writes=[brs])

    t32 = [k.sb(f"t32_{i}", [128, 512]) for i in range(2)]; bt32 = [Buf("t32a"), Buf("t32b")]

    def modulate(H, bH, X, bX, rs, brs, W, ia, ib, who):
        for c in range(8):
            t_ = t32[c % 2]; bt_ = bt32[c % 2]
            k.op("dve", lambda e, c=c, t_=t_: e.scalar_tensor_tensor(out=t_[:, :W], in0=X[:, c, :W], scalar=coef[:, ia, c, who:who + 1],
                                                                   in1=rs[:, :W], op0=ALU.mult, op1=ALU.mult),
                 reads=[bX, brs, bcoef], writes=[bt_])
            k.op("act", lambda e, c=c, t_=t_: e.activation(out=H[:, c, :W], in_=t_[:, :W], func=AF.Identity, bias=coef[:, ib, c, who:who + 1]),
                 reads=[bcoef, bt_], writes=[bH] if c == 0 else (), adds=[bH] if c else ())

    castrr = [0]

    def cast(out, in_, reads, writes=(), adds=()):
        castrr[0] = (castrr[0] + 1) % 2
        if castrr[0] == 0:
            k.op("act", lambda e: e.copy(out=out, in_=in_), reads=reads, writes=writes, adds=adds)
        else:
            k.op("dve", lambda e: e.tensor_copy(out=out, in_=in_), reads=reads, writes=writes, adds=adds)

    def resid_update(X, bX, Y, bY, rs, brs, W, ig, who):
        for j in range(8):
            k.op("dve", lambda e, j=j: e.scalar_tensor_tensor(out=Y[:, j, :W], in0=Y[:, j, :W], scalar=coef[:, ig, j, who:who + 1],
                                                             in1=rs[:, :W], op0=ALU.mult, op1=ALU.mult),
                 reads=[brs, bcoef], writes=[bY])
            k.op("dve", lambda e, j=j: e.tensor_tensor(out=X[:, j, :W], in0=X[:, j, :W], in1=Y[:, j, :W], op=ALU.add),
                 reads=[bY], writes=[bX])

    evflip = [0]

    def evac(out, in_, reads, writes=(), adds=()):
        evflip[0] ^= 1
        if evflip[0]:
            k.op("act", lambda e: e.copy(out=out, in_=in_), reads=reads, writes=writes, adds=adds)
        else:
            k.op("dve", lambda e: e.tensor_copy(out=out, in_=in_), reads=reads, writes=writes, adds=adds)

    for l in range(nlayers):
        last = (l == L - 1)
        CURL[0] = l
        with ExitStack() as st:
          if i_phase_on(0):
            k.dma(pp[:], ppd.ap()[l], writes=[bpp])
            wm = [k.sb(f"wm{i}", [128, 8, 1024], stack=st) for i in range(2)]; bwm = [Buf("wm0"), Buf("wm1")]
            for m in range(6):
                k.dma(wm[m % 2][:], w_mod.ap()[l][:, m * 1024:(m + 1) * 1024].rearrange("(kc p) n -> p kc n", p=128), writes=[bwm[m % 2]],
                      q="sp" if m % 2 == 0 else "pool")
                for c in range(8):
                    bk, bb = k.bank()
                    for kc in range(8):
                        k.op("pe", lambda e, kc=kc, c=c, m=m: e.matmul(bk[:, 0:2], lhsT=wm[m % 2][:, kc, c * 128:(c + 1) * 128], rhs=cact[:, kc, :],
                                                                      start=(kc == 0), stop=(kc == 7)), reads=[bwm[m % 2], bca], writes=[bb])
                    j = m * 8 + c
                    k.op("dve", lambda e, j=j: e.tensor_scalar(out=modt[:, j, :], in0=bk[:, 0:2], scalar1=pp[:, BM + j:BM + j + 1], scalar2=None, op0=ALU.add),
                         reads=[bb, bpp], adds=[bmod])
            for who in range(2):
                for (ic, isc, ig) in ((0, 1, 0), (3, 4, 2)):
                    k.op("dve", lambda e, who=who, ic=ic, isc=isc, ig=ig: e.tensor_tensor(out=coef[:, ic, :, who], in0=modt[:, isc * 8:isc * 8 + 8, who],
                                                                                        in1=pp[:, NG + ig * 8:NG + ig * 8 + 8], op=ALU.mult),
                         reads=[bmod, bpp], writes=[bcoef])
                    k.op("dve", lambda e, who=who, ic=ic, ig=ig: e.tensor_tensor(out=coef[:, ic, :, who], in0=coef[:, ic, :, who],
                                                                               in1=pp[:, NG + ig * 8:NG + ig * 8 + 8], op=ALU.add), writes=[bcoef])
                for (ic, ish) in ((1, 0), (4, 3)):
                    k.op("dve", lambda e, who=who, ic=ic, ish=ish: e.tensor_copy(out=coef[:, ic, :, who], in_=modt[:, ish * 8:ish * 8 + 8, who]),
                         reads=[bmod], writes=[bcoef])
                for (ic, igt, ig) in ((2, 2, 1), (5, 5, 3)):
                    k.op("dve", lambda e, who=who, ic=ic, igt=igt, ig=ig: e.tensor_tensor(out=coef[:, ic, :, who], in0=modt[:, igt * 8:igt * 8 + 8, who],
                                                                                        in1=pp[:, NG + ig * 8:NG + ig * 8 + 8], op=ALU.mult),
                         reads=[bmod, bpp], writes=[bcoef])
            k.op("dve", lambda e: e.tensor_scalar(out=cfs[:, :, 0], in0=pp[:, MU:MU + 9], scalar1=-1.0, scalar2=1.0, op0=ALU.mult, op1=ALU.add),
                 reads=[bpp], writes=[bcfs])
            for dd in range(4):
                k.op("dve", lambda e, dd=dd: e.tensor_tensor(out=cfs[:, :, 1 + dd], in0=pp[:, MU:MU + 9], in1=smLs[:, :, dd], op=ALU.mult),
                     reads=[bpp, bsm], writes=[bcfs])
            for dd in range(2):
                k.op("dve", lambda e, dd=dd: e.tensor_tensor(out=cfs[:, :, 5 + dd], in0=pp[:, MU:MU + 9], in1=smCs[:, :, dd], op=ALU.mult),
                     reads=[bpp, bsm], writes=[bcfs])
            k.op("dve", lambda e: e.tensor_scalar(out=omka[:], in0=pp[:, KA:KA + 2], scalar1=-1.0, scalar2=1.0, op0=ALU.mult, op1=ALU.add),
                 reads=[bpp], writes=[bomka])
            k.barrier()

        with ExitStack() as st:
          if i_phase_on(1):
            win = k.sb("win", [128, 8, 2432], BF16, stack=st); bwin = Buf("win")
            wst = [k.sb(f"wst{i}", [128, 2432], stack=st) for i in range(2)]; bwst = [Buf("wst0"), Buf("wst1")]
            wv = w_in.ap()[l].rearrange("(kc p) n -> p kc n", p=128)
            for kc in range(8):
                k.dma(wst[kc % 2][:], wv[:, kc, :], writes=[bwst[kc % 2]], q="sp" if kc % 2 == 0 else "pool")
                cast(win[:, kc, :], wst[kc % 2][:], [bwst[kc % 2]], adds=[bwin])
            X = k.sb("X", [128, 8, 512], stack=st); bX = Buf("X")
            SQ = k.sb("SQ", [128, 8, 512], stack=st); bSQ = Buf("SQ")
            H = k.sb("H", [128, 8, 512], BF16, stack=st); bH = Buf("H")
            rs = k.sb("rs", [128, 512], stack=st); brs = Buf("rs")
            stg = [k.sb(f"stg{i}", [128, 512], stack=st) for i in range(2)]; bstg = [Buf("s0"), Buf("s1")]
            xv = xsrc(l).ap().rearrange("c p t -> p c t")
            for (t0, W, who) in TILES:
                k.dma(X[:, :, :W], xv[:, :, t0:t0 + W], reads=[bxs], writes=[bX])
                rmsn(st, X, bX, W, SQ, bSQ, rs, brs)
                modulate(H, bH, X, bX, rs, brs, W, 0, 1, who)
                for j in range(19):
                    bk, bb = k.bank()
                    for kc in range(8):
                        k.op("pe", lambda e, kc=kc, j=j: e.matmul(bk[:, :W], lhsT=win[:, kc, j * 128:(j + 1) * 128], rhs=H[:, kc, :W],
                                                                 start=(kc == 0), stop=(kc == 7)), reads=[bwin, bH], writes=[bb])
                    s = stg[j % 2]
                    evac(s[:, :W], bk[:, :W], [bb], writes=[bstg[j % 2]])
                    k.dma(pxT.ap()[j][:, t0:t0 + W], s[:, :W], reads=[bstg[j % 2]], adds=[bpx])
            k.barrier()

        with ExitStack() as st:
          if i_phase_on(2):
            fb = [k.sb(f"fb{i}", [128, TT], stack=st) for i in range(2)]; bfb = [Buf("fb0"), Buf("fb1")]
            ob = [k.sb(f"ob{i}", [128, TT], stack=st) for i in range(2)]; bob = [Buf("ob0"), Buf("ob1")]
            for ci in range(9):
                f = fb[ci % 2]; o = ob[ci % 2]; bf = bfb[ci % 2]; bo_ = bob[ci % 2]
                k.dma(f[:], pxT.ap()[SRC_CH[ci]], reads=[bpx], writes=[bf], q="sp" if ci % 2 == 0 else "pool")
                k.op("act", lambda e, f=f, o=o, ci=ci: e.activation(out=o[:], in_=f[:], func=AF.Identity, scale=cfs[:, ci, 0:1]), reads=[bf, bcfs], writes=[bo_])
                fl = f[:, CT:].rearrange("p (r w) -> p r w", w=64)
                ol = o[:, CT:].rearrange("p (r w) -> p r w", w=64)
                views = [(ol[:, :, 1:64], fl[:, :, 0:63]), (ol[:, :, 0:63], fl[:, :, 1:64]), (ol[:, 1:64, :], fl[:, 0:63, :]), (ol[:, 0:63, :], fl[:, 1:64, :]),
                         (o[:, 1:CT], f[:, 0:CT - 1]), (o[:, 0:CT - 1], f[:, 1:CT])]
                for di, (ov, fv) in enumerate(views):
                    eng = "dve"
                    k.op(eng, lambda e, ov=ov, fv=fv, ci=ci, di=di: e.scalar_tensor_tensor(out=ov, in0=fv, scalar=cfs[:, ci, 1 + di:2 + di], in1=ov,
                                                                                          op0=ALU.mult, op1=ALU.add), reads=[bf, bcfs], writes=[bo_])
                k.dma(rmT.ap()[ci], o[:], reads=[bo_], adds=[brm])
            k.barrier()

        with ExitStack() as st:
          if i_phase_on(3):
            w2 = k.sb("w2s", [128, 256], stack=st); a2 = k.sb("a2s", [128, 256], stack=st); bw2 = Buf("w2")
            k.dma(w2[:], w2d.ap()[l], adds=[bw2]); k.dma(a2[:], a2d.ap()[l], adds=[bw2])
            rsm = k.sb("rsm", [128, 512], stack=st); brsm = Buf("rsm")
            k.dma(rsm[:], rsd.ap(), writes=[brsm])
            rm = k.sb("rm", [128, 9, 512], stack=st); brmt = Buf("rm")
            tw = k.sb("tw", [128, 512], stack=st); btw = Buf("tw")
            kkr = k.sb("kkr", [128, 2, 512], stack=st); bkkr = Buf("kkr")
            sqk = k.sb("sqk", [128, 2, 512], stack=st); bsqk = Buf("sqk")
            kk = k.sb("kk", [128, 2, 512], stack=st); bkk = Buf("kk")
            nr = k.sb("nr", [128, 512], stack=st); bnr = Buf("nr")
            sg = k.sb("sg", [128, 512], stack=st); bsg = Buf("sg")
            aa = k.sb("aa", [128, 512], stack=st); baa = Buf("aa")
            cs = k.sb("cs", [128, 512], stack=st); bcs = Buf("cs")
            Ei = k.sb("Ei", [128, 512], stack=st); bEi = Buf("Ei")
            Ex = k.sb("Ex", [128, 512], stack=st); bEx = Buf("Ex")
            gi = k.sb("gi", [128, 512], stack=st); bgi = Buf("gi")
            gx = k.sb("gx", [128, 512], stack=st); bgx = Buf("gx")
            gv = k.sb("gv", [128, 512], stack=st); bgv = Buf("gv")
            gc = k.sb("gc", [128, 8], stack=st); bgc = Buf("gc")
            kd = k.sb("kd", [128, 512], stack=st); bkd = Buf("kd")
            bt = k.sb("bt", [128, 512], stack=st); bbt = Buf("bt")
            kds = k.sb("kds", [128, 2, 512], stack=st); bkds = Buf("kds")
            so = [k.sb(f"so{i}", [128, 4, 512], stack=st) for i in range(2)]; bso = [Buf("so0"), Buf("so1")]
            it = 0
            for (t0, W, who) in TILES:
                nck = W // 64
                k.dma(rm[:, :, :W], rmT.ap().rearrange("c p t -> p c t")[:, :, t0:t0 + W], reads=[brm], writes=[brmt])
                k.op("act", lambda e: e.activation(out=tw[:, :W], in_=rm[:, 7, :W], func=AF.Tanh), reads=[brmt], writes=[btw])
                for hp in range(2):
                    k.op("dve", lambda e, hp=hp: e.tensor_scalar(out=kkr[:, hp, :W], in0=rm[:, 3 + hp, :W], scalar1=pp[:, KK + hp:KK + hp + 1], scalar2=None, op0=ALU.mult),
                         reads=[brmt, bpp], adds=[bkkr])
                k.op("act", lambda e: e.activation(out=sqk[:, :, :W], in_=kkr[:, :, :W], func=AF.Square), reads=[bkkr], writes=[bsqk])
                for hp in range(2):
                    bk, bb = k.bank()
                    k.op("pe", lambda e, hp=hp: e.matmul(bk[:, :W], lhsT=bones, rhs=sqk[:, hp, :W], start=True, stop=True), reads=[bsqk, bc], writes=[bb])
                    k.op("act", lambda e: e.activation(out=nr[:, :W], in_=bk[:, :W], func=AF.Sqrt), reads=[bb], writes=[bnr])
                    k.op("dve", lambda e: e.tensor_scalar(out=nr[:, :W], in0=nr[:, :W], scalar1=1e-12, scalar2=None, op0=ALU.max), writes=[bnr])
                    k.op("dve", lambda e: e.reciprocal(out=nr[:, :W], in_=nr[:, :W]), writes=[bnr])
                    k.op("dve", lambda e, hp=hp: e.tensor_tensor(out=kk[:, hp, :W], in0=kkr[:, hp, :W], in1=nr[:, :W], op=ALU.mult), reads=[bkkr, bnr], adds=[bkk])
                for hp in range(2):
                    for d in range(2):
                        bk, bb = k.bank()
                        k.op("pe", lambda e, d=d, hp=hp: e.matmul(bk[:, :W], lhsT=w2[d * 64:(d + 1) * 64, hp * 128:(hp + 1) * 128], rhs=tw[d * 64:(d + 1) * 64, :W],
                                                                 start=True, stop=True), reads=[bw2, btw], writes=[bb])
                        k.op("act", lambda e, d=d, hp=hp: e.activation(out=sg[:, :W], in_=bk[:, :W], func=AF.Sigmoid, bias=pp[:, W0 + d * 2 + hp:W0 + d * 2 + hp + 1]),
                             reads=[bb, bpp], writes=[bsg])
                        bk2, bb2 = k.bank()
                        k.op("pe", lambda e, d=d, hp=hp: e.matmul(bk2[:, :W], lhsT=a2[d * 64:(d + 1) * 64, hp * 128:(hp + 1) * 128], rhs=rm[d * 64:(d + 1) * 64, 8, :W],
                                                                 start=True, stop=True), reads=[bw2, brmt], writes=[bb2])
                        k.op("act", lambda e, d=d, hp=hp: e.activation(out=aa[:, :W], in_=bk2[:, :W], func=AF.Sigmoid, bias=pp[:, A0 + d * 2 + hp:A0 + d * 2 + hp + 1]),
                             reads=[bb2, bpp], writes=[baa])
                        k.op("dve", lambda e: e.tensor_tensor_scan(out=cs[:, :W], data0=rsm[:, :W], data1=sg[:, :W], initial=0.0, op0=ALU.mult, op1=ALU.add),
                             reads=[brsm, bsg], writes=[bcs])
                        cs3 = cs[:, :W].rearrange("p (c t) -> p c t", t=64)
                        if d == 0:
                            k.op("dve", lambda e: e.tensor_tensor(out=Ex[:, :W], in0=cs[:, :W], in1=sg[:, :W], op=ALU.subtract), reads=[bcs, bsg], writes=[bEx])
                            Eiv = cs; bEiv = bcs
                        else:
                            Ex3 = Ex[:, :W].rearrange("p (c t) -> p c t", t=64)
                            k.op("dve", lambda e, cs3=cs3, Ex3=Ex3: e.tensor_tensor(out=Ex3, in0=cs3[:, :, 63:64].broadcast_to([128, nck, 64]), in1=cs3, op=ALU.subtract),
                                 reads=[bcs], writes=[bEx])
                            k.op("dve", lambda e: e.tensor_tensor(out=Ei[:, :W], in0=Ex[:, :W], in1=sg[:, :W], op=ALU.add), reads=[bEx, bsg], writes=[bEi])
                            Eiv = Ei; bEiv = bEi
                        k.op("act", lambda e, cs3=cs3: e.activation(out=gc[:, :nck], in_=cs3[:, :, 63], func=AF.Exp, scale=-C0), reads=[bcs], writes=[bgc])
                        k.op("act", lambda e, Eiv=Eiv: e.activation(out=gi[:, :W], in_=Eiv[:, :W], func=AF.Exp, scale=-C0), reads=[bEiv], writes=[bgi])
                        k.op("act", lambda e: e.activation(out=gx[:, :W], in_=Ex[:, :W], func=AF.Exp, scale=-C0), reads=[bEx], writes=[bgx])
                        k.op("act", lambda e, Eiv=Eiv: e.activation(out=gv[:, :W], in_=Eiv[:, :W], func=AF.Exp, scale=C0), reads=[bEiv], writes=[bgv])
                        s_ = so[it % 2]; bs_ = bso[it % 2]; it += 1
                        k.op("dve", lambda e, s_=s_, hp=hp: e.tensor_tensor(out=s_[:, 0, :W], in0=kk[:, hp, :W], in1=gx[:, :W], op=ALU.mult), reads=[bkk, bgx], writes=[bs_])
                        k.op("dve", lambda e, s_=s_, hp=hp: e.tensor_tensor(out=s_[:, 1, :W], in0=rm[:, hp, :W], in1=gi[:, :W], op=ALU.mult), reads=[brmt, bgi], adds=[bs_])
                        k.op("dve", lambda e, hp=hp: e.tensor_scalar(out=kd[:, :W], in0=aa[:, :W], scalar1=pp[:, KA + hp:KA + hp + 1], scalar2=omka[:, hp:hp + 1],
                                                                    op0=ALU.mult, op1=ALU.add), reads=[baa, bpp, bomka], writes=[bkd])
                        k.op("dve", lambda e, hp=hp: e.tensor_tensor(out=kd[:, :W], in0=kd[:, :W], in1=rm[:, 3 + hp, :W], op=ALU.mult), reads=[brmt], writes=[bkd])
                        k.op("dve", lambda e, s_=s_: e.tensor_tensor(out=s_[:, 2, :W], in0=kd[:, :W], in1=gv[:, :W], op=ALU.mult), reads=[bkd, bgv], adds=[bs_])
                        k.op("dve", lambda e, hp=hp: e.tensor_tensor(out=bt[:, :W], in0=aa[:, :W], in1=kk[:, hp, :W], op=ALU.mult), reads=[baa, bkk], writes=[bbt])
                        k.op("dve", lambda e, s_=s_: e.tensor_tensor(out=s_[:, 3, :W], in0=bt[:, :W], in1=gv[:, :W], op=ALU.mult), reads=[bbt, bgv], adds=[bs_])
                        if d == 0:
                            k.op("dve", lambda e, hp=hp: e.tensor_copy(out=kds[:, hp, :W], in_=kd[:, :W]), reads=[bkd], writes=[bkds])
                        else:
                            k.op("dve", lambda e, hp=hp: e.tensor_tensor(out=kds[:, hp, :W], in0=kds[:, hp, :W], in1=kd[:, :W], op=ALU.add), reads=[bkd], writes=[bkds])
                        k.dma(scin.ap()[d * 2 + hp][:, :, t0:t0 + W], s_[:, :, :W], reads=[bs_], adds=[bscin])
                        k.dma(gcd.ap()[d * 2 + hp][:, t0 // 64:t0 // 64 + nck], gc[:, :nck], reads=[bgc], adds=[bgcd])
                    k.op("dve", lambda e, hp=hp: e.scalar_tensor_tensor(out=kds[:, hp, :W], in0=kds[:, hp, :W], scalar=pp[:, RK + hp:RK + hp + 1], in1=rm[:, hp, :W],
                                                                       op0=ALU.mult, op1=ALU.mult), reads=[brmt, bpp], writes=[bkds])
                k.dma(rksT.ap().rearrange("c p t -> p c t")[:, :, t0:t0 + W], kds[:, :, :W], reads=[bkds], adds=[brks])
            k.barrier()

        with ExitStack() as st:
          if i_phase_on(4):
            cm = k.sb("cm", [64, 4, 512], stack=st); bcm = Buf("cm")
            k.dma(cm[:], cmd.ap(), writes=[bcm])
            inT = [k.sb(f"inT{d}", [64, 4, 4, 512], stack=st) for d in range(2)]; binT = [Buf("inT0"), Buf("inT1")]
            gct = [k.sb(f"gct{d}", [64, 4, 8], stack=st) for d in range(2)]; bgct = [Buf("g0"), Buf("g1")]
            vt = [k.sb(f"vt{d}", [64, 4, 512], stack=st) for d in range(2)]; bvt = [Buf("vt0"), Buf("vt1")]
            yst = [k.sb(f"yst{d}", [64, 4, 512], stack=st) for d in range(2)]; byst = [Buf("y0"), Buf("y1")]
            Hs = k.sb("Hs", [64, 8, 64], stack=st); bHs = Buf("Hs")
            k.op("dve", lambda e: e.memset(Hs[:], 0.0), writes=[bHs])
            Ktm = k.sb("Ktm", [64, 512], stack=st); bKtm = Buf("Ktm")
            NBtm = k.sb("NBtm", [64, 512], stack=st); bNB = Buf("NBtm")
            Vtm = k.sb("Vtm", [64, 512], stack=st); bVtm = Buf("Vtm")
            Pm = [k.sb(f"Pm{i}", [64, 512], stack=st) for i in range(2)]; bPm = [Buf("P0"), Buf("P1")]
            Nm = [k.sb(f"Nm{i}", [64, 512], stack=st) for i in range(2)]; bNm = [Buf("N0"), Buf("N1")]
            Am = [k.sb(f"Am{i}", [64, 512], stack=st) for i in range(2)]; bAm = [Buf("A0"), Buf("A1")]
            AKT = k.sb("AKT", [64, 512], stack=st); bAKT = Buf("AKT")
            RKT = k.sb("RKT", [64, 512], stack=st); bRKT = Buf("RKT")
            RBT = k.sb("RBT", [64, 512], stack=st); bRBT = Buf("RBT")
            RHSs = k.sb("RHSs", [64, 512], stack=st); bRHS = Buf("RHS")
            Us = k.sb("Us", [64, 512], stack=st); bUs = Buf("Us")
            Ht = k.sb("Ht", [64, 8, 64], stack=st); bHt = Buf("Ht")
            Fc = [(0, ci) for ci in range(4)] + [(ti, ci) for ti in range(1, 9) for ci in range(8)]
            Bc = [(0, ci) for ci in range(3, -1, -1)] + [(ti, ci) for ti in range(8, 0, -1) for ci in range(7, -1, -1)]
            cur = [None, None]

            def flush(d):
                ti = cur[d]
                t0, W, _ = TILES[ti]
                yv = yT.ap()[d].rearrange("(h v) t -> v h t", v=64)
                k.dma(yv[:, :, t0:t0 + W], yst[d][:, :, :W], reads=[byst[d]], adds=[byT])

            def load(d, ti):
                t0, W, _ = TILES[ti]
                for h in range(4):
                    hp = h // 2; e_ = h % 2
                    k.dma(inT[d][:, h, :, :W], scin.ap()[d * 2 + hp][e_ * 64:(e_ + 1) * 64, :, t0:t0 + W], reads=[bscin],
                          writes=[binT[d]] if h == 0 else (), adds=[binT[d]] if h else ())
                    k.dma(gct[d][:, h, :W // 64], gcd.ap()[d * 2 + hp][e_ * 64:(e_ + 1) * 64, t0 // 64:t0 // 64 + W // 64], reads=[bgcd],
                          writes=[bgct[d]] if h == 0 else (), adds=[bgct[d]] if h else ())
                    k.dma(vt[d][:, h, :W], rmT.ap()[5 + hp][e_ * 64:(e_ + 1) * 64, t0:t0 + W], reads=[brm],
                          writes=[bvt[d]] if h == 0 else (), adds=[bvt[d]] if h else ())

            def opd(d, h, kind, ci):
                return inT[d][:, h, kind, ci * 64:(ci + 1) * 64]

            def blk(t, q):
                return t[:, q * 64:(q + 1) * 64]

            import os as _os
            R3S = int(_os.environ.get('R3S', '99')); R3N = int(_os.environ.get('R3N', str(NCHK)))
            for step in range(R3N):
                tcs = [Fc[step], Bc[step]]
                for d in range(2):
                    if cur[d] != tcs[d][0]:
                        if cur[d] is not None:
                            flush(d)
                        load(d, tcs[d][0])
                        cur[d] = tcs[d][0]
                cis = [tcs[0][1], tcs[1][1]]
                if R3S < 1:
                    continue
                for kind, dst, bdst, neg in ((2, Ktm, bKtm, False), (3, NBtm, bNB, True)):
                    bk, bb = k.bank()
                    for d in range(2):
                        for h in range(4):
                            q = d * 4 + h
                            k.op("pe", lambda e, d=d, h=h, q=q, kind=kind: e.transpose(bk[0:64, q * 64:(q + 1) * 64], opd(d, h, kind, cis[d]), ident[0:64, 0:64]),
                                 reads=[binT[d], bc], writes=[bb])
                    if neg:
                        k.op("act", lambda e, dst=dst: e.activation(out=dst[:], in_=bk[0:64, :], func=AF.Copy, scale=-1.0), reads=[bb], writes=[bdst])
                    else:
                        k.op("dve", lambda e, dst=dst: e.tensor_copy(out=dst[:], in_=bk[0:64, :]), reads=[bb], writes=[bdst])
                bk, bb = k.bank()
                for d in range(2):
                    for h in range(4):
                        q = d * 4 + h
                        k.op("pe", lambda e, d=d, h=h, q=q: e.transpose(bk[0:64, q * 64:(q + 1) * 64], vt[d][:, h, cis[d] * 64:(cis[d] + 1) * 64], ident[0:64, 0:64]),
                             reads=[bvt[d], bc], writes=[bb])
                k.op("dve", lambda e: e.tensor_copy(out=Vtm[:], in_=bk[0:64, :]), reads=[bb], writes=[bVtm])

                if R3S < 2:
                    continue
                def prod(la, ra, dst, bdst, mki, neg, eng):
                    bk, bb = k.bank()
                    for d in range(2):
                        for h in range(4):
                            q = d * 4 + h
                            k.op("pe", lambda e, d=d, h=h, q=q: e.matmul(bk[0:64, q * 64:(q + 1) * 64], lhsT=opd(d, h, la, cis[d]), rhs=opd(d, h, ra, cis[d]),
                                                                        start=True, stop=True), reads=[binT[d]], writes=[bb])
                    k.op(eng, lambda e: e.scalar_tensor_tensor(out=dst[:], in0=bk[0:64, :], scalar=(-1.0 if neg else 1.0), in1=cm[:, mki, :], op0=ALU.mult, op1=ALU.mult),
                         reads=[bb, bcm], writes=[bdst])
                prod(3, 0, Pm[0], bPm[0], 0, True, "dve")
                prod(0, 3, Nm[0], bNm[0], 1, True, "dve")
                prod(2, 0, AKT, bAKT, 0, False, "dve")
                prod(2, 1, RKT, bRKT, 2, False, "dve")
                prod(3, 1, RBT, bRBT, 2, True, "dve")
                if R3S < 3:
                    continue
                k.op("dve", lambda e: e.tensor_tensor(out=Am[0][:], in0=Pm[0][:], in1=cm[:, 3, :], op=ALU.add), reads=[bPm[0], bcm], writes=[bAm[0]])
                pc = 0; ac = 0
                for j in range(1, 6):
                    nn = 1 - pc
                    bk, bb = k.bank()
                    for q in range(8):
                        k.op("pe", lambda e, q=q, pc=pc: e.matmul(blk(bk[0:64, :], q), lhsT=blk(Pm[pc], q), rhs=blk(Nm[pc], q), start=True, stop=True),
                             reads=[bPm[pc], bNm[pc]], writes=[bb])
                    if j < 5:
                        bk2, bb2 = k.bank()
                        for q in range(8):
                            k.op("pe", lambda e, q=q, pc=pc: e.matmul(blk(bk2[0:64, :], q), lhsT=blk(Nm[pc], q), rhs=blk(Pm[pc], q), start=True, stop=True),
                                 reads=[bPm[pc], bNm[pc]], writes=[bb2])
                    k.op("act", lambda e, nn=nn, bk=bk: e.copy(out=Nm[nn][:], in_=bk[0:64, :]), reads=[bb], writes=[bNm[nn]])
                    if j < 5:
                        k.op("dve", lambda e, nn=nn, bk2=bk2: e.tensor_copy(out=Pm[nn][:], in_=bk2[0:64, :]), reads=[bb2], writes=[bPm[nn]])
                    bk3, bb3 = k.bank()
                    for q in range(8):
                        k.op("pe", lambda e, q=q, nn=nn, ac=ac: e.matmul(blk(bk3[0:64, :], q), lhsT=blk(Nm[nn], q), rhs=blk(Am[ac], q), start=True, stop=True),
                             reads=[bNm[nn], bAm[ac]], writes=[bb3])
                    k.op("dve", lambda e, ac=ac, bk3=bk3: e.tensor_tensor(out=Am[1 - ac][:], in0=bk3[0:64, :], in1=Am[ac][:], op=ALU.add), reads=[bb3, bAm[ac]], writes=[bAm[1 - ac]])
                    pc = nn; ac = 1 - ac
                A = Am[ac]; bA = bAm[ac]
                if R3S < 4:
                    continue
                bk, bb = k.bank()
                for d in range(2):
                    for h in range(4):
                        q = d * 4 + h
                        k.op("pe", lambda e, d=d, h=h, q=q: e.matmul(blk(bk[0:64, :], q), lhsT=opd(d, h, 0, cis[d]), rhs=Hs[:, q, :], start=True, stop=False),
                             reads=[binT[d], bHs], writes=[bb])
                        k.op("pe", lambda e, q=q: e.matmul(blk(bk[0:64, :], q), lhsT=blk(AKT, q), rhs=blk(Vtm, q), start=False, stop=True),
                             reads=[bAKT, bVtm], writes=[bb])
                k.op("act", lambda e, bk=bk: e.copy(out=RHSs[:], in_=bk[0:64, :]), reads=[bb], writes=[bRHS])
                bk, bb = k.bank()
                for q in range(8):
                    k.op("pe", lambda e, q=q: e.matmul(blk(bk[0:64, :], q), lhsT=blk(A, q), rhs=blk(RHSs, q), start=True, stop=True), reads=[bA, bRHS], writes=[bb])
                k.op("dve", lambda e, bk=bk: e.tensor_copy(out=Us[:], in_=bk[0:64, :]), reads=[bb], writes=[bUs])
                if R3S < 5:
                    continue
                for d in range(2):
                    bk, bb = k.bank()
                    for h in range(4):
                        q = d * 4 + h
                        k.op("pe", lambda e, d=d, h=h, q=q, bk=bk: e.matmul(blk(bk[0:64, :], h), lhsT=Hs[:, q, :], rhs=opd(d, h, 1, cis[d]), start=True, stop=False),
                             reads=[binT[d], bHs], writes=[bb])
                        k.op("pe", lambda e, q=q, h=h, bk=bk: e.matmul(blk(bk[0:64, :], h), lhsT=blk(Vtm, q), rhs=blk(RKT, q), start=False, stop=False), reads=[bVtm, bRKT], writes=[bb])
                        k.op("pe", lambda e, q=q, h=h, bk=bk: e.matmul(blk(bk[0:64, :], h), lhsT=blk(Us, q), rhs=blk(RBT, q), start=False, stop=True), reads=[bUs, bRBT], writes=[bb])
                    for h in range(4):
                        if d == 0:
                            k.op("act", lambda e, d=d, h=h, bk=bk: e.copy(out=yst[d][:, h, cis[d] * 64:(cis[d] + 1) * 64], in_=bk[0:64, h * 64:(h + 1) * 64]), reads=[bb], adds=[byst[d]])
                        else:
                            k.op("dve", lambda e, d=d, h=h, bk=bk: e.tensor_copy(out=yst[d][:, h, cis[d] * 64:(cis[d] + 1) * 64], in_=bk[0:64, h * 64:(h + 1) * 64]), reads=[bb], adds=[byst[d]])
                bk, bb = k.bank()
                for q in range(8):
                    k.op("pe", lambda e, q=q: e.matmul(blk(bk[0:64, :], q), lhsT=blk(Ktm, q), rhs=blk(Vtm, q), start=True, stop=False), reads=[bKtm, bVtm], writes=[bb])
                    k.op("pe", lambda e, q=q: e.matmul(blk(bk[0:64, :], q), lhsT=blk(NBtm, q), rhs=blk(Us, q), start=False, stop=True), reads=[bNB, bUs], writes=[bb])
                k.op("dve", lambda e, bk=bk: e.tensor_tensor(out=Ht[:].rearrange("p a b -> p (a b)"), in0=bk[0:64, :], in1=Hs[:].rearrange("p a b -> p (a b)"), op=ALU.add),
                     reads=[bb, bHs], writes=[bHt])
                for d in range(2):
                    for h in range(4):
                        q = d * 4 + h
                        k.op("dve", lambda e, d=d, h=h, q=q: e.tensor_scalar(out=Hs[:, q, :], in0=Ht[:, q, :], scalar1=gct[d][:, h, cis[d]:cis[d] + 1], scalar2=None, op0=ALU.mult),
                             reads=[bHt, bgct[d]], writes=[bHs] if q == 0 else (), adds=[bHs] if q else ())
            flush(0); flush(1)
            k.barrier()

        with ExitStack() as st:
          if i_phase_on(5):
            g2 = k.sb("g2s", [128, 256], stack=st); bg2 = Buf("g2")
            k.dma(g2[:], g2d.ap()[l], writes=[bg2])
            yf = k.sb("yf", [128, 2, 512], stack=st); byf = Buf("yf")
            yb = k.sb("yb", [128, 2, 512], stack=st); byb = Buf("yb")
            rmo = k.sb("rmo", [128, 7, 512], stack=st); brmo = Buf("rmo")
            rk_ = k.sb("rks", [128, 2, 512], stack=st); brk_ = Buf("rks")
            sq = k.sb("sq4", [128, 512], stack=st); bsq = Buf("sq4")
            mu = k.sb("mu4", [128, 512], stack=st); bmu = Buf("mu4")
            va = k.sb("va4", [128, 512], stack=st); bva = Buf("va4")
            sgl = k.sb("sgl", [128, 512], stack=st); bsgl = Buf("sgl")
            orw = k.sb("orw", [128, 2, 512], stack=st); borw = Buf("orw")
            bn = k.sb("bn4", [128, 512], stack=st); bbn = Buf("bn4")
            for (t0, W, who) in TILES:
                if last and who == 1:
                    continue
                yv = yT.ap().rearrange("d (hp p) t -> d p hp t", p=128)
                k.dma(yf[:, :, :W], yv[0][:, :, t0:t0 + W], reads=[byT], writes=[byf])
                k.dma(yb[:, :, :W], yv[1][:, :, t0:t0 + W], reads=[byT], writes=[byb])
                k.dma(rmo[:, :, :W], rmT.ap().rearrange("c p t -> p c t")[:, 0:7, t0:t0 + W], reads=[brm], writes=[brmo])
                k.dma(rk_[:, :, :W], rksT.ap().rearrange("c p t -> p c t")[:, :, t0:t0 + W], reads=[brks], writes=[brk_])
                k.op("dve", lambda e: e.tensor_tensor(out=yf[:, :, :W], in0=yf[:, :, :W], in1=yb[:, :, :W], op=ALU.add), reads=[byb], writes=[byf])
                k.op("act", lambda e: e.activation(out=sgl[:, :W], in_=rmo[:, 2, :W], func=AF.Sigmoid), reads=[brmo], writes=[bsgl])
                for hp in range(2):
                    bk1, bb1 = k.bank()
                    k.op("pe", lambda e, hp=hp: e.matmul(bk1[:, :W], lhsT=bones, rhs=yf[:, hp, :W], start=True, stop=True), reads=[byf, bc], writes=[bb1])
                    k.op("act", lambda e, hp=hp: e.activation(out=sq[:, :W], in_=yf[:, hp, :W], func=AF.Square), reads=[byf], writes=[bsq])
                    bk2, bb2 = k.bank()
                    k.op("pe", lambda e: e.matmul(bk2[:, :W], lhsT=bones, rhs=sq[:, :W], start=True, stop=True), reads=[bsq, bc], writes=[bb2])
                    k.op("dve", lambda e: e.tensor_scalar(out=mu[:, :W], in0=bk1[:, :W], scalar1=1.0 / 64, scalar2=None, op0=ALU.mult), reads=[bb1], writes=[bmu])
                    k.op("dve", lambda e: e.tensor_tensor(out=va[:, :W], in0=mu[:, :W], in1=mu[:, :W], op=ALU.mult), reads=[bmu], writes=[bva])
                    k.op("dve", lambda e: e.scalar_tensor_tensor(out=va[:, :W], in0=bk2[:, :W], scalar=1.0 / 64, in1=va[:, :W], op0=ALU.mult, op1=ALU.subtract),
                         reads=[bb2], writes=[bva])
                    k.op("dve", lambda e: e.tensor_scalar(out=va[:, :W], in0=va[:, :W], scalar1=64e-5, scalar2=None, op0=ALU.add), writes=[bva])
                    k.op("act", lambda e: e.activation(out=va[:, :W], in_=va[:, :W], func=AF.Sqrt), writes=[bva])
                    k.op("dve", lambda e: e.reciprocal(out=va[:, :W], in_=va[:, :W]), writes=[bva])
                    k.op("dve", lambda e, hp=hp: e.tensor_tensor(out=yf[:, hp, :W], in0=yf[:, hp, :W], in1=mu[:, :W], op=ALU.subtract), reads=[bmu], writes=[byf])
                    k.op("dve", lambda e, hp=hp: e.tensor_tensor(out=yf[:, hp, :W], in0=yf[:, hp, :W], in1=va[:, :W], op=ALU.mult), reads=[bva], writes=[byf])
                    k.op("dve", lambda e, hp=hp: e.tensor_scalar(out=yf[:, hp, :W], in0=yf[:, hp, :W], scalar1=pp[:, LXG + hp:LXG + hp + 1], scalar2=pp[:, LXB + hp:LXB + hp + 1],
                                                                op0=ALU.mult, op1=ALU.add), reads=[bpp], writes=[byf])
                    bk3, bb3 = k.bank()
                    k.op("pe", lambda e, hp=hp: e.matmul(bk3[:, :W], lhsT=bones, rhs=rk_[:, hp, :W], start=True, stop=True), reads=[brk_, bc], writes=[bb3])
                    k.op("dve", lambda e, hp=hp: e.tensor_tensor(out=bn[:, :W], in0=bk3[:, :W], in1=rmo[:, 5 + hp, :W], op=ALU.mult), reads=[bb3, brmo], writes=[bbn])
                    k.op("dve", lambda e, hp=hp: e.tensor_tensor(out=bn[:, :W], in0=bn[:, :W], in1=yf[:, hp, :W], op=ALU.add), reads=[byf], writes=[bbn])
                    bk4, bb4 = k.bank()
                    k.op("pe", lambda e, hp=hp: e.matmul(bk4[:, :W], lhsT=g2[:, hp * 128:(hp + 1) * 128], rhs=sgl[:, :W], start=True, stop=True), reads=[bg2, bsgl], writes=[bb4])
                    k.op("dve", lambda e, hp=hp: e.tensor_tensor(out=orw[:, hp, :W], in0=bk4[:, :W], in1=bn[:, :W], op=ALU.mult), reads=[bb4, bbn], writes=[borw])
                k.dma(mixT.ap().rearrange("c p t -> p c t")[:, 6:8, t0:t0 + W], orw[:, :, :W], reads=[borw], adds=[bmix])
            k.barrier()

        with ExitStack() as st:
          if i_phase_on(6):
            wsT = k.sb("wsT", [128, 4, 128], stack=st); bws = Buf("wsT")
            k.dma(wsT[:], sgwT.ap()[l], writes=[bws])
            bsb = k.sb("bsb", [128, 512], stack=st); bbsb = Buf("bsb")
            k.dma(bsb[:], sgb.ap()[l].partition_broadcast(128), writes=[bbsb])
            uv = k.sb("uv", [128, 4, 512], stack=st); buv = Buf("uv")
            sq = k.sb("sqg", [128, 2, 512], stack=st); bsq = Buf("sqg")
            mn = k.sb("mng", [128, 512], stack=st); bmn = Buf("mng")
            va = k.sb("vag", [128, 512], stack=st); bva = Buf("vag")
            vtm = k.sb("vtmg", [128, 256], stack=st); bvtm = Buf("vtmg")
            tmp = k.sb("tmpg", [128, 128], stack=st); btmp = Buf("tmpg")
            osg = k.sb("osg", [128, 2, 512], stack=st); bosg = Buf("osg")
            for (t0, W, who) in TILES:
                if last and who == 1:
                    continue
                k.dma(uv[:, :, :W], pxT.ap().rearrange("c p t -> p c t")[:, 0:4, t0:t0 + W], reads=[bpx], writes=[buv])
                k.op("act", lambda e: e.activation(out=uv[:, :, :W], in_=uv[:, :, :W], func=AF.Gelu), writes=[buv])
                k.op("act", lambda e: e.activation(out=sq[:, :, :W], in_=uv[:, 2:4, :W], func=AF.Square), reads=[buv], writes=[bsq])
                bk1, bb1 = k.bank(); bk2, bb2 = k.bank()
                for c in range(2):
                    k.op("pe", lambda e, c=c: e.matmul(bk1[:, :W], lhsT=ones[:], rhs=uv[:, 2 + c, :W], start=(c == 0), stop=(c == 1)), reads=[buv, bo], writes=[bb1])
                for c in range(2):
                    k.op("pe", lambda e, c=c: e.matmul(bk2[:, :W], lhsT=ones[:], rhs=sq[:, c, :W], start=(c == 0), stop=(c == 1)), reads=[bsq, bo], writes=[bb2])
                k.op("dve", lambda e: e.tensor_scalar(out=mn[:, :W], in0=bk1[:, :W], scalar1=1.0 / 256, scalar2=None, op0=ALU.mult), reads=[bb1], writes=[bmn])
                k.op("dve", lambda e: e.tensor_tensor(out=va[:, :W], in0=mn[:, :W], in1=mn[:, :W], op=ALU.mult), reads=[bmn], writes=[bva])
                k.op("dve", lambda e: e.scalar_tensor_tensor(out=va[:, :W], in0=bk2[:, :W], scalar=1.0 / 256, in1=va[:, :W], op0=ALU.mult, op1=ALU.subtract), reads=[bb2], writes=[bva])
                k.op("dve", lambda e: e.tensor_scalar(out=va[:, :W], in0=va[:, :W], scalar1=1e-5, scalar2=None, op0=ALU.add), writes=[bva])
                k.op("act", lambda e: e.activation(out=va[:, :W], in_=va[:, :W], func=AF.Sqrt), writes=[bva])
                k.op("dve", lambda e: e.reciprocal(out=va[:, :W], in_=va[:, :W]), writes=[bva])
                for c in range(2):
                    k.op("dve", lambda e, c=c: e.tensor_tensor(out=uv[:, 2 + c, :W], in0=uv[:, 2 + c, :W], in1=mn[:, :W], op=ALU.subtract), reads=[bmn], writes=[buv])
                    k.op("dve", lambda e, c=c: e.tensor_tensor(out=uv[:, 2 + c, :W], in0=uv[:, 2 + c, :W], in1=va[:, :W], op=ALU.mult), reads=[bva], writes=[buv])
                    k.op("dve", lambda e, c=c: e.tensor_scalar(out=uv[:, 2 + c, :W], in0=uv[:, 2 + c, :W], scalar1=pp[:, SLG + c:SLG + c + 1], scalar2=pp[:, SLB + c:SLB + c + 1],
                                                              op0=ALU.mult, op1=ALU.add), reads=[bpp], writes=[buv])
                for nb in range(W // 128):
                    cols = slice(nb * 128, (nb + 1) * 128)
                    bkT, bbT = k.bank()
                    for c in range(2):
                        k.op("pe", lambda e, c=c, cols=cols: e.transpose(bkT[:, c * 128:(c + 1) * 128], uv[:, 2 + c, cols], ident), reads=[buv, bc], writes=[bbT])
                    k.op("act", lambda e: e.copy(out=vtm[:], in_=bkT[:, 0:256]), reads=[bbT], writes=[bvtm])
                    bkM, bbM = k.bank()
                    for g in range(4):
                        k.op("pe", lambda e, g=g: e.matmul(bkM[:, g * 128:(g + 1) * 128], lhsT=vtm[:, (g // 2) * 128:(g // 2 + 1) * 128], rhs=wsT[:, g, :], start=True, stop=True),
                             reads=[bvtm, bws], writes=[bbM])
                    for g in range(4):
                        ps_ = slice((g % 2) * 64, (g % 2 + 1) * 64)
                        k.op("dve", lambda e, g=g, ps_=ps_: e.tensor_tensor(out=tmp[ps_, :], in0=bkM[ps_, g * 128:(g + 1) * 128], in1=bsb[ps_, g * 128:(g + 1) * 128], op=ALU.add),
                             reads=[bbM, bbsb], writes=[btmp])
                        k.op("dve", lambda e, g=g, ps_=ps_, cols=cols: e.tensor_tensor(out=osg[ps_, g // 2, cols], in0=tmp[ps_, :], in1=uv[ps_, g // 2, cols], op=ALU.mult),
                             reads=[btmp, buv], writes=[bosg])
                k.dma(mixT.ap().rearrange("c p t -> p c t")[:, 0:2, t0:t0 + W], osg[:, :, :W], reads=[bosg], adds=[bmix])
            k.barrier()

        with ExitStack() as st:
          if i_phase_on(7):
            cosT = k.sb("cosT", [128, T], stack=st); sinT = k.sb("sinT", [128, T], stack=st); btab = Buf("tab")
            k.dma(cosT[:], cosd.ap(), adds=[btab]); k.dma(sinT[:], sind.ap(), adds=[btab], q="pool")
            am = k.sb("am", [128, 2, 512], stack=st); bam = Buf("am")
            k.dma(am[:], amd.ap(), writes=[bam])
            esb = k.sb("esb", [64, 8], stack=st); besb = Buf("esb")
            k.dma(esb[:], sink.ap()[l].partition_broadcast(64), writes=[besb])
            k.op("act", lambda e: e.activation(out=esb[:], in_=esb[:], func=AF.Exp), writes=[besb])
            KT2 = k.sb("KT2", [128, 2, TT], stack=st); bKT = Buf("KT2")
            Vt = k.sb("Vt", [128, NTOKB, 128], stack=st); bVt = Buf("Vt")
            kraw = k.sb("kraw", [128, TT], stack=st); bkr = Buf("kraw")
            t1 = k.sb("t1", [128, 512], stack=st); bt1 = Buf("t1")
            k.dma(kraw[:], pxT.ap()[11], reads=[bpx], writes=[bkr])

            def rope(dst, src, bsrc, tl, W, wr_reads, bdst):
                bk, bb = k.bank()
                k.op("pe", lambda e: e.matmul(bk[:, :W], lhsT=Rm, rhs=src, start=True, stop=True), reads=[bsrc, bc], writes=[bb])
                k.op("dve", lambda e: e.tensor_tensor(out=t1[:, :W], in0=bk[:, :W], in1=sinT[:, tl:tl + W], op=ALU.mult), reads=[bb, btab], writes=[bt1])
                k.op("dve", lambda e: e.tensor_tensor(out=dst, in0=src, in1=cosT[:, tl:tl + W], op=ALU.mult), reads=[bsrc, btab], writes=[bdst])
                k.op("dve", lambda e: e.tensor_tensor(out=dst, in0=dst, in1=t1[:, :W], op=ALU.add), reads=[bt1], writes=[bdst])

            for i in range(8):
                sl = slice(CT + i * 512, CT + (i + 1) * 512)
                rope(kraw[:, sl], kraw[:, sl], bkr, i * 512, 512, [], bkr)
            for hk in range(2):
                ps_ = slice(hk * 64, (hk + 1) * 64)
                for e_ in range(2):
                    po = slice(e_ * 64, (e_ + 1) * 64)
                    k.op("dve" if e_ == 0 else "act", (lambda e, hk=hk, ps_=ps_, po=po: e.tensor_copy(out=KT2[po, hk, :], in_=kraw[ps_, :])) if e_ == 0 else (lambda e, hk=hk, ps_=ps_, po=po: e.copy(out=KT2[po, hk, :], in_=kraw[ps_, :])), reads=[bkr], adds=[bKT])
            vraw = kraw
            k.dma(vraw[:], pxT.ap()[12], reads=[bpx], writes=[bkr])
            for b4 in range(0, NTOKB, 4):
                nb_ = min(4, NTOKB - b4)
                bk, bb = k.bank()
                for i in range(nb_):
                    k.op("pe", lambda e, i=i, b4=b4: e.transpose(bk[:, i * 128:(i + 1) * 128], vraw[:, (b4 + i) * 128:(b4 + i + 1) * 128], ident), reads=[bkr, bc], writes=[bb])
                evac(Vt[:, b4:b4 + nb_, :].rearrange("p a b -> p (a b)"), bk[:, :nb_ * 128], [bb], adds=[bVt])
            qraw = k.sb("qraw", [128, 4, 512], stack=st); bq = Buf("qraw")
            qr = k.sb("qr", [128, 4, 512], stack=st); bqr = Buf("qr")
            PT = [k.sb(f"PT{i}", [128, 512], stack=st) for i in range(2)]; bPT = [Buf("PT0"), Buf("PT1")]
            rc = k.sb("rc", [64, 512], stack=st); brc = Buf("rc")
            obt = k.sb("obt", [64, 2, 4, 512], stack=st); bobt = Buf("obt")
            pti = 0; sbi = [0]
            for (t0, W, who) in TILES:
                if last and who == 1:
                    continue
                k.dma(qraw[:, :, :W], pxT.ap().rearrange("c p t -> p c t")[:, 4:8, t0:t0 + W], reads=[bpx], writes=[bq])
                if who == 0:
                    for c in range(4):
                        rope(qr[:, c, :W], qraw[:, c, :W], bq, t0 - CT, W, [], bqr)
                    Q = qr; bQ = bqr
                else:
                    Q = qraw; bQ = bq
                for nb in range(W // 128):
                    n = t0 // 128 + nb
                    cols = slice(nb * 128, (nb + 1) * 128)
                    if who == 1:
                        kbs = [(0, None), (1, None)]
                    else:
                        kbs = [(0, None), (1, None)]
                        if n - 1 >= 2:
                            kbs.append((n - 1, 0))
                        kbs.append((n, None))
                        if n + 1 < NTOKB:
                            kbs.append((n + 1, 1))
                    GO = [0, 2, 1, 3]
                    for hk in range(2):
                        bkO, bbO = k.banks[(hk % 2) * 2]; bkD, bbD = k.banks[(hk % 2) * 2 + 1]
                        for idx, (kb, mk) in enumerate(kbs):
                            bkS = [None, None]; bbS = [None, None]
                            bkS[0], bbS[0] = k.banks[4 + (sbi[0] % 2) * 2]; bkS[1], bbS[1] = k.banks[5 + (sbi[0] % 2) * 2]; sbi[0] += 1
                            for pos in range(4):
                                g = GO[pos]; h = hk * 4 + g; e_ = h % 2; chn = h // 2
                                assert e_ == pos // 2
                                k.op("pe", lambda e, pos=pos, e_=e_, chn=chn, kb=kb, hk=hk: e.matmul(bkS[e_][:, (pos % 2) * 128:(pos % 2 + 1) * 128],
                                                                                                    lhsT=KT2[e_ * 64:(e_ + 1) * 64, hk, kb * 128:(kb + 1) * 128],
                                                                                                    rhs=Q[e_ * 64:(e_ + 1) * 64, chn, cols], start=True, stop=True),
                                     reads=[bKT, bQ], writes=[bbS[e_]])
                            P_ = PT[pti % 2]; bP_ = bPT[pti % 2]; pti += 1
                            k.op("act", lambda e, P_=P_, b0=bkS[0]: e.activation(out=P_[:, 0:256], in_=b0[:, 0:256], func=AF.Exp, scale=0.125), reads=[bbS[0]], writes=[bP_])
                            k.op("act", lambda e, P_=P_, b1=bkS[1]: e.activation(out=P_[:, 256:512], in_=b1[:, 0:256], func=AF.Exp, scale=0.125), reads=[bbS[1]], adds=[bP_])
                            if mk is not None:
                                k.op("dve", lambda e, P_=P_, mk=mk: e.tensor_tensor(out=P_[:], in0=P_[:], in1=am[:, mk, :], op=ALU.mult), reads=[bam], writes=[bP_])
                            k.op("pe", lambda e, P_=P_, kb=kb, hk=hk, idx=idx: e.matmul(bkO[0:64, :], lhsT=Vt[:, kb, hk * 64:(hk + 1) * 64], rhs=P_[:], start=(idx == 0), stop=(idx == len(kbs) - 1)),
                                 reads=[bVt, bP_], writes=[bbO])
                            k.op("pe", lambda e, P_=P_, idx=idx: e.matmul(bkD[0:64, :], lhsT=ones[:, 0:64], rhs=P_[:], start=(idx == 0), stop=(idx == len(kbs) - 1)), reads=[bo, bP_], writes=[bbD])
                        for pos in range(4):
                            h = hk * 4 + GO[pos]
                            k.op("dve", lambda e, pos=pos, h=h, bkD=bkD: e.tensor_scalar(out=rc[:, pos * 128:(pos + 1) * 128], in0=bkD[0:64, pos * 128:(pos + 1) * 128],
                                                                                       scalar1=esb[:, h:h + 1], scalar2=None, op0=ALU.add),
                                 reads=[bbD, besb], writes=[brc] if pos == 0 else (), adds=[brc] if pos else ())
                        k.op("dve", lambda e: e.reciprocal(out=rc[:], in_=rc[:]), writes=[brc])
                        for pos in range(4):
                            g = GO[pos]
                            k.op("dve", lambda e, pos=pos, g=g, bkO=bkO, hk=hk, cols=cols: e.tensor_tensor(out=obt[:, hk, g, cols], in0=bkO[0:64, pos * 128:(pos + 1) * 128],
                                                                                                      in1=rc[:, pos * 128:(pos + 1) * 128], op=ALU.mult), reads=[bbO, brc], adds=[bobt])
                for hk in range(2):
                    for g in range(4):
                        h = hk * 4 + g
                        k.dma(mixT.ap()[2 + h // 2][(h % 2) * 64:(h % 2 + 1) * 64, t0:t0 + W], obt[:, hk, g, :W], reads=[bobt], adds=[bmix], q="sp" if g % 2 == 0 else "pool")
                k.op("dve", lambda e: e.memset(rc[:, 0:1], 0.0), reads=[], writes=[bobt, brc])
            k.barrier()

        with ExitStack() as st:
          if i_phase_on(8):
            wo = k.sb("wo", [128, 8, 1024], BF16, stack=st); bwo = Buf("wo")
            wst = [k.sb(f"wst5{i}", [128, 1024], stack=st) for i in range(2)]; bwst = [Buf("wst0"), Buf("wst1")]
            wv = w_out.ap()[l].rearrange("(kc p) n -> p kc n", p=128)
            for kc in range(8):
                k.dma(wst[kc % 2][:], wv[:, kc, :], writes=[bwst[kc % 2]], q="sp" if kc % 2 == 0 else "pool")
                cast(wo[:, kc, :], wst[kc % 2][:], [bwst[kc % 2]], adds=[bwo])
            Mb = k.sb("Mb5", [128, 8, 512], BF16, stack=st); bMb = Buf("Mb5")
            X = k.sb("X5", [128, 8, 512], stack=st); bX = Buf("X5")
            M = k.sb("M5", [128, 8, 512], stack=st); bM = Buf("M5")
            Y = k.sb("Y5", [128, 8, 512], stack=st); bY = Buf("Y5")
            SQ = k.sb("SQ5", [128, 8, 512], stack=st); bSQ = Buf("SQ5")
            rs = k.sb("rs5", [128, 512], stack=st); brs = Buf("rs5")
            xv = xsrc(l).ap().rearrange("c p t -> p c t")
            xo = xs.ap().rearrange("c p t -> p c t")
            for (t0, W, who) in TILES:
                if last and who == 1:
                    continue
                k.dma(X[:, :, :W], xv[:, :, t0:t0 + W], reads=[bxs], writes=[bX])
                k.dma(M[:, :, :W], mixT.ap().rearrange("c p t -> p c t")[:, :, t0:t0 + W], reads=[bmix], writes=[bM], q="pool")
                for kc in range(8):
                    cast(Mb[:, kc, :W], M[:, kc, :W], [bM], writes=[bMb] if kc == 0 else (), adds=[bMb] if kc else ())
                for j in range(8):
                    bk, bb = k.bank()
                    for kc in range(8):
                        k.op("pe", lambda e, kc=kc, j=j: e.matmul(bk[:, :W], lhsT=wo[:, kc, j * 128:(j + 1) * 128], rhs=Mb[:, kc, :W], start=(kc == 0), stop=(kc == 7)),
                             reads=[bwo, bMb], writes=[bb])
                    evac(Y[:, j, :W], bk[:, :W], [bb], adds=[bY])
                rmsn(st, Y, bY, W, SQ, bSQ, rs, brs)
                resid_update(X, bX, Y, bY, rs, brs, W, 2, who)
                k.dma(xo[:, :, t0:t0 + W], X[:, :, :W], reads=[bX], adds=[bxs])
            k.barrier()

        with ExitStack() as st:
          if i_phase_on(9):
            wgb = k.sb("wgb", [128, 8, 5632], BF16, stack=st); bwgb = Buf("wgb")
            wdb = k.sb("wdb", [128, 22, 1024], BF16, stack=st); bwdb = Buf("wdb")
            wst = [k.sb(f"wst6{i}", [128, 1024], stack=st) for i in range(2)]; bwst = [Buf("wst0"), Buf("wst1")]
            gv_ = w_gu.ap()[l].rearrange("(kc p) n -> p kc n", p=128)
            dv_ = w_down.ap()[l].rearrange("(j p) n -> p j n", p=128)
            wi = 0
            for kc in range(8):
                for cb in range(6):
                    c0_ = cb * 1024; cw = min(1024, 5632 - c0_)
                    k.dma(wst[wi % 2][:, :cw], gv_[:, kc, c0_:c0_ + cw], writes=[bwst[wi % 2]], q="sp" if wi % 2 == 0 else "pool")
                    cast(wgb[:, kc, c0_:c0_ + cw], wst[wi % 2][:, :cw], [bwst[wi % 2]], adds=[bwgb])
                    wi += 1
            for j in range(22):
                k.dma(wst[wi % 2][:], dv_[:, j, :], writes=[bwst[wi % 2]], q="sp" if wi % 2 == 0 else "pool")
                cast(wdb[:, j, :], wst[wi % 2][:], [bwst[wi % 2]], adds=[bwdb])
                wi += 1
            X = k.sb("X6", [128, 8, 256], stack=st); bX = Buf("X6")
            H = k.sb("H6", [128, 8, 256], BF16, stack=st); bH = Buf("H6")
            SQ = k.sb("SQ6", [128, 8, 256], stack=st); bSQ = Buf("SQ6")
            Y = k.sb("Y6", [128, 8, 256], stack=st); bY = Buf("Y6")
            rs = k.sb("rs6", [128, 256], stack=st); brs = Buf("rs6")
            actb = k.sb("actb", [128, 22, 256], BF16, stack=st); bact = Buf("actb")
            sl = [k.sb(f"sl6{i}", [128, 256], stack=st) for i in range(2)]; bsl = [Buf("sl0"), Buf("sl1")]
            xo = xs.ap().rearrange("c p t -> p c t")
            for (t0, W, who) in TILES5:
                if last and who == 1:
                    continue
                k.dma(X[:, :, :W], xo[:, :, t0:t0 + W], reads=[bxs], writes=[bX])
                rmsn(st, X, bX, W, SQ, bSQ, rs, brs)
                modulate(H, bH, X, bX, rs, brs, W, 3, 4, who)
                for j in range(22):
                    bkg, bbg = k.bank(); bku, bbu = k.bank()
                    for kc in range(8):
                        k.op("pe", lambda e, kc=kc, j=j: e.matmul(bkg[:, :W], lhsT=wgb[:, kc, j * 128:(j + 1) * 128], rhs=H[:, kc, :W], start=(kc == 0), stop=(kc == 7)),
                             reads=[bwgb, bH], writes=[bbg])
                    for kc in range(8):
                        k.op("pe", lambda e, kc=kc, j=j: e.matmul(bku[:, :W], lhsT=wgb[:, kc, 2816 + j * 128:2816 + (j + 1) * 128], rhs=H[:, kc, :W], start=(kc == 0), stop=(kc == 7)),
                             reads=[bwgb, bH], writes=[bbu])
                    s_ = sl[j % 2]; bs_ = bsl[j % 2]
                    k.op("act", lambda e, bkg=bkg, s_=s_: e.activation(out=s_[:, :W], in_=bkg[:, :W], func=AF.Silu), reads=[bbg], writes=[bs_])
                    k.op("dve", lambda e, j=j, bku=bku, s_=s_: e.tensor_tensor(out=actb[:, j, :W], in0=bku[:, :W], in1=s_[:, :W], op=ALU.mult), reads=[bbu, bs_],
                         writes=[bact] if j == 0 else (), adds=[bact] if j else ())
                for jo in range(8):
                    bk, bb = k.bank()
                    for j in range(22):
                        k.op("pe", lambda e, j=j, jo=jo: e.matmul(bk[:, :W], lhsT=wdb[:, j, jo * 128:(jo + 1) * 128], rhs=actb[:, j, :W], start=(j == 0), stop=(j == 21)),
                             reads=[bwdb, bact], writes=[bb])
                    evac(Y[:, jo, :W], bk[:, :W], [bb], writes=[bY] if jo == 0 else (), adds=[bY] if jo else ())
                rmsn(st, Y, bY, W, SQ, bSQ, rs, brs)
                resid_update(X, bX, Y, bY, rs, brs, W, 5, who)
                if l == L - 1:
                    k.dma(outT.ap().rearrange("c p t -> p c t")[:, :, t0 - CT:t0 - CT + W], X[:, :, :W], reads=[bX], adds=[bout])
                else:
                    k.dma(xo[:, :, t0:t0 + W], X[:, :, :W], reads=[bX], adds=[bxs])
            k.barrier()
    k.barrier()
    return nc, k


def host_consts():
    ident = np.eye(128, dtype=np.float32)
    bones = np.zeros((128, 128), np.float32); bones[:64, :64] = 1; bones[64:, 64:] = 1
    Rm = np.zeros((128, 128), np.float32)
    for m in range(128):
        i = m % 32
        if i < 16:
            Rm[m + 16, m] = -1.0
        else:
            Rm[m - 16, m] = 1.0
    cc = np.concatenate([ident, bones, Rm], axis=1)
    smL = np.zeros((128, 9, 4), np.float32); smC = np.zeros((128, 9, 2), np.float32)
    for ci in range(9):
        for p in range(128):
            if ci < 3:
                ch = ci * 128 + p; qd = ch // 96; half = 0 if ch < 192 else 1
            else:
                ch = (ci - 3) * 128 + p; qd = ch // 192; half = 0 if ch < 384 else 1
            smL[p, ci, qd] = 1; smC[p, ci, half] = 1
    s = np.arange(64)[:, None]; t = np.arange(64)[None, :]
    mS = [(s < t), (s > t)]; mST = [(s > t), (s < t)]; mI = [(s <= t), (s >= t)]
    cm = np.zeros((64, 4, 512), np.float32)
    for d in range(2):
        for h in range(4):
            q = d * 4 + h
            cm[:, 0, q * 64:(q + 1) * 64] = mS[d]
            cm[:, 1, q * 64:(q + 1) * 64] = mST[d]
            cm[:, 2, q * 64:(q + 1) * 64] = mI[d]
            cm[:, 3, q * 64:(q + 1) * 64] = np.eye(64)
    j = np.arange(128)[:, None]; i = np.arange(128)[None, :]
    am = np.zeros((128, 2, 512), np.float32)
    for g in range(4):
        am[:, 0, g * 128:(g + 1) * 128] = (j >= i)
        am[:, 1, g * 128:(g + 1) * 128] = (j <= i)
    rs = np.ones((128, 512), np.float32); rs[:, ::64] = 0
    tt = np.arange(T)
    pos = np.stack([tt // 64, tt % 64], -1).astype(np.float32)
    inv = (10000.0 ** (-np.arange(16, dtype=np.float32) / 16)).astype(np.float32)
    ang = pos[:, :, None] * inv
    cosT = np.zeros((128, T), np.float32); sinT = np.zeros((128, T), np.float32)
    for p in range(128):
        dd = p % 64; ax = dd // 32; fr = dd % 16
        cosT[p] = np.cos(ang[:, ax, fr]); sinT[p] = np.sin(ang[:, ax, fr])
    return dict(cc=cc, smL=smL, smC=smC, cm=cm, am=am, rs=rs, cosT=cosT, sinT=sinT)


def cmaj(v, n):
    return np.ascontiguousarray(np.asarray(v, np.float32).reshape(n, 128).T)


def host_inputs(inp, b):
    f = lambda a: np.ascontiguousarray(np.asarray(a, np.float32))
    xall = np.concatenate([inp["ctx"][b], inp["x"][b]], axis=0)
    xin = np.ascontiguousarray(xall.T.reshape(8, 128, TT))
    cin = np.stack([cmaj(inp["c"][b], 8), cmaj(inp["c_ctx"], 8)], axis=-1)
    pp = np.zeros((L, 128, NP), np.float32)
    for l in range(L):
        for i in range(4):
            pp[l, :, NG + i * 8:NG + i * 8 + 8] = cmaj(inp["norm_g"][l, i], 8)
        pp[l, :, BM:BM + 48] = cmaj(inp["b_mod"][l], 48)
        pp[l, :, MU:MU + 9] = cmaj(inp["rw_mu"][l], 9)
        pp[l, :, KK:KK + 2] = cmaj(inp["rw_kk"][l], 2)
        pp[l, :, KA:KA + 2] = cmaj(inp["rw_ka"][l], 2)
        pp[l, :, RK:RK + 2] = cmaj(inp["rw_rk"][l].reshape(-1), 2)
        pp[l, :, LXG:LXG + 2] = cmaj(inp["rw_lnx_g"][l], 2)
        pp[l, :, LXB:LXB + 2] = cmaj(inp["rw_lnx_b"][l], 2)
        pp[l, :, W0:W0 + 4] = cmaj(inp["rw_w0"][l].reshape(-1), 4)
        pp[l, :, A0:A0 + 4] = cmaj(inp["rw_a0"][l].reshape(-1), 4)
        pp[l, :, SLG:SLG + 2] = cmaj(inp["sg_ln_g"][l], 2)
        pp[l, :, SLB:SLB + 2] = cmaj(inp["sg_ln_b"][l], 2)
    m = dict(xin=xin, cin=f(cin), w_mod=f(inp["w_mod"]), w_in=f(inp["w_in"]), w_out=f(inp["w_out"]), w_gu=f(inp["w_gu"]), w_down=f(inp["w_down"]),
             pp=pp, sgwT=f(np.transpose(inp["sg_w"], (0, 3, 1, 2))), sgb=f(inp["sg_b"]).reshape(L, 1, 512), sink=f(inp["attn_sink"]).reshape(L, 1, 8),
             w2=f(inp["rw_w2"]).reshape(L, 128, 256), a2=f(inp["rw_a2"]).reshape(L, 128, 256), g2=f(inp["rw_g2"]))
    m.update(host_consts())
    return m


def kernel(**inp):
    inp = {k_: np.asarray(v) for k_, v in inp.items()}
    nc, _ = build()
    maps = [host_inputs(inp, b % 4) for b in range(4)]
    in_maps = [maps[i % 4] for i in range(8)]
    res = run_bass_kernel_spmd(nc, in_maps, core_ids=list(range(8)))
    out = np.zeros((4, T, D), np.float32)
    for b in range(4):
        o = res.results[b]["outT"]
        out[b] = o.reshape(D, T).T
    return out
```

```python
import numpy as np
import concourse.bass as bass
import concourse.mybir as mybir
from concourse.bass_utils import run_bass_kernel_spmd
from contextlib import ExitStack

F32 = mybir.dt.float32
BF16 = mybir.dt.bfloat16
AF = mybir.ActivationFunctionType
ALU = mybir.AluOpType

D = 1024; T = 4096; CT = 256; TT = T + CT; L = 4
NTOKB = TT // 128
NCHK = TT // 64
C0 = float(np.exp(-0.5))
TILES = [(0, 256, 1)] + [(256 + 512 * i, 512, 0) for i in range(8)]
TILES5 = [(0, 256, 1)] + [(256 + 256 * i, 256, 0) for i in range(16)]
NG = 0; BM = 32; MU = 80; KK = 89; KA = 91; RK = 93; LXG = 95; LXB = 97; W0 = 99; A0 = 103; SLG = 107; SLB = 109; NP = 111
SRC_CH = [8, 9, 10, 13, 14, 15, 16, 17, 18]

SAME_ENGINE_SYNC = True
EPOCH = 12000
DMA_EPOCH = 1500
NDMASEM = 6


class Buf:
    __slots__ = ("name", "w", "r")

    def __init__(self, name):
        self.name = name
        self.w = {}
        self.r = {}


class KB:
    def __init__(self, nc):
        self.nc = nc
        self.es = ExitStack()
        self.E = {"pe": nc.tensor, "act": nc.scalar, "dve": nc.vector, "pool": nc.gpsimd, "sp": nc.sync}
        self.sem = {}
        self.cnt = {}
        self.seen = {e: {} for e in self.E}
        self.nsem = 0
        for e in self.E:
            self._new_sem(e)
        self.dsem = {}
        self.dcnt = {}
        self.drr = {}
        for q in ("sp", "pool"):
            self.dsem[q] = [self._alloc_sem(f"d{q}{i}") for i in range(NDMASEM)]
            self.dcnt[q] = [0] * NDMASEM
            self.drr[q] = 0
        self.ninstr = 0
        self.banks = []
        self.bi = 0

    def _alloc_sem(self, name):
        self.nsem += 1
        h = self.nc.alloc_semaphore(name=f"{name}_{self.nsem}")
        self._keep = getattr(self, "_keep", [])
        self._keep.append(h)
        return h

    def _new_sem(self, e):
        self.sem[e] = self._alloc_sem(f"s{e}")
        self.cnt[e] = 0

    def sb(self, name, shape, dtype=F32, stack=None):
        self.nsb = getattr(self, "nsb", 0) + 1
        return (stack or self.es).enter_context(self.nc.sbuf_tensor(f"sb{self.nsb}_{name}", list(shape), dtype))

    def mkbanks(self):
        for i in range(8):
            t = self.es.enter_context(self.nc.psum_tensor(f"bank{i}", [128, 512], F32))
            self.banks.append((t, Buf(f"bank{i}")))

    def bank(self):
        b = self.banks[self.bi]
        self.bi = (self.bi + 1) % 8
        return b

    def _wait(self, e, ev):
        sem, val, src = ev
        if src == e and (not SAME_ENGINE_SYNC or e == "pe"):
            return
        key = id(sem)
        if self.seen[e].get(key, 0) >= val:
            return
        self.E[e].wait_ge(sem, val)
        self.seen[e][key] = val

    def _deps(self, e, reads, writes, adds, isdma=False):
        for b in reads:
            for ev in b.w.values():
                self._wait(e, ev)
        for b in writes:
            for ev in b.w.values():
                self._wait(e, ev)
            for ev in b.r.values():
                self._wait(e, ev)
        for b in adds:
            for ev in b.r.values():
                self._wait(e, ev)
            for ev in b.w.values():
                if not (isdma and ev[2] == "dma"):
                    self._wait(e, ev)

    def _commit(self, key, ev, reads, writes, adds):
        for b in reads:
            b.r[key] = ev
        for b in writes:
            b.w = {key: ev}
            b.r = {}
        for b in adds:
            b.w[key] = ev

    def op(self, e, fn, reads=(), writes=(), adds=()):
        self._deps(e, reads, writes, adds)
        ins = fn(self.E[e])
        if self.cnt[e] >= EPOCH:
            self._new_sem(e)
        self.cnt[e] += 1
        ins.then_inc(self.sem[e], 1)
        ev = (self.sem[e], self.cnt[e], e)
        self._commit(e, ev, reads, writes, adds)
        self.ninstr += 1
        return ev

    def dma(self, out, in_, reads=(), writes=(), adds=(), q="sp"):
        self._deps(q, reads, writes, adds, True)
        j = self.drr[q]
        self.drr[q] = (j + 1) % NDMASEM
        if self.dcnt[q][j] > 0:
            self._wait(q, (self.dsem[q][j], self.dcnt[q][j] * 16, "dma"))
        if self.dcnt[q][j] >= DMA_EPOCH:
            self.dsem[q][j] = self._alloc_sem(f"d{q}{j}")
            self.dcnt[q][j] = 0
        sem = self.dsem[q][j]
        self.E[q].dma_start(out=out, in_=in_).then_inc(sem, 16)
        self.dcnt[q][j] += 1
        ev = (sem, self.dcnt[q][j] * 16, "dma")
        self._commit(id(sem), ev, reads, writes, adds)
        self.ninstr += 1
        return ev

    def barrier(self):
        if getattr(self, "ptick", None) is not None:
            self.op("pool", lambda e: e.memset(self.ptick[:], 0.0), writes=[self.ptickb])
        evs = []
        for e in ("pe", "act", "dve", "pool", "sp"):
            if self.cnt[e] > 0:
                evs.append((self.sem[e], self.cnt[e], e))
        for q in self.dsem:
            for j in range(NDMASEM):
                if self.dcnt[q][j] > 0:
                    evs.append((self.dsem[q][j], self.dcnt[q][j] * 16, "dma"))
        for e in self.E:
            for ev in evs:
                if ev[2] != e:
                    self._wait(e, ev)


PHASES = None


CURL = [0]
PHASES_L = None


def i_phase_on(i):
    if PHASES_L is not None:
        return i in PHASES_L[CURL[0]]
    return PHASES is None or i in PHASES


def build(nlayers=L, debug=False):
    nc = bass.Bass("TRN2", target_bir_lowering=False)
    k = KB(nc)
    k.mkbanks()

    def din(name, shape):
        return nc.dram_tensor(name, list(shape), F32, kind="ExternalInput")

    def dscr(name, shape, out=False):
        return nc.dram_tensor(name, list(shape), F32, kind="ExternalOutput" if (out and debug) else "Internal")

    xin = din("xin", [8, 128, TT])
    cin = din("cin", [128, 8, 2])
    w_mod = din("w_mod", [L, D, 6 * D]); w_in = din("w_in", [L, D, 2432]); w_out = din("w_out", [L, D, D])
    w_gu = din("w_gu", [L, D, 5632]); w_down = din("w_down", [L, 2816, D])
    ppd = din("pp", [L, 128, NP])
    sgwT = din("sgwT", [L, 128, 4, 128]); sgb = din("sgb", [L, 1, 512]); sink = din("sink", [L, 1, 8])
    w2d = din("w2", [L, 128, 256]); a2d = din("a2", [L, 128, 256]); g2d = din("g2", [L, 128, 256])
    cc = din("cc", [128, 384]); smL = din("smL", [128, 9, 4]); smC = din("smC", [128, 9, 2])
    cmd = din("cm", [64, 4, 512]); amd = din("am", [128, 2, 512]); rsd = din("rs", [128, 512])
    cosd = din("cosT", [128, T]); sind = din("sinT", [128, T])
    outT = nc.dram_tensor("outT", [8, 128, T], F32, kind="ExternalOutput")

    xs = dscr("xs", [8, 128, TT], True); bxs = Buf("xs")
    pxT = dscr("pxT", [19, 128, TT], True); bpx = Buf("pxT")
    mixT = dscr("mixT", [8, 128, TT], True); bmix = Buf("mixT")
    rmT = dscr("rmT", [9, 128, TT], True); brm = Buf("rmT")
    scin = nc.dram_tensor("scin", [4, 128, 5, TT], BF16, kind="Internal"); bscin = Buf("scin")
    gcd = dscr("gcd", [4, 128, NCHK], True); bgcd = Buf("gcd")
    rksT = dscr("rksT", [2, 128, TT], True); brks = Buf("rksT")
    yT = dscr("yT", [2, 256, TT], True); byT = Buf("yT")
    bout = Buf("outT")

    cst = k.sb("cst", [128, 384]); bc = Buf("cst")
    k.dma(cst[:], cc.ap(), writes=[bc])
    ident = cst[:, 0:128]; bones = cst[:, 128:256]; Rm = cst[:, 256:384]
    k.ptick = k.sb("ptick", [128, 8]); k.ptickb = Buf("ptick")
    ones = k.sb("ones", [128, 128]); bo = Buf("ones")
    k.op("dve", lambda e: e.memset(ones[:], 1.0), writes=[bo])
    cact = k.sb("cact", [128, 8, 2]); bca = Buf("cact")
    k.dma(cact[:], cin.ap(), writes=[bca])
    k.op("act", lambda e: e.activation(out=cact[:], in_=cact[:], func=AF.Silu), writes=[bca])
    pp = k.sb("ppt", [128, NP]); bpp = Buf("pp")
    modt = k.sb("modt", [128, 48, 2]); bmod = Buf("modt")
    coef = k.sb("coef", [128, 6, 8, 2]); bcoef = Buf("coef")
    cfs = k.sb("cfs", [128, 9, 7]); bcfs = Buf("cfs")
    smLs = k.sb("smLs", [128, 9, 4]); smCs = k.sb("smCs", [128, 9, 2]); bsm = Buf("sm")
    k.dma(smLs[:], smL.ap(), adds=[bsm]); k.dma(smCs[:], smC.ap(), adds=[bsm])
    omka = k.sb("omka", [128, 2]); bomka = Buf("omka")
    k.barrier()

    def xsrc(l):
        return xin if l == 0 else xs

    def rmsn(st, X, bX, W, SQ, bSQ, rs, brs):
        k.op("act", lambda e: e.activation(out=SQ[:, :, :W], in_=X[:, :, :W], func=AF.Square), reads=[bX], writes=[bSQ])
        bk, bb = k.bank()
        for c in range(8):
            k.op("pe", lambda e, c=c: e.matmul(bk[:, :W], lhsT=ones[:], rhs=SQ[:, c, :W], start=(c == 0), stop=(c == 7)),
                 reads=[bSQ, bo], writes=[bb])
        k.op("dve", lambda e: e.tensor_scalar(out=rs[:, :W], in0=bk[:, :W], scalar1=1.0 / D, scalar2=1e-6, op0=ALU.mult, op1=ALU.add),
             reads=[bb], writes=[brs])
        k.op("act", lambda e: e.activation(out=rs[:, :W], in_=rs[:, :W], func=AF.Sqrt), writes=[brs])
        k.op("dve", lambda e: e.reciprocal(out=rs[:, :W], in_=rs[:, :W]), writes=[brs])

    t32 = [k.sb(f"t32_{i}", [128, 512]) for i in range(2)]; bt32 = [Buf("t32a"), Buf("t32b")]

    def modulate(H, bH, X, bX, rs, brs, W, ia, ib, who):
        for c in range(8):
            t_ = t32[c % 2]; bt_ = bt32[c % 2]
            k.op("dve", lambda e, c=c, t_=t_: e.scalar_tensor_tensor(out=t_[:, :W], in0=X[:, c, :W], scalar=coef[:, ia, c, who:who + 1],
                                                                   in1=rs[:, :W], op0=ALU.mult, op1=ALU.mult),
                 reads=[bX, brs, bcoef], writes=[bt_])
            k.op("act", lambda e, c=c, t_=t_: e.activation(out=H[:, c, :W], in_=t_[:, :W], func=AF.Identity, bias=coef[:, ib, c, who:who + 1]),
                 reads=[bcoef, bt_], writes=[bH] if c == 0 else (), adds=[bH] if c else ())

    castrr = [0]

    def cast(out, in_, reads, writes=(), adds=()):
        castrr[0] = (castrr[0] + 1) % 2
        if castrr[0] == 0:
            k.op("act", lambda e: e.copy(out=out, in_=in_), reads=reads, writes=writes, adds=adds)
        else:
            k.op("dve", lambda e: e.tensor_copy(out=out, in_=in_), reads=reads, writes=writes, adds=adds)

    def resid_update(X, bX, Y, bY, rs, brs, W, ig, who):
        for j in range(8):
            k.op("dve", lambda e, j=j: e.scalar_tensor_tensor(out=Y[:, j, :W], in0=Y[:, j, :W], scalar=coef[:, ig, j, who:who + 1],
                                                             in1=rs[:, :W], op0=ALU.mult, op1=ALU.mult),
                 reads=[brs, bcoef], writes=[bY])
            k.op("dve", lambda e, j=j: e.tensor_tensor(out=X[:, j, :W], in0=X[:, j, :W], in1=Y[:, j, :W], op=ALU.add),
                 reads=[bY], writes=[bX])

    evflip = [0]

    def evac(out, in_, reads, writes=(), adds=()):
        evflip[0] ^= 1
        if evflip[0]:
            k.op("act", lambda e: e.copy(out=out, in_=in_), reads=reads, writes=writes, adds=adds)
        else:
            k.op("dve", lambda e: e.tensor_copy(out=out, in_=in_), reads=reads, writes=writes, adds=adds)

    for l in range(nlayers):
        last = (l == L - 1)
        CURL[0] = l
        with ExitStack() as st:
          if i_phase_on(0):
            k.dma(pp[:], ppd.ap()[l], writes=[bpp])
            wm = [k.sb(f"wm{i}", [128, 8, 1024], stack=st) for i in range(2)]; bwm = [Buf("wm0"), Buf("wm1")]
            for m in range(6):
                k.dma(wm[m % 2][:], w_mod.ap()[l][:, m * 1024:(m + 1) * 1024].rearrange("(kc p) n -> p kc n", p=128), writes=[bwm[m % 2]],
                      q="sp" if m % 2 == 0 else "pool")
                for c in range(8):
                    bk, bb = k.bank()
                    for kc in range(8):
                        k.op("pe", lambda e, kc=kc, c=c, m=m: e.matmul(bk[:, 0:2], lhsT=wm[m % 2][:, kc, c * 128:(c + 1) * 128], rhs=cact[:, kc, :],
                                                                      start=(kc == 0), stop=(kc == 7)), reads=[bwm[m % 2], bca], writes=[bb])
                    j = m * 8 + c
                    k.op("dve", lambda e, j=j: e.tensor_scalar(out=modt[:, j, :], in0=bk[:, 0:2], scalar1=pp[:, BM + j:BM + j + 1], scalar2=None, op0=ALU.add),
                         reads=[bb, bpp], adds=[bmod])
            for who in range(2):
                for (ic, isc, ig) in ((0, 1, 0), (3, 4, 2)):
                    k.op("dve", lambda e, who=who, ic=ic, isc=isc, ig=ig: e.tensor_tensor(out=coef[:, ic, :, who], in0=modt[:, isc * 8:isc * 8 + 8, who],
                                                                                        in1=pp[:, NG + ig * 8:NG + ig * 8 + 8], op=ALU.mult),
                         reads=[bmod, bpp], writes=[bcoef])
                    k.op("dve", lambda e, who=who, ic=ic, ig=ig: e.tensor_tensor(out=coef[:, ic, :, who], in0=coef[:, ic, :, who],
                                                                               in1=pp[:, NG + ig * 8:NG + ig * 8 + 8], op=ALU.add), writes=[bcoef])
                for (ic, ish) in ((1, 0), (4, 3)):
                    k.op("dve", lambda e, who=who, ic=ic, ish=ish: e.tensor_copy(out=coef[:, ic, :, who], in_=modt[:, ish * 8:ish * 8 + 8, who]),
                         reads=[bmod], writes=[bcoef])
                for (ic, igt, ig) in ((2, 2, 1), (5, 5, 3)):
                    k.op("dve", lambda e, who=who, ic=ic, igt=igt, ig=ig: e.tensor_tensor(out=coef[:, ic, :, who], in0=modt[:, igt * 8:igt * 8 + 8, who],
                                                                                        in1=pp[:, NG + ig * 8:NG + ig * 8 + 8], op=ALU.mult),
                         reads=[bmod, bpp], writes=[bcoef])
            k.op("dve", lambda e: e.tensor_scalar(out=cfs[:, :, 0], in0=pp[:, MU:MU + 9], scalar1=-1.0, scalar2=1.0, op0=ALU.mult, op1=ALU.add),
                 reads=[bpp], writes=[bcfs])
            for dd in range(4):
                k.op("dve", lambda e, dd=dd: e.tensor_tensor(out=cfs[:, :, 1 + dd], in0=pp[:, MU:MU + 9], in1=smLs[:, :, dd], op=ALU.mult),
                     reads=[bpp, bsm], writes=[bcfs])
            for dd in range(2):
                k.op("dve", lambda e, dd=dd: e.tensor_tensor(out=cfs[:, :, 5 + dd], in0=pp[:, MU:MU + 9], in1=smCs[:, :, dd], op=ALU.mult),
                     reads=[bpp, bsm], writes=[bcfs])
            k.op("dve", lambda e: e.tensor_scalar(out=omka[:], in0=pp[:, KA:KA + 2], scalar1=-1.0, scalar2=1.0, op0=ALU.mult, op1=ALU.add),
                 reads=[bpp], writes=[bomka])
            k.barrier()

        with ExitStack() as st:
          if i_phase_on(1):
            win = k.sb("win", [128, 8, 2432], BF16, stack=st); bwin = Buf("win")
            wst = [k.sb(f"wst{i}", [128, 2432], stack=st) for i in range(2)]; bwst = [Buf("wst0"), Buf("wst1")]
            wv = w_in.ap()[l].rearrange("(kc p) n -> p kc n", p=128)
            for kc in range(8):
                k.dma(wst[kc % 2][:], wv[:, kc, :], writes=[bwst[kc % 2]], q="sp" if kc % 2 == 0 else "pool")
                cast(win[:, kc, :], wst[kc % 2][:], [bwst[kc % 2]], adds=[bwin])
            X = k.sb("X", [128, 8, 512], stack=st); bX = Buf("X")
            SQ = k.sb("SQ", [128, 8, 512], stack=st); bSQ = Buf("SQ")
            H = k.sb("H", [128, 8, 512], BF16, stack=st); bH = Buf("H")
            rs = k.sb("rs", [128, 512], stack=st); brs = Buf("rs")
            stg = [k.sb(f"stg{i}", [128, 512], stack=st) for i in range(2)]; bstg = [Buf("s0"), Buf("s1")]
            xv = xsrc(l).ap().rearrange("c p t -> p c t")
            for (t0, W, who) in TILES:
                k.dma(X[:, :, :W], xv[:, :, t0:t0 + W], reads=[bxs], writes=[bX])
                rmsn(st, X, bX, W, SQ, bSQ, rs, brs)
                modulate(H, bH, X, bX, rs, brs, W, 0, 1, who)
                for j in range(19):
                    bk, bb = k.bank()
                    for kc in range(8):
                        k.op("pe", lambda e, kc=kc, j=j: e.matmul(bk[:, :W], lhsT=win[:, kc, j * 128:(j + 1) * 128], rhs=H[:, kc, :W],
                                                                 start=(kc == 0), stop=(kc == 7)), reads=[bwin, bH], writes=[bb])
                    s = stg[j % 2]
                    evac(s[:, :W], bk[:, :W], [bb], writes=[bstg[j % 2]])
                    k.dma(pxT.ap()[j][:, t0:t0 + W], s[:, :W], reads=[bstg[j % 2]], adds=[bpx])
            k.barrier()

        with ExitStack() as st:
          if i_phase_on(2):
            fb = [k.sb(f"fb{i}", [128, TT], stack=st) for i in range(2)]; bfb = [Buf("fb0"), Buf("fb1")]
            ob = [k.sb(f"ob{i}", [128, TT], stack=st) for i in range(2)]; bob = [Buf("ob0"), Buf("ob1")]
            for ci in range(9):
                f = fb[ci % 2]; o = ob[ci % 2]; bf = bfb[ci % 2]; bo_ = bob[ci % 2]
                k.dma(f[:], pxT.ap()[SRC_CH[ci]], reads=[bpx], writes=[bf], q="sp" if ci % 2 == 0 else "pool")
                k.op("act", lambda e, f=f, o=o, ci=ci: e.activation(out=o[:], in_=f[:], func=AF.Identity, scale=cfs[:, ci, 0:1]), reads=[bf, bcfs], writes=[bo_])
                fl = f[:, CT:].rearrange("p (r w) -> p r w", w=64)
                ol = o[:, CT:].rearrange("p (r w) -> p r w", w=64)
                views = [(ol[:, :, 1:64], fl[:, :, 0:63]), (ol[:, :, 0:63], fl[:, :, 1:64]), (ol[:, 1:64, :], fl[:, 0:63, :]), (ol[:, 0:63, :], fl[:, 1:64, :]),
                         (o[:, 1:CT], f[:, 0:CT - 1]), (o[:, 0:CT - 1], f[:, 1:CT])]
                for di, (ov, fv) in enumerate(views):
                    eng = "dve"
                    k.op(eng, lambda e, ov=ov, fv=fv, ci=ci, di=di: e.scalar_tensor_tensor(out=ov, in0=fv, scalar=cfs[:, ci, 1 + di:2 + di], in1=ov,
                                                                                          op0=ALU.mult, op1=ALU.add), reads=[bf, bcfs], writes=[bo_])
                k.dma(rmT.ap()[ci], o[:], reads=[bo_], adds=[brm])
            k.barrier()

        with ExitStack() as st:
          if i_phase_on(3):
            w2 = k.sb("w2s", [128, 256], stack=st); a2 = k.sb("a2s", [128, 256], stack=st); bw2 = Buf("w2")
            k.dma(w2[:], w2d.ap()[l], adds=[bw2]); k.dma(a2[:], a2d.ap()[l], adds=[bw2])
            rsm = k.sb("rsm", [128, 512], stack=st); brsm = Buf("rsm")
            k.dma(rsm[:], rsd.ap(), writes=[brsm])
            rm = k.sb("rm", [128, 9, 512], stack=st); brmt = Buf("rm")
            tw = k.sb("tw", [128, 512], stack=st); btw = Buf("tw")
            kkr = k.sb("kkr", [128, 2, 512], stack=st); bkkr = Buf("kkr")
            sqk = k.sb("sqk", [128, 2, 512], stack=st); bsqk = Buf("sqk")
            kk = k.sb("kk", [128, 2, 512], stack=st); bkk = Buf("kk")
            nr = k.sb("nr", [128, 512], stack=st); bnr = Buf("nr")
            sg = k.sb("sg", [128, 512], stack=st); bsg = Buf("sg")
            aa = k.sb("aa", [128, 512], stack=st); baa = Buf("aa")
            cs = k.sb("cs", [128, 512], stack=st); bcs = Buf("cs")
            Ei = k.sb("Ei", [128, 512], stack=st); bEi = Buf("Ei")
            Ex = k.sb("Ex", [128, 512], stack=st); bEx = Buf("Ex")
            gi = k.sb("gi", [128, 512], stack=st); bgi = Buf("gi")
            gx = k.sb("gx", [128, 512], stack=st); bgx = Buf("gx")
            gv = k.sb("gv", [128, 512], stack=st); bgv = Buf("gv")
            gc = k.sb("gc", [128, 8], stack=st); bgc = Buf("gc")
            kd = k.sb("kd", [128, 512], stack=st); bkd = Buf("kd")
            bt = k.sb("bt", [128, 512], stack=st); bbt = Buf("bt")
            kds = k.sb("kds", [128, 2, 512], stack=st); bkds = Buf("kds")
            so = [k.sb(f"so{i}", [128, 5, 512], BF16, stack=st) for i in range(2)]; bso = [Buf("so0"), Buf("so1")]
            it = 0
            for (t0, W, who) in TILES:
                nck = W // 64
                k.dma(rm[:, :, :W], rmT.ap().rearrange("c p t -> p c t")[:, :, t0:t0 + W], reads=[brm], writes=[brmt])
                k.op("act", lambda e: e.activation(out=tw[:, :W], in_=rm[:, 7, :W], func=AF.Tanh), reads=[brmt], writes=[btw])
                for hp in range(2):
                    k.op("dve", lambda e, hp=hp: e.tensor_scalar(out=kkr[:, hp, :W], in0=rm[:, 3 + hp, :W], scalar1=pp[:, KK + hp:KK + hp + 1], scalar2=None, op0=ALU.mult),
                         reads=[brmt, bpp], adds=[bkkr])
                k.op("act", lambda e: e.activation(out=sqk[:, :, :W], in_=kkr[:, :, :W], func=AF.Square), reads=[bkkr], writes=[bsqk])
                for hp in range(2):
                    bk, bb = k.bank()
                    k.op("pe", lambda e, hp=hp: e.matmul(bk[:, :W], lhsT=bones, rhs=sqk[:, hp, :W], start=True, stop=True), reads=[bsqk, bc], writes=[bb])
                    k.op("act", lambda e: e.activation(out=nr[:, :W], in_=bk[:, :W], func=AF.Sqrt), reads=[bb], writes=[bnr])
                    k.op("dve", lambda e: e.tensor_scalar(out=nr[:, :W], in0=nr[:, :W], scalar1=1e-12, scalar2=None, op0=ALU.max), writes=[bnr])
                    k.op("dve", lambda e: e.reciprocal(out=nr[:, :W], in_=nr[:, :W]), writes=[bnr])
                    k.op("dve", lambda e, hp=hp: e.tensor_tensor(out=kk[:, hp, :W], in0=kkr[:, hp, :W], in1=nr[:, :W], op=ALU.mult), reads=[bkkr, bnr], adds=[bkk])
                for hp in range(2):
                    for d in range(2):
                        bk, bb = k.bank()
                        k.op("pe", lambda e, d=d, hp=hp: e.matmul(bk[:, :W], lhsT=w2[d * 64:(d + 1) * 64, hp * 128:(hp + 1) * 128], rhs=tw[d * 64:(d + 1) * 64, :W],
                                                                 start=True, stop=True), reads=[bw2, btw], writes=[bb])
                        k.op("act", lambda e, d=d, hp=hp: e.activation(out=sg[:, :W], in_=bk[:, :W], func=AF.Sigmoid, bias=pp[:, W0 + d * 2 + hp:W0 + d * 2 + hp + 1]),
                             reads=[bb, bpp], writes=[bsg])
                        bk2, bb2 = k.bank()
                        k.op("pe", lambda e, d=d, hp=hp: e.matmul(bk2[:, :W], lhsT=a2[d * 64:(d + 1) * 64, hp * 128:(hp + 1) * 128], rhs=rm[d * 64:(d + 1) * 64, 8, :W],
                                                                 start=True, stop=True), reads=[bw2, brmt], writes=[bb2])
                        k.op("act", lambda e, d=d, hp=hp: e.activation(out=aa[:, :W], in_=bk2[:, :W], func=AF.Sigmoid, bias=pp[:, A0 + d * 2 + hp:A0 + d * 2 + hp + 1]),
                             reads=[bb2, bpp], writes=[baa])
                        k.op("dve", lambda e: e.tensor_tensor_scan(out=cs[:, :W], data0=rsm[:, :W], data1=sg[:, :W], initial=0.0, op0=ALU.mult, op1=ALU.add),
                             reads=[brsm, bsg], writes=[bcs])
                        cs3 = cs[:, :W].rearrange("p (c t) -> p c t", t=64)
                        if d == 0:
                            k.op("dve", lambda e: e.tensor_tensor(out=Ex[:, :W], in0=cs[:, :W], in1=sg[:, :W], op=ALU.subtract), reads=[bcs, bsg], writes=[bEx])
                            Eiv = cs; bEiv = bcs
                        else:
                            Ex3 = Ex[:, :W].rearrange("p (c t) -> p c t", t=64)
                            k.op("dve", lambda e, cs3=cs3, Ex3=Ex3: e.tensor_tensor(out=Ex3, in0=cs3[:, :, 63:64].broadcast_to([128, nck, 64]), in1=cs3, op=ALU.subtract),
                                 reads=[bcs], writes=[bEx])
                            k.op("dve", lambda e: e.tensor_tensor(out=Ei[:, :W], in0=Ex[:, :W], in1=sg[:, :W], op=ALU.add), reads=[bEx, bsg], writes=[bEi])
                            Eiv = Ei; bEiv = bEi
                        k.op("act", lambda e, cs3=cs3: e.activation(out=gc[:, :nck], in_=cs3[:, :, 63], func=AF.Exp, scale=-C0), reads=[bcs], writes=[bgc])
                        k.op("act", lambda e, Eiv=Eiv: e.activation(out=gi[:, :W], in_=Eiv[:, :W], func=AF.Exp, scale=-C0), reads=[bEiv], writes=[bgi])
                        k.op("act", lambda e: e.activation(out=gx[:, :W], in_=Ex[:, :W], func=AF.Exp, scale=-C0), reads=[bEx], writes=[bgx])
                        k.op("act", lambda e, Eiv=Eiv: e.activation(out=gv[:, :W], in_=Eiv[:, :W], func=AF.Exp, scale=C0), reads=[bEiv], writes=[bgv])
                        s_ = so[it % 2]; bs_ = bso[it % 2]; it += 1
                        k.op("dve", lambda e, s_=s_, hp=hp: e.tensor_tensor(out=s_[:, 0, :W], in0=kk[:, hp, :W], in1=gx[:, :W], op=ALU.mult), reads=[bkk, bgx], writes=[bs_])
                        k.op("dve", lambda e, s_=s_, hp=hp: e.tensor_tensor(out=s_[:, 1, :W], in0=rm[:, hp, :W], in1=gi[:, :W], op=ALU.mult), reads=[brmt, bgi], adds=[bs_])
                        k.op("dve", lambda e, hp=hp: e.tensor_scalar(out=kd[:, :W], in0=aa[:, :W], scalar1=pp[:, KA + hp:KA + hp + 1], scalar2=omka[:, hp:hp + 1],
                                                                    op0=ALU.mult, op1=ALU.add), reads=[baa, bpp, bomka], writes=[bkd])
                        k.op("dve", lambda e, hp=hp: e.tensor_tensor(out=kd[:, :W], in0=kd[:, :W], in1=rm[:, 3 + hp, :W], op=ALU.mult), reads=[brmt], writes=[bkd])
                        k.op("dve", lambda e, s_=s_: e.tensor_tensor(out=s_[:, 2, :W], in0=kd[:, :W], in1=gv[:, :W], op=ALU.mult), reads=[bkd, bgv], adds=[bs_])
                        k.op("dve", lambda e, hp=hp: e.tensor_tensor(out=bt[:, :W], in0=aa[:, :W], in1=kk[:, hp, :W], op=ALU.mult), reads=[baa, bkk], writes=[bbt])
                        k.op("dve", lambda e, s_=s_: e.tensor_tensor(out=s_[:, 3, :W], in0=bt[:, :W], in1=gv[:, :W], op=ALU.mult), reads=[bbt, bgv], adds=[bs_])
                        if d == 0:
                            k.op("dve", lambda e, hp=hp: e.tensor_copy(out=kds[:, hp, :W], in_=kd[:, :W]), reads=[bkd], writes=[bkds])
                        else:
                            k.op("dve", lambda e, hp=hp: e.tensor_tensor(out=kds[:, hp, :W], in0=kds[:, hp, :W], in1=kd[:, :W], op=ALU.add), reads=[bkd], writes=[bkds])
                        k.op("act", lambda e, s_=s_, hp=hp: e.copy(out=s_[:, 4, :W], in_=rm[:, 5 + hp, :W]), reads=[brmt], adds=[bs_])
                        k.dma(scin.ap()[d * 2 + hp][:, :, t0:t0 + W], s_[:, :, :W], reads=[bs_], adds=[bscin])
                        k.dma(gcd.ap()[d * 2 + hp][:, t0 // 64:t0 // 64 + nck], gc[:, :nck], reads=[bgc], adds=[bgcd])
                    k.op("dve", lambda e, hp=hp: e.scalar_tensor_tensor(out=kds[:, hp, :W], in0=kds[:, hp, :W], scalar=pp[:, RK + hp:RK + hp + 1], in1=rm[:, hp, :W],
                                                                       op0=ALU.mult, op1=ALU.mult), reads=[brmt, bpp], writes=[bkds])
                k.dma(rksT.ap().rearrange("c p t -> p c t")[:, :, t0:t0 + W], kds[:, :, :W], reads=[bkds], adds=[brks])
            k.barrier()

        with ExitStack() as st:
          if i_phase_on(4):
            cm = k.sb("cm", [64, 4, 512], stack=st); bcm = Buf("cm")
            k.dma(cm[:], cmd.ap(), writes=[bcm])
            idb = k.sb("idb", [64, 64], BF16, stack=st); bidb = Buf("idb")
            k.op("dve", lambda e: e.tensor_copy(out=idb[:], in_=ident[0:64, 0:64]), reads=[bc], writes=[bidb])
            inT = [k.sb(f"inT{d}", [64, 4, 5, 512], BF16, stack=st) for d in range(2)]; binT = [Buf("inT0"), Buf("inT1")]
            gct = [k.sb(f"gct{d}", [64, 4, 8], stack=st) for d in range(2)]; bgct = [Buf("g0"), Buf("g1")]
            yst = [k.sb(f"yst{d}", [64, 4, 512], stack=st) for d in range(2)]; byst = [Buf("y0"), Buf("y1")]
            Hs = k.sb("Hs", [64, 8, 64], stack=st); bHs = Buf("Hs")
            Hb = k.sb("Hb", [64, 8, 64], BF16, stack=st); bHb = Buf("Hb")
            k.op("dve", lambda e: e.memset(Hs[:], 0.0), writes=[bHs])
            k.op("dve", lambda e: e.memset(Hb[:], 0.0), writes=[bHb])

            def b16(name):
                return k.sb(name, [64, 512], BF16, stack=st), Buf(name)
            Ktm, bKtm = b16("Ktm"); NBtm, bNB = b16("NBtm"); Vtm, bVtm = b16("Vtm")
            Pm = [None, None]; bPm = [None, None]; Nm = [None, None]; bNm = [None, None]; Ab = [None, None]; bAb = [None, None]
            for i in range(2):
                Pm[i], bPm[i] = b16(f"Pm{i}"); Nm[i], bNm[i] = b16(f"Nm{i}"); Ab[i], bAb[i] = b16(f"Ab{i}")
            A32 = [k.sb(f"A32{i}", [64, 512], stack=st) for i in range(2)]; bA32 = [Buf("A32a"), Buf("A32b")]
            AKT, bAKT = b16("AKT"); RKT, bRKT = b16("RKT"); RBT, bRBT = b16("RBT"); RHSs, bRHS = b16("RHSs"); Us, bUs = b16("Us")
            Ht = k.sb("Ht", [64, 8, 64], stack=st); bHt = Buf("Ht")
            Fc = [(0, ci) for ci in range(4)] + [(ti, ci) for ti in range(1, 9) for ci in range(8)]
            Bc = [(0, ci) for ci in range(3, -1, -1)] + [(ti, ci) for ti in range(8, 0, -1) for ci in range(7, -1, -1)]
            cur = [None, None]

            def flush(d):
                ti = cur[d]
                t0, W, _ = TILES[ti]
                yv = yT.ap()[d].rearrange("(h v) t -> v h t", v=64)
                k.dma(yv[:, :, t0:t0 + W], yst[d][:, :, :W], reads=[byst[d]], adds=[byT])

            def load(d, ti):
                t0, W, _ = TILES[ti]
                for h in range(4):
                    hp = h // 2; e_ = h % 2
                    k.dma(inT[d][:, h, :, :W], scin.ap()[d * 2 + hp][e_ * 64:(e_ + 1) * 64, :, t0:t0 + W], reads=[bscin],
                          adds=[binT[d]], q="sp" if h % 2 == 0 else "pool")
                    k.dma(gct[d][:, h, :W // 64], gcd.ap()[d * 2 + hp][e_ * 64:(e_ + 1) * 64, t0 // 64:t0 // 64 + W // 64], reads=[bgcd],
                          writes=[bgct[d]] if h == 0 else (), adds=[bgct[d]] if h else ())

            def opd(d, h, kind, ci):
                return inT[d][:, h, kind, ci * 64:(ci + 1) * 64]

            def blk(t, q):
                return t[:, q * 64:(q + 1) * 64]

            import os as _os
            R3N = int(_os.environ.get('R3N', str(NCHK)))
            for step in range(R3N):
                tcs = [Fc[step], Bc[step]]
                for d in range(2):
                    if cur[d] != tcs[d][0]:
                        if cur[d] is not None:
                            flush(d)
                        load(d, tcs[d][0])
                        cur[d] = tcs[d][0]
                cis = [tcs[0][1], tcs[1][1]]
                for kind, dst, bdst, neg in ((2, Ktm, bKtm, False), (3, NBtm, bNB, True), (4, Vtm, bVtm, False)):
                    bk, bb = k.bank()
                    for d in range(2):
                        for h in range(4):
                            q = d * 4 + h
                            k.op("pe", lambda e, d=d, h=h, q=q, kind=kind, bk=bk: e.matmul(blk(bk[0:64, :], q), lhsT=opd(d, h, kind, cis[d]), rhs=idb[:], start=True, stop=True),
                                 reads=[binT[d], bidb], writes=[bb])
                    if neg:
                        k.op("act", lambda e, dst=dst, bk=bk: e.activation(out=dst[:], in_=bk[0:64, :], func=AF.Copy, scale=-1.0), reads=[bb], writes=[bdst])
                    elif kind == 2:
                        k.op("dve", lambda e, dst=dst, bk=bk: e.tensor_copy(out=dst[:], in_=bk[0:64, :]), reads=[bb], writes=[bdst])
                    else:
                        k.op("act", lambda e, dst=dst, bk=bk: e.copy(out=dst[:], in_=bk[0:64, :]), reads=[bb], writes=[bdst])

                def prod(la, ra, dst, bdst, mki, neg):
                    bk, bb = k.bank()
                    for d in range(2):
                        for h in range(4):
                            q = d * 4 + h
                            k.op("pe", lambda e, d=d, h=h, q=q: e.matmul(blk(bk[0:64, :], q), lhsT=opd(d, h, la, cis[d]), rhs=opd(d, h, ra, cis[d]),
                                                                        start=True, stop=True), reads=[binT[d]], writes=[bb])
                    k.op("dve", lambda e: e.scalar_tensor_tensor(out=dst[:], in0=bk[0:64, :], scalar=(-1.0 if neg else 1.0), in1=cm[:, mki, :], op0=ALU.mult, op1=ALU.mult),
                         reads=[bb, bcm], writes=[bdst])
                prod(3, 0, Pm[0], bPm[0], 0, True)
                prod(0, 3, Nm[0], bNm[0], 1, True)
                prod(2, 0, AKT, bAKT, 0, False)
                prod(2, 1, RKT, bRKT, 2, False)
                prod(3, 1, RBT, bRBT, 2, True)
                k.op("dve", lambda e: e.tensor_tensor(out=A32[0][:], in0=Pm[0][:], in1=cm[:, 3, :], op=ALU.add), reads=[bPm[0], bcm], writes=[bA32[0]])
                k.op("act", lambda e: e.copy(out=Ab[0][:], in_=A32[0][:]), reads=[bA32[0]], writes=[bAb[0]])
                pc = 0; ac = 0
                for j in range(1, 6):
                    nn = 1 - pc
                    bk, bb = k.bank()
                    for q in range(8):
                        k.op("pe", lambda e, q=q, pc=pc: e.matmul(blk(bk[0:64, :], q), lhsT=blk(Pm[pc], q), rhs=blk(Nm[pc], q), start=True, stop=True),
                             reads=[bPm[pc], bNm[pc]], writes=[bb])
                    if j < 5:
                        bk2, bb2 = k.bank()
                        for q in range(8):
                            k.op("pe", lambda e, q=q, pc=pc: e.matmul(blk(bk2[0:64, :], q), lhsT=blk(Nm[pc], q), rhs=blk(Pm[pc], q), start=True, stop=True),
                                 reads=[bPm[pc], bNm[pc]], writes=[bb2])
                    k.op("act", lambda e, nn=nn, bk=bk: e.copy(out=Nm[nn][:], in_=bk[0:64, :]), reads=[bb], writes=[bNm[nn]])
                    if j < 5:
                        k.op("dve", lambda e, nn=nn, bk2=bk2: e.tensor_copy(out=Pm[nn][:], in_=bk2[0:64, :]), reads=[bb2], writes=[bPm[nn]])
                    bk3, bb3 = k.bank()
                    for q in range(8):
                        k.op("pe", lambda e, q=q, nn=nn, ac=ac: e.matmul(blk(bk3[0:64, :], q), lhsT=blk(Nm[nn], q), rhs=blk(Ab[ac], q), start=True, stop=True),
                             reads=[bNm[nn], bAb[ac]], writes=[bb3])
                    k.op("dve", lambda e, ac=ac, bk3=bk3: e.tensor_tensor(out=A32[1 - ac][:], in0=bk3[0:64, :], in1=A32[ac][:], op=ALU.add), reads=[bb3, bA32[ac]], writes=[bA32[1 - ac]])
                    k.op("act", lambda e, ac=ac: e.copy(out=Ab[1 - ac][:], in_=A32[1 - ac][:]), reads=[bA32[1 - ac]], writes=[bAb[1 - ac]])
                    pc = nn; ac = 1 - ac
                A = Ab[ac]; bA = bAb[ac]
                bk, bb = k.bank()
                for d in range(2):
                    for h in range(4):
                        q = d * 4 + h
                        k.op("pe", lambda e, d=d, h=h, q=q: e.matmul(blk(bk[0:64, :], q), lhsT=opd(d, h, 0, cis[d]), rhs=Hb[:, q, :], start=True, stop=False),
                             reads=[binT[d], bHb], writes=[bb])
                        k.op("pe", lambda e, q=q: e.matmul(blk(bk[0:64, :], q), lhsT=blk(AKT, q), rhs=blk(Vtm, q), start=False, stop=True),
                             reads=[bAKT, bVtm], writes=[bb])
                k.op("act", lambda e, bk=bk: e.copy(out=RHSs[:], in_=bk[0:64, :]), reads=[bb], writes=[bRHS])
                bk, bb = k.bank()
                for q in range(8):
                    k.op("pe", lambda e, q=q: e.matmul(blk(bk[0:64, :], q), lhsT=blk(A, q), rhs=blk(RHSs, q), start=True, stop=True), reads=[bA, bRHS], writes=[bb])
                k.op("dve", lambda e, bk=bk: e.tensor_copy(out=Us[:], in_=bk[0:64, :]), reads=[bb], writes=[bUs])
                for d in range(2):
                    bk, bb = k.bank()
                    for h in range(4):
                        q = d * 4 + h
                        k.op("pe", lambda e, d=d, h=h, q=q, bk=bk: e.matmul(blk(bk[0:64, :], h), lhsT=Hb[:, q, :], rhs=opd(d, h, 1, cis[d]), start=True, stop=False),
                             reads=[binT[d], bHb], writes=[bb])
                        k.op("pe", lambda e, q=q, h=h, bk=bk: e.matmul(blk(bk[0:64, :], h), lhsT=blk(Vtm, q), rhs=blk(RKT, q), start=False, stop=False), reads=[bVtm, bRKT], writes=[bb])
                        k.op("pe", lambda e, q=q, h=h, bk=bk: e.matmul(blk(bk[0:64, :], h), lhsT=blk(Us, q), rhs=blk(RBT, q), start=False, stop=True), reads=[bUs, bRBT], writes=[bb])
                    if d == 0:
                        k.op("act", lambda e, d=d, bk=bk: e.copy(out=yst[d][:, 0, cis[d] * 64:(cis[d] + 1) * 64], in_=bk[0:64, 0:64]), reads=[bb], adds=[byst[d]])
                        for h in range(1, 4):
                            k.op("act", lambda e, d=d, h=h, bk=bk: e.copy(out=yst[d][:, h, cis[d] * 64:(cis[d] + 1) * 64], in_=bk[0:64, h * 64:(h + 1) * 64]), reads=[bb], adds=[byst[d]])
                    else:
                        for h in range(4):
                            k.op("dve", lambda e, d=d, h=h, bk=bk: e.tensor_copy(out=yst[d][:, h, cis[d] * 64:(cis[d] + 1) * 64], in_=bk[0:64, h * 64:(h + 1) * 64]), reads=[bb], adds=[byst[d]])
                bk, bb = k.bank()
                for q in range(8):
                    k.op("pe", lambda e, q=q: e.matmul(blk(bk[0:64, :], q), lhsT=blk(Ktm, q), rhs=blk(Vtm, q), start=True, stop=False), reads=[bKtm, bVtm], writes=[bb])
                    k.op("pe", lambda e, q=q: e.matmul(blk(bk[0:64, :], q), lhsT=blk(NBtm, q), rhs=blk(Us, q), start=False, stop=True), reads=[bNB, bUs], writes=[bb])
                k.op("dve", lambda e, bk=bk: e.tensor_tensor(out=Ht[:].rearrange("p a b -> p (a b)"), in0=bk[0:64, :], in1=Hs[:].rearrange("p a b -> p (a b)"), op=ALU.add),
                     reads=[bb, bHs], writes=[bHt])
                for d in range(2):
                    for h in range(4):
                        q = d * 4 + h
                        k.op("dve", lambda e, d=d, h=h, q=q: e.tensor_scalar(out=Hs[:, q, :], in0=Ht[:, q, :], scalar1=gct[d][:, h, cis[d]:cis[d] + 1], scalar2=None, op0=ALU.mult),
                             reads=[bHt, bgct[d]], writes=[bHs] if q == 0 else (), adds=[bHs] if q else ())
                k.op("act", lambda e: e.copy(out=Hb[:].rearrange("p a b -> p (a b)"), in_=Hs[:].rearrange("p a b -> p (a b)")), reads=[bHs], writes=[bHb])
            flush(0); flush(1)
            k.barrier()

        with ExitStack() as st:
          if i_phase_on(5):
            g2 = k.sb("g2s", [128, 256], stack=st); bg2 = Buf("g2")
            k.dma(g2[:], g2d.ap()[l], writes=[bg2])
            yf = k.sb("yf", [128, 2, 512], stack=st); byf = Buf("yf")
            yb = k.sb("yb", [128, 2, 512], stack=st); byb = Buf("yb")
            rmo = k.sb("rmo", [128, 7, 512], stack=st); brmo = Buf("rmo")
            rk_ = k.sb("rks", [128, 2, 512], stack=st); brk_ = Buf("rks")
            sq = k.sb("sq4", [128, 512], stack=st); bsq = Buf("sq4")
            mu = k.sb("mu4", [128, 512], stack=st); bmu = Buf("mu4")
            va = k.sb("va4", [128, 512], stack=st); bva = Buf("va4")
            sgl = k.sb("sgl", [128, 512], stack=st); bsgl = Buf("sgl")
            orw = k.sb("orw", [128, 2, 512], stack=st); borw = Buf("orw")
            bn = k.sb("bn4", [128, 512], stack=st); bbn = Buf("bn4")
            for (t0, W, who) in TILES:
                if last and who == 1:
                    continue
                yv = yT.ap().rearrange("d (hp p) t -> d p hp t", p=128)
                k.dma(yf[:, :, :W], yv[0][:, :, t0:t0 + W], reads=[byT], writes=[byf])
                k.dma(yb[:, :, :W], yv[1][:, :, t0:t0 + W], reads=[byT], writes=[byb])
                k.dma(rmo[:, :, :W], rmT.ap().rearrange("c p t -> p c t")[:, 0:7, t0:t0 + W], reads=[brm], writes=[brmo])
                k.dma(rk_[:, :, :W], rksT.ap().rearrange("c p t -> p c t")[:, :, t0:t0 + W], reads=[brks], writes=[brk_])
                k.op("dve", lambda e: e.tensor_tensor(out=yf[:, :, :W], in0=yf[:, :, :W], in1=yb[:, :, :W], op=ALU.add), reads=[byb], writes=[byf])
                k.op("act", lambda e: e.activation(out=sgl[:, :W], in_=rmo[:, 2, :W], func=AF.Sigmoid), reads=[brmo], writes=[bsgl])
                for hp in range(2):
                    bk1, bb1 = k.bank()
                    k.op("pe", lambda e, hp=hp: e.matmul(bk1[:, :W], lhsT=bones, rhs=yf[:, hp, :W], start=True, stop=True), reads=[byf, bc], writes=[bb1])
                    k.op("act", lambda e, hp=hp: e.activation(out=sq[:, :W], in_=yf[:, hp, :W], func=AF.Square), reads=[byf], writes=[bsq])
                    bk2, bb2 = k.bank()
                    k.op("pe", lambda e: e.matmul(bk2[:, :W], lhsT=bones, rhs=sq[:, :W], start=True, stop=True), reads=[bsq, bc], writes=[bb2])
                    k.op("dve", lambda e: e.tensor_scalar(out=mu[:, :W], in0=bk1[:, :W], scalar1=1.0 / 64, scalar2=None, op0=ALU.mult), reads=[bb1], writes=[bmu])
                    k.op("dve", lambda e: e.tensor_tensor(out=va[:, :W], in0=mu[:, :W], in1=mu[:, :W], op=ALU.mult), reads=[bmu], writes=[bva])
                    k.op("dve", lambda e: e.scalar_tensor_tensor(out=va[:, :W], in0=bk2[:, :W], scalar=1.0 / 64, in1=va[:, :W], op0=ALU.mult, op1=ALU.subtract),
                         reads=[bb2], writes=[bva])
                    k.op("dve", lambda e: e.tensor_scalar(out=va[:, :W], in0=va[:, :W], scalar1=64e-5, scalar2=None, op0=ALU.add), writes=[bva])
                    k.op("act", lambda e: e.activation(out=va[:, :W], in_=va[:, :W], func=AF.Sqrt), writes=[bva])
                    k.op("dve", lambda e: e.reciprocal(out=va[:, :W], in_=va[:, :W]), writes=[bva])
                    k.op("dve", lambda e, hp=hp: e.tensor_tensor(out=yf[:, hp, :W], in0=yf[:, hp, :W], in1=mu[:, :W], op=ALU.subtract), reads=[bmu], writes=[byf])
                    k.op("dve", lambda e, hp=hp: e.tensor_tensor(out=yf[:, hp, :W], in0=yf[:, hp, :W], in1=va[:, :W], op=ALU.mult), reads=[bva], writes=[byf])
                    k.op("dve", lambda e, hp=hp: e.tensor_scalar(out=yf[:, hp, :W], in0=yf[:, hp, :W], scalar1=pp[:, LXG + hp:LXG + hp + 1], scalar2=pp[:, LXB + hp:LXB + hp + 1],
                                                                op0=ALU.mult, op1=ALU.add), reads=[bpp], writes=[byf])
                    bk3, bb3 = k.bank()
                    k.op("pe", lambda e, hp=hp: e.matmul(bk3[:, :W], lhsT=bones, rhs=rk_[:, hp, :W], start=True, stop=True), reads=[brk_, bc], writes=[bb3])
                    k.op("dve", lambda e, hp=hp: e.tensor_tensor(out=bn[:, :W], in0=bk3[:, :W], in1=rmo[:, 5 + hp, :W], op=ALU.mult), reads=[bb3, brmo], writes=[bbn])
                    k.op("dve", lambda e, hp=hp: e.tensor_tensor(out=bn[:, :W], in0=bn[:, :W], in1=yf[:, hp, :W], op=ALU.add), reads=[byf], writes=[bbn])
                    bk4, bb4 = k.bank()
                    k.op("pe", lambda e, hp=hp: e.matmul(bk4[:, :W], lhsT=g2[:, hp * 128:(hp + 1) * 128], rhs=sgl[:, :W], start=True, stop=True), reads=[bg2, bsgl], writes=[bb4])
                    k.op("dve", lambda e, hp=hp: e.tensor_tensor(out=orw[:, hp, :W], in0=bk4[:, :W], in1=bn[:, :W], op=ALU.mult), reads=[bb4, bbn], writes=[borw])
                k.dma(mixT.ap().rearrange("c p t -> p c t")[:, 6:8, t0:t0 + W], orw[:, :, :W], reads=[borw], adds=[bmix])
            k.barrier()

        with ExitStack() as st:
          if i_phase_on(6):
            wsT = k.sb("wsT", [128, 4, 128], stack=st); bws = Buf("wsT")
            k.dma(wsT[:], sgwT.ap()[l], writes=[bws])
            bsb = k.sb("bsb", [128, 512], stack=st); bbsb = Buf("bsb")
            k.dma(bsb[:], sgb.ap()[l].partition_broadcast(128), writes=[bbsb])
            uv = k.sb("uv", [128, 4, 512], stack=st); buv = Buf("uv")
            sq = k.sb("sqg", [128, 2, 512], stack=st); bsq = Buf("sqg")
            mn = k.sb("mng", [128, 512], stack=st); bmn = Buf("mng")
            va = k.sb("vag", [128, 512], stack=st); bva = Buf("vag")
            vtm = k.sb("vtmg", [128, 256], stack=st); bvtm = Buf("vtmg")
            tmp = k.sb("tmpg", [128, 128], stack=st); btmp = Buf("tmpg")
            osg = k.sb("osg", [128, 2, 512], stack=st); bosg = Buf("osg")
            for (t0, W, who) in TILES:
                if last and who == 1:
                    continue
                k.dma(uv[:, :, :W], pxT.ap().rearrange("c p t -> p c t")[:, 0:4, t0:t0 + W], reads=[bpx], writes=[buv])
                k.op("act", lambda e: e.activation(out=uv[:, :, :W], in_=uv[:, :, :W], func=AF.Gelu), writes=[buv])
                k.op("act", lambda e: e.activation(out=sq[:, :, :W], in_=uv[:, 2:4, :W], func=AF.Square), reads=[buv], writes=[bsq])
                bk1, bb1 = k.bank(); bk2, bb2 = k.bank()
                for c in range(2):
                    k.op("pe", lambda e, c=c: e.matmul(bk1[:, :W], lhsT=ones[:], rhs=uv[:, 2 + c, :W], start=(c == 0), stop=(c == 1)), reads=[buv, bo], writes=[bb1])
                for c in range(2):
                    k.op("pe", lambda e, c=c: e.matmul(bk2[:, :W], lhsT=ones[:], rhs=sq[:, c, :W], start=(c == 0), stop=(c == 1)), reads=[bsq, bo], writes=[bb2])
                k.op("dve", lambda e: e.tensor_scalar(out=mn[:, :W], in0=bk1[:, :W], scalar1=1.0 / 256, scalar2=None, op0=ALU.mult), reads=[bb1], writes=[bmn])
                k.op("dve", lambda e: e.tensor_tensor(out=va[:, :W], in0=mn[:, :W], in1=mn[:, :W], op=ALU.mult), reads=[bmn], writes=[bva])
                k.op("dve", lambda e: e.scalar_tensor_tensor(out=va[:, :W], in0=bk2[:, :W], scalar=1.0 / 256, in1=va[:, :W], op0=ALU.mult, op1=ALU.subtract), reads=[bb2], writes=[bva])
                k.op("dve", lambda e: e.tensor_scalar(out=va[:, :W], in0=va[:, :W], scalar1=1e-5, scalar2=None, op0=ALU.add), writes=[bva])
                k.op("act", lambda e: e.activation(out=va[:, :W], in_=va[:, :W], func=AF.Sqrt), writes=[bva])
                k.op("dve", lambda e: e.reciprocal(out=va[:, :W], in_=va[:, :W]), writes=[bva])
                for c in range(2):
                    k.op("dve", lambda e, c=c: e.tensor_tensor(out=uv[:, 2 + c, :W], in0=uv[:, 2 + c, :W], in1=mn[:, :W], op=ALU.subtract), reads=[bmn], writes=[buv])
                    k.op("dve", lambda e, c=c: e.tensor_tensor(out=uv[:, 2 + c, :W], in0=uv[:, 2 + c, :W], in1=va[:, :W], op=ALU.mult), reads=[bva], writes=[buv])
                    k.op("dve", lambda e, c=c: e.tensor_scalar(out=uv[:, 2 + c, :W], in0=uv[:, 2 + c, :W], scalar1=pp[:, SLG + c:SLG + c + 1], scalar2=pp[:, SLB + c:SLB + c + 1],
                                                              op0=ALU.mult, op1=ALU.add), reads=[bpp], writes=[buv])
                for nb in range(W // 128):
                    cols = slice(nb * 128, (nb + 1) * 128)
                    bkT, bbT = k.bank()
                    for c in range(2):
                        k.op("pe", lambda e, c=c, cols=cols: e.transpose(bkT[:, c * 128:(c + 1) * 128], uv[:, 2 + c, cols], ident), reads=[buv, bc], writes=[bbT])
                    k.op("act", lambda e: e.copy(out=vtm[:], in_=bkT[:, 0:256]), reads=[bbT], writes=[bvtm])
                    bkM, bbM = k.bank()
                    for g in range(4):
                        k.op("pe", lambda e, g=g: e.matmul(bkM[:, g * 128:(g + 1) * 128], lhsT=vtm[:, (g // 2) * 128:(g // 2 + 1) * 128], rhs=wsT[:, g, :], start=True, stop=True),
                             reads=[bvtm, bws], writes=[bbM])
                    for g in range(4):
                        ps_ = slice((g % 2) * 64, (g % 2 + 1) * 64)
                        k.op("dve", lambda e, g=g, ps_=ps_: e.tensor_tensor(out=tmp[ps_, :], in0=bkM[ps_, g * 128:(g + 1) * 128], in1=bsb[ps_, g * 128:(g + 1) * 128], op=ALU.add),
                             reads=[bbM, bbsb], writes=[btmp])
                        k.op("dve", lambda e, g=g, ps_=ps_, cols=cols: e.tensor_tensor(out=osg[ps_, g // 2, cols], in0=tmp[ps_, :], in1=uv[ps_, g // 2, cols], op=ALU.mult),
                             reads=[btmp, buv], writes=[bosg])
                k.dma(mixT.ap().rearrange("c p t -> p c t")[:, 0:2, t0:t0 + W], osg[:, :, :W], reads=[bosg], adds=[bmix])
            k.barrier()

        with ExitStack() as st:
          if i_phase_on(7):
            cosT = k.sb("cosT", [128, T], stack=st); sinT = k.sb("sinT", [128, T], stack=st); btab = Buf("tab")
            k.dma(cosT[:], cosd.ap(), adds=[btab]); k.dma(sinT[:], sind.ap(), adds=[btab], q="pool")
            am = k.sb("am", [128, 2, 512], stack=st); bam = Buf("am")
            k.dma(am[:], amd.ap(), writes=[bam])
            esb = k.sb("esb", [64, 8], stack=st); besb = Buf("esb")
            k.dma(esb[:], sink.ap()[l].partition_broadcast(64), writes=[besb])
            k.op("act", lambda e: e.activation(out=esb[:], in_=esb[:], func=AF.Exp), writes=[besb])
            KT2 = k.sb("KT2", [128, 2, TT], BF16, stack=st); bKT = Buf("KT2")
            Vt = k.sb("Vt", [128, NTOKB, 128], BF16, stack=st); bVt = Buf("Vt")
            onesb = k.sb("onesb", [128, 64], BF16, stack=st); bonb = Buf("onesb")
            k.op("dve", lambda e: e.memset(onesb[:], 1.0), writes=[bonb])
            t2 = k.sb("t2r", [128, 512], stack=st); bt2 = Buf("t2r")
            kraw = k.sb("kraw", [128, TT], stack=st); bkr = Buf("kraw")
            t1 = k.sb("t1", [128, 512], stack=st); bt1 = Buf("t1")
            k.dma(kraw[:], pxT.ap()[11], reads=[bpx], writes=[bkr])

            def rope(dst, src, bsrc, tl, W, wr_reads, bdst):
                bk, bb = k.bank()
                k.op("pe", lambda e: e.matmul(bk[:, :W], lhsT=Rm, rhs=src, start=True, stop=True), reads=[bsrc, bc], writes=[bb])
                k.op("dve", lambda e: e.tensor_tensor(out=t1[:, :W], in0=bk[:, :W], in1=sinT[:, tl:tl + W], op=ALU.mult), reads=[bb, btab], writes=[bt1])
                k.op("dve", lambda e: e.tensor_tensor(out=t2[:, :W], in0=src, in1=cosT[:, tl:tl + W], op=ALU.mult), reads=[bsrc, btab], writes=[bt2])
                k.op("dve", lambda e: e.tensor_tensor(out=dst, in0=t2[:, :W], in1=t1[:, :W], op=ALU.add), reads=[bt1, bt2], writes=[bdst])

            for i in range(8):
                sl = slice(CT + i * 512, CT + (i + 1) * 512)
                rope(kraw[:, sl], kraw[:, sl], bkr, i * 512, 512, [], bkr)
            for hk in range(2):
                ps_ = slice(hk * 64, (hk + 1) * 64)
                for e_ in range(2):
                    po = slice(e_ * 64, (e_ + 1) * 64)
                    k.op("dve" if e_ == 0 else "act", (lambda e, hk=hk, ps_=ps_, po=po: e.tensor_copy(out=KT2[po, hk, :], in_=kraw[ps_, :])) if e_ == 0 else (lambda e, hk=hk, ps_=ps_, po=po: e.copy(out=KT2[po, hk, :], in_=kraw[ps_, :])), reads=[bkr], adds=[bKT])
            vraw = kraw
            k.dma(vraw[:], pxT.ap()[12], reads=[bpx], writes=[bkr])
            for b4 in range(0, NTOKB, 4):
                nb_ = min(4, NTOKB - b4)
                bk, bb = k.bank()
                for i in range(nb_):
                    k.op("pe", lambda e, i=i, b4=b4: e.transpose(bk[:, i * 128:(i + 1) * 128], vraw[:, (b4 + i) * 128:(b4 + i + 1) * 128], ident), reads=[bkr, bc], writes=[bb])
                evac(Vt[:, b4:b4 + nb_, :].rearrange("p a b -> p (a b)"), bk[:, :nb_ * 128], [bb], adds=[bVt])
            qraw = k.sb("qraw", [128, 4, 512], stack=st); bq = Buf("qraw")
            qr = k.sb("qr", [128, 4, 512], BF16, stack=st); bqr = Buf("qr")
            PT = [k.sb(f"PT{i}", [128, 512], BF16, stack=st) for i in range(2)]; bPT = [Buf("PT0"), Buf("PT1")]
            rc = k.sb("rc", [64, 512], stack=st); brc = Buf("rc")
            obt = k.sb("obt", [64, 2, 4, 512], stack=st); bobt = Buf("obt")
            pti = 0; sbi = [0]
            for (t0, W, who) in TILES:
                if last and who == 1:
                    continue
                k.dma(qraw[:, :, :W], pxT.ap().rearrange("c p t -> p c t")[:, 4:8, t0:t0 + W], reads=[bpx], writes=[bq])
                if who == 0:
                    for c in range(4):
                        rope(qr[:, c, :W], qraw[:, c, :W], bq, t0 - CT, W, [], bqr)
                    Q = qr; bQ = bqr
                else:
                    for c in range(4):
                        cast(qr[:, c, :W], qraw[:, c, :W], [bq], writes=[bqr] if c == 0 else (), adds=[bqr] if c else ())
                    Q = qr; bQ = bqr
                for nb in range(W // 128):
                    n = t0 // 128 + nb
                    cols = slice(nb * 128, (nb + 1) * 128)
                    if who == 1:
                        kbs = [(0, None), (1, None)]
                    else:
                        kbs = [(0, None), (1, None)]
                        if n - 1 >= 2:
                            kbs.append((n - 1, 0))
                        kbs.append((n, None))
                        if n + 1 < NTOKB:
                            kbs.append((n + 1, 1))
                    GO = [0, 2, 1, 3]
                    for hk in range(2):
                        bkO, bbO = k.banks[(hk % 2) * 2]; bkD, bbD = k.banks[(hk % 2) * 2 + 1]
                        for idx, (kb, mk) in enumerate(kbs):
                            bkS = [None, None]; bbS = [None, None]
                            bkS[0], bbS[0] = k.banks[4 + (sbi[0] % 2) * 2]; bkS[1], bbS[1] = k.banks[5 + (sbi[0] % 2) * 2]; sbi[0] += 1
                            for pos in range(4):
                                g = GO[pos]; h = hk * 4 + g; e_ = h % 2; chn = h // 2
                                assert e_ == pos // 2
                                k.op("pe", lambda e, pos=pos, e_=e_, chn=chn, kb=kb, hk=hk: e.matmul(bkS[e_][:, (pos % 2) * 128:(pos % 2 + 1) * 128],
                                                                                                    lhsT=KT2[e_ * 64:(e_ + 1) * 64, hk, kb * 128:(kb + 1) * 128],
                                                                                                    rhs=Q[e_ * 64:(e_ + 1) * 64, chn, cols], start=True, stop=True),
                                     reads=[bKT, bQ], writes=[bbS[e_]])
                            P_ = PT[pti % 2]; bP_ = bPT[pti % 2]; pti += 1
                            k.op("act", lambda e, P_=P_, b0=bkS[0]: e.activation(out=P_[:, 0:256], in_=b0[:, 0:256], func=AF.Exp, scale=0.125), reads=[bbS[0]], writes=[bP_])
                            k.op("act", lambda e, P_=P_, b1=bkS[1]: e.activation(out=P_[:, 256:512], in_=b1[:, 0:256], func=AF.Exp, scale=0.125), reads=[bbS[1]], adds=[bP_])
                            if mk is not None:
                                k.op("dve", lambda e, P_=P_, mk=mk: e.tensor_tensor(out=P_[:], in0=P_[:], in1=am[:, mk, :], op=ALU.mult), reads=[bam], writes=[bP_])
                            k.op("pe", lambda e, P_=P_, kb=kb, hk=hk, idx=idx: e.matmul(bkO[0:64, :], lhsT=Vt[:, kb, hk * 64:(hk + 1) * 64], rhs=P_[:], start=(idx == 0), stop=(idx == len(kbs) - 1)),
                                 reads=[bVt, bP_], writes=[bbO])
                            k.op("pe", lambda e, P_=P_, idx=idx: e.matmul(bkD[0:64, :], lhsT=onesb[:], rhs=P_[:], start=(idx == 0), stop=(idx == len(kbs) - 1)), reads=[bonb, bP_], writes=[bbD])
                        for pos in range(4):
                            h = hk * 4 + GO[pos]
                            k.op("dve", lambda e, pos=pos, h=h, bkD=bkD: e.tensor_scalar(out=rc[:, pos * 128:(pos + 1) * 128], in0=bkD[0:64, pos * 128:(pos + 1) * 128],
                                                                                       scalar1=esb[:, h:h + 1], scalar2=None, op0=ALU.add),
                                 reads=[bbD, besb], writes=[brc] if pos == 0 else (), adds=[brc] if pos else ())
                        k.op("dve", lambda e: e.reciprocal(out=rc[:], in_=rc[:]), writes=[brc])
                        for pos in range(4):
                            g = GO[pos]
                            k.op("dve", lambda e, pos=pos, g=g, bkO=bkO, hk=hk, cols=cols: e.tensor_tensor(out=obt[:, hk, g, cols], in0=bkO[0:64, pos * 128:(pos + 1) * 128],
                                                                                                      in1=rc[:, pos * 128:(pos + 1) * 128], op=ALU.mult), reads=[bbO, brc], adds=[bobt])
                for hk in range(2):
                    for g in range(4):
                        h = hk * 4 + g
                        k.dma(mixT.ap()[2 + h // 2][(h % 2) * 64:(h % 2 + 1) * 64, t0:t0 + W], obt[:, hk, g, :W], reads=[bobt], adds=[bmix], q="sp" if g % 2 == 0 else "pool")
                k.op("dve", lambda e: e.memset(rc[:, 0:1], 0.0), reads=[], writes=[bobt, brc])
            k.barrier()

        with ExitStack() as st:
          if i_phase_on(8):
            wo = k.sb("wo", [128, 8, 1024], BF16, stack=st); bwo = Buf("wo")
            wst = [k.sb(f"wst5{i}", [128, 1024], stack=st) for i in range(2)]; bwst = [Buf("wst0"), Buf("wst1")]
            wv = w_out.ap()[l].rearrange("(kc p) n -> p kc n", p=128)
            for kc in range(8):
                k.dma(wst[kc % 2][:], wv[:, kc, :], writes=[bwst[kc % 2]], q="sp" if kc % 2 == 0 else "pool")
                cast(wo[:, kc, :], wst[kc % 2][:], [bwst[kc % 2]], adds=[bwo])
            Mb = k.sb("Mb5", [128, 8, 512], BF16, stack=st); bMb = Buf("Mb5")
            X = k.sb("X5", [128, 8, 512], stack=st); bX = Buf("X5")
            M = k.sb("M5", [128, 8, 512], stack=st); bM = Buf("M5")
            Y = k.sb("Y5", [128, 8, 512], stack=st); bY = Buf("Y5")
            SQ = k.sb("SQ5", [128, 8, 512], stack=st); bSQ = Buf("SQ5")
            rs = k.sb("rs5", [128, 512], stack=st); brs = Buf("rs5")
            xv = xsrc(l).ap().rearrange("c p t -> p c t")
            xo = xs.ap().rearrange("c p t -> p c t")
            for (t0, W, who) in TILES:
                if last and who == 1:
                    continue
                k.dma(X[:, :, :W], xv[:, :, t0:t0 + W], reads=[bxs], writes=[bX])
                k.dma(M[:, :, :W], mixT.ap().rearrange("c p t -> p c t")[:, :, t0:t0 + W], reads=[bmix], writes=[bM], q="pool")
                for kc in range(8):
                    cast(Mb[:, kc, :W], M[:, kc, :W], [bM], writes=[bMb] if kc == 0 else (), adds=[bMb] if kc else ())
                for j in range(8):
                    bk, bb = k.bank()
                    for kc in range(8):
                        k.op("pe", lambda e, kc=kc, j=j: e.matmul(bk[:, :W], lhsT=wo[:, kc, j * 128:(j + 1) * 128], rhs=Mb[:, kc, :W], start=(kc == 0), stop=(kc == 7)),
                             reads=[bwo, bMb], writes=[bb])
                    evac(Y[:, j, :W], bk[:, :W], [bb], adds=[bY])
                rmsn(st, Y, bY, W, SQ, bSQ, rs, brs)
                resid_update(X, bX, Y, bY, rs, brs, W, 2, who)
                k.dma(xo[:, :, t0:t0 + W], X[:, :, :W], reads=[bX], adds=[bxs])
            k.barrier()

        with ExitStack() as st:
          if i_phase_on(9):
            wgb = k.sb("wgb", [128, 8, 5632], BF16, stack=st); bwgb = Buf("wgb")
            wdb = k.sb("wdb", [128, 22, 1024], BF16, stack=st); bwdb = Buf("wdb")
            wst = [k.sb(f"wst6{i}", [128, 1024], stack=st) for i in range(2)]; bwst = [Buf("wst0"), Buf("wst1")]
            gv_ = w_gu.ap()[l].rearrange("(kc p) n -> p kc n", p=128)
            dv_ = w_down.ap()[l].rearrange("(j p) n -> p j n", p=128)
            wi = 0
            for kc in range(8):
                for cb in range(6):
                    c0_ = cb * 1024; cw = min(1024, 5632 - c0_)
                    k.dma(wst[wi % 2][:, :cw], gv_[:, kc, c0_:c0_ + cw], writes=[bwst[wi % 2]], q="sp" if wi % 2 == 0 else "pool")
                    cast(wgb[:, kc, c0_:c0_ + cw], wst[wi % 2][:, :cw], [bwst[wi % 2]], adds=[bwgb])
                    wi += 1
            for j in range(22):
                k.dma(wst[wi % 2][:], dv_[:, j, :], writes=[bwst[wi % 2]], q="sp" if wi % 2 == 0 else "pool")
                cast(wdb[:, j, :], wst[wi % 2][:], [bwst[wi % 2]], adds=[bwdb])
                wi += 1
            X = k.sb("X6", [128, 8, 256], stack=st); bX = Buf("X6")
            H = k.sb("H6", [128, 8, 256], BF16, stack=st); bH = Buf("H6")
            SQ = k.sb("SQ6", [128, 8, 256], stack=st); bSQ = Buf("SQ6")
            Y = k.sb("Y6", [128, 8, 256], stack=st); bY = Buf("Y6")
            rs = k.sb("rs6", [128, 256], stack=st); brs = Buf("rs6")
            actb = k.sb("actb", [128, 22, 256], BF16, stack=st); bact = Buf("actb")
            sl = [k.sb(f"sl6{i}", [128, 256], stack=st) for i in range(2)]; bsl = [Buf("sl0"), Buf("sl1")]
            xo = xs.ap().rearrange("c p t -> p c t")
            for (t0, W, who) in TILES5:
                if last and who == 1:
                    continue
                k.dma(X[:, :, :W], xo[:, :, t0:t0 + W], reads=[bxs], writes=[bX])
                rmsn(st, X, bX, W, SQ, bSQ, rs, brs)
                modulate(H, bH, X, bX, rs, brs, W, 3, 4, who)
                for j in range(22):
                    bkg, bbg = k.bank(); bku, bbu = k.bank()
                    for kc in range(8):
                        k.op("pe", lambda e, kc=kc, j=j: e.matmul(bkg[:, :W], lhsT=wgb[:, kc, j * 128:(j + 1) * 128], rhs=H[:, kc, :W], start=(kc == 0), stop=(kc == 7)),
                             reads=[bwgb, bH], writes=[bbg])
                    for kc in range(8):
                        k.op("pe", lambda e, kc=kc, j=j: e.matmul(bku[:, :W], lhsT=wgb[:, kc, 2816 + j * 128:2816 + (j + 1) * 128], rhs=H[:, kc, :W], start=(kc == 0), stop=(kc == 7)),
                             reads=[bwgb, bH], writes=[bbu])
                    s_ = sl[j % 2]; bs_ = bsl[j % 2]
                    k.op("act", lambda e, bkg=bkg, s_=s_: e.activation(out=s_[:, :W], in_=bkg[:, :W], func=AF.Silu), reads=[bbg], writes=[bs_])
                    k.op("dve", lambda e, j=j, bku=bku, s_=s_: e.tensor_tensor(out=actb[:, j, :W], in0=bku[:, :W], in1=s_[:, :W], op=ALU.mult), reads=[bbu, bs_],
                         writes=[bact] if j == 0 else (), adds=[bact] if j else ())
                for jo in range(8):
                    bk, bb = k.bank()
                    for j in range(22):
                        k.op("pe", lambda e, j=j, jo=jo: e.matmul(bk[:, :W], lhsT=wdb[:, j, jo * 128:(jo + 1) * 128], rhs=actb[:, j, :W], start=(j == 0), stop=(j == 21)),
                             reads=[bwdb, bact], writes=[bb])
                    evac(Y[:, jo, :W], bk[:, :W], [bb], writes=[bY] if jo == 0 else (), adds=[bY] if jo else ())
                rmsn(st, Y, bY, W, SQ, bSQ, rs, brs)
                resid_update(X, bX, Y, bY, rs, brs, W, 5, who)
                if l == L - 1:
                    k.dma(outT.ap().rearrange("c p t -> p c t")[:, :, t0 - CT:t0 - CT + W], X[:, :, :W], reads=[bX], adds=[bout])
                else:
                    k.dma(xo[:, :, t0:t0 + W], X[:, :, :W], reads=[bX], adds=[bxs])
            k.barrier()
    k.barrier()
    return nc, k


def host_consts():
    ident = np.eye(128, dtype=np.float32)
    bones = np.zeros((128, 128), np.float32); bones[:64, :64] = 1; bones[64:, 64:] = 1
    Rm = np.zeros((128, 128), np.float32)
    for m in range(128):
        i = m % 32
        if i < 16:
            Rm[m + 16, m] = -1.0
        else:
            Rm[m - 16, m] = 1.0
    cc = np.concatenate([ident, bones, Rm], axis=1)
    smL = np.zeros((128, 9, 4), np.float32); smC = np.zeros((128, 9, 2), np.float32)
    for ci in range(9):
        for p in range(128):
            if ci < 3:
                ch = ci * 128 + p; qd = ch // 96; half = 0 if ch < 192 else 1
            else:
                ch = (ci - 3) * 128 + p; qd = ch // 192; half = 0 if ch < 384 else 1
            smL[p, ci, qd] = 1; smC[p, ci, half] = 1
    s = np.arange(64)[:, None]; t = np.arange(64)[None, :]
    mS = [(s < t), (s > t)]; mST = [(s > t), (s < t)]; mI = [(s <= t), (s >= t)]
    cm = np.zeros((64, 4, 512), np.float32)
    for d in range(2):
        for h in range(4):
            q = d * 4 + h
            cm[:, 0, q * 64:(q + 1) * 64] = mS[d]
            cm[:, 1, q * 64:(q + 1) * 64] = mST[d]
            cm[:, 2, q * 64:(q + 1) * 64] = mI[d]
            cm[:, 3, q * 64:(q + 1) * 64] = np.eye(64)
    j = np.arange(128)[:, None]; i = np.arange(128)[None, :]
    am = np.zeros((128, 2, 512), np.float32)
    for g in range(4):
        am[:, 0, g * 128:(g + 1) * 128] = (j >= i)
        am[:, 1, g * 128:(g + 1) * 128] = (j <= i)
    rs = np.ones((128, 512), np.float32); rs[:, ::64] = 0
    tt = np.arange(T)
    pos = np.stack([tt // 64, tt % 64], -1).astype(np.float32)
    inv = (10000.0 ** (-np.arange(16, dtype=np.float32) / 16)).astype(np.float32)
    ang = pos[:, :, None] * inv
    cosT = np.zeros((128, T), np.float32); sinT = np.zeros((128, T), np.float32)
    for p in range(128):
        dd = p % 64; ax = dd // 32; fr = dd % 16
        cosT[p] = np.cos(ang[:, ax, fr]); sinT[p] = np.sin(ang[:, ax, fr])
    return dict(cc=cc, smL=smL, smC=smC, cm=cm, am=am, rs=rs, cosT=cosT, sinT=sinT)


def cmaj(v, n):
    return np.ascontiguousarray(np.asarray(v, np.float32).reshape(n, 128).T)


def host_inputs(inp, b):
    f = lambda a: np.ascontiguousarray(np.asarray(a, np.float32))
    xall = np.concatenate([inp["ctx"][b], inp["x"][b]], axis=0)
    xin = np.ascontiguousarray(xall.T.reshape(8, 128, TT))
    cin = np.stack([cmaj(inp["c"][b], 8), cmaj(inp["c_ctx"], 8)], axis=-1)
    pp = np.zeros((L, 128, NP), np.float32)
    for l in range(L):
        for i in range(4):
            pp[l, :, NG + i * 8:NG + i * 8 + 8] = cmaj(inp["norm_g"][l, i], 8)
        pp[l, :, BM:BM + 48] = cmaj(inp["b_mod"][l], 48)
        pp[l, :, MU:MU + 9] = cmaj(inp["rw_mu"][l], 9)
        pp[l, :, KK:KK + 2] = cmaj(inp["rw_kk"][l], 2)
        pp[l, :, KA:KA + 2] = cmaj(inp["rw_ka"][l], 2)
        pp[l, :, RK:RK + 2] = cmaj(inp["rw_rk"][l].reshape(-1), 2)
        pp[l, :, LXG:LXG + 2] = cmaj(inp["rw_lnx_g"][l], 2)
        pp[l, :, LXB:LXB + 2] = cmaj(inp["rw_lnx_b"][l], 2)
        pp[l, :, W0:W0 + 4] = cmaj(inp["rw_w0"][l].reshape(-1), 4)
        pp[l, :, A0:A0 + 4] = cmaj(inp["rw_a0"][l].reshape(-1), 4)
        pp[l, :, SLG:SLG + 2] = cmaj(inp["sg_ln_g"][l], 2)
        pp[l, :, SLB:SLB + 2] = cmaj(inp["sg_ln_b"][l], 2)
    m = dict(xin=xin, cin=f(cin), w_mod=f(inp["w_mod"]), w_in=f(inp["w_in"]), w_out=f(inp["w_out"]), w_gu=f(inp["w_gu"]), w_down=f(inp["w_down"]),
             pp=pp, sgwT=f(np.transpose(inp["sg_w"], (0, 3, 1, 2))), sgb=f(inp["sg_b"]).reshape(L, 1, 512), sink=f(inp["attn_sink"]).reshape(L, 1, 8),
             w2=f(inp["rw_w2"]).reshape(L, 128, 256), a2=f(inp["rw_a2"]).reshape(L, 128, 256), g2=f(inp["rw_g2"]))
    m.update(host_consts())
    return m


def kernel(**inp):
    inp = {k_: np.asarray(v) for k_, v in inp.items()}
    nc, _ = build()
    maps = [host_inputs(inp, b % 4) for b in range(4)]
    in_maps = [maps[i % 4] for i in range(8)]
    res = run_bass_kernel_spmd(nc, in_maps, core_ids=list(range(8)))
    out = np.zeros((4, T, D), np.float32)
    for b in range(4):
        o = res.results[b]["outT"]
        out[b] = o.reshape(D, T).T
    return out
```

```python
import numpy as np
import concourse.bass as bass
import concourse.mybir as mybir
from concourse.bass_utils import run_bass_kernel_spmd
from contextlib import ExitStack

F32 = mybir.dt.float32
BF16 = mybir.dt.bfloat16
AF = mybir.ActivationFunctionType
ALU = mybir.AluOpType

D = 1024; T = 4096; CT = 256; TT = T + CT; L = 4
NTOKB = TT // 128
NCHK = TT // 64
C0 = float(np.exp(-0.5))
TILES = [(0, 256, 1)] + [(256 + 512 * i, 512, 0) for i in range(8)]
TILES5 = [(0, 256, 1)] + [(256 + 256 * i, 256, 0) for i in range(16)]
NG = 0; BM = 32; MU = 80; KK = 89; KA = 91; RK = 93; LXG = 95; LXB = 97; W0 = 99; A0 = 103; SLG = 107; SLB = 109; NP = 111
SRC_CH = [8, 9, 10, 13, 14, 15, 16, 17, 18]

SAME_ENGINE_SYNC = True
EPOCH = 12000
DMA_EPOCH = 1500
NDMASEM = 6


class Buf:
    __slots__ = ("name", "w", "r")

    def __init__(self, name):
        self.name = name
        self.w = {}
        self.r = {}


class KB:
    def __init__(self, nc):
        self.nc = nc
        self.es = ExitStack()
        self.E = {"pe": nc.tensor, "act": nc.scalar, "dve": nc.vector, "pool": nc.gpsimd, "sp": nc.sync}
        self.sem = {}
        self.cnt = {}
        self.seen = {e: {} for e in self.E}
        self.nsem = 0
        for e in self.E:
            self._new_sem(e)
        self.dsem = {}
        self.dcnt = {}
        self.drr = {}
        for q in ("sp", "pool"):
            self.dsem[q] = [self._alloc_sem(f"d{q}{i}") for i in range(NDMASEM)]
            self.dcnt[q] = [0] * NDMASEM
            self.drr[q] = 0
        self.ninstr = 0
        self.banks = []
        self.bi = 0

    def _alloc_sem(self, name):
        self.nsem += 1
        h = self.nc.alloc_semaphore(name=f"{name}_{self.nsem}")
        self._keep = getattr(self, "_keep", [])
        self._keep.append(h)
        return h

    def _new_sem(self, e):
        self.sem[e] = self._alloc_sem(f"s{e}")
        self.cnt[e] = 0

    def sb(self, name, shape, dtype=F32, stack=None):
        self.nsb = getattr(self, "nsb", 0) + 1
        return (stack or self.es).enter_context(self.nc.sbuf_tensor(f"sb{self.nsb}_{name}", list(shape), dtype))

    def mkbanks(self):
        for i in range(8):
            t = self.es.enter_context(self.nc.psum_tensor(f"bank{i}", [128, 512], F32))
            self.banks.append((t, Buf(f"bank{i}")))

    def bank(self):
        b = self.banks[self.bi]
        self.bi = (self.bi + 1) % 8
        return b

    def _wait(self, e, ev):
        sem, val, src = ev
        if src == e and (not SAME_ENGINE_SYNC or e == "pe"):
            return
        key = id(sem)
        if self.seen[e].get(key, 0) >= val:
            return
        self.E[e].wait_ge(sem, val)
        self.seen[e][key] = val

    def _deps(self, e, reads, writes, adds, isdma=False):
        for b in reads:
            for ev in b.w.values():
                self._wait(e, ev)
        for b in writes:
            for ev in b.w.values():
                self._wait(e, ev)
            for ev in b.r.values():
                self._wait(e, ev)
        for b in adds:
            for ev in b.r.values():
                self._wait(e, ev)
            for ev in b.w.values():
                if not (isdma and ev[2] == "dma"):
                    self._wait(e, ev)

    def _commit(self, key, ev, reads, writes, adds):
        for b in reads:
            b.r[key] = ev
        for b in writes:
            b.w = {key: ev}
            b.r = {}
        for b in adds:
            b.w[key] = ev

    def op(self, e, fn, reads=(), writes=(), adds=()):
        self._deps(e, reads, writes, adds)
        ins = fn(self.E[e])
        if self.cnt[e] >= EPOCH:
            self._new_sem(e)
        self.cnt[e] += 1
        ins.then_inc(self.sem[e], 1)
        ev = (self.sem[e], self.cnt[e], e)
        self._commit(e, ev, reads, writes, adds)
        self.ninstr += 1
        return ev

    def dma(self, out, in_, reads=(), writes=(), adds=(), q="sp"):
        self._deps(q, reads, writes, adds, True)
        j = self.drr[q]
        self.drr[q] = (j + 1) % NDMASEM
        if self.dcnt[q][j] > 0:
            self._wait(q, (self.dsem[q][j], self.dcnt[q][j] * 16, "dma"))
        if self.dcnt[q][j] >= DMA_EPOCH:
            self.dsem[q][j] = self._alloc_sem(f"d{q}{j}")
            self.dcnt[q][j] = 0
        sem = self.dsem[q][j]
        self.E[q].dma_start(out=out, in_=in_).then_inc(sem, 16)
        self.dcnt[q][j] += 1
        ev = (sem, self.dcnt[q][j] * 16, "dma")
        self._commit(id(sem), ev, reads, writes, adds)
        self.ninstr += 1
        return ev

    def barrier(self):
        if getattr(self, "ptick", None) is not None:
            self.op("pool", lambda e: e.memset(self.ptick[:], 0.0), writes=[self.ptickb])
        evs = []
        for e in ("pe", "act", "dve", "pool", "sp"):
            if self.cnt[e] > 0:
                evs.append((self.sem[e], self.cnt[e], e))
        for q in self.dsem:
            for j in range(NDMASEM):
                if self.dcnt[q][j] > 0:
                    evs.append((self.dsem[q][j], self.dcnt[q][j] * 16, "dma"))
        for e in self.E:
            for ev in evs:
                if ev[2] != e:
                    self._wait(e, ev)


PHASES = None


CURL = [0]
PHASES_L = None


def i_phase_on(i):
    if PHASES_L is not None:
        return i in PHASES_L[CURL[0]]
    return PHASES is None or i in PHASES


def build(nlayers=L, debug=False):
    nc = bass.Bass("TRN2", target_bir_lowering=False)
    k = KB(nc)
    k.mkbanks()

    def din(name, shape):
        return nc.dram_tensor(name, list(shape), F32, kind="ExternalInput")

    def dscr(name, shape, out=False):
        return nc.dram_tensor(name, list(shape), F32, kind="ExternalOutput" if (out and debug) else "Internal")

    xin = din("xin", [8, 128, TT])
    cin = din("cin", [128, 8, 2])
    w_mod = din("w_mod", [L, D, 6 * D]); w_in = din("w_in", [L, D, 2432]); w_out = din("w_out", [L, D, D])
    w_gu = din("w_gu", [L, D, 5632]); w_down = din("w_down", [L, 2816, D])
    ppd = din("pp", [L, 128, NP])
    sgwT = din("sgwT", [L, 128, 4, 128]); sgb = din("sgb", [L, 1, 512]); sink = din("sink", [L, 1, 8])
    w2d = din("w2", [L, 128, 256]); a2d = din("a2", [L, 128, 256]); g2d = din("g2", [L, 128, 256])
    cc = din("cc", [128, 384]); smL = din("smL", [128, 9, 4]); smC = din("smC", [128, 9, 2])
    cmd = din("cm", [64, 4, 512]); amd = din("am", [128, 2, 512]); rsd = din("rs", [128, 512])
    cosd = din("cosT", [128, T]); sind = din("sinT", [128, T])
    outT = nc.dram_tensor("outT", [8, 128, T], F32, kind="ExternalOutput")

    xs = dscr("xs", [8, 128, TT], True); bxs = Buf("xs")
    pxT = dscr("pxT", [19, 128, TT], True); bpx = Buf("pxT")
    mixT = dscr("mixT", [8, 128, TT], True); bmix = Buf("mixT")
    rmT = dscr("rmT", [9, 128, TT], True); brm = Buf("rmT")
    scin = nc.dram_tensor("scin", [4, 128, 5, TT], BF16, kind="Internal"); bscin = Buf("scin")
    gcd = dscr("gcd", [4, 128, NCHK], True); bgcd = Buf("gcd")
    rksT = dscr("rksT", [2, 128, TT], True); brks = Buf("rksT")
    yT = dscr("yT", [2, 256, TT], True); byT = Buf("yT")
    bout = Buf("outT")

    cst = k.sb("cst", [128, 384]); bc = Buf("cst")
    k.dma(cst[:], cc.ap(), writes=[bc])
    ident = cst[:, 0:128]; bones = cst[:, 128:256]; Rm = cst[:, 256:384]
    k.ptick = k.sb("ptick", [128, 8]); k.ptickb = Buf("ptick")
    ones = k.sb("ones", [128, 128]); bo = Buf("ones")
    k.op("dve", lambda e: e.memset(ones[:], 1.0), writes=[bo])
    cact = k.sb("cact", [128, 8, 2]); bca = Buf("cact")
    k.dma(cact[:], cin.ap(), writes=[bca])
    k.op("act", lambda e: e.activation(out=cact[:], in_=cact[:], func=AF.Silu), writes=[bca])
    pp = k.sb("ppt", [128, NP]); bpp = Buf("pp")
    modt = k.sb("modt", [128, 48, 2]); bmod = Buf("modt")
    coef = k.sb("coef", [128, 6, 8, 2]); bcoef = Buf("coef")
    cfs = k.sb("cfs", [128, 9, 7]); bcfs = Buf("cfs")
    smLs = k.sb("smLs", [128, 9, 4]); smCs = k.sb("smCs", [128, 9, 2]); bsm = Buf("sm")
    k.dma(smLs[:], smL.ap(), adds=[bsm]); k.dma(smCs[:], smC.ap(), adds=[bsm])
    omka = k.sb("omka", [128, 2]); bomka = Buf("omka")
    k.barrier()

    def xsrc(l):
        return xin if l == 0 else xs

    def rmsn(st, X, bX, W, SQ, bSQ, rs, brs):
        k.op("act", lambda e: e.activation(out=SQ[:, :, :W], in_=X[:, :, :W], func=AF.Square), reads=[bX], writes=[bSQ])
        bk, bb = k.bank()
        for c in range(8):
            k.op("pe", lambda e, c=c: e.matmul(bk[:, :W], lhsT=ones[:], rhs=SQ[:, c, :W], start=(c == 0), stop=(c == 7)),
                 reads=[bSQ, bo], writes=[bb])
        k.op("dve", lambda e: e.tensor_scalar(out=rs[:, :W], in0=bk[:, :W], scalar1=1.0 / D, scalar2=1e-6, op0=ALU.mult, op1=ALU.add),
             reads=[bb], writes=[brs])
        k.op("act", lambda e: e.activation(out=rs[:, :W], in_=rs[:, :W], func=AF.Sqrt), writes=[brs])
        k.op("dve", lambda e: e.reciprocal(out=rs[:, :W], in_=rs[:, :W]), writes=[brs])

    t32 = [k.sb(f"t32_{i}", [128, 512]) for i in range(2)]; bt32 = [Buf("t32a"), Buf("t32b")]

    def modulate(H, bH, X, bX, rs, brs, W, ia, ib, who):
        for c in range(8):
            t_ = t32[c % 2]; bt_ = bt32[c % 2]
            k.op("dve", lambda e, c=c, t_=t_: e.scalar_tensor_tensor(out=t_[:, :W], in0=X[:, c, :W], scalar=coef[:, ia, c, who:who + 1],
                                                                   in1=rs[:, :W], op0=ALU.mult, op1=ALU.mult),
                 reads=[bX, brs, bcoef], writes=[bt_])
            k.op("act", lambda e, c=c, t_=t_: e.activation(out=H[:, c, :W], in_=t_[:, :W], func=AF.Identity, bias=coef[:, ib, c, who:who + 1]),
                 reads=[bcoef, bt_], writes=[bH] if c == 0 else (), adds=[bH] if c else ())

    castrr = [0]

    def cast(out, in_, reads, writes=(), adds=()):
        castrr[0] = (castrr[0] + 1) % 2
        if castrr[0] == 0:
            k.op("act", lambda e: e.copy(out=out, in_=in_), reads=reads, writes=writes, adds=adds)
        else:
            k.op("dve", lambda e: e.tensor_copy(out=out, in_=in_), reads=reads, writes=writes, adds=adds)

    def resid_update(X, bX, Y, bY, rs, brs, W, ig, who):
        for j in range(8):
            k.op("dve", lambda e, j=j: e.scalar_tensor_tensor(out=Y[:, j, :W], in0=Y[:, j, :W], scalar=coef[:, ig, j, who:who + 1],
                                                             in1=rs[:, :W], op0=ALU.mult, op1=ALU.mult),
                 reads=[brs, bcoef], writes=[bY])
            k.op("dve", lambda e, j=j: e.tensor_tensor(out=X[:, j, :W], in0=X[:, j, :W], in1=Y[:, j, :W], op=ALU.add),
                 reads=[bY], writes=[bX])

    evflip = [0]

    def evac(out, in_, reads, writes=(), adds=()):
        evflip[0] ^= 1
        if evflip[0]:
            k.op("act", lambda e: e.copy(out=out, in_=in_), reads=reads, writes=writes, adds=adds)
        else:
            k.op("dve", lambda e: e.tensor_copy(out=out, in_=in_), reads=reads, writes=writes, adds=adds)

    for l in range(nlayers):
        last = (l == L - 1)
        CURL[0] = l
        with ExitStack() as st:
          if i_phase_on(0):
            k.dma(pp[:], ppd.ap()[l], writes=[bpp])
            wm = [k.sb(f"wm{i}", [128, 8, 1024], stack=st) for i in range(2)]; bwm = [Buf("wm0"), Buf("wm1")]
            for m in range(6):
                k.dma(wm[m % 2][:], w_mod.ap()[l][:, m * 1024:(m + 1) * 1024].rearrange("(kc p) n -> p kc n", p=128), writes=[bwm[m % 2]],
                      q="sp" if m % 2 == 0 else "pool")
                for c in range(8):
                    bk, bb = k.bank()
                    for kc in range(8):
                        k.op("pe", lambda e, kc=kc, c=c, m=m: e.matmul(bk[:, 0:2], lhsT=wm[m % 2][:, kc, c * 128:(c + 1) * 128], rhs=cact[:, kc, :],
                                                                      start=(kc == 0), stop=(kc == 7)), reads=[bwm[m % 2], bca], writes=[bb])
                    j = m * 8 + c
                    k.op("dve", lambda e, j=j: e.tensor_scalar(out=modt[:, j, :], in0=bk[:, 0:2], scalar1=pp[:, BM + j:BM + j + 1], scalar2=None, op0=ALU.add),
                         reads=[bb, bpp], adds=[bmod])
            for who in range(2):
                for (ic, isc, ig) in ((0, 1, 0), (3, 4, 2)):
                    k.op("dve", lambda e, who=who, ic=ic, isc=isc, ig=ig: e.tensor_tensor(out=coef[:, ic, :, who], in0=modt[:, isc * 8:isc * 8 + 8, who],
                                                                                        in1=pp[:, NG + ig * 8:NG + ig * 8 + 8], op=ALU.mult),
                         reads=[bmod, bpp], writes=[bcoef])
                    k.op("dve", lambda e, who=who, ic=ic, ig=ig: e.tensor_tensor(out=coef[:, ic, :, who], in0=coef[:, ic, :, who],
                                                                               in1=pp[:, NG + ig * 8:NG + ig * 8 + 8], op=ALU.add), writes=[bcoef])
                for (ic, ish) in ((1, 0), (4, 3)):
                    k.op("dve", lambda e, who=who, ic=ic, ish=ish: e.tensor_copy(out=coef[:, ic, :, who], in_=modt[:, ish * 8:ish * 8 + 8, who]),
                         reads=[bmod], writes=[bcoef])
                for (ic, igt, ig) in ((2, 2, 1), (5, 5, 3)):
                    k.op("dve", lambda e, who=who, ic=ic, igt=igt, ig=ig: e.tensor_tensor(out=coef[:, ic, :, who], in0=modt[:, igt * 8:igt * 8 + 8, who],
                                                                                        in1=pp[:, NG + ig * 8:NG + ig * 8 + 8], op=ALU.mult),
                         reads=[bmod, bpp], writes=[bcoef])
            k.op("dve", lambda e: e.tensor_scalar(out=cfs[:, :, 0], in0=pp[:, MU:MU + 9], scalar1=-1.0, scalar2=1.0, op0=ALU.mult, op1=ALU.add),
                 reads=[bpp], writes=[bcfs])
            for dd in range(4):
                k.op("dve", lambda e, dd=dd: e.tensor_tensor(out=cfs[:, :, 1 + dd], in0=pp[:, MU:MU + 9], in1=smLs[:, :, dd], op=ALU.mult),
                     reads=[bpp, bsm], writes=[bcfs])
            for dd in range(2):
                k.op("dve", lambda e, dd=dd: e.tensor_tensor(out=cfs[:, :, 5 + dd], in0=pp[:, MU:MU + 9], in1=smCs[:, :, dd], op=ALU.mult),
                     reads=[bpp, bsm], writes=[bcfs])
            k.op("dve", lambda e: e.tensor_scalar(out=omka[:], in0=pp[:, KA:KA + 2], scalar1=-1.0, scalar2=1.0, op0=ALU.mult, op1=ALU.add),
                 reads=[bpp], writes=[bomka])
            k.barrier()

        with ExitStack() as st:
          if i_phase_on(1):
            win = k.sb("win", [128, 8, 2432], BF16, stack=st); bwin = Buf("win")
            wst = [k.sb(f"wst{i}", [128, 2432], stack=st) for i in range(2)]; bwst = [Buf("wst0"), Buf("wst1")]
            wv = w_in.ap()[l].rearrange("(kc p) n -> p kc n", p=128)
            for kc in range(8):
                k.dma(wst[kc % 2][:], wv[:, kc, :], writes=[bwst[kc % 2]], q="sp" if kc % 2 == 0 else "pool")
                cast(win[:, kc, :], wst[kc % 2][:], [bwst[kc % 2]], adds=[bwin])
            X = k.sb("X", [128, 8, 512], stack=st); bX = Buf("X")
            SQ = k.sb("SQ", [128, 8, 512], stack=st); bSQ = Buf("SQ")
            H = k.sb("H", [128, 8, 512], BF16, stack=st); bH = Buf("H")
            rs = k.sb("rs", [128, 512], stack=st); brs = Buf("rs")
            stg = [k.sb(f"stg{i}", [128, 512], stack=st) for i in range(2)]; bstg = [Buf("s0"), Buf("s1")]
            xv = xsrc(l).ap().rearrange("c p t -> p c t")
            for (t0, W, who) in TILES:
                k.dma(X[:, :, :W], xv[:, :, t0:t0 + W], reads=[bxs], writes=[bX])
                rmsn(st, X, bX, W, SQ, bSQ, rs, brs)
                modulate(H, bH, X, bX, rs, brs, W, 0, 1, who)
                for j in range(19):
                    bk, bb = k.bank()
                    for kc in range(8):
                        k.op("pe", lambda e, kc=kc, j=j: e.matmul(bk[:, :W], lhsT=win[:, kc, j * 128:(j + 1) * 128], rhs=H[:, kc, :W],
                                                                 start=(kc == 0), stop=(kc == 7)), reads=[bwin, bH], writes=[bb])
                    s = stg[j % 2]
                    evac(s[:, :W], bk[:, :W], [bb], writes=[bstg[j % 2]])
                    k.dma(pxT.ap()[j][:, t0:t0 + W], s[:, :W], reads=[bstg[j % 2]], adds=[bpx])
            k.barrier()

        with ExitStack() as st:
          if i_phase_on(2):
            fb = [k.sb(f"fb{i}", [128, TT], stack=st) for i in range(2)]; bfb = [Buf("fb0"), Buf("fb1")]
            ob = [k.sb(f"ob{i}", [128, TT], stack=st) for i in range(2)]; bob = [Buf("ob0"), Buf("ob1")]
            for ci in range(9):
                f = fb[ci % 2]; o = ob[ci % 2]; bf = bfb[ci % 2]; bo_ = bob[ci % 2]
                k.dma(f[:], pxT.ap()[SRC_CH[ci]], reads=[bpx], writes=[bf], q="sp" if ci % 2 == 0 else "pool")
                k.op("act", lambda e, f=f, o=o, ci=ci: e.activation(out=o[:], in_=f[:], func=AF.Identity, scale=cfs[:, ci, 0:1]), reads=[bf, bcfs], writes=[bo_])
                fl = f[:, CT:].rearrange("p (r w) -> p r w", w=64)
                ol = o[:, CT:].rearrange("p (r w) -> p r w", w=64)
                views = [(ol[:, :, 1:64], fl[:, :, 0:63]), (ol[:, :, 0:63], fl[:, :, 1:64]), (ol[:, 1:64, :], fl[:, 0:63, :]), (ol[:, 0:63, :], fl[:, 1:64, :]),
                         (o[:, 1:CT], f[:, 0:CT - 1]), (o[:, 0:CT - 1], f[:, 1:CT])]
                for di, (ov, fv) in enumerate(views):
                    eng = "dve"
                    k.op(eng, lambda e, ov=ov, fv=fv, ci=ci, di=di: e.scalar_tensor_tensor(out=ov, in0=fv, scalar=cfs[:, ci, 1 + di:2 + di], in1=ov,
                                                                                          op0=ALU.mult, op1=ALU.add), reads=[bf, bcfs], writes=[bo_])
                k.dma(rmT.ap()[ci], o[:], reads=[bo_], adds=[brm])
            k.barrier()

        with ExitStack() as st:
          if i_phase_on(3):
            w2 = k.sb("w2s", [128, 256], stack=st); a2 = k.sb("a2s", [128, 256], stack=st); bw2 = Buf("w2")
            k.dma(w2[:], w2d.ap()[l], adds=[bw2]); k.dma(a2[:], a2d.ap()[l], adds=[bw2])
            rsm = k.sb("rsm", [128, 512], stack=st); brsm = Buf("rsm")
            k.dma(rsm[:], rsd.ap(), writes=[brsm])
            rm = k.sb("rm", [128, 9, 512], stack=st); brmt = Buf("rm")
            tw = k.sb("tw", [128, 512], stack=st); btw = Buf("tw")
            kkr = k.sb("kkr", [128, 2, 512], stack=st); bkkr = Buf("kkr")
            sqk = k.sb("sqk", [128, 2, 512], stack=st); bsqk = Buf("sqk")
            kk = k.sb("kk", [128, 2, 512], stack=st); bkk = Buf("kk")
            nr = k.sb("nr", [128, 512], stack=st); bnr = Buf("nr")
            sg = k.sb("sg", [128, 512], stack=st); bsg = Buf("sg")
            aa = k.sb("aa", [128, 512], stack=st); baa = Buf("aa")
            cs = k.sb("cs", [128, 512], stack=st); bcs = Buf("cs")
            Ei = k.sb("Ei", [128, 512], stack=st); bEi = Buf("Ei")
            Ex = k.sb("Ex", [128, 512], stack=st); bEx = Buf("Ex")
            gi = k.sb("gi", [128, 512], stack=st); bgi = Buf("gi")
            gx = k.sb("gx", [128, 512], stack=st); bgx = Buf("gx")
            gv = k.sb("gv", [128, 512], stack=st); bgv = Buf("gv")
            gc = k.sb("gc", [128, 8], stack=st); bgc = Buf("gc")
            kd = k.sb("kd", [128, 512], stack=st); bkd = Buf("kd")
            bt = k.sb("bt", [128, 512], stack=st); bbt = Buf("bt")
            kds = k.sb("kds", [128, 2, 512], stack=st); bkds = Buf("kds")
            so = [k.sb(f"so{i}", [128, 5, 512], BF16, stack=st) for i in range(2)]; bso = [Buf("so0"), Buf("so1")]
            it = 0
            for (t0, W, who) in TILES:
                nck = W // 64
                k.dma(rm[:, :, :W], rmT.ap().rearrange("c p t -> p c t")[:, :, t0:t0 + W], reads=[brm], writes=[brmt])
                k.op("act", lambda e: e.activation(out=tw[:, :W], in_=rm[:, 7, :W], func=AF.Tanh), reads=[brmt], writes=[btw])
                for hp in range(2):
                    k.op("dve", lambda e, hp=hp: e.tensor_scalar(out=kkr[:, hp, :W], in0=rm[:, 3 + hp, :W], scalar1=pp[:, KK + hp:KK + hp + 1], scalar2=None, op0=ALU.mult),
                         reads=[brmt, bpp], adds=[bkkr])
                k.op("act", lambda e: e.activation(out=sqk[:, :, :W], in_=kkr[:, :, :W], func=AF.Square), reads=[bkkr], writes=[bsqk])
                for hp in range(2):
                    bk, bb = k.bank()
                    k.op("pe", lambda e, hp=hp: e.matmul(bk[:, :W], lhsT=bones, rhs=sqk[:, hp, :W], start=True, stop=True), reads=[bsqk, bc], writes=[bb])
                    k.op("act", lambda e: e.activation(out=nr[:, :W], in_=bk[:, :W], func=AF.Sqrt), reads=[bb], writes=[bnr])
                    k.op("dve", lambda e: e.tensor_scalar(out=nr[:, :W], in0=nr[:, :W], scalar1=1e-12, scalar2=None, op0=ALU.max), writes=[bnr])
                    k.op("dve", lambda e: e.reciprocal(out=nr[:, :W], in_=nr[:, :W]), writes=[bnr])
                    k.op("dve", lambda e, hp=hp: e.tensor_tensor(out=kk[:, hp, :W], in0=kkr[:, hp, :W], in1=nr[:, :W], op=ALU.mult), reads=[bkkr, bnr], adds=[bkk])
                for hp in range(2):
                    for d in range(2):
                        bk, bb = k.bank()
                        k.op("pe", lambda e, d=d, hp=hp: e.matmul(bk[:, :W], lhsT=w2[d * 64:(d + 1) * 64, hp * 128:(hp + 1) * 128], rhs=tw[d * 64:(d + 1) * 64, :W],
                                                                 start=True, stop=True), reads=[bw2, btw], writes=[bb])
                        k.op("act", lambda e, d=d, hp=hp: e.activation(out=sg[:, :W], in_=bk[:, :W], func=AF.Sigmoid, bias=pp[:, W0 + d * 2 + hp:W0 + d * 2 + hp + 1]),
                             reads=[bb, bpp], writes=[bsg])
                        bk2, bb2 = k.bank()
                        k.op("pe", lambda e, d=d, hp=hp: e.matmul(bk2[:, :W], lhsT=a2[d * 64:(d + 1) * 64, hp * 128:(hp + 1) * 128], rhs=rm[d * 64:(d + 1) * 64, 8, :W],
                                                                 start=True, stop=True), reads=[bw2, brmt], writes=[bb2])
                        k.op("act", lambda e, d=d, hp=hp: e.activation(out=aa[:, :W], in_=bk2[:, :W], func=AF.Sigmoid, bias=pp[:, A0 + d * 2 + hp:A0 + d * 2 + hp + 1]),
                             reads=[bb2, bpp], writes=[baa])
                        k.op("dve", lambda e: e.tensor_tensor_scan(out=cs[:, :W], data0=rsm[:, :W], data1=sg[:, :W], initial=0.0, op0=ALU.mult, op1=ALU.add),
                             reads=[brsm, bsg], writes=[bcs])
                        cs3 = cs[:, :W].rearrange("p (c t) -> p c t", t=64)
                        if d == 0:
                            k.op("dve", lambda e: e.tensor_tensor(out=Ex[:, :W], in0=cs[:, :W], in1=sg[:, :W], op=ALU.subtract), reads=[bcs, bsg], writes=[bEx])
                            Eiv = cs; bEiv = bcs
                        else:
                            Ex3 = Ex[:, :W].rearrange("p (c t) -> p c t", t=64)
                            k.op("dve", lambda e, cs3=cs3, Ex3=Ex3: e.tensor_tensor(out=Ex3, in0=cs3[:, :, 63:64].broadcast_to([128, nck, 64]), in1=cs3, op=ALU.subtract),
                                 reads=[bcs], writes=[bEx])
                            k.op("dve", lambda e: e.tensor_tensor(out=Ei[:, :W], in0=Ex[:, :W], in1=sg[:, :W], op=ALU.add), reads=[bEx, bsg], writes=[bEi])
                            Eiv = Ei; bEiv = bEi
                        k.op("act", lambda e, cs3=cs3: e.activation(out=gc[:, :nck], in_=cs3[:, :, 63], func=AF.Exp, scale=-C0), reads=[bcs], writes=[bgc])
                        k.op("act", lambda e, Eiv=Eiv: e.activation(out=gi[:, :W], in_=Eiv[:, :W], func=AF.Exp, scale=-C0), reads=[bEiv], writes=[bgi])
                        k.op("act", lambda e: e.activation(out=gx[:, :W], in_=Ex[:, :W], func=AF.Exp, scale=-C0), reads=[bEx], writes=[bgx])
                        k.op("act", lambda e, Eiv=Eiv: e.activation(out=gv[:, :W], in_=Eiv[:, :W], func=AF.Exp, scale=C0), reads=[bEiv], writes=[bgv])
                        s_ = so[it % 2]; bs_ = bso[it % 2]; it += 1
                        k.op("dve", lambda e, s_=s_, hp=hp: e.tensor_tensor(out=s_[:, 0, :W], in0=kk[:, hp, :W], in1=gx[:, :W], op=ALU.mult), reads=[bkk, bgx], writes=[bs_])
                        k.op("dve", lambda e, s_=s_, hp=hp: e.tensor_tensor(out=s_[:, 1, :W], in0=rm[:, hp, :W], in1=gi[:, :W], op=ALU.mult), reads=[brmt, bgi], adds=[bs_])
                        k.op("dve", lambda e, hp=hp: e.tensor_scalar(out=kd[:, :W], in0=aa[:, :W], scalar1=pp[:, KA + hp:KA + hp + 1], scalar2=omka[:, hp:hp + 1],
                                                                    op0=ALU.mult, op1=ALU.add), reads=[baa, bpp, bomka], writes=[bkd])
                        k.op("dve", lambda e, hp=hp: e.tensor_tensor(out=kd[:, :W], in0=kd[:, :W], in1=rm[:, 3 + hp, :W], op=ALU.mult), reads=[brmt], writes=[bkd])
                        k.op("dve", lambda e, s_=s_: e.tensor_tensor(out=s_[:, 2, :W], in0=kd[:, :W], in1=gv[:, :W], op=ALU.mult), reads=[bkd, bgv], adds=[bs_])
                        k.op("dve", lambda e, hp=hp: e.tensor_tensor(out=bt[:, :W], in0=aa[:, :W], in1=kk[:, hp, :W], op=ALU.mult), reads=[baa, bkk], writes=[bbt])
                        k.op("dve", lambda e, s_=s_: e.tensor_tensor(out=s_[:, 3, :W], in0=bt[:, :W], in1=gv[:, :W], op=ALU.mult), reads=[bbt, bgv], adds=[bs_])
                        if d == 0:
                            k.op("dve", lambda e, hp=hp: e.tensor_copy(out=kds[:, hp, :W], in_=kd[:, :W]), reads=[bkd], writes=[bkds])
                        else:
                            k.op("dve", lambda e, hp=hp: e.tensor_tensor(out=kds[:, hp, :W], in0=kds[:, hp, :W], in1=kd[:, :W], op=ALU.add), reads=[bkd], writes=[bkds])
                        k.op("act", lambda e, s_=s_, hp=hp: e.copy(out=s_[:, 4, :W], in_=rm[:, 5 + hp, :W]), reads=[brmt], adds=[bs_])
                        k.dma(scin.ap()[d * 2 + hp][:, :, t0:t0 + W], s_[:, :, :W], reads=[bs_], adds=[bscin])
                        k.dma(gcd.ap()[d * 2 + hp][:, t0 // 64:t0 // 64 + nck], gc[:, :nck], reads=[bgc], adds=[bgcd])
                    k.op("dve", lambda e, hp=hp: e.scalar_tensor_tensor(out=kds[:, hp, :W], in0=kds[:, hp, :W], scalar=pp[:, RK + hp:RK + hp + 1], in1=rm[:, hp, :W],
                                                                       op0=ALU.mult, op1=ALU.mult), reads=[brmt, bpp], writes=[bkds])
                k.dma(rksT.ap().rearrange("c p t -> p c t")[:, :, t0:t0 + W], kds[:, :, :W], reads=[bkds], adds=[brks])
            k.barrier()

        with ExitStack() as st:
          if i_phase_on(4):
            cm = k.sb("cm", [64, 4, 512], stack=st); bcm = Buf("cm")
            k.dma(cm[:], cmd.ap(), writes=[bcm])
            idb = k.sb("idb", [64, 64], BF16, stack=st); bidb = Buf("idb")
            k.op("dve", lambda e: e.tensor_copy(out=idb[:], in_=ident[0:64, 0:64]), reads=[bc], writes=[bidb])
            inT = [k.sb(f"inT{d}", [64, 4, 5, 512], BF16, stack=st) for d in range(2)]; binT = [Buf("inT0"), Buf("inT1")]
            gct = [k.sb(f"gct{d}", [64, 4, 8], stack=st) for d in range(2)]; bgct = [Buf("g0"), Buf("g1")]
            yst = [k.sb(f"yst{d}", [64, 4, 512], stack=st) for d in range(2)]; byst = [Buf("y0"), Buf("y1")]
            Hs = k.sb("Hs", [64, 8, 64], stack=st); bHs = Buf("Hs")
            Hb = k.sb("Hb", [64, 8, 64], BF16, stack=st); bHb = Buf("Hb")
            k.op("dve", lambda e: e.memset(Hs[:], 0.0), writes=[bHs])
            k.op("dve", lambda e: e.memset(Hb[:], 0.0), writes=[bHb])

            def b16(name):
                return k.sb(name, [64, 512], BF16, stack=st), Buf(name)
            Ktm, bKtm = b16("Ktm"); NBtm, bNB = b16("NBtm"); Vtm, bVtm = b16("Vtm")
            Pm = [None, None]; bPm = [None, None]; Nm = [None, None]; bNm = [None, None]; Ab = [None, None]; bAb = [None, None]
            for i in range(2):
                Pm[i], bPm[i] = b16(f"Pm{i}"); Nm[i], bNm[i] = b16(f"Nm{i}"); Ab[i], bAb[i] = b16(f"Ab{i}")
            A32 = [k.sb(f"A32{i}", [64, 512], stack=st) for i in range(2)]; bA32 = [Buf("A32a"), Buf("A32b")]
            AKT, bAKT = b16("AKT"); RKT, bRKT = b16("RKT"); RBT, bRBT = b16("RBT"); RHSs, bRHS = b16("RHSs"); Us, bUs = b16("Us")
            Ht = k.sb("Ht", [64, 8, 64], stack=st); bHt = Buf("Ht")
            Fc = [(0, ci) for ci in range(4)] + [(ti, ci) for ti in range(1, 9) for ci in range(8)]
            Bc = [(0, ci) for ci in range(3, -1, -1)] + [(ti, ci) for ti in range(8, 0, -1) for ci in range(7, -1, -1)]
            cur = [None, None]

            def flush(d):
                ti = cur[d]
                t0, W, _ = TILES[ti]
                yv = yT.ap()[d].rearrange("(h v) t -> v h t", v=64)
                k.dma(yv[:, :, t0:t0 + W], yst[d][:, :, :W], reads=[byst[d]], adds=[byT])

            def load(d, ti):
                t0, W, _ = TILES[ti]
                for h in range(4):
                    hp = h // 2; e_ = h % 2
                    k.dma(inT[d][:, h, :, :W], scin.ap()[d * 2 + hp][e_ * 64:(e_ + 1) * 64, :, t0:t0 + W], reads=[bscin],
                          adds=[binT[d]], q="sp" if h % 2 == 0 else "pool")
                    k.dma(gct[d][:, h, :W // 64], gcd.ap()[d * 2 + hp][e_ * 64:(e_ + 1) * 64, t0 // 64:t0 // 64 + W // 64], reads=[bgcd],
                          writes=[bgct[d]] if h == 0 else (), adds=[bgct[d]] if h else ())

            def opd(d, h, kind, ci):
                return inT[d][:, h, kind, ci * 64:(ci + 1) * 64]

            def blk(t, q):
                return t[:, q * 64:(q + 1) * 64]

            import os as _os
            R3N = int(_os.environ.get('R3N', str(NCHK)))
            for step in range(R3N):
                tcs = [Fc[step], Bc[step]]
                for d in range(2):
                    if cur[d] != tcs[d][0]:
                        if cur[d] is not None:
                            flush(d)
                        load(d, tcs[d][0])
                        cur[d] = tcs[d][0]
                cis = [tcs[0][1], tcs[1][1]]
                for kind, dst, bdst, neg in ((2, Ktm, bKtm, False), (3, NBtm, bNB, True), (4, Vtm, bVtm, False)):
                    bk, bb = k.bank()
                    for d in range(2):
                        for h in range(4):
                            q = d * 4 + h
                            k.op("pe", lambda e, d=d, h=h, q=q, kind=kind, bk=bk: e.matmul(blk(bk[0:64, :], q), lhsT=opd(d, h, kind, cis[d]), rhs=idb[:], start=True, stop=True),
                                 reads=[binT[d], bidb], writes=[bb])
                    if neg:
                        k.op("act", lambda e, dst=dst, bk=bk: e.activation(out=dst[:], in_=bk[0:64, :], func=AF.Copy, scale=-1.0), reads=[bb], writes=[bdst])
                    elif kind == 2:
                        k.op("dve", lambda e, dst=dst, bk=bk: e.tensor_copy(out=dst[:], in_=bk[0:64, :]), reads=[bb], writes=[bdst])
                    else:
                        k.op("act", lambda e, dst=dst, bk=bk: e.copy(out=dst[:], in_=bk[0:64, :]), reads=[bb], writes=[bdst])

                def prod(la, ra, dst, bdst, mki, neg):
                    bk, bb = k.bank()
                    for d in range(2):
                        for h in range(4):
                            q = d * 4 + h
                            k.op("pe", lambda e, d=d, h=h, q=q: e.matmul(blk(bk[0:64, :], q), lhsT=opd(d, h, la, cis[d]), rhs=opd(d, h, ra, cis[d]),
                                                                        start=True, stop=True), reads=[binT[d]], writes=[bb])
                    k.op("dve", lambda e: e.scalar_tensor_tensor(out=dst[:], in0=bk[0:64, :], scalar=(-1.0 if neg else 1.0), in1=cm[:, mki, :], op0=ALU.mult, op1=ALU.mult),
                         reads=[bb, bcm], writes=[bdst])
                prod(3, 0, Pm[0], bPm[0], 0, True)
                prod(0, 3, Nm[0], bNm[0], 1, True)
                prod(2, 0, AKT, bAKT, 0, False)
                prod(2, 1, RKT, bRKT, 2, False)
                prod(3, 1, RBT, bRBT, 2, True)
                k.op("dve", lambda e: e.tensor_tensor(out=A32[0][:], in0=Pm[0][:], in1=cm[:, 3, :], op=ALU.add), reads=[bPm[0], bcm], writes=[bA32[0]])
                k.op("act", lambda e: e.copy(out=Ab[0][:], in_=A32[0][:]), reads=[bA32[0]], writes=[bAb[0]])
                pc = 0; ac = 0
                for j in range(1, 6):
                    nn = 1 - pc
                    bk, bb = k.bank()
                    for q in range(8):
                        k.op("pe", lambda e, q=q, pc=pc: e.matmul(blk(bk[0:64, :], q), lhsT=blk(Pm[pc], q), rhs=blk(Nm[pc], q), start=True, stop=True),
                             reads=[bPm[pc], bNm[pc]], writes=[bb])
                    if j < 5:
                        bk2, bb2 = k.bank()
                        for q in range(8):
                            k.op("pe", lambda e, q=q, pc=pc: e.matmul(blk(bk2[0:64, :], q), lhsT=blk(Nm[pc], q), rhs=blk(Pm[pc], q), start=True, stop=True),
                                 reads=[bPm[pc], bNm[pc]], writes=[bb2])
                    k.op("act", lambda e, nn=nn, bk=bk: e.copy(out=Nm[nn][:], in_=bk[0:64, :]), reads=[bb], writes=[bNm[nn]])
                    if j < 5:
                        k.op("dve", lambda e, nn=nn, bk2=bk2: e.tensor_copy(out=Pm[nn][:], in_=bk2[0:64, :]), reads=[bb2], writes=[bPm[nn]])
                    bk3, bb3 = k.bank()
                    for q in range(8):
                        k.op("pe", lambda e, q=q, nn=nn, ac=ac: e.matmul(blk(bk3[0:64, :], q), lhsT=blk(Nm[nn], q), rhs=blk(Ab[ac], q), start=True, stop=True),
                             reads=[bNm[nn], bAb[ac]], writes=[bb3])
                    k.op("dve", lambda e, ac=ac, bk3=bk3: e.tensor_tensor(out=A32[1 - ac][:], in0=bk3[0:64, :], in1=A32[ac][:], op=ALU.add), reads=[bb3, bA32[ac]], writes=[bA32[1 - ac]])
                    k.op("act", lambda e, ac=ac: e.copy(out=Ab[1 - ac][:], in_=A32[1 - ac][:]), reads=[bA32[1 - ac]], writes=[bAb[1 - ac]])
                    pc = nn; ac = 1 - ac
                A = Ab[ac]; bA = bAb[ac]
                bk, bb = k.bank()
                for d in range(2):
                    for h in range(4):
                        q = d * 4 + h
                        k.op("pe", lambda e, d=d, h=h, q=q: e.matmul(blk(bk[0:64, :], q), lhsT=opd(d, h, 0, cis[d]), rhs=Hb[:, q, :], start=True, stop=False),
                             reads=[binT[d], bHb], writes=[bb])
                        k.op("pe", lambda e, q=q: e.matmul(blk(bk[0:64, :], q), lhsT=blk(AKT, q), rhs=blk(Vtm, q), start=False, stop=True),
                             reads=[bAKT, bVtm], writes=[bb])
                k.op("act", lambda e, bk=bk: e.copy(out=RHSs[:], in_=bk[0:64, :]), reads=[bb], writes=[bRHS])
                bk, bb = k.bank()
                for q in range(8):
                    k.op("pe", lambda e, q=q: e.matmul(blk(bk[0:64, :], q), lhsT=blk(A, q), rhs=blk(RHSs, q), start=True, stop=True), reads=[bA, bRHS], writes=[bb])
                k.op("dve", lambda e, bk=bk: e.tensor_copy(out=Us[:], in_=bk[0:64, :]), reads=[bb], writes=[bUs])
                for d in range(2):
                    bk, bb = k.bank()
                    for h in range(4):
                        q = d * 4 + h
                        k.op("pe", lambda e, d=d, h=h, q=q, bk=bk: e.matmul(blk(bk[0:64, :], h), lhsT=Hb[:, q, :], rhs=opd(d, h, 1, cis[d]), start=True, stop=False),
                             reads=[binT[d], bHb], writes=[bb])
                        k.op("pe", lambda e, q=q, h=h, bk=bk: e.matmul(blk(bk[0:64, :], h), lhsT=blk(Vtm, q), rhs=blk(RKT, q), start=False, stop=False), reads=[bVtm, bRKT], writes=[bb])
                        k.op("pe", lambda e, q=q, h=h, bk=bk: e.matmul(blk(bk[0:64, :], h), lhsT=blk(Us, q), rhs=blk(RBT, q), start=False, stop=True), reads=[bUs, bRBT], writes=[bb])
                    if d == 0:
                        k.op("act", lambda e, d=d, bk=bk: e.copy(out=yst[d][:, 0, cis[d] * 64:(cis[d] + 1) * 64], in_=bk[0:64, 0:64]), reads=[bb], adds=[byst[d]])
                        for h in range(1, 4):
                            k.op("act", lambda e, d=d, h=h, bk=bk: e.copy(out=yst[d][:, h, cis[d] * 64:(cis[d] + 1) * 64], in_=bk[0:64, h * 64:(h + 1) * 64]), reads=[bb], adds=[byst[d]])
                    else:
                        for h in range(4):
                            k.op("dve", lambda e, d=d, h=h, bk=bk: e.tensor_copy(out=yst[d][:, h, cis[d] * 64:(cis[d] + 1) * 64], in_=bk[0:64, h * 64:(h + 1) * 64]), reads=[bb], adds=[byst[d]])
                bk, bb = k.bank()
                for q in range(8):
                    k.op("pe", lambda e, q=q: e.matmul(blk(bk[0:64, :], q), lhsT=blk(Ktm, q), rhs=blk(Vtm, q), start=True, stop=False), reads=[bKtm, bVtm], writes=[bb])
                    k.op("pe", lambda e, q=q: e.matmul(blk(bk[0:64, :], q), lhsT=blk(NBtm, q), rhs=blk(Us, q), start=False, stop=True), reads=[bNB, bUs], writes=[bb])
                k.op("dve", lambda e, bk=bk: e.tensor_tensor(out=Ht[:].rearrange("p a b -> p (a b)"), in0=bk[0:64, :], in1=Hs[:].rearrange("p a b -> p (a b)"), op=ALU.add),
                     reads=[bb, bHs], writes=[bHt])
                for d in range(2):
                    for h in range(4):
                        q = d * 4 + h
                        k.op("dve", lambda e, d=d, h=h, q=q: e.tensor_scalar(out=Hs[:, q, :], in0=Ht[:, q, :], scalar1=gct[d][:, h, cis[d]:cis[d] + 1], scalar2=None, op0=ALU.mult),
                             reads=[bHt, bgct[d]], writes=[bHs] if q == 0 else (), adds=[bHs] if q else ())
                k.op("act", lambda e: e.copy(out=Hb[:].rearrange("p a b -> p (a b)"), in_=Hs[:].rearrange("p a b -> p (a b)")), reads=[bHs], writes=[bHb])
            flush(0); flush(1)
            k.barrier()

        with ExitStack() as st:
          if i_phase_on(5):
            g2 = k.sb("g2s", [128, 256], stack=st); bg2 = Buf("g2")
            k.dma(g2[:], g2d.ap()[l], writes=[bg2])
            yf = k.sb("yf", [128, 2, 512], stack=st); byf = Buf("yf")
            yb = k.sb("yb", [128, 2, 512], stack=st); byb = Buf("yb")
            rmo = k.sb("rmo", [128, 7, 512], stack=st); brmo = Buf("rmo")
            rk_ = k.sb("rks", [128, 2, 512], stack=st); brk_ = Buf("rks")
            sq = k.sb("sq4", [128, 512], stack=st); bsq = Buf("sq4")
            mu = k.sb("mu4", [128, 512], stack=st); bmu = Buf("mu4")
            va = k.sb("va4", [128, 512], stack=st); bva = Buf("va4")
            sgl = k.sb("sgl", [128, 512], stack=st); bsgl = Buf("sgl")
            orw = k.sb("orw", [128, 2, 512], stack=st); borw = Buf("orw")
            bn = k.sb("bn4", [128, 512], stack=st); bbn = Buf("bn4")
            for (t0, W, who) in TILES:
                if last and who == 1:
                    continue
                yv = yT.ap().rearrange("d (hp p) t -> d p hp t", p=128)
                k.dma(yf[:, :, :W], yv[0][:, :, t0:t0 + W], reads=[byT], writes=[byf])
                k.dma(yb[:, :, :W], yv[1][:, :, t0:t0 + W], reads=[byT], writes=[byb])
                k.dma(rmo[:, :, :W], rmT.ap().rearrange("c p t -> p c t")[:, 0:7, t0:t0 + W], reads=[brm], writes=[brmo])
                k.dma(rk_[:, :, :W], rksT.ap().rearrange("c p t -> p c t")[:, :, t0:t0 + W], reads=[brks], writes=[brk_])
                k.op("dve", lambda e: e.tensor_tensor(out=yf[:, :, :W], in0=yf[:, :, :W], in1=yb[:, :, :W], op=ALU.add), reads=[byb], writes=[byf])
                k.op("act", lambda e: e.activation(out=sgl[:, :W], in_=rmo[:, 2, :W], func=AF.Sigmoid), reads=[brmo], writes=[bsgl])
                for hp in range(2):
                    bk1, bb1 = k.bank()
                    k.op("pe", lambda e, hp=hp: e.matmul(bk1[:, :W], lhsT=bones, rhs=yf[:, hp, :W], start=True, stop=True), reads=[byf, bc], writes=[bb1])
                    k.op("act", lambda e, hp=hp: e.activation(out=sq[:, :W], in_=yf[:, hp, :W], func=AF.Square), reads=[byf], writes=[bsq])
                    bk2, bb2 = k.bank()
                    k.op("pe", lambda e: e.matmul(bk2[:, :W], lhsT=bones, rhs=sq[:, :W], start=True, stop=True), reads=[bsq, bc], writes=[bb2])
                    k.op("dve", lambda e: e.tensor_scalar(out=mu[:, :W], in0=bk1[:, :W], scalar1=1.0 / 64, scalar2=None, op0=ALU.mult), reads=[bb1], writes=[bmu])
                    k.op("dve", lambda e: e.tensor_tensor(out=va[:, :W], in0=mu[:, :W], in1=mu[:, :W], op=ALU.mult), reads=[bmu], writes=[bva])
                    k.op("dve", lambda e: e.scalar_tensor_tensor(out=va[:, :W], in0=bk2[:, :W], scalar=1.0 / 64, in1=va[:, :W], op0=ALU.mult, op1=ALU.subtract),
                         reads=[bb2], writes=[bva])
                    k.op("dve", lambda e: e.tensor_scalar(out=va[:, :W], in0=va[:, :W], scalar1=64e-5, scalar2=None, op0=ALU.add), writes=[bva])
                    k.op("act", lambda e: e.activation(out=va[:, :W], in_=va[:, :W], func=AF.Sqrt), writes=[bva])
                    k.op("dve", lambda e: e.reciprocal(out=va[:, :W], in_=va[:, :W]), writes=[bva])
                    k.op("dve", lambda e, hp=hp: e.tensor_tensor(out=yf[:, hp, :W], in0=yf[:, hp, :W], in1=mu[:, :W], op=ALU.subtract), reads=[bmu], writes=[byf])
                    k.op("dve", lambda e, hp=hp: e.tensor_tensor(out=yf[:, hp, :W], in0=yf[:, hp, :W], in1=va[:, :W], op=ALU.mult), reads=[bva], writes=[byf])
                    k.op("dve", lambda e, hp=hp: e.tensor_scalar(out=yf[:, hp, :W], in0=yf[:, hp, :W], scalar1=pp[:, LXG + hp:LXG + hp + 1], scalar2=pp[:, LXB + hp:LXB + hp + 1],
                                                                op0=ALU.mult, op1=ALU.add), reads=[bpp], writes=[byf])
                    bk3, bb3 = k.bank()
                    k.op("pe", lambda e, hp=hp: e.matmul(bk3[:, :W], lhsT=bones, rhs=rk_[:, hp, :W], start=True, stop=True), reads=[brk_, bc], writes=[bb3])
                    k.op("dve", lambda e, hp=hp: e.tensor_tensor(out=bn[:, :W], in0=bk3[:, :W], in1=rmo[:, 5 + hp, :W], op=ALU.mult), reads=[bb3, brmo], writes=[bbn])
                    k.op("dve", lambda e, hp=hp: e.tensor_tensor(out=bn[:, :W], in0=bn[:, :W], in1=yf[:, hp, :W], op=ALU.add), reads=[byf], writes=[bbn])
                    bk4, bb4 = k.bank()
                    k.op("pe", lambda e, hp=hp: e.matmul(bk4[:, :W], lhsT=g2[:, hp * 128:(hp + 1) * 128], rhs=sgl[:, :W], start=True, stop=True), reads=[bg2, bsgl], writes=[bb4])
                    k.op("dve", lambda e, hp=hp: e.tensor_tensor(out=orw[:, hp, :W], in0=bk4[:, :W], in1=bn[:, :W], op=ALU.mult), reads=[bb4, bbn], writes=[borw])
                k.dma(mixT.ap().rearrange("c p t -> p c t")[:, 6:8, t0:t0 + W], orw[:, :, :W], reads=[borw], adds=[bmix])
            k.barrier()

        with ExitStack() as st:
          if i_phase_on(6):
            wsT = k.sb("wsT", [128, 4, 128], stack=st); bws = Buf("wsT")
            k.dma(wsT[:], sgwT.ap()[l], writes=[bws])
            bsb = k.sb("bsb", [128, 512], stack=st); bbsb = Buf("bsb")
            k.dma(bsb[:], sgb.ap()[l].partition_broadcast(128), writes=[bbsb])
            uv = k.sb("uv", [128, 4, 512], stack=st); buv = Buf("uv")
            sq = k.sb("sqg", [128, 2, 512], stack=st); bsq = Buf("sqg")
            mn = k.sb("mng", [128, 512], stack=st); bmn = Buf("mng")
            va = k.sb("vag", [128, 512], stack=st); bva = Buf("vag")
            vtm = k.sb("vtmg", [128, 256], stack=st); bvtm = Buf("vtmg")
            tmp = k.sb("tmpg", [128, 128], stack=st); btmp = Buf("tmpg")
            osg = k.sb("osg", [128, 2, 512], stack=st); bosg = Buf("osg")
            for (t0, W, who) in TILES:
                if last and who == 1:
                    continue
                k.dma(uv[:, :, :W], pxT.ap().rearrange("c p t -> p c t")[:, 0:4, t0:t0 + W], reads=[bpx], writes=[buv])
                k.op("act", lambda e: e.activation(out=uv[:, :, :W], in_=uv[:, :, :W], func=AF.Gelu), writes=[buv])
                k.op("act", lambda e: e.activation(out=sq[:, :, :W], in_=uv[:, 2:4, :W], func=AF.Square), reads=[buv], writes=[bsq])
                bk1, bb1 = k.bank(); bk2, bb2 = k.bank()
                for c in range(2):
                    k.op("pe", lambda e, c=c: e.matmul(bk1[:, :W], lhsT=ones[:], rhs=uv[:, 2 + c, :W], start=(c == 0), stop=(c == 1)), reads=[buv, bo], writes=[bb1])
                for c in range(2):
                    k.op("pe", lambda e, c=c: e.matmul(bk2[:, :W], lhsT=ones[:], rhs=sq[:, c, :W], start=(c == 0), stop=(c == 1)), reads=[bsq, bo], writes=[bb2])
                k.op("dve", lambda e: e.tensor_scalar(out=mn[:, :W], in0=bk1[:, :W], scalar1=1.0 / 256, scalar2=None, op0=ALU.mult), reads=[bb1], writes=[bmn])
                k.op("dve", lambda e: e.tensor_tensor(out=va[:, :W], in0=mn[:, :W], in1=mn[:, :W], op=ALU.mult), reads=[bmn], writes=[bva])
                k.op("dve", lambda e: e.scalar_tensor_tensor(out=va[:, :W], in0=bk2[:, :W], scalar=1.0 / 256, in1=va[:, :W], op0=ALU.mult, op1=ALU.subtract), reads=[bb2], writes=[bva])
                k.op("dve", lambda e: e.tensor_scalar(out=va[:, :W], in0=va[:, :W], scalar1=1e-5, scalar2=None, op0=ALU.add), writes=[bva])
                k.op("act", lambda e: e.activation(out=va[:, :W], in_=va[:, :W], func=AF.Sqrt), writes=[bva])
                k.op("dve", lambda e: e.reciprocal(out=va[:, :W], in_=va[:, :W]), writes=[bva])
                for c in range(2):
                    k.op("dve", lambda e, c=c: e.tensor_tensor(out=uv[:, 2 + c, :W], in0=uv[:, 2 + c, :W], in1=mn[:, :W], op=ALU.subtract), reads=[bmn], writes=[buv])
                    k.op("dve", lambda e, c=c: e.tensor_tensor(out=uv[:, 2 + c, :W], in0=uv[:, 2 + c, :W], in1=va[:, :W], op=ALU.mult), reads=[bva], writes=[buv])
                    k.op("dve", lambda e, c=c: e.tensor_scalar(out=uv[:, 2 + c, :W], in0=uv[:, 2 + c, :W], scalar1=pp[:, SLG + c:SLG + c + 1], scalar2=pp[:, SLB + c:SLB + c + 1],
                                                              op0=ALU.mult, op1=ALU.add), reads=[bpp], writes=[buv])
                for nb in range(W // 128):
                    cols = slice(nb * 128, (nb + 1) * 128)
                    bkT, bbT = k.bank()
                    for c in range(2):
                        k.op("pe", lambda e, c=c, cols=cols: e.transpose(bkT[:, c * 128:(c + 1) * 128], uv[:, 2 + c, cols], ident), reads=[buv, bc], writes=[bbT])
                    k.op("act", lambda e: e.copy(out=vtm[:], in_=bkT[:, 0:256]), reads=[bbT], writes=[bvtm])
                    bkM, bbM = k.bank()
                    for g in range(4):
                        k.op("pe", lambda e, g=g: e.matmul(bkM[:, g * 128:(g + 1) * 128], lhsT=vtm[:, (g // 2) * 128:(g // 2 + 1) * 128], rhs=wsT[:, g, :], start=True, stop=True),
                             reads=[bvtm, bws], writes=[bbM])
                    for g in range(4):
                        ps_ = slice((g % 2) * 64, (g % 2 + 1) * 64)
                        k.op("dve", lambda e, g=g, ps_=ps_: e.tensor_tensor(out=tmp[ps_, :], in0=bkM[ps_, g * 128:(g + 1) * 128], in1=bsb[ps_, g * 128:(g + 1) * 128], op=ALU.add),
                             reads=[bbM, bbsb], writes=[btmp])
                        k.op("dve", lambda e, g=g, ps_=ps_, cols=cols: e.tensor_tensor(out=osg[ps_, g // 2, cols], in0=tmp[ps_, :], in1=uv[ps_, g // 2, cols], op=ALU.mult),
                             reads=[btmp, buv], writes=[bosg])
                k.dma(mixT.ap().rearrange("c p t -> p c t")[:, 0:2, t0:t0 + W], osg[:, :, :W], reads=[bosg], adds=[bmix])
            k.barrier()

        with ExitStack() as st:
          if i_phase_on(7):
            cosT = k.sb("cosT", [128, T], stack=st); sinT = k.sb("sinT", [128, T], stack=st); btab = Buf("tab")
            k.dma(cosT[:], cosd.ap(), adds=[btab]); k.dma(sinT[:], sind.ap(), adds=[btab], q="pool")
            am = k.sb("am", [128, 2, 512], stack=st); bam = Buf("am")
            k.dma(am[:], amd.ap(), writes=[bam])
            esb = k.sb("esb", [64, 8], stack=st); besb = Buf("esb")
            k.dma(esb[:], sink.ap()[l].partition_broadcast(64), writes=[besb])
            k.op("act", lambda e: e.activation(out=esb[:], in_=esb[:], func=AF.Exp), writes=[besb])
            KT2 = k.sb("KT2", [128, 2, TT], BF16, stack=st); bKT = Buf("KT2")
            Vt = k.sb("Vt", [128, NTOKB, 128], BF16, stack=st); bVt = Buf("Vt")
            onesb = k.sb("onesb", [128, 64], BF16, stack=st); bonb = Buf("onesb")
            k.op("dve", lambda e: e.memset(onesb[:], 1.0), writes=[bonb])
            t2 = k.sb("t2r", [128, 512], stack=st); bt2 = Buf("t2r")
            kraw = k.sb("kraw", [128, TT], stack=st); bkr = Buf("kraw")
            t1 = k.sb("t1", [128, 512], stack=st); bt1 = Buf("t1")
            k.dma(kraw[:], pxT.ap()[11], reads=[bpx], writes=[bkr])

            def rope(dst, src, bsrc, tl, W, wr_reads, bdst):
                bk, bb = k.bank()
                k.op("pe", lambda e: e.matmul(bk[:, :W], lhsT=Rm, rhs=src, start=True, stop=True), reads=[bsrc, bc], writes=[bb])
                k.op("dve", lambda e: e.tensor_tensor(out=t1[:, :W], in0=bk[:, :W], in1=sinT[:, tl:tl + W], op=ALU.mult), reads=[bb, btab], writes=[bt1])
                k.op("dve", lambda e: e.tensor_tensor(out=t2[:, :W], in0=src, in1=cosT[:, tl:tl + W], op=ALU.mult), reads=[bsrc, btab], writes=[bt2])
                k.op("dve", lambda e: e.tensor_tensor(out=dst, in0=t2[:, :W], in1=t1[:, :W], op=ALU.add), reads=[bt1, bt2], writes=[bdst])

            for i in range(8):
                sl = slice(CT + i * 512, CT + (i + 1) * 512)
                rope(kraw[:, sl], kraw[:, sl], bkr, i * 512, 512, [], bkr)
            for hk in range(2):
                ps_ = slice(hk * 64, (hk + 1) * 64)
                for e_ in range(2):
                    po = slice(e_ * 64, (e_ + 1) * 64)
                    k.op("dve" if e_ == 0 else "act", (lambda e, hk=hk, ps_=ps_, po=po: e.tensor_copy(out=KT2[po, hk, :], in_=kraw[ps_, :])) if e_ == 0 else (lambda e, hk=hk, ps_=ps_, po=po: e.copy(out=KT2[po, hk, :], in_=kraw[ps_, :])), reads=[bkr], adds=[bKT])
            vraw = kraw
            k.dma(vraw[:], pxT.ap()[12], reads=[bpx], writes=[bkr])
            for b4 in range(0, NTOKB, 4):
                nb_ = min(4, NTOKB - b4)
                bk, bb = k.bank()
                for i in range(nb_):
                    k.op("pe", lambda e, i=i, b4=b4: e.transpose(bk[:, i * 128:(i + 1) * 128], vraw[:, (b4 + i) * 128:(b4 + i + 1) * 128], ident), reads=[bkr, bc], writes=[bb])
                evac(Vt[:, b4:b4 + nb_, :].rearrange("p a b -> p (a b)"), bk[:, :nb_ * 128], [bb], adds=[bVt])
            qraw = k.sb("qraw", [128, 4, 512], stack=st); bq = Buf("qraw")
            qr = k.sb("qr", [128, 4, 512], BF16, stack=st); bqr = Buf("qr")
            PT = [k.sb(f"PT{i}", [128, 512], BF16, stack=st) for i in range(2)]; bPT = [Buf("PT0"), Buf("PT1")]
            rc = k.sb("rc", [64, 512], stack=st); brc = Buf("rc")
            obt = k.sb("obt", [64, 2, 4, 512], stack=st); bobt = Buf("obt")
            pti = 0; sbi = [0]
            for (t0, W, who) in TILES:
                if last and who == 1:
                    continue
                k.dma(qraw[:, :, :W], pxT.ap().rearrange("c p t -> p c t")[:, 4:8, t0:t0 + W], reads=[bpx], writes=[bq])
                if who == 0:
                    for c in range(4):
                        rope(qr[:, c, :W], qraw[:, c, :W], bq, t0 - CT, W, [], bqr)
                    Q = qr; bQ = bqr
                else:
                    for c in range(4):
                        cast(qr[:, c, :W], qraw[:, c, :W], [bq], writes=[bqr] if c == 0 else (), adds=[bqr] if c else ())
                    Q = qr; bQ = bqr
                for nb in range(W // 128):
                    n = t0 // 128 + nb
                    cols = slice(nb * 128, (nb + 1) * 128)
                    if who == 1:
                        kbs = [(0, None), (1, None)]
                    else:
                        kbs = [(0, None), (1, None)]
                        if n - 1 >= 2:
                            kbs.append((n - 1, 0))
                        kbs.append((n, None))
                        if n + 1 < NTOKB:
                            kbs.append((n + 1, 1))
                    GO = [0, 2, 1, 3]
                    for hk in range(2):
                        bkO, bbO = k.banks[(hk % 2) * 2]; bkD, bbD = k.banks[(hk % 2) * 2 + 1]
                        for idx, (kb, mk) in enumerate(kbs):
                            bkS = [None, None]; bbS = [None, None]
                            bkS[0], bbS[0] = k.banks[4 + (sbi[0] % 2) * 2]; bkS[1], bbS[1] = k.banks[5 + (sbi[0] % 2) * 2]; sbi[0] += 1
                            for pos in range(4):
                                g = GO[pos]; h = hk * 4 + g; e_ = h % 2; chn = h // 2
                                assert e_ == pos // 2
                                k.op("pe", lambda e, pos=pos, e_=e_, chn=chn, kb=kb, hk=hk: e.matmul(bkS[e_][:, (pos % 2) * 128:(pos % 2 + 1) * 128],
                                                                                                    lhsT=KT2[e_ * 64:(e_ + 1) * 64, hk, kb * 128:(kb + 1) * 128],
                                                                                                    rhs=Q[e_ * 64:(e_ + 1) * 64, chn, cols], start=True, stop=True),
                                     reads=[bKT, bQ], writes=[bbS[e_]])
                            P_ = PT[pti % 2]; bP_ = bPT[pti % 2]; pti += 1
                            k.op("act", lambda e, P_=P_, b0=bkS[0]: e.activation(out=P_[:, 0:256], in_=b0[:, 0:256], func=AF.Exp, scale=0.125), reads=[bbS[0]], writes=[bP_])
                            k.op("act", lambda e, P_=P_, b1=bkS[1]: e.activation(out=P_[:, 256:512], in_=b1[:, 0:256], func=AF.Exp, scale=0.125), reads=[bbS[1]], adds=[bP_])
                            if mk is not None:
                                k.op("dve", lambda e, P_=P_, mk=mk: e.tensor_tensor(out=P_[:], in0=P_[:], in1=am[:, mk, :], op=ALU.mult), reads=[bam], writes=[bP_])
                            k.op("pe", lambda e, P_=P_, kb=kb, hk=hk, idx=idx: e.matmul(bkO[0:64, :], lhsT=Vt[:, kb, hk * 64:(hk + 1) * 64], rhs=P_[:], start=(idx == 0), stop=(idx == len(kbs) - 1)),
                                 reads=[bVt, bP_], writes=[bbO])
                            k.op("pe", lambda e, P_=P_, idx=idx: e.matmul(bkD[0:64, :], lhsT=onesb[:], rhs=P_[:], start=(idx == 0), stop=(idx == len(kbs) - 1)), reads=[bonb, bP_], writes=[bbD])
                        for pos in range(4):
                            h = hk * 4 + GO[pos]
                            k.op("dve", lambda e, pos=pos, h=h, bkD=bkD: e.tensor_scalar(out=rc[:, pos * 128:(pos + 1) * 128], in0=bkD[0:64, pos * 128:(pos + 1) * 128],
                                                                                       scalar1=esb[:, h:h + 1], scalar2=None, op0=ALU.add),
                                 reads=[bbD, besb], writes=[brc] if pos == 0 else (), adds=[brc] if pos else ())
                        k.op("dve", lambda e: e.reciprocal(out=rc[:], in_=rc[:]), writes=[brc])
                        for pos in range(4):
                            g = GO[pos]
                            k.op("dve", lambda e, pos=pos, g=g, bkO=bkO, hk=hk, cols=cols: e.tensor_tensor(out=obt[:, hk, g, cols], in0=bkO[0:64, pos * 128:(pos + 1) * 128],
                                                                                                      in1=rc[:, pos * 128:(pos + 1) * 128], op=ALU.mult), reads=[bbO, brc], adds=[bobt])
                for hk in range(2):
                    for g in range(4):
                        h = hk * 4 + g
                        k.dma(mixT.ap()[2 + h // 2][(h % 2) * 64:(h % 2 + 1) * 64, t0:t0 + W], obt[:, hk, g, :W], reads=[bobt], adds=[bmix], q="sp" if g % 2 == 0 else "pool")
                k.op("dve", lambda e: e.memset(rc[:, 0:1], 0.0), reads=[], writes=[bobt, brc])
            k.barrier()

        with ExitStack() as st:
          if i_phase_on(8):
            wo = k.sb("wo", [128, 8, 1024], BF16, stack=st); bwo = Buf("wo")
            wst = [k.sb(f"wst5{i}", [128, 1024], stack=st) for i in range(2)]; bwst = [Buf("wst0"), Buf("wst1")]
            wv = w_out.ap()[l].rearrange("(kc p) n -> p kc n", p=128)
            for kc in range(8):
                k.dma(wst[kc % 2][:], wv[:, kc, :], writes=[bwst[kc % 2]], q="sp" if kc % 2 == 0 else "pool")
                cast(wo[:, kc, :], wst[kc % 2][:], [bwst[kc % 2]], adds=[bwo])
            Mb = k.sb("Mb5", [128, 8, 512], BF16, stack=st); bMb = Buf("Mb5")
            X = k.sb("X5", [128, 8, 512], stack=st); bX = Buf("X5")
            M = k.sb("M5", [128, 8, 512], stack=st); bM = Buf("M5")
            Y = k.sb("Y5", [128, 8, 512], stack=st); bY = Buf("Y5")
            SQ = k.sb("SQ5", [128, 8, 512], stack=st); bSQ = Buf("SQ5")
            rs = k.sb("rs5", [128, 512], stack=st); brs = Buf("rs5")
            xv = xsrc(l).ap().rearrange("c p t -> p c t")
            xo = xs.ap().rearrange("c p t -> p c t")
            for (t0, W, who) in TILES:
                if last and who == 1:
                    continue
                k.dma(X[:, :, :W], xv[:, :, t0:t0 + W], reads=[bxs], writes=[bX])
                k.dma(M[:, :, :W], mixT.ap().rearrange("c p t -> p c t")[:, :, t0:t0 + W], reads=[bmix], writes=[bM], q="pool")
                for kc in range(8):
                    cast(Mb[:, kc, :W], M[:, kc, :W], [bM], writes=[bMb] if kc == 0 else (), adds=[bMb] if kc else ())
                for j in range(8):
                    bk, bb = k.bank()
                    for kc in range(8):
                        k.op("pe", lambda e, kc=kc, j=j: e.matmul(bk[:, :W], lhsT=wo[:, kc, j * 128:(j + 1) * 128], rhs=Mb[:, kc, :W], start=(kc == 0), stop=(kc == 7)),
                             reads=[bwo, bMb], writes=[bb])
                    evac(Y[:, j, :W], bk[:, :W], [bb], adds=[bY])
                rmsn(st, Y, bY, W, SQ, bSQ, rs, brs)
                resid_update(X, bX, Y, bY, rs, brs, W, 2, who)
                k.dma(xo[:, :, t0:t0 + W], X[:, :, :W], reads=[bX], adds=[bxs])
            k.barrier()

        with ExitStack() as st:
          if i_phase_on(9):
            wgb = k.sb("wgb", [128, 8, 5632], BF16, stack=st); bwgb = Buf("wgb")
            NST = 3
            wst = [k.sb(f"wst6{i}", [128, 1024], stack=st) for i in range(NST)]; bwst = [Buf(f"wst{i}") for i in range(NST)]
            wdj = [k.sb(f"wdj{i}", [128, 1024], BF16, stack=st) for i in range(2)]; bwdj = [Buf("wdj0"), Buf("wdj1")]
            gv_ = w_gu.ap()[l].rearrange("(kc p) n -> p kc n", p=128)
            dv_ = w_down.ap()[l].rearrange("(j p) n -> p j n", p=128)
            wi = 0
            for kc in range(8):
                for cb in range(6):
                    c0_ = cb * 1024; cw = min(1024, 5632 - c0_)
                    k.dma(wst[wi % NST][:, :cw], gv_[:, kc, c0_:c0_ + cw], writes=[bwst[wi % NST]], q="sp" if wi % 2 == 0 else "pool")
                    cast(wgb[:, kc, c0_:c0_ + cw], wst[wi % NST][:, :cw], [bwst[wi % NST]], adds=[bwgb])
                    wi += 1
            X = k.sb("X6", [128, 8, 512], stack=st); bX = Buf("X6")
            H = k.sb("H6", [128, 8, 512], BF16, stack=st); bH = Buf("H6")
            SQ = k.sb("SQ6", [128, 8, 512], stack=st); bSQ = Buf("SQ6")
            Y = k.sb("Y6", [128, 8, 512], stack=st); bY = Buf("Y6")
            rs = k.sb("rs6", [128, 512], stack=st); brs = Buf("rs6")
            actb = k.sb("actb", [128, 22, 512], BF16, stack=st); bact = Buf("actb")
            sl = [k.sb(f"sl6{i}", [128, 512], stack=st) for i in range(2)]; bsl = [Buf("sl0"), Buf("sl1")]
            xo = xs.ap().rearrange("c p t -> p c t")
            dj = 0
            for (t0, W, who) in TILES:
                if last and who == 1:
                    continue
                k.dma(X[:, :, :W], xo[:, :, t0:t0 + W], reads=[bxs], writes=[bX])
                rmsn(st, X, bX, W, SQ, bSQ, rs, brs)
                modulate(H, bH, X, bX, rs, brs, W, 3, 4, who)
                for j in range(22):
                    bkg, bbg = k.bank(); bku, bbu = k.bank()
                    for kc in range(8):
                        k.op("pe", lambda e, kc=kc, j=j: e.matmul(bkg[:, :W], lhsT=wgb[:, kc, j * 128:(j + 1) * 128], rhs=H[:, kc, :W], start=(kc == 0), stop=(kc == 7)),
                             reads=[bwgb, bH], writes=[bbg])
                    for kc in range(8):
                        k.op("pe", lambda e, kc=kc, j=j: e.matmul(bku[:, :W], lhsT=wgb[:, kc, 2816 + j * 128:2816 + (j + 1) * 128], rhs=H[:, kc, :W], start=(kc == 0), stop=(kc == 7)),
                             reads=[bwgb, bH], writes=[bbu])
                    s_ = sl[j % 2]; bs_ = bsl[j % 2]
                    k.op("act", lambda e, bkg=bkg, s_=s_: e.activation(out=s_[:, :W], in_=bkg[:, :W], func=AF.Silu), reads=[bbg], writes=[bs_])
                    k.op("dve", lambda e, j=j, bku=bku, s_=s_: e.tensor_tensor(out=actb[:, j, :W], in0=bku[:, :W], in1=s_[:, :W], op=ALU.mult), reads=[bbu, bs_],
                         writes=[bact] if j == 0 else (), adds=[bact] if j else ())
                for j in range(22):
                    w_ = wst[wi % NST]; bw_ = bwst[wi % NST]
                    k.dma(w_[:], dv_[:, j, :], writes=[bw_], q="sp" if wi % 2 == 0 else "pool")
                    wi += 1
                    d_ = wdj[dj % 2]; bd_ = bwdj[dj % 2]; dj += 1
                    cast(d_[:], w_[:], [bw_], writes=[bd_])
                    for jo in range(8):
                        bk, bb = k.banks[jo]
                        k.op("pe", lambda e, j=j, jo=jo, d_=d_, bk=bk: e.matmul(bk[:, :W], lhsT=d_[:, jo * 128:(jo + 1) * 128], rhs=actb[:, j, :W], start=(j == 0), stop=(j == 21)),
                             reads=[bd_, bact], writes=[bb])
                for jo in range(8):
                    bk, bb = k.banks[jo]
                    evac(Y[:, jo, :W], bk[:, :W], [bb], writes=[bY] if jo == 0 else (), adds=[bY] if jo else ())
                rmsn(st, Y, bY, W, SQ, bSQ, rs, brs)
                resid_update(X, bX, Y, bY, rs, brs, W, 5, who)
                if l == L - 1:
                    k.dma(outT.ap().rearrange("c p t -> p c t")[:, :, t0 - CT:t0 - CT + W], X[:, :, :W], reads=[bX], adds=[bout])
                else:
                    k.dma(xo[:, :, t0:t0 + W], X[:, :, :W], reads=[bX], adds=[bxs])
            k.barrier()
    k.barrier()
    return nc, k


def host_consts():
    ident = np.eye(128, dtype=np.float32)
    bones = np.zeros((128, 128), np.float32); bones[:64, :64] = 1; bones[64:, 64:] = 1
    Rm = np.zeros((128, 128), np.float32)
    for m in range(128):
        i = m % 32
        if i < 16:
            Rm[m + 16, m] = -1.0
        else:
            Rm[m - 16, m] = 1.0
    cc = np.concatenate([ident, bones, Rm], axis=1)
    smL = np.zeros((128, 9, 4), np.float32); smC = np.zeros((128, 9, 2), np.float32)
    for ci in range(9):
        for p in range(128):
            if ci < 3:
                ch = ci * 128 + p; qd = ch // 96; half = 0 if ch < 192 else 1
            else:
                ch = (ci - 3) * 128 + p; qd = ch // 192; half = 0 if ch < 384 else 1
            smL[p, ci, qd] = 1; smC[p, ci, half] = 1
    s = np.arange(64)[:, None]; t = np.arange(64)[None, :]
    mS = [(s < t), (s > t)]; mST = [(s > t), (s < t)]; mI = [(s <= t), (s >= t)]
    cm = np.zeros((64, 4, 512), np.float32)
    for d in range(2):
        for h in range(4):
            q = d * 4 + h
            cm[:, 0, q * 64:(q + 1) * 64] = mS[d]
            cm[:, 1, q * 64:(q + 1) * 64] = mST[d]
            cm[:, 2, q * 64:(q + 1) * 64] = mI[d]
            cm[:, 3, q * 64:(q + 1) * 64] = np.eye(64)
    j = np.arange(128)[:, None]; i = np.arange(128)[None, :]
    am = np.zeros((128, 2, 512), np.float32)
    for g in range(4):
        am[:, 0, g * 128:(g + 1) * 128] = (j >= i)
        am[:, 1, g * 128:(g + 1) * 128] = (j <= i)
    rs = np.ones((128, 512), np.float32); rs[:, ::64] = 0
    tt = np.arange(T)
    pos = np.stack([tt // 64, tt % 64], -1).astype(np.float32)
    inv = (10000.0 ** (-np.arange(16, dtype=np.float32) / 16)).astype(np.float32)
    ang = pos[:, :, None] * inv
    cosT = np.zeros((128, T), np.float32); sinT = np.zeros((128, T), np.float32)
    for p in range(128):
        dd = p % 64; ax = dd // 32; fr = dd % 16
        cosT[p] = np.cos(ang[:, ax, fr]); sinT[p] = np.sin(ang[:, ax, fr])
    return dict(cc=cc, smL=smL, smC=smC, cm=cm, am=am, rs=rs, cosT=cosT, sinT=sinT)


def cmaj(v, n):
    return np.ascontiguousarray(np.asarray(v, np.float32).reshape(n, 128).T)


def host_inputs(inp, b):
    f = lambda a: np.ascontiguousarray(np.asarray(a, np.float32))
    xall = np.concatenate([inp["ctx"][b], inp["x"][b]], axis=0)
    xin = np.ascontiguousarray(xall.T.reshape(8, 128, TT))
    cin = np.stack([cmaj(inp["c"][b], 8), cmaj(inp["c_ctx"], 8)], axis=-1)
    pp = np.zeros((L, 128, NP), np.float32)
    for l in range(L):
        for i in range(4):
            pp[l, :, NG + i * 8:NG + i * 8 + 8] = cmaj(inp["norm_g"][l, i], 8)
        pp[l, :, BM:BM + 48] = cmaj(inp["b_mod"][l], 48)
        pp[l, :, MU:MU + 9] = cmaj(inp["rw_mu"][l], 9)
        pp[l, :, KK:KK + 2] = cmaj(inp["rw_kk"][l], 2)
        pp[l, :, KA:KA + 2] = cmaj(inp["rw_ka"][l], 2)
        pp[l, :, RK:RK + 2] = cmaj(inp["rw_rk"][l].reshape(-1), 2)
        pp[l, :, LXG:LXG + 2] = cmaj(inp["rw_lnx_g"][l], 2)
        pp[l, :, LXB:LXB + 2] = cmaj(inp["rw_lnx_b"][l], 2)
        pp[l, :, W0:W0 + 4] = cmaj(inp["rw_w0"][l].reshape(-1), 4)
        pp[l, :, A0:A0 + 4] = cmaj(inp["rw_a0"][l].reshape(-1), 4)
        pp[l, :, SLG:SLG + 2] = cmaj(inp["sg_ln_g"][l], 2)
        pp[l, :, SLB:SLB + 2] = cmaj(inp["sg_ln_b"][l], 2)
    m = dict(xin=xin, cin=f(cin), w_mod=f(inp["w_mod"]), w_in=f(inp["w_in"]), w_out=f(inp["w_out"]), w_gu=f(inp["w_gu"]), w_down=f(inp["w_down"]),
             pp=pp, sgwT=f(np.transpose(inp["sg_w"], (0, 3, 1, 2))), sgb=f(inp["sg_b"]).reshape(L, 1, 512), sink=f(inp["attn_sink"]).reshape(L, 1, 8),
             w2=f(inp["rw_w2"]).reshape(L, 128, 256), a2=f(inp["rw_a2"]).reshape(L, 128, 256), g2=f(inp["rw_g2"]))
    m.update(host_consts())
    return m


def kernel(**inp):
    inp = {k_: np.asarray(v) for k_, v in inp.items()}
    nc, _ = build()
    maps = [host_inputs(inp, b % 4) for b in range(4)]
    in_maps = [maps[i % 4] for i in range(8)]
    res = run_bass_kernel_spmd(nc, in_maps, core_ids=list(range(8)))
    out = np.zeros((4, T, D), np.float32)
    for b in range(4):
        o = res.results[b]["outT"]
        out[b] = o.reshape(D, T).T
    return out
```

```python
import numpy as np
import concourse.bass as bass
import concourse.mybir as mybir
from concourse.bass_utils import run_bass_kernel_spmd
from contextlib import ExitStack

F32 = mybir.dt.float32
BF16 = mybir.dt.bfloat16
AF = mybir.ActivationFunctionType
ALU = mybir.AluOpType

D = 1024; T = 4096; CT = 256; TT = T + CT; L = 4
NTOKB = TT // 128
NCHK = TT // 64
C0 = float(np.exp(-0.5))
TILES = [(0, 256, 1)] + [(256 + 512 * i, 512, 0) for i in range(8)]
TILES5 = [(0, 256, 1)] + [(256 + 256 * i, 256, 0) for i in range(16)]
NG = 0; BM = 32; MU = 80; KK = 89; KA = 91; RK = 93; LXG = 95; LXB = 97; W0 = 99; A0 = 103; SLG = 107; SLB = 109; NP = 111
SRC_CH = [8, 9, 10, 13, 14, 15, 16, 17, 18]

SAME_ENGINE_SYNC = True
EPOCH = 12000
DMA_EPOCH = 1500
NDMASEM = 6


class Buf:
    __slots__ = ("name", "w", "r")

    def __init__(self, name):
        self.name = name
        self.w = {}
        self.r = {}


class KB:
    def __init__(self, nc):
        self.nc = nc
        self.es = ExitStack()
        self.E = {"pe": nc.tensor, "act": nc.scalar, "dve": nc.vector, "pool": nc.gpsimd, "sp": nc.sync}
        self.sem = {}
        self.cnt = {}
        self.seen = {e: {} for e in self.E}
        self.nsem = 0
        for e in self.E:
            self._new_sem(e)
        self.dsem = {}
        self.dcnt = {}
        self.drr = {}
        for q in ("sp", "pool"):
            self.dsem[q] = [self._alloc_sem(f"d{q}{i}") for i in range(NDMASEM)]
            self.dcnt[q] = [0] * NDMASEM
            self.drr[q] = 0
        self.ninstr = 0
        self.banks = []
        self.bi = 0

    def _alloc_sem(self, name):
        self.nsem += 1
        h = self.nc.alloc_semaphore(name=f"{name}_{self.nsem}")
        self._keep = getattr(self, "_keep", [])
        self._keep.append(h)
        return h

    def _new_sem(self, e):
        self.sem[e] = self._alloc_sem(f"s{e}")
        self.cnt[e] = 0

    def sb(self, name, shape, dtype=F32, stack=None):
        self.nsb = getattr(self, "nsb", 0) + 1
        return (stack or self.es).enter_context(self.nc.sbuf_tensor(f"sb{self.nsb}_{name}", list(shape), dtype))

    def mkbanks(self):
        for i in range(8):
            t = self.es.enter_context(self.nc.psum_tensor(f"bank{i}", [128, 512], F32))
            self.banks.append((t, Buf(f"bank{i}")))

    def bank(self):
        b = self.banks[self.bi]
        self.bi = (self.bi + 1) % 8
        return b

    def _wait(self, e, ev):
        sem, val, src = ev
        if src == e and (not SAME_ENGINE_SYNC or e == "pe"):
            return
        key = id(sem)
        if self.seen[e].get(key, 0) >= val:
            return
        self.E[e].wait_ge(sem, val)
        self.seen[e][key] = val

    def _deps(self, e, reads, writes, adds, isdma=False):
        for b in reads:
            for ev in b.w.values():
                self._wait(e, ev)
        for b in writes:
            for ev in b.w.values():
                self._wait(e, ev)
            for ev in b.r.values():
                self._wait(e, ev)
        for b in adds:
            for ev in b.r.values():
                self._wait(e, ev)
            for ev in b.w.values():
                if not (isdma and ev[2] == "dma"):
                    self._wait(e, ev)

    def _commit(self, key, ev, reads, writes, adds):
        for b in reads:
            b.r[key] = ev
        for b in writes:
            b.w = {key: ev}
            b.r = {}
        for b in adds:
            b.w[key] = ev

    def op(self, e, fn, reads=(), writes=(), adds=()):
        self._deps(e, reads, writes, adds)
        ins = fn(self.E[e])
        if self.cnt[e] >= EPOCH:
            self._new_sem(e)
        self.cnt[e] += 1
        ins.then_inc(self.sem[e], 1)
        ev = (self.sem[e], self.cnt[e], e)
        self._commit(e, ev, reads, writes, adds)
        self.ninstr += 1
        return ev

    def dma(self, out, in_, reads=(), writes=(), adds=(), q="sp"):
        self._deps(q, reads, writes, adds, True)
        j = self.drr[q]
        self.drr[q] = (j + 1) % NDMASEM
        if self.dcnt[q][j] > 0:
            self._wait(q, (self.dsem[q][j], self.dcnt[q][j] * 16, "dma"))
        if self.dcnt[q][j] >= DMA_EPOCH:
            self.dsem[q][j] = self._alloc_sem(f"d{q}{j}")
            self.dcnt[q][j] = 0
        sem = self.dsem[q][j]
        self.E[q].dma_start(out=out, in_=in_).then_inc(sem, 16)
        self.dcnt[q][j] += 1
        ev = (sem, self.dcnt[q][j] * 16, "dma")
        self._commit(id(sem), ev, reads, writes, adds)
        self.ninstr += 1
        return ev

    def barrier(self):
        if getattr(self, "ptick", None) is not None:
            self.op("pool", lambda e: e.memset(self.ptick[:], 0.0), writes=[self.ptickb])
        evs = []
        for e in ("pe", "act", "dve", "pool", "sp"):
            if self.cnt[e] > 0:
                evs.append((self.sem[e], self.cnt[e], e))
        for q in self.dsem:
            for j in range(NDMASEM):
                if self.dcnt[q][j] > 0:
                    evs.append((self.dsem[q][j], self.dcnt[q][j] * 16, "dma"))
        for e in self.E:
            for ev in evs:
                if ev[2] != e:
                    self._wait(e, ev)


PHASES = None


CURL = [0]
PHASES_L = None


def i_phase_on(i):
    if PHASES_L is not None:
        return i in PHASES_L[CURL[0]]
    return PHASES is None or i in PHASES


def build(nlayers=L, debug=False):
    nc = bass.Bass("TRN2", target_bir_lowering=False)
    k = KB(nc)
    k.mkbanks()

    def din(name, shape):
        return nc.dram_tensor(name, list(shape), F32, kind="ExternalInput")

    def dscr(name, shape, out=False):
        return nc.dram_tensor(name, list(shape), F32, kind="ExternalOutput" if (out and debug) else "Internal")

    xin = din("xin", [8, 128, TT])
    cin = din("cin", [128, 8, 2])
    w_mod = din("w_mod", [L, D, 6 * D]); w_in = din("w_in", [L, D, 2432]); w_out = din("w_out", [L, D, D])
    w_gu = din("w_gu", [L, D, 5632]); w_down = din("w_down", [L, 2816, D])
    ppd = din("pp", [L, 128, NP])
    sgwT = din("sgwT", [L, 128, 4, 128]); sgb = din("sgb", [L, 1, 512]); sink = din("sink", [L, 1, 8])
    w2d = din("w2", [L, 128, 256]); a2d = din("a2", [L, 128, 256]); g2d = din("g2", [L, 128, 256])
    cc = din("cc", [128, 384]); smL = din("smL", [128, 9, 4]); smC = din("smC", [128, 9, 2])
    cmd = din("cm", [64, 4, 512]); amd = din("am", [128, 2, 512]); rsd = din("rs", [128, 512])
    cosd = din("cosT", [128, T]); sind = din("sinT", [128, T])
    outT = nc.dram_tensor("outT", [8, 128, T], F32, kind="ExternalOutput")

    xs = dscr("xs", [8, 128, TT], True); bxs = Buf("xs")
    pxT = dscr("pxT", [19, 128, TT], True); bpx = Buf("pxT")
    mixT = dscr("mixT", [8, 128, TT], True); bmix = Buf("mixT")
    rmT = dscr("rmT", [9, 128, TT], True); brm = Buf("rmT")
    scin = nc.dram_tensor("scin", [4, 128, 5, TT], BF16, kind="Internal"); bscin = Buf("scin")
    gcd = dscr("gcd", [4, 128, NCHK], True); bgcd = Buf("gcd")
    rksT = dscr("rksT", [2, 128, TT], True); brks = Buf("rksT")
    yT = dscr("yT", [2, 256, TT], True); byT = Buf("yT")
    bout = Buf("outT")

    cst = k.sb("cst", [128, 384]); bc = Buf("cst")
    k.dma(cst[:], cc.ap(), writes=[bc])
    ident = cst[:, 0:128]; bones = cst[:, 128:256]; Rm = cst[:, 256:384]
    k.ptick = k.sb("ptick", [128, 8]); k.ptickb = Buf("ptick")
    ones = k.sb("ones", [128, 128]); bo = Buf("ones")
    k.op("dve", lambda e: e.memset(ones[:], 1.0), writes=[bo])
    cact = k.sb("cact", [128, 8, 2]); bca = Buf("cact")
    k.dma(cact[:], cin.ap(), writes=[bca])
    k.op("act", lambda e: e.activation(out=cact[:], in_=cact[:], func=AF.Silu), writes=[bca])
    pp = k.sb("ppt", [128, NP]); bpp = Buf("pp")
    modt = k.sb("modt", [128, 48, 2]); bmod = Buf("modt")
    coef = k.sb("coef", [128, 6, 8, 2]); bcoef = Buf("coef")
    cfs = k.sb("cfs", [128, 9, 7]); bcfs = Buf("cfs")
    smLs = k.sb("smLs", [128, 9, 4]); smCs = k.sb("smCs", [128, 9, 2]); bsm = Buf("sm")
    k.dma(smLs[:], smL.ap(), adds=[bsm]); k.dma(smCs[:], smC.ap(), adds=[bsm])
    omka = k.sb("omka", [128, 2]); bomka = Buf("omka")
    k.barrier()

    def xsrc(l):
        return xin if l == 0 else xs

    def rmsn(st, X, bX, W, SQ, bSQ, rs, brs):
        k.op("act", lambda e: e.activation(out=SQ[:, :, :W], in_=X[:, :, :W], func=AF.Square), reads=[bX], writes=[bSQ])
        bk, bb = k.bank()
        for c in range(8):
            k.op("pe", lambda e, c=c: e.matmul(bk[:, :W], lhsT=ones[:], rhs=SQ[:, c, :W], start=(c == 0), stop=(c == 7)),
                 reads=[bSQ, bo], writes=[bb])
        k.op("dve", lambda e: e.tensor_scalar(out=rs[:, :W], in0=bk[:, :W], scalar1=1.0 / D, scalar2=1e-6, op0=ALU.mult, op1=ALU.add),
             reads=[bb], writes=[brs])
        k.op("act", lambda e: e.activation(out=rs[:, :W], in_=rs[:, :W], func=AF.Sqrt), writes=[brs])
        k.op("dve", lambda e: e.reciprocal(out=rs[:, :W], in_=rs[:, :W]), writes=[brs])

    t32 = [k.sb(f"t32_{i}", [128, 512]) for i in range(2)]; bt32 = [Buf("t32a"), Buf("t32b")]

    def modulate(H, bH, X, bX, rs, brs, W, ia, ib, who):
        for c in range(8):
            t_ = t32[c % 2]; bt_ = bt32[c % 2]
            k.op("dve", lambda e, c=c, t_=t_: e.scalar_tensor_tensor(out=t_[:, :W], in0=X[:, c, :W], scalar=coef[:, ia, c, who:who + 1],
                                                                   in1=rs[:, :W], op0=ALU.mult, op1=ALU.mult),
                 reads=[bX, brs, bcoef], writes=[bt_])
            k.op("act", lambda e, c=c, t_=t_: e.activation(out=H[:, c, :W], in_=t_[:, :W], func=AF.Identity, bias=coef[:, ib, c, who:who + 1]),
                 reads=[bcoef, bt_], writes=[bH] if c == 0 else (), adds=[bH] if c else ())

    castrr = [0]

    def cast(out, in_, reads, writes=(), adds=()):
        castrr[0] = (castrr[0] + 1) % 2
        if castrr[0] == 0:
            k.op("act", lambda e: e.copy(out=out, in_=in_), reads=reads, writes=writes, adds=adds)
        else:
            k.op("dve", lambda e: e.tensor_copy(out=out, in_=in_), reads=reads, writes=writes, adds=adds)

    def resid_update(X, bX, Y, bY, rs, brs, W, ig, who):
        for j in range(8):
            k.op("dve", lambda e, j=j: e.scalar_tensor_tensor(out=Y[:, j, :W], in0=Y[:, j, :W], scalar=coef[:, ig, j, who:who + 1],
                                                             in1=rs[:, :W], op0=ALU.mult, op1=ALU.mult),
                 reads=[brs, bcoef], writes=[bY])
            k.op("dve", lambda e, j=j: e.tensor_tensor(out=X[:, j, :W], in0=X[:, j, :W], in1=Y[:, j, :W], op=ALU.add),
                 reads=[bY], writes=[bX])

    evflip = [0]

    def evac(out, in_, reads, writes=(), adds=()):
        evflip[0] ^= 1
        if evflip[0]:
            k.op("act", lambda e: e.copy(out=out, in_=in_), reads=reads, writes=writes, adds=adds)
        else:
            k.op("dve", lambda e: e.tensor_copy(out=out, in_=in_), reads=reads, writes=writes, adds=adds)

    for l in range(nlayers):
        last = (l == L - 1)
        CURL[0] = l
        with ExitStack() as st:
          if i_phase_on(0):
            k.dma(pp[:], ppd.ap()[l], writes=[bpp])
            wm = [k.sb(f"wm{i}", [128, 8, 1024], stack=st) for i in range(2)]; bwm = [Buf("wm0"), Buf("wm1")]
            for m in range(6):
                k.dma(wm[m % 2][:], w_mod.ap()[l][:, m * 1024:(m + 1) * 1024].rearrange("(kc p) n -> p kc n", p=128), writes=[bwm[m % 2]],
                      q="sp" if m % 2 == 0 else "pool")
                for c in range(8):
                    bk, bb = k.bank()
                    for kc in range(8):
                        k.op("pe", lambda e, kc=kc, c=c, m=m: e.matmul(bk[:, 0:2], lhsT=wm[m % 2][:, kc, c * 128:(c + 1) * 128], rhs=cact[:, kc, :],
                                                                      start=(kc == 0), stop=(kc == 7)), reads=[bwm[m % 2], bca], writes=[bb])
                    j = m * 8 + c
                    k.op("dve", lambda e, j=j: e.tensor_scalar(out=modt[:, j, :], in0=bk[:, 0:2], scalar1=pp[:, BM + j:BM + j + 1], scalar2=None, op0=ALU.add),
                         reads=[bb, bpp], adds=[bmod])
            for who in range(2):
                for (ic, isc, ig) in ((0, 1, 0), (3, 4, 2)):
                    k.op("dve", lambda e, who=who, ic=ic, isc=isc, ig=ig: e.tensor_tensor(out=coef[:, ic, :, who], in0=modt[:, isc * 8:isc * 8 + 8, who],
                                                                                        in1=pp[:, NG + ig * 8:NG + ig * 8 + 8], op=ALU.mult),
                         reads=[bmod, bpp], writes=[bcoef])
                    k.op("dve", lambda e, who=who, ic=ic, ig=ig: e.tensor_tensor(out=coef[:, ic, :, who], in0=coef[:, ic, :, who],
                                                                               in1=pp[:, NG + ig * 8:NG + ig * 8 + 8], op=ALU.add), writes=[bcoef])
                for (ic, ish) in ((1, 0), (4, 3)):
                    k.op("dve", lambda e, who=who, ic=ic, ish=ish: e.tensor_copy(out=coef[:, ic, :, who], in_=modt[:, ish * 8:ish * 8 + 8, who]),
                         reads=[bmod], writes=[bcoef])
                for (ic, igt, ig) in ((2, 2, 1), (5, 5, 3)):
                    k.op("dve", lambda e, who=who, ic=ic, igt=igt, ig=ig: e.tensor_tensor(out=coef[:, ic, :, who], in0=modt[:, igt * 8:igt * 8 + 8, who],
                                                                                        in1=pp[:, NG + ig * 8:NG + ig * 8 + 8], op=ALU.mult),
                         reads=[bmod, bpp], writes=[bcoef])
            k.op("dve", lambda e: e.tensor_scalar(out=cfs[:, :, 0], in0=pp[:, MU:MU + 9], scalar1=-1.0, scalar2=1.0, op0=ALU.mult, op1=ALU.add),
                 reads=[bpp], writes=[bcfs])
            for dd in range(4):
                k.op("dve", lambda e, dd=dd: e.tensor_tensor(out=cfs[:, :, 1 + dd], in0=pp[:, MU:MU + 9], in1=smLs[:, :, dd], op=ALU.mult),
                     reads=[bpp, bsm], writes=[bcfs])
            for dd in range(2):
                k.op("dve", lambda e, dd=dd: e.tensor_tensor(out=cfs[:, :, 5 + dd], in0=pp[:, MU:MU + 9], in1=smCs[:, :, dd], op=ALU.mult),
                     reads=[bpp, bsm], writes=[bcfs])
            k.op("dve", lambda e: e.tensor_scalar(out=omka[:], in0=pp[:, KA:KA + 2], scalar1=-1.0, scalar2=1.0, op0=ALU.mult, op1=ALU.add),
                 reads=[bpp], writes=[bomka])
            k.barrier()

        with ExitStack() as st:
          if i_phase_on(1):
            win = k.sb("win", [128, 8, 2432], BF16, stack=st); bwin = Buf("win")
            wst = [k.sb(f"wst{i}", [128, 2432], stack=st) for i in range(2)]; bwst = [Buf("wst0"), Buf("wst1")]
            wv = w_in.ap()[l].rearrange("(kc p) n -> p kc n", p=128)
            for kc in range(8):
                k.dma(wst[kc % 2][:], wv[:, kc, :], writes=[bwst[kc % 2]], q="sp" if kc % 2 == 0 else "pool")
                cast(win[:, kc, :], wst[kc % 2][:], [bwst[kc % 2]], adds=[bwin])
            X = k.sb("X", [128, 8, 512], stack=st); bX = Buf("X")
            SQ = k.sb("SQ", [128, 8, 512], stack=st); bSQ = Buf("SQ")
            H = k.sb("H", [128, 8, 512], BF16, stack=st); bH = Buf("H")
            rs = k.sb("rs", [128, 512], stack=st); brs = Buf("rs")
            stg = [k.sb(f"stg{i}", [128, 512], stack=st) for i in range(2)]; bstg = [Buf("s0"), Buf("s1")]
            xv = xsrc(l).ap().rearrange("c p t -> p c t")
            for (t0, W, who) in TILES:
                k.dma(X[:, :, :W], xv[:, :, t0:t0 + W], reads=[bxs], writes=[bX])
                rmsn(st, X, bX, W, SQ, bSQ, rs, brs)
                modulate(H, bH, X, bX, rs, brs, W, 0, 1, who)
                for j in range(19):
                    bk, bb = k.bank()
                    for kc in range(8):
                        k.op("pe", lambda e, kc=kc, j=j: e.matmul(bk[:, :W], lhsT=win[:, kc, j * 128:(j + 1) * 128], rhs=H[:, kc, :W],
                                                                 start=(kc == 0), stop=(kc == 7)), reads=[bwin, bH], writes=[bb])
                    s = stg[j % 2]
                    evac(s[:, :W], bk[:, :W], [bb], writes=[bstg[j % 2]])
                    k.dma(pxT.ap()[j][:, t0:t0 + W], s[:, :W], reads=[bstg[j % 2]], adds=[bpx])
            k.barrier()

        with ExitStack() as st:
          if i_phase_on(2):
            fb = [k.sb(f"fb{i}", [128, TT], stack=st) for i in range(2)]; bfb = [Buf("fb0"), Buf("fb1")]
            ob = [k.sb(f"ob{i}", [128, TT], stack=st) for i in range(2)]; bob = [Buf("ob0"), Buf("ob1")]
            for ci in range(9):
                f = fb[ci % 2]; o = ob[ci % 2]; bf = bfb[ci % 2]; bo_ = bob[ci % 2]
                k.dma(f[:], pxT.ap()[SRC_CH[ci]], reads=[bpx], writes=[bf], q="sp" if ci % 2 == 0 else "pool")
                k.op("act", lambda e, f=f, o=o, ci=ci: e.activation(out=o[:], in_=f[:], func=AF.Identity, scale=cfs[:, ci, 0:1]), reads=[bf, bcfs], writes=[bo_])
                fl = f[:, CT:].rearrange("p (r w) -> p r w", w=64)
                ol = o[:, CT:].rearrange("p (r w) -> p r w", w=64)
                views = [(ol[:, :, 1:64], fl[:, :, 0:63]), (ol[:, :, 0:63], fl[:, :, 1:64]), (ol[:, 1:64, :], fl[:, 0:63, :]), (ol[:, 0:63, :], fl[:, 1:64, :]),
                         (o[:, 1:CT], f[:, 0:CT - 1]), (o[:, 0:CT - 1], f[:, 1:CT])]
                for di, (ov, fv) in enumerate(views):
                    eng = "dve"
                    k.op(eng, lambda e, ov=ov, fv=fv, ci=ci, di=di: e.scalar_tensor_tensor(out=ov, in0=fv, scalar=cfs[:, ci, 1 + di:2 + di], in1=ov,
                                                                                          op0=ALU.mult, op1=ALU.add), reads=[bf, bcfs], writes=[bo_])
                k.dma(rmT.ap()[ci], o[:], reads=[bo_], adds=[brm])
            k.barrier()

        with ExitStack() as st:
          if i_phase_on(3):
            w2 = k.sb("w2s", [128, 256], stack=st); a2 = k.sb("a2s", [128, 256], stack=st); bw2 = Buf("w2")
            k.dma(w2[:], w2d.ap()[l], adds=[bw2]); k.dma(a2[:], a2d.ap()[l], adds=[bw2])
            rsm = k.sb("rsm", [128, 512], stack=st); brsm = Buf("rsm")
            k.dma(rsm[:], rsd.ap(), writes=[brsm])
            rm = k.sb("rm", [128, 9, 512], stack=st); brmt = Buf("rm")
            tw = k.sb("tw", [128, 512], stack=st); btw = Buf("tw")
            kkr = k.sb("kkr", [128, 2, 512], stack=st); bkkr = Buf("kkr")
            sqk = k.sb("sqk", [128, 2, 512], stack=st); bsqk = Buf("sqk")
            kk = k.sb("kk", [128, 2, 512], stack=st); bkk = Buf("kk")
            nr = k.sb("nr", [128, 512], stack=st); bnr = Buf("nr")
            sg = k.sb("sg", [128, 512], stack=st); bsg = Buf("sg")
            aa = k.sb("aa", [128, 512], stack=st); baa = Buf("aa")
            cs = k.sb("cs", [128, 512], stack=st); bcs = Buf("cs")
            Ei = k.sb("Ei", [128, 512], stack=st); bEi = Buf("Ei")
            Ex = k.sb("Ex", [128, 512], stack=st); bEx = Buf("Ex")
            gi = k.sb("gi", [128, 512], stack=st); bgi = Buf("gi")
            gx = k.sb("gx", [128, 512], stack=st); bgx = Buf("gx")
            gv = k.sb("gv", [128, 512], stack=st); bgv = Buf("gv")
            gc = k.sb("gc", [128, 8], stack=st); bgc = Buf("gc")
            kd = k.sb("kd", [128, 512], stack=st); bkd = Buf("kd")
            bt = k.sb("bt", [128, 512], stack=st); bbt = Buf("bt")
            kds = k.sb("kds", [128, 2, 512], stack=st); bkds = Buf("kds")
            so = [k.sb(f"so{i}", [128, 5, 512], BF16, stack=st) for i in range(2)]; bso = [Buf("so0"), Buf("so1")]
            it = 0
            for (t0, W, who) in TILES:
                nck = W // 64
                k.dma(rm[:, :, :W], rmT.ap().rearrange("c p t -> p c t")[:, :, t0:t0 + W], reads=[brm], writes=[brmt])
                k.op("act", lambda e: e.activation(out=tw[:, :W], in_=rm[:, 7, :W], func=AF.Tanh), reads=[brmt], writes=[btw])
                for hp in range(2):
                    k.op("dve", lambda e, hp=hp: e.tensor_scalar(out=kkr[:, hp, :W], in0=rm[:, 3 + hp, :W], scalar1=pp[:, KK + hp:KK + hp + 1], scalar2=None, op0=ALU.mult),
                         reads=[brmt, bpp], adds=[bkkr])
                k.op("act", lambda e: e.activation(out=sqk[:, :, :W], in_=kkr[:, :, :W], func=AF.Square), reads=[bkkr], writes=[bsqk])
                for hp in range(2):
                    bk, bb = k.bank()
                    k.op("pe", lambda e, hp=hp: e.matmul(bk[:, :W], lhsT=bones, rhs=sqk[:, hp, :W], start=True, stop=True), reads=[bsqk, bc], writes=[bb])
                    k.op("act", lambda e: e.activation(out=nr[:, :W], in_=bk[:, :W], func=AF.Sqrt), reads=[bb], writes=[bnr])
                    k.op("dve", lambda e: e.tensor_scalar(out=nr[:, :W], in0=nr[:, :W], scalar1=1e-12, scalar2=None, op0=ALU.max), writes=[bnr])
                    k.op("dve", lambda e: e.reciprocal(out=nr[:, :W], in_=nr[:, :W]), writes=[bnr])
                    k.op("dve", lambda e, hp=hp: e.tensor_tensor(out=kk[:, hp, :W], in0=kkr[:, hp, :W], in1=nr[:, :W], op=ALU.mult), reads=[bkkr, bnr], adds=[bkk])
                for hp in range(2):
                    for d in range(2):
                        bk, bb = k.bank()
                        k.op("pe", lambda e, d=d, hp=hp: e.matmul(bk[:, :W], lhsT=w2[d * 64:(d + 1) * 64, hp * 128:(hp + 1) * 128], rhs=tw[d * 64:(d + 1) * 64, :W],
                                                                 start=True, stop=True), reads=[bw2, btw], writes=[bb])
                        k.op("act", lambda e, d=d, hp=hp: e.activation(out=sg[:, :W], in_=bk[:, :W], func=AF.Sigmoid, bias=pp[:, W0 + d * 2 + hp:W0 + d * 2 + hp + 1]),
                             reads=[bb, bpp], writes=[bsg])
                        bk2, bb2 = k.bank()
                        k.op("pe", lambda e, d=d, hp=hp: e.matmul(bk2[:, :W], lhsT=a2[d * 64:(d + 1) * 64, hp * 128:(hp + 1) * 128], rhs=rm[d * 64:(d + 1) * 64, 8, :W],
                                                                 start=True, stop=True), reads=[bw2, brmt], writes=[bb2])
                        k.op("act", lambda e, d=d, hp=hp: e.activation(out=aa[:, :W], in_=bk2[:, :W], func=AF.Sigmoid, bias=pp[:, A0 + d * 2 + hp:A0 + d * 2 + hp + 1]),
                             reads=[bb2, bpp], writes=[baa])
                        k.op("dve", lambda e: e.tensor_tensor_scan(out=cs[:, :W], data0=rsm[:, :W], data1=sg[:, :W], initial=0.0, op0=ALU.mult, op1=ALU.add),
                             reads=[brsm, bsg], writes=[bcs])
                        cs3 = cs[:, :W].rearrange("p (c t) -> p c t", t=64)
                        if d == 0:
                            k.op("dve", lambda e: e.tensor_tensor(out=Ex[:, :W], in0=cs[:, :W], in1=sg[:, :W], op=ALU.subtract), reads=[bcs, bsg], writes=[bEx])
                            Eiv = cs; bEiv = bcs
                        else:
                            Ex3 = Ex[:, :W].rearrange("p (c t) -> p c t", t=64)
                            k.op("dve", lambda e, cs3=cs3, Ex3=Ex3: e.tensor_tensor(out=Ex3, in0=cs3[:, :, 63:64].broadcast_to([128, nck, 64]), in1=cs3, op=ALU.subtract),
                                 reads=[bcs], writes=[bEx])
                            k.op("dve", lambda e: e.tensor_tensor(out=Ei[:, :W], in0=Ex[:, :W], in1=sg[:, :W], op=ALU.add), reads=[bEx, bsg], writes=[bEi])
                            Eiv = Ei; bEiv = bEi
                        k.op("act", lambda e, cs3=cs3: e.activation(out=gc[:, :nck], in_=cs3[:, :, 63], func=AF.Exp, scale=-C0), reads=[bcs], writes=[bgc])
                        k.op("act", lambda e, Eiv=Eiv: e.activation(out=gi[:, :W], in_=Eiv[:, :W], func=AF.Exp, scale=-C0), reads=[bEiv], writes=[bgi])
                        k.op("act", lambda e: e.activation(out=gx[:, :W], in_=Ex[:, :W], func=AF.Exp, scale=-C0), reads=[bEx], writes=[bgx])
                        k.op("act", lambda e, Eiv=Eiv: e.activation(out=gv[:, :W], in_=Eiv[:, :W], func=AF.Exp, scale=C0), reads=[bEiv], writes=[bgv])
                        s_ = so[it % 2]; bs_ = bso[it % 2]; it += 1
                        k.op("dve", lambda e, s_=s_, hp=hp: e.tensor_tensor(out=s_[:, 0, :W], in0=kk[:, hp, :W], in1=gx[:, :W], op=ALU.mult), reads=[bkk, bgx], writes=[bs_])
                        k.op("dve", lambda e, s_=s_, hp=hp: e.tensor_tensor(out=s_[:, 1, :W], in0=rm[:, hp, :W], in1=gi[:, :W], op=ALU.mult), reads=[brmt, bgi], adds=[bs_])
                        k.op("dve", lambda e, hp=hp: e.tensor_scalar(out=kd[:, :W], in0=aa[:, :W], scalar1=pp[:, KA + hp:KA + hp + 1], scalar2=omka[:, hp:hp + 1],
                                                                    op0=ALU.mult, op1=ALU.add), reads=[baa, bpp, bomka], writes=[bkd])
                        k.op("dve", lambda e, hp=hp: e.tensor_tensor(out=kd[:, :W], in0=kd[:, :W], in1=rm[:, 3 + hp, :W], op=ALU.mult), reads=[brmt], writes=[bkd])
                        k.op("dve", lambda e, s_=s_: e.tensor_tensor(out=s_[:, 2, :W], in0=kd[:, :W], in1=gv[:, :W], op=ALU.mult), reads=[bkd, bgv], adds=[bs_])
                        k.op("dve", lambda e, hp=hp: e.tensor_tensor(out=bt[:, :W], in0=aa[:, :W], in1=kk[:, hp, :W], op=ALU.mult), reads=[baa, bkk], writes=[bbt])
                        k.op("dve", lambda e, s_=s_: e.tensor_tensor(out=s_[:, 3, :W], in0=bt[:, :W], in1=gv[:, :W], op=ALU.mult), reads=[bbt, bgv], adds=[bs_])
                        if d == 0:
                            k.op("dve", lambda e, hp=hp: e.tensor_copy(out=kds[:, hp, :W], in_=kd[:, :W]), reads=[bkd], writes=[bkds])
                        else:
                            k.op("dve", lambda e, hp=hp: e.tensor_tensor(out=kds[:, hp, :W], in0=kds[:, hp, :W], in1=kd[:, :W], op=ALU.add), reads=[bkd], writes=[bkds])
                        k.op("act", lambda e, s_=s_, hp=hp: e.copy(out=s_[:, 4, :W], in_=rm[:, 5 + hp, :W]), reads=[brmt], adds=[bs_])
                        k.dma(scin.ap()[d * 2 + hp][:, :, t0:t0 + W], s_[:, :, :W], reads=[bs_], adds=[bscin])
                        k.dma(gcd.ap()[d * 2 + hp][:, t0 // 64:t0 // 64 + nck], gc[:, :nck], reads=[bgc], adds=[bgcd])
                    k.op("dve", lambda e, hp=hp: e.scalar_tensor_tensor(out=kds[:, hp, :W], in0=kds[:, hp, :W], scalar=pp[:, RK + hp:RK + hp + 1], in1=rm[:, hp, :W],
                                                                       op0=ALU.mult, op1=ALU.mult), reads=[brmt, bpp], writes=[bkds])
                k.dma(rksT.ap().rearrange("c p t -> p c t")[:, :, t0:t0 + W], kds[:, :, :W], reads=[bkds], adds=[brks])
            k.barrier()

        with ExitStack() as st:
          if i_phase_on(4):
            cm = k.sb("cm", [64, 4, 512], stack=st); bcm = Buf("cm")
            k.dma(cm[:], cmd.ap(), writes=[bcm])
            idb = k.sb("idb", [64, 64], BF16, stack=st); bidb = Buf("idb")
            k.op("dve", lambda e: e.tensor_copy(out=idb[:], in_=ident[0:64, 0:64]), reads=[bc], writes=[bidb])
            inT = [k.sb(f"inT{d}", [64, 4, 5, 512], BF16, stack=st) for d in range(2)]; binT = [Buf("inT0"), Buf("inT1")]
            gct = [k.sb(f"gct{d}", [64, 4, 8], stack=st) for d in range(2)]; bgct = [Buf("g0"), Buf("g1")]
            yst = [k.sb(f"yst{d}", [64, 4, 512], stack=st) for d in range(2)]; byst = [Buf("y0"), Buf("y1")]
            Hs = k.sb("Hs", [64, 8, 64], stack=st); bHs = Buf("Hs")
            Hb = k.sb("Hb", [64, 8, 64], BF16, stack=st); bHb = Buf("Hb")
            k.op("dve", lambda e: e.memset(Hs[:], 0.0), writes=[bHs])
            k.op("dve", lambda e: e.memset(Hb[:], 0.0), writes=[bHb])

            def b16(name):
                return k.sb(name, [64, 512], BF16, stack=st), Buf(name)
            Ktm, bKtm = b16("Ktm"); NBtm, bNB = b16("NBtm"); Vtm, bVtm = b16("Vtm")
            Pm = [None, None]; bPm = [None, None]; Nm = [None, None]; bNm = [None, None]; Ab = [None, None]; bAb = [None, None]
            for i in range(2):
                Pm[i], bPm[i] = b16(f"Pm{i}"); Nm[i], bNm[i] = b16(f"Nm{i}"); Ab[i], bAb[i] = b16(f"Ab{i}")
            A32 = [k.sb(f"A32{i}", [64, 512], stack=st) for i in range(2)]; bA32 = [Buf("A32a"), Buf("A32b")]
            AKT, bAKT = b16("AKT"); RKT, bRKT = b16("RKT"); RBT, bRBT = b16("RBT"); RHSs, bRHS = b16("RHSs"); Us, bUs = b16("Us")
            Ht = k.sb("Ht", [64, 8, 64], stack=st); bHt = Buf("Ht")
            Fc = [(0, ci) for ci in range(4)] + [(ti, ci) for ti in range(1, 9) for ci in range(8)]
            Bc = [(0, ci) for ci in range(3, -1, -1)] + [(ti, ci) for ti in range(8, 0, -1) for ci in range(7, -1, -1)]
            cur = [None, None]

            def flush(d):
                ti = cur[d]
                t0, W, _ = TILES[ti]
                yv = yT.ap()[d].rearrange("(h v) t -> v h t", v=64)
                k.dma(yv[:, :, t0:t0 + W], yst[d][:, :, :W], reads=[byst[d]], adds=[byT])

            def load(d, ti):
                t0, W, _ = TILES[ti]
                for h in range(4):
                    hp = h // 2; e_ = h % 2
                    k.dma(inT[d][:, h, :, :W], scin.ap()[d * 2 + hp][e_ * 64:(e_ + 1) * 64, :, t0:t0 + W], reads=[bscin],
                          adds=[binT[d]], q="sp" if h % 2 == 0 else "pool")
                    k.dma(gct[d][:, h, :W // 64], gcd.ap()[d * 2 + hp][e_ * 64:(e_ + 1) * 64, t0 // 64:t0 // 64 + W // 64], reads=[bgcd],
                          writes=[bgct[d]] if h == 0 else (), adds=[bgct[d]] if h else ())

            def opd(d, h, kind, ci):
                return inT[d][:, h, kind, ci * 64:(ci + 1) * 64]

            def blk(t, q):
                return t[:, q * 64:(q + 1) * 64]

            import os as _os
            R3N = int(_os.environ.get('R3N', str(NCHK)))
            for step in range(R3N):
                tcs = [Fc[step], Bc[step]]
                for d in range(2):
                    if cur[d] != tcs[d][0]:
                        if cur[d] is not None:
                            flush(d)
                        load(d, tcs[d][0])
                        cur[d] = tcs[d][0]
                cis = [tcs[0][1], tcs[1][1]]
                for kind, dst, bdst, neg in ((2, Ktm, bKtm, False), (3, NBtm, bNB, True), (4, Vtm, bVtm, False)):
                    bk, bb = k.bank()
                    for d in range(2):
                        for h in range(4):
                            q = d * 4 + h
                            k.op("pe", lambda e, d=d, h=h, q=q, kind=kind, bk=bk: e.matmul(blk(bk[0:64, :], q), lhsT=opd(d, h, kind, cis[d]), rhs=idb[:], start=True, stop=True),
                                 reads=[binT[d], bidb], writes=[bb])
                    if neg:
                        k.op("act", lambda e, dst=dst, bk=bk: e.activation(out=dst[:], in_=bk[0:64, :], func=AF.Copy, scale=-1.0), reads=[bb], writes=[bdst])
                    elif kind == 2:
                        k.op("dve", lambda e, dst=dst, bk=bk: e.tensor_copy(out=dst[:], in_=bk[0:64, :]), reads=[bb], writes=[bdst])
                    else:
                        k.op("act", lambda e, dst=dst, bk=bk: e.copy(out=dst[:], in_=bk[0:64, :]), reads=[bb], writes=[bdst])

                def prod(la, ra, dst, bdst, mki, neg):
                    bk, bb = k.bank()
                    for d in range(2):
                        for h in range(4):
                            q = d * 4 + h
                            k.op("pe", lambda e, d=d, h=h, q=q: e.matmul(blk(bk[0:64, :], q), lhsT=opd(d, h, la, cis[d]), rhs=opd(d, h, ra, cis[d]),
                                                                        start=True, stop=True), reads=[binT[d]], writes=[bb])
                    k.op("dve", lambda e: e.scalar_tensor_tensor(out=dst[:], in0=bk[0:64, :], scalar=(-1.0 if neg else 1.0), in1=cm[:, mki, :], op0=ALU.mult, op1=ALU.mult),
                         reads=[bb, bcm], writes=[bdst])
                prod(3, 0, Pm[0], bPm[0], 0, True)
                prod(0, 3, Nm[0], bNm[0], 1, True)
                prod(2, 0, AKT, bAKT, 0, False)
                prod(2, 1, RKT, bRKT, 2, False)
                prod(3, 1, RBT, bRBT, 2, True)
                k.op("dve", lambda e: e.tensor_tensor(out=A32[0][:], in0=Pm[0][:], in1=cm[:, 3, :], op=ALU.add), reads=[bPm[0], bcm], writes=[bA32[0]])
                k.op("act", lambda e: e.copy(out=Ab[0][:], in_=A32[0][:]), reads=[bA32[0]], writes=[bAb[0]])
                pc = 0; ac = 0
                for j in range(1, 6):
                    nn = 1 - pc
                    bk, bb = k.bank()
                    for q in range(8):
                        k.op("pe", lambda e, q=q, pc=pc: e.matmul(blk(bk[0:64, :], q), lhsT=blk(Pm[pc], q), rhs=blk(Nm[pc], q), start=True, stop=True),
                             reads=[bPm[pc], bNm[pc]], writes=[bb])
                    if j < 5:
                        bk2, bb2 = k.bank()
                        for q in range(8):
                            k.op("pe", lambda e, q=q, pc=pc: e.matmul(blk(bk2[0:64, :], q), lhsT=blk(Nm[pc], q), rhs=blk(Pm[pc], q), start=True, stop=True),
                                 reads=[bPm[pc], bNm[pc]], writes=[bb2])
                    k.op("act", lambda e, nn=nn, bk=bk: e.copy(out=Nm[nn][:], in_=bk[0:64, :]), reads=[bb], writes=[bNm[nn]])
                    if j < 5:
                        k.op("dve", lambda e, nn=nn, bk2=bk2: e.tensor_copy(out=Pm[nn][:], in_=bk2[0:64, :]), reads=[bb2], writes=[bPm[nn]])
                    bk3, bb3 = k.bank()
                    for q in range(8):
                        k.op("pe", lambda e, q=q, nn=nn, ac=ac: e.matmul(blk(bk3[0:64, :], q), lhsT=blk(Nm[nn], q), rhs=blk(Ab[ac], q), start=True, stop=True),
                             reads=[bNm[nn], bAb[ac]], writes=[bb3])
                    k.op("dve", lambda e, ac=ac, bk3=bk3: e.tensor_tensor(out=A32[1 - ac][:], in0=bk3[0:64, :], in1=A32[ac][:], op=ALU.add), reads=[bb3, bA32[ac]], writes=[bA32[1 - ac]])
                    k.op("act", lambda e, ac=ac: e.copy(out=Ab[1 - ac][:], in_=A32[1 - ac][:]), reads=[bA32[1 - ac]], writes=[bAb[1 - ac]])
                    pc = nn; ac = 1 - ac
                A = Ab[ac]; bA = bAb[ac]
                bk, bb = k.bank()
                for d in range(2):
                    for h in range(4):
                        q = d * 4 + h
                        k.op("pe", lambda e, d=d, h=h, q=q: e.matmul(blk(bk[0:64, :], q), lhsT=opd(d, h, 0, cis[d]), rhs=Hb[:, q, :], start=True, stop=False),
                             reads=[binT[d], bHb], writes=[bb])
                        k.op("pe", lambda e, q=q: e.matmul(blk(bk[0:64, :], q), lhsT=blk(AKT, q), rhs=blk(Vtm, q), start=False, stop=True),
                             reads=[bAKT, bVtm], writes=[bb])
                k.op("act", lambda e, bk=bk: e.copy(out=RHSs[:], in_=bk[0:64, :]), reads=[bb], writes=[bRHS])
                bk, bb = k.bank()
                for q in range(8):
                    k.op("pe", lambda e, q=q: e.matmul(blk(bk[0:64, :], q), lhsT=blk(A, q), rhs=blk(RHSs, q), start=True, stop=True), reads=[bA, bRHS], writes=[bb])
                k.op("dve", lambda e, bk=bk: e.tensor_copy(out=Us[:], in_=bk[0:64, :]), reads=[bb], writes=[bUs])
                for d in range(2):
                    bk, bb = k.bank()
                    for h in range(4):
                        q = d * 4 + h
                        k.op("pe", lambda e, d=d, h=h, q=q, bk=bk: e.matmul(blk(bk[0:64, :], h), lhsT=Hb[:, q, :], rhs=opd(d, h, 1, cis[d]), start=True, stop=False),
                             reads=[binT[d], bHb], writes=[bb])
                        k.op("pe", lambda e, q=q, h=h, bk=bk: e.matmul(blk(bk[0:64, :], h), lhsT=blk(Vtm, q), rhs=blk(RKT, q), start=False, stop=False), reads=[bVtm, bRKT], writes=[bb])
                        k.op("pe", lambda e, q=q, h=h, bk=bk: e.matmul(blk(bk[0:64, :], h), lhsT=blk(Us, q), rhs=blk(RBT, q), start=False, stop=True), reads=[bUs, bRBT], writes=[bb])
                    if d == 0:
                        k.op("act", lambda e, d=d, bk=bk: e.copy(out=yst[d][:, 0, cis[d] * 64:(cis[d] + 1) * 64], in_=bk[0:64, 0:64]), reads=[bb], adds=[byst[d]])
                        for h in range(1, 4):
                            k.op("act", lambda e, d=d, h=h, bk=bk: e.copy(out=yst[d][:, h, cis[d] * 64:(cis[d] + 1) * 64], in_=bk[0:64, h * 64:(h + 1) * 64]), reads=[bb], adds=[byst[d]])
                    else:
                        for h in range(4):
                            k.op("dve", lambda e, d=d, h=h, bk=bk: e.tensor_copy(out=yst[d][:, h, cis[d] * 64:(cis[d] + 1) * 64], in_=bk[0:64, h * 64:(h + 1) * 64]), reads=[bb], adds=[byst[d]])
                bk, bb = k.bank()
                for q in range(8):
                    k.op("pe", lambda e, q=q: e.matmul(blk(bk[0:64, :], q), lhsT=blk(Ktm, q), rhs=blk(Vtm, q), start=True, stop=False), reads=[bKtm, bVtm], writes=[bb])
                    k.op("pe", lambda e, q=q: e.matmul(blk(bk[0:64, :], q), lhsT=blk(NBtm, q), rhs=blk(Us, q), start=False, stop=True), reads=[bNB, bUs], writes=[bb])
                k.op("dve", lambda e, bk=bk: e.tensor_tensor(out=Ht[:].rearrange("p a b -> p (a b)"), in0=bk[0:64, :], in1=Hs[:].rearrange("p a b -> p (a b)"), op=ALU.add),
                     reads=[bb, bHs], writes=[bHt])
                for d in range(2):
                    for h in range(4):
                        q = d * 4 + h
                        k.op("dve", lambda e, d=d, h=h, q=q: e.tensor_scalar(out=Hs[:, q, :], in0=Ht[:, q, :], scalar1=gct[d][:, h, cis[d]:cis[d] + 1], scalar2=None, op0=ALU.mult),
                             reads=[bHt, bgct[d]], writes=[bHs] if q == 0 else (), adds=[bHs] if q else ())
                k.op("act", lambda e: e.copy(out=Hb[:].rearrange("p a b -> p (a b)"), in_=Hs[:].rearrange("p a b -> p (a b)")), reads=[bHs], writes=[bHb])
            flush(0); flush(1)
            k.barrier()

        with ExitStack() as st:
          if i_phase_on(5):
            g2 = k.sb("g2s", [128, 256], stack=st); bg2 = Buf("g2")
            k.dma(g2[:], g2d.ap()[l], writes=[bg2])
            yf = k.sb("yf", [128, 2, 512], stack=st); byf = Buf("yf")
            yb = k.sb("yb", [128, 2, 512], stack=st); byb = Buf("yb")
            rmo = k.sb("rmo", [128, 7, 512], stack=st); brmo = Buf("rmo")
            rk_ = k.sb("rks", [128, 2, 512], stack=st); brk_ = Buf("rks")
            sq = k.sb("sq4", [128, 512], stack=st); bsq = Buf("sq4")
            mu = k.sb("mu4", [128, 512], stack=st); bmu = Buf("mu4")
            va = k.sb("va4", [128, 512], stack=st); bva = Buf("va4")
            sgl = k.sb("sgl", [128, 512], stack=st); bsgl = Buf("sgl")
            orw = k.sb("orw", [128, 2, 512], stack=st); borw = Buf("orw")
            bn = k.sb("bn4", [128, 512], stack=st); bbn = Buf("bn4")
            for (t0, W, who) in TILES:
                if last and who == 1:
                    continue
                yv = yT.ap().rearrange("d (hp p) t -> d p hp t", p=128)
                k.dma(yf[:, :, :W], yv[0][:, :, t0:t0 + W], reads=[byT], writes=[byf])
                k.dma(yb[:, :, :W], yv[1][:, :, t0:t0 + W], reads=[byT], writes=[byb])
                k.dma(rmo[:, :, :W], rmT.ap().rearrange("c p t -> p c t")[:, 0:7, t0:t0 + W], reads=[brm], writes=[brmo])
                k.dma(rk_[:, :, :W], rksT.ap().rearrange("c p t -> p c t")[:, :, t0:t0 + W], reads=[brks], writes=[brk_])
                k.op("dve", lambda e: e.tensor_tensor(out=yf[:, :, :W], in0=yf[:, :, :W], in1=yb[:, :, :W], op=ALU.add), reads=[byb], writes=[byf])
                k.op("act", lambda e: e.activation(out=sgl[:, :W], in_=rmo[:, 2, :W], func=AF.Sigmoid), reads=[brmo], writes=[bsgl])
                for hp in range(2):
                    bk1, bb1 = k.bank()
                    k.op("pe", lambda e, hp=hp: e.matmul(bk1[:, :W], lhsT=bones, rhs=yf[:, hp, :W], start=True, stop=True), reads=[byf, bc], writes=[bb1])
                    k.op("act", lambda e, hp=hp: e.activation(out=sq[:, :W], in_=yf[:, hp, :W], func=AF.Square), reads=[byf], writes=[bsq])
                    bk2, bb2 = k.bank()
                    k.op("pe", lambda e: e.matmul(bk2[:, :W], lhsT=bones, rhs=sq[:, :W], start=True, stop=True), reads=[bsq, bc], writes=[bb2])
                    k.op("dve", lambda e: e.tensor_scalar(out=mu[:, :W], in0=bk1[:, :W], scalar1=1.0 / 64, scalar2=None, op0=ALU.mult), reads=[bb1], writes=[bmu])
                    k.op("dve", lambda e: e.tensor_tensor(out=va[:, :W], in0=mu[:, :W], in1=mu[:, :W], op=ALU.mult), reads=[bmu], writes=[bva])
                    k.op("dve", lambda e: e.scalar_tensor_tensor(out=va[:, :W], in0=bk2[:, :W], scalar=1.0 / 64, in1=va[:, :W], op0=ALU.mult, op1=ALU.subtract),
                         reads=[bb2], writes=[bva])
                    k.op("dve", lambda e: e.tensor_scalar(out=va[:, :W], in0=va[:, :W], scalar1=64e-5, scalar2=None, op0=ALU.add), writes=[bva])
                    k.op("act", lambda e: e.activation(out=va[:, :W], in_=va[:, :W], func=AF.Sqrt), writes=[bva])
                    k.op("dve", lambda e: e.reciprocal(out=va[:, :W], in_=va[:, :W]), writes=[bva])
                    k.op("dve", lambda e, hp=hp: e.tensor_tensor(out=yf[:, hp, :W], in0=yf[:, hp, :W], in1=mu[:, :W], op=ALU.subtract), reads=[bmu], writes=[byf])
                    k.op("dve", lambda e, hp=hp: e.tensor_tensor(out=yf[:, hp, :W], in0=yf[:, hp, :W], in1=va[:, :W], op=ALU.mult), reads=[bva], writes=[byf])
                    k.op("dve", lambda e, hp=hp: e.tensor_scalar(out=yf[:, hp, :W], in0=yf[:, hp, :W], scalar1=pp[:, LXG + hp:LXG + hp + 1], scalar2=pp[:, LXB + hp:LXB + hp + 1],
                                                                op0=ALU.mult, op1=ALU.add), reads=[bpp], writes=[byf])
                    bk3, bb3 = k.bank()
                    k.op("pe", lambda e, hp=hp: e.matmul(bk3[:, :W], lhsT=bones, rhs=rk_[:, hp, :W], start=True, stop=True), reads=[brk_, bc], writes=[bb3])
                    k.op("dve", lambda e, hp=hp: e.tensor_tensor(out=bn[:, :W], in0=bk3[:, :W], in1=rmo[:, 5 + hp, :W], op=ALU.mult), reads=[bb3, brmo], writes=[bbn])
                    k.op("dve", lambda e, hp=hp: e.tensor_tensor(out=bn[:, :W], in0=bn[:, :W], in1=yf[:, hp, :W], op=ALU.add), reads=[byf], writes=[bbn])
                    bk4, bb4 = k.bank()
                    k.op("pe", lambda e, hp=hp: e.matmul(bk4[:, :W], lhsT=g2[:, hp * 128:(hp + 1) * 128], rhs=sgl[:, :W], start=True, stop=True), reads=[bg2, bsgl], writes=[bb4])
                    k.op("dve", lambda e, hp=hp: e.tensor_tensor(out=orw[:, hp, :W], in0=bk4[:, :W], in1=bn[:, :W], op=ALU.mult), reads=[bb4, bbn], writes=[borw])
                k.dma(mixT.ap().rearrange("c p t -> p c t")[:, 6:8, t0:t0 + W], orw[:, :, :W], reads=[borw], adds=[bmix])
            k.barrier()

        with ExitStack() as st:
          if i_phase_on(6):
            wsT = k.sb("wsT", [128, 4, 128], stack=st); bws = Buf("wsT")
            k.dma(wsT[:], sgwT.ap()[l], writes=[bws])
            bsb = k.sb("bsb", [128, 512], stack=st); bbsb = Buf("bsb")
            k.dma(bsb[:], sgb.ap()[l].partition_broadcast(128), writes=[bbsb])
            uv = k.sb("uv", [128, 4, 512], stack=st); buv = Buf("uv")
            sq = k.sb("sqg", [128, 2, 512], stack=st); bsq = Buf("sqg")
            mn = k.sb("mng", [128, 512], stack=st); bmn = Buf("mng")
            va = k.sb("vag", [128, 512], stack=st); bva = Buf("vag")
            vtm = k.sb("vtmg", [128, 256], stack=st); bvtm = Buf("vtmg")
            tmp = k.sb("tmpg", [128, 128], stack=st); btmp = Buf("tmpg")
            osg = k.sb("osg", [128, 2, 512], stack=st); bosg = Buf("osg")
            for (t0, W, who) in TILES:
                if last and who == 1:
                    continue
                k.dma(uv[:, :, :W], pxT.ap().rearrange("c p t -> p c t")[:, 0:4, t0:t0 + W], reads=[bpx], writes=[buv])
                k.op("act", lambda e: e.activation(out=uv[:, :, :W], in_=uv[:, :, :W], func=AF.Gelu), writes=[buv])
                k.op("act", lambda e: e.activation(out=sq[:, :, :W], in_=uv[:, 2:4, :W], func=AF.Square), reads=[buv], writes=[bsq])
                bk1, bb1 = k.bank(); bk2, bb2 = k.bank()
                for c in range(2):
                    k.op("pe", lambda e, c=c: e.matmul(bk1[:, :W], lhsT=ones[:], rhs=uv[:, 2 + c, :W], start=(c == 0), stop=(c == 1)), reads=[buv, bo], writes=[bb1])
                for c in range(2):
                    k.op("pe", lambda e, c=c: e.matmul(bk2[:, :W], lhsT=ones[:], rhs=sq[:, c, :W], start=(c == 0), stop=(c == 1)), reads=[bsq, bo], writes=[bb2])
                k.op("dve", lambda e: e.tensor_scalar(out=mn[:, :W], in0=bk1[:, :W], scalar1=1.0 / 256, scalar2=None, op0=ALU.mult), reads=[bb1], writes=[bmn])
                k.op("dve", lambda e: e.tensor_tensor(out=va[:, :W], in0=mn[:, :W], in1=mn[:, :W], op=ALU.mult), reads=[bmn], writes=[bva])
                k.op("dve", lambda e: e.scalar_tensor_tensor(out=va[:, :W], in0=bk2[:, :W], scalar=1.0 / 256, in1=va[:, :W], op0=ALU.mult, op1=ALU.subtract), reads=[bb2], writes=[bva])
                k.op("dve", lambda e: e.tensor_scalar(out=va[:, :W], in0=va[:, :W], scalar1=1e-5, scalar2=None, op0=ALU.add), writes=[bva])
                k.op("act", lambda e: e.activation(out=va[:, :W], in_=va[:, :W], func=AF.Sqrt), writes=[bva])
                k.op("dve", lambda e: e.reciprocal(out=va[:, :W], in_=va[:, :W]), writes=[bva])
                for c in range(2):
                    k.op("dve", lambda e, c=c: e.tensor_tensor(out=uv[:, 2 + c, :W], in0=uv[:, 2 + c, :W], in1=mn[:, :W], op=ALU.subtract), reads=[bmn], writes=[buv])
                    k.op("dve", lambda e, c=c: e.tensor_tensor(out=uv[:, 2 + c, :W], in0=uv[:, 2 + c, :W], in1=va[:, :W], op=ALU.mult), reads=[bva], writes=[buv])
                    k.op("dve", lambda e, c=c: e.tensor_scalar(out=uv[:, 2 + c, :W], in0=uv[:, 2 + c, :W], scalar1=pp[:, SLG + c:SLG + c + 1], scalar2=pp[:, SLB + c:SLB + c + 1],
                                                              op0=ALU.mult, op1=ALU.add), reads=[bpp], writes=[buv])
                for nb in range(W // 128):
                    cols = slice(nb * 128, (nb + 1) * 128)
                    bkT, bbT = k.bank()
                    for c in range(2):
                        k.op("pe", lambda e, c=c, cols=cols: e.transpose(bkT[:, c * 128:(c + 1) * 128], uv[:, 2 + c, cols], ident), reads=[buv, bc], writes=[bbT])
                    k.op("act", lambda e: e.copy(out=vtm[:], in_=bkT[:, 0:256]), reads=[bbT], writes=[bvtm])
                    bkM, bbM = k.bank()
                    for g in range(4):
                        k.op("pe", lambda e, g=g: e.matmul(bkM[:, g * 128:(g + 1) * 128], lhsT=vtm[:, (g // 2) * 128:(g // 2 + 1) * 128], rhs=wsT[:, g, :], start=True, stop=True),
                             reads=[bvtm, bws], writes=[bbM])
                    for g in range(4):
                        ps_ = slice((g % 2) * 64, (g % 2 + 1) * 64)
                        k.op("dve", lambda e, g=g, ps_=ps_: e.tensor_tensor(out=tmp[ps_, :], in0=bkM[ps_, g * 128:(g + 1) * 128], in1=bsb[ps_, g * 128:(g + 1) * 128], op=ALU.add),
                             reads=[bbM, bbsb], writes=[btmp])
                        k.op("dve", lambda e, g=g, ps_=ps_, cols=cols: e.tensor_tensor(out=osg[ps_, g // 2, cols], in0=tmp[ps_, :], in1=uv[ps_, g // 2, cols], op=ALU.mult),
                             reads=[btmp, buv], writes=[bosg])
                k.dma(mixT.ap().rearrange("c p t -> p c t")[:, 0:2, t0:t0 + W], osg[:, :, :W], reads=[bosg], adds=[bmix])
            k.barrier()

        with ExitStack() as st:
          if i_phase_on(7):
            cosT = k.sb("cosT", [128, T], stack=st); sinT = k.sb("sinT", [128, T], stack=st); btab = Buf("tab")
            k.dma(cosT[:], cosd.ap(), adds=[btab]); k.dma(sinT[:], sind.ap(), adds=[btab], q="pool")
            am = k.sb("am", [128, 2, 512], stack=st); bam = Buf("am")
            k.dma(am[:], amd.ap(), writes=[bam])
            esb = k.sb("esb", [64, 8], stack=st); besb = Buf("esb")
            k.dma(esb[:], sink.ap()[l].partition_broadcast(64), writes=[besb])
            k.op("act", lambda e: e.activation(out=esb[:], in_=esb[:], func=AF.Exp), writes=[besb])
            KT2 = k.sb("KT2", [128, 2, TT], BF16, stack=st); bKT = Buf("KT2")
            Vt = k.sb("Vt", [128, NTOKB, 128], BF16, stack=st); bVt = Buf("Vt")
            onesb = k.sb("onesb", [128, 64], BF16, stack=st); bonb = Buf("onesb")
            k.op("dve", lambda e: e.memset(onesb[:], 1.0), writes=[bonb])
            t2 = k.sb("t2r", [128, 512], stack=st); bt2 = Buf("t2r")
            kraw = k.sb("kraw", [128, TT], stack=st); bkr = Buf("kraw")
            t1 = k.sb("t1", [128, 512], stack=st); bt1 = Buf("t1")
            k.dma(kraw[:], pxT.ap()[11], reads=[bpx], writes=[bkr])

            def rope(dst, src, bsrc, tl, W, wr_reads, bdst):
                bk, bb = k.bank()
                k.op("pe", lambda e: e.matmul(bk[:, :W], lhsT=Rm, rhs=src, start=True, stop=True), reads=[bsrc, bc], writes=[bb])
                k.op("dve", lambda e: e.tensor_tensor(out=t1[:, :W], in0=bk[:, :W], in1=sinT[:, tl:tl + W], op=ALU.mult), reads=[bb, btab], writes=[bt1])
                k.op("dve", lambda e: e.tensor_tensor(out=t2[:, :W], in0=src, in1=cosT[:, tl:tl + W], op=ALU.mult), reads=[bsrc, btab], writes=[bt2])
                k.op("dve", lambda e: e.tensor_tensor(out=dst, in0=t2[:, :W], in1=t1[:, :W], op=ALU.add), reads=[bt1, bt2], writes=[bdst])

            for i in range(8):
                sl = slice(CT + i * 512, CT + (i + 1) * 512)
                rope(kraw[:, sl], kraw[:, sl], bkr, i * 512, 512, [], bkr)
            for hk in range(2):
                ps_ = slice(hk * 64, (hk + 1) * 64)
                for e_ in range(2):
                    po = slice(e_ * 64, (e_ + 1) * 64)
                    k.op("dve" if e_ == 0 else "act", (lambda e, hk=hk, ps_=ps_, po=po: e.tensor_copy(out=KT2[po, hk, :], in_=kraw[ps_, :])) if e_ == 0 else (lambda e, hk=hk, ps_=ps_, po=po: e.copy(out=KT2[po, hk, :], in_=kraw[ps_, :])), reads=[bkr], adds=[bKT])
            vraw = kraw
            k.dma(vraw[:], pxT.ap()[12], reads=[bpx], writes=[bkr])
            for b4 in range(0, NTOKB, 4):
                nb_ = min(4, NTOKB - b4)
                bk, bb = k.bank()
                for i in range(nb_):
                    k.op("pe", lambda e, i=i, b4=b4: e.transpose(bk[:, i * 128:(i + 1) * 128], vraw[:, (b4 + i) * 128:(b4 + i + 1) * 128], ident), reads=[bkr, bc], writes=[bb])
                evac(Vt[:, b4:b4 + nb_, :].rearrange("p a b -> p (a b)"), bk[:, :nb_ * 128], [bb], adds=[bVt])
            qraw = k.sb("qraw", [128, 4, 512], stack=st); bq = Buf("qraw")
            qr = k.sb("qr", [128, 4, 512], BF16, stack=st); bqr = Buf("qr")
            PT = [k.sb(f"PT{i}", [128, 512], BF16, stack=st) for i in range(2)]; bPT = [Buf("PT0"), Buf("PT1")]
            rc = k.sb("rc", [64, 512], stack=st); brc = Buf("rc")
            obt = k.sb("obt", [64, 2, 4, 512], stack=st); bobt = Buf("obt")
            pti = 0; sbi = [0]
            for (t0, W, who) in TILES:
                if last and who == 1:
                    continue
                k.dma(qraw[:, :, :W], pxT.ap().rearrange("c p t -> p c t")[:, 4:8, t0:t0 + W], reads=[bpx], writes=[bq])
                if who == 0:
                    for c in range(4):
                        rope(qr[:, c, :W], qraw[:, c, :W], bq, t0 - CT, W, [], bqr)
                    Q = qr; bQ = bqr
                else:
                    for c in range(4):
                        cast(qr[:, c, :W], qraw[:, c, :W], [bq], writes=[bqr] if c == 0 else (), adds=[bqr] if c else ())
                    Q = qr; bQ = bqr
                for nb in range(W // 128):
                    n = t0 // 128 + nb
                    cols = slice(nb * 128, (nb + 1) * 128)
                    if who == 1:
                        kbs = [(0, None), (1, None)]
                    else:
                        kbs = [(0, None), (1, None)]
                        if n - 1 >= 2:
                            kbs.append((n - 1, 0))
                        kbs.append((n, None))
                        if n + 1 < NTOKB:
                            kbs.append((n + 1, 1))
                    GO = [0, 2, 1, 3]
                    for hk in range(2):
                        bkO, bbO = k.banks[(hk % 2) * 2]; bkD, bbD = k.banks[(hk % 2) * 2 + 1]
                        for idx, (kb, mk) in enumerate(kbs):
                            bkS = [None, None]; bbS = [None, None]
                            bkS[0], bbS[0] = k.banks[4 + (sbi[0] % 2) * 2]; bkS[1], bbS[1] = k.banks[5 + (sbi[0] % 2) * 2]; sbi[0] += 1
                            for pos in range(4):
                                g = GO[pos]; h = hk * 4 + g; e_ = h % 2; chn = h // 2
                                assert e_ == pos // 2
                                k.op("pe", lambda e, pos=pos, e_=e_, chn=chn, kb=kb, hk=hk: e.matmul(bkS[e_][:, (pos % 2) * 128:(pos % 2 + 1) * 128],
                                                                                                    lhsT=KT2[e_ * 64:(e_ + 1) * 64, hk, kb * 128:(kb + 1) * 128],
                                                                                                    rhs=Q[e_ * 64:(e_ + 1) * 64, chn, cols], start=True, stop=True),
                                     reads=[bKT, bQ], writes=[bbS[e_]])
                            P_ = PT[pti % 2]; bP_ = bPT[pti % 2]; pti += 1
                            k.op("act", lambda e, P_=P_, b0=bkS[0]: e.activation(out=P_[:, 0:256], in_=b0[:, 0:256], func=AF.Exp, scale=0.125), reads=[bbS[0]], writes=[bP_])
                            k.op("act", lambda e, P_=P_, b1=bkS[1]: e.activation(out=P_[:, 256:512], in_=b1[:, 0:256], func=AF.Exp, scale=0.125), reads=[bbS[1]], adds=[bP_])
                            if mk is not None:
                                k.op("dve", lambda e, P_=P_, mk=mk: e.tensor_tensor(out=P_[:], in0=P_[:], in1=am[:, mk, :], op=ALU.mult), reads=[bam], writes=[bP_])
                            k.op("pe", lambda e, P_=P_, kb=kb, hk=hk, idx=idx: e.matmul(bkO[0:64, :], lhsT=Vt[:, kb, hk * 64:(hk + 1) * 64], rhs=P_[:], start=(idx == 0), stop=(idx == len(kbs) - 1)),
                                 reads=[bVt, bP_], writes=[bbO])
                            k.op("pe", lambda e, P_=P_, idx=idx: e.matmul(bkD[0:64, :], lhsT=onesb[:], rhs=P_[:], start=(idx == 0), stop=(idx == len(kbs) - 1)), reads=[bonb, bP_], writes=[bbD])
                        for pos in range(4):
                            h = hk * 4 + GO[pos]
                            k.op("dve", lambda e, pos=pos, h=h, bkD=bkD: e.tensor_scalar(out=rc[:, pos * 128:(pos + 1) * 128], in0=bkD[0:64, pos * 128:(pos + 1) * 128],
                                                                                       scalar1=esb[:, h:h + 1], scalar2=None, op0=ALU.add),
                                 reads=[bbD, besb], writes=[brc] if pos == 0 else (), adds=[brc] if pos else ())
                        k.op("dve", lambda e: e.reciprocal(out=rc[:], in_=rc[:]), writes=[brc])
                        for pos in range(4):
                            g = GO[pos]
                            k.op("dve", lambda e, pos=pos, g=g, bkO=bkO, hk=hk, cols=cols: e.tensor_tensor(out=obt[:, hk, g, cols], in0=bkO[0:64, pos * 128:(pos + 1) * 128],
                                                                                                      in1=rc[:, pos * 128:(pos + 1) * 128], op=ALU.mult), reads=[bbO, brc], adds=[bobt])
                for hk in range(2):
                    for g in range(4):
                        h = hk * 4 + g
                        k.dma(mixT.ap()[2 + h // 2][(h % 2) * 64:(h % 2 + 1) * 64, t0:t0 + W], obt[:, hk, g, :W], reads=[bobt], adds=[bmix], q="sp" if g % 2 == 0 else "pool")
                k.op("dve", lambda e: e.memset(rc[:, 0:1], 0.0), reads=[], writes=[bobt, brc])
            k.barrier()

        with ExitStack() as st:
          if i_phase_on(8):
            wo = k.sb("wo", [128, 8, 1024], BF16, stack=st); bwo = Buf("wo")
            wst = [k.sb(f"wst5{i}", [128, 1024], stack=st) for i in range(2)]; bwst = [Buf("wst0"), Buf("wst1")]
            wv = w_out.ap()[l].rearrange("(kc p) n -> p kc n", p=128)
            for kc in range(8):
                k.dma(wst[kc % 2][:], wv[:, kc, :], writes=[bwst[kc % 2]], q="sp" if kc % 2 == 0 else "pool")
                cast(wo[:, kc, :], wst[kc % 2][:], [bwst[kc % 2]], adds=[bwo])
            Mb = k.sb("Mb5", [128, 8, 512], BF16, stack=st); bMb = Buf("Mb5")
            X = k.sb("X5", [128, 8, 512], stack=st); bX = Buf("X5")
            M = k.sb("M5", [128, 8, 512], stack=st); bM = Buf("M5")
            Y = k.sb("Y5", [128, 8, 512], stack=st); bY = Buf("Y5")
            SQ = k.sb("SQ5", [128, 8, 512], stack=st); bSQ = Buf("SQ5")
            rs = k.sb("rs5", [128, 512], stack=st); brs = Buf("rs5")
            xv = xsrc(l).ap().rearrange("c p t -> p c t")
            xo = xs.ap().rearrange("c p t -> p c t")
            for (t0, W, who) in TILES:
                if last and who == 1:
                    continue
                k.dma(X[:, :, :W], xv[:, :, t0:t0 + W], reads=[bxs], writes=[bX])
                k.dma(M[:, :, :W], mixT.ap().rearrange("c p t -> p c t")[:, :, t0:t0 + W], reads=[bmix], writes=[bM], q="pool")
                for kc in range(8):
                    cast(Mb[:, kc, :W], M[:, kc, :W], [bM], writes=[bMb] if kc == 0 else (), adds=[bMb] if kc else ())
                for j in range(8):
                    bk, bb = k.bank()
                    for kc in range(8):
                        k.op("pe", lambda e, kc=kc, j=j: e.matmul(bk[:, :W], lhsT=wo[:, kc, j * 128:(j + 1) * 128], rhs=Mb[:, kc, :W], start=(kc == 0), stop=(kc == 7)),
                             reads=[bwo, bMb], writes=[bb])
                    evac(Y[:, j, :W], bk[:, :W], [bb], adds=[bY])
                rmsn(st, Y, bY, W, SQ, bSQ, rs, brs)
                resid_update(X, bX, Y, bY, rs, brs, W, 2, who)
                k.dma(xo[:, :, t0:t0 + W], X[:, :, :W], reads=[bX], adds=[bxs])
            k.barrier()

        with ExitStack() as st:
          if i_phase_on(9):
            wgb = k.sb("wgb", [128, 8, 5632], BF16, stack=st); bwgb = Buf("wgb")
            NST = 3
            wst = [k.sb(f"wst6{i}", [128, 1024], stack=st) for i in range(NST)]; bwst = [Buf(f"wst{i}") for i in range(NST)]
            wdj = [k.sb(f"wdj{i}", [128, 1024], BF16, stack=st) for i in range(2)]; bwdj = [Buf("wdj0"), Buf("wdj1")]
            gv_ = w_gu.ap()[l].rearrange("(kc p) n -> p kc n", p=128)
            dv_ = w_down.ap()[l].rearrange("(j p) n -> p j n", p=128)
            wi = 0
            for kc in range(8):
                for cb in range(6):
                    c0_ = cb * 1024; cw = min(1024, 5632 - c0_)
                    k.dma(wst[wi % NST][:, :cw], gv_[:, kc, c0_:c0_ + cw], writes=[bwst[wi % NST]], q="sp" if wi % 2 == 0 else "pool")
                    cast(wgb[:, kc, c0_:c0_ + cw], wst[wi % NST][:, :cw], [bwst[wi % NST]], adds=[bwgb])
                    wi += 1
            X = k.sb("X6", [128, 8, 512], stack=st); bX = Buf("X6")
            H = k.sb("H6", [128, 8, 512], BF16, stack=st); bH = Buf("H6")
            SQ = k.sb("SQ6", [128, 8, 512], stack=st); bSQ = Buf("SQ6")
            Y = k.sb("Y6", [128, 8, 512], stack=st); bY = Buf("Y6")
            rs = k.sb("rs6", [128, 512], stack=st); brs = Buf("rs6")
            actb = k.sb("actb", [128, 22, 512], BF16, stack=st); bact = Buf("actb")
            sl = [k.sb(f"sl6{i}", [128, 512], stack=st) for i in range(2)]; bsl = [Buf("sl0"), Buf("sl1")]
            xo = xs.ap().rearrange("c p t -> p c t")
            dj = 0
            for (t0, W, who) in TILES:
                if last and who == 1:
                    continue
                k.dma(X[:, :, :W], xo[:, :, t0:t0 + W], reads=[bxs], writes=[bX])
                rmsn(st, X, bX, W, SQ, bSQ, rs, brs)
                modulate(H, bH, X, bX, rs, brs, W, 3, 4, who)
                for j in range(22):
                    bkg, bbg = k.bank(); bku, bbu = k.bank()
                    for kc in range(8):
                        k.op("pe", lambda e, kc=kc, j=j: e.matmul(bkg[:, :W], lhsT=wgb[:, kc, j * 128:(j + 1) * 128], rhs=H[:, kc, :W], start=(kc == 0), stop=(kc == 7)),
                             reads=[bwgb, bH], writes=[bbg])
                    for kc in range(8):
                        k.op("pe", lambda e, kc=kc, j=j: e.matmul(bku[:, :W], lhsT=wgb[:, kc, 2816 + j * 128:2816 + (j + 1) * 128], rhs=H[:, kc, :W], start=(kc == 0), stop=(kc == 7)),
                             reads=[bwgb, bH], writes=[bbu])
                    s_ = sl[j % 2]; bs_ = bsl[j % 2]
                    k.op("act", lambda e, bkg=bkg, s_=s_: e.activation(out=s_[:, :W], in_=bkg[:, :W], func=AF.Silu), reads=[bbg], writes=[bs_])
                    k.op("dve", lambda e, j=j, bku=bku, s_=s_: e.tensor_tensor(out=actb[:, j, :W], in0=bku[:, :W], in1=s_[:, :W], op=ALU.mult), reads=[bbu, bs_],
                         writes=[bact] if j == 0 else (), adds=[bact] if j else ())
                for j in range(22):
                    w_ = wst[wi % NST]; bw_ = bwst[wi % NST]
                    k.dma(w_[:], dv_[:, j, :], writes=[bw_], q="sp" if wi % 2 == 0 else "pool")
                    wi += 1
                    d_ = wdj[dj % 2]; bd_ = bwdj[dj % 2]; dj += 1
                    cast(d_[:], w_[:], [bw_], writes=[bd_])
                    for jo in range(8):
                        bk, bb = k.banks[jo]
                        k.op("pe", lambda e, j=j, jo=jo, d_=d_, bk=bk: e.matmul(bk[:, :W], lhsT=d_[:, jo * 128:(jo + 1) * 128], rhs=actb[:, j, :W], start=(j == 0), stop=(j == 21)),
                             reads=[bd_, bact], writes=[bb])
                for jo in range(8):
                    bk, bb = k.banks[jo]
                    evac(Y[:, jo, :W], bk[:, :W], [bb], writes=[bY] if jo == 0 else (), adds=[bY] if jo else ())
                rmsn(st, Y, bY, W, SQ, bSQ, rs, brs)
                resid_update(X, bX, Y, bY, rs, brs, W, 5, who)
                if l == L - 1:
                    k.dma(outT.ap().rearrange("c p t -> p c t")[:, :, t0 - CT:t0 - CT + W], X[:, :, :W], reads=[bX], adds=[bout])
                else:
                    k.dma(xo[:, :, t0:t0 + W], X[:, :, :W], reads=[bX], adds=[bxs])
            k.barrier()
    k.barrier()
    return nc, k


def host_consts():
    ident = np.eye(128, dtype=np.float32)
    bones = np.zeros((128, 128), np.float32); bones[:64, :64] = 1; bones[64:, 64:] = 1
    Rm = np.zeros((128, 128), np.float32)
    for m in range(128):
        i = m % 32
        if i < 16:
            Rm[m + 16, m] = -1.0
        else:
            Rm[m - 16, m] = 1.0
    cc = np.concatenate([ident, bones, Rm], axis=1)
    smL = np.zeros((128, 9, 4), np.float32); smC = np.zeros((128, 9, 2), np.float32)
    for ci in range(9):
        for p in range(128):
            if ci < 3:
                ch = ci * 128 + p; qd = ch // 96; half = 0 if ch < 192 else 1
            else:
                ch = (ci - 3) * 128 + p; qd = ch // 192; half = 0 if ch < 384 else 1
            smL[p, ci, qd] = 1; smC[p, ci, half] = 1
    s = np.arange(64)[:, None]; t = np.arange(64)[None, :]
    mS = [(s < t), (s > t)]; mST = [(s > t), (s < t)]; mI = [(s <= t), (s >= t)]
    cm = np.zeros((64, 4, 512), np.float32)
    for d in range(2):
        for h in range(4):
            q = d * 4 + h
            cm[:, 0, q * 64:(q + 1) * 64] = mS[d]
            cm[:, 1, q * 64:(q + 1) * 64] = mST[d]
            cm[:, 2, q * 64:(q + 1) * 64] = mI[d]
            cm[:, 3, q * 64:(q + 1) * 64] = np.eye(64)
    j = np.arange(128)[:, None]; i = np.arange(128)[None, :]
    am = np.zeros((128, 2, 512), np.float32)
    for g in range(4):
        am[:, 0, g * 128:(g + 1) * 128] = (j >= i)
        am[:, 1, g * 128:(g + 1) * 128] = (j <= i)
    rs = np.ones((128, 512), np.float32); rs[:, ::64] = 0
    tt = np.arange(T)
    pos = np.stack([tt // 64, tt % 64], -1).astype(np.float32)
    inv = (10000.0 ** (-np.arange(16, dtype=np.float32) / 16)).astype(np.float32)
    ang = pos[:, :, None] * inv
    cosT = np.zeros((128, T), np.float32); sinT = np.zeros((128, T), np.float32)
    for p in range(128):
        dd = p % 64; ax = dd // 32; fr = dd % 16
        cosT[p] = np.cos(ang[:, ax, fr]); sinT[p] = np.sin(ang[:, ax, fr])
    return dict(cc=cc, smL=smL, smC=smC, cm=cm, am=am, rs=rs, cosT=cosT, sinT=sinT)


def cmaj(v, n):
    return np.ascontiguousarray(np.asarray(v, np.float32).reshape(n, 128).T)


def host_inputs(inp, b):
    f = lambda a: np.ascontiguousarray(np.asarray(a, np.float32))
    xall = np.concatenate([inp["ctx"][b], inp["x"][b]], axis=0)
    xin = np.ascontiguousarray(xall.T.reshape(8, 128, TT))
    cin = np.stack([cmaj(inp["c"][b], 8), cmaj(inp["c_ctx"], 8)], axis=-1)
    pp = np.zeros((L, 128, NP), np.float32)
    for l in range(L):
        for i in range(4):
            pp[l, :, NG + i * 8:NG + i * 8 + 8] = cmaj(inp["norm_g"][l, i], 8)
        pp[l, :, BM:BM + 48] = cmaj(inp["b_mod"][l], 48)
        pp[l, :, MU:MU + 9] = cmaj(inp["rw_mu"][l], 9)
        pp[l, :, KK:KK + 2] = cmaj(inp["rw_kk"][l], 2)
        pp[l, :, KA:KA + 2] = cmaj(inp["rw_ka"][l], 2)
        pp[l, :, RK:RK + 2] = cmaj(inp["rw_rk"][l].reshape(-1), 2)
        pp[l, :, LXG:LXG + 2] = cmaj(inp["rw_lnx_g"][l], 2)
        pp[l, :, LXB:LXB + 2] = cmaj(inp["rw_lnx_b"][l], 2)
        pp[l, :, W0:W0 + 4] = cmaj(inp["rw_w0"][l].reshape(-1), 4)
        pp[l, :, A0:A0 + 4] = cmaj(inp["rw_a0"][l].reshape(-1), 4)
        pp[l, :, SLG:SLG + 2] = cmaj(inp["sg_ln_g"][l], 2)
        pp[l, :, SLB:SLB + 2] = cmaj(inp["sg_ln_b"][l], 2)
    m = dict(xin=xin, cin=f(cin), w_mod=f(inp["w_mod"]), w_in=f(inp["w_in"]), w_out=f(inp["w_out"]), w_gu=f(inp["w_gu"]), w_down=f(inp["w_down"]),
             pp=pp, sgwT=f(np.transpose(inp["sg_w"], (0, 3, 1, 2))), sgb=f(inp["sg_b"]).reshape(L, 1, 512), sink=f(inp["attn_sink"]).reshape(L, 1, 8),
             w2=f(inp["rw_w2"]).reshape(L, 128, 256), a2=f(inp["rw_a2"]).reshape(L, 128, 256), g2=f(inp["rw_g2"]))
    m.update(host_consts())
    return m


REAL_CORES = [0, 1, 4, 5]


def kernel(**inp):
    inp = {k_: np.asarray(v) for k_, v in inp.items()}
    nc, _ = build()
    maps = [host_inputs(inp, b) for b in range(4)]
    zmap = {k_: np.zeros_like(v) for k_, v in maps[0].items()}
    in_maps = [zmap] * 8
    in_maps = list(in_maps)
    for b, c in enumerate(REAL_CORES):
        in_maps[c] = maps[b]
    res = run_bass_kernel_spmd(nc, in_maps, core_ids=list(range(8)))
    out = np.zeros((4, T, D), np.float32)
    for b, c in enumerate(REAL_CORES):
        o = res.results[c]["outT"]
        out[b] = o.reshape(D, T).T
    return out
```

```python
import numpy as np
import concourse.bass as bass
import concourse.mybir as mybir
from concourse.bass_utils import run_bass_kernel_spmd
from contextlib import ExitStack

F32 = mybir.dt.float32
BF16 = mybir.dt.bfloat16
AF = mybir.ActivationFunctionType
ALU = mybir.AluOpType

D = 1024; T = 4096; CT = 256; TT = T + CT; L = 4
NTOKB = TT // 128
NCHK = TT // 64
C0 = float(np.exp(-0.5))
TILES = [(0, 256, 1)] + [(256 + 512 * i, 512, 0) for i in range(8)]
TILES5 = [(0, 256, 1)] + [(256 + 256 * i, 256, 0) for i in range(16)]
NG = 0; BM = 32; MU = 80; KK = 89; KA = 91; RK = 93; LXG = 95; LXB = 97; W0 = 99; A0 = 103; SLG = 107; SLB = 109; NP = 111
SRC_CH = [8, 9, 10, 13, 14, 15, 16, 17, 18]

SAME_ENGINE_SYNC = True
EPOCH = 12000
DMA_EPOCH = 1500
NDMASEM = 6


class Buf:
    __slots__ = ("name", "w", "r")

    def __init__(self, name):
        self.name = name
        self.w = {}
        self.r = {}


class KB:
    def __init__(self, nc):
        self.nc = nc
        self.es = ExitStack()
        self.E = {"pe": nc.tensor, "act": nc.scalar, "dve": nc.vector, "pool": nc.gpsimd, "sp": nc.sync}
        self.sem = {}
        self.cnt = {}
        self.seen = {e: {} for e in self.E}
        self.nsem = 0
        for e in self.E:
            self._new_sem(e)
        self.dsem = {}
        self.dcnt = {}
        self.drr = {}
        for q in ("sp", "pool"):
            self.dsem[q] = [self._alloc_sem(f"d{q}{i}") for i in range(NDMASEM)]
            self.dcnt[q] = [0] * NDMASEM
            self.drr[q] = 0
        self.ninstr = 0
        self.banks = []
        self.bi = 0

    def _alloc_sem(self, name):
        self.nsem += 1
        h = self.nc.alloc_semaphore(name=f"{name}_{self.nsem}")
        self._keep = getattr(self, "_keep", [])
        self._keep.append(h)
        return h

    def _new_sem(self, e):
        self.sem[e] = self._alloc_sem(f"s{e}")
        self.cnt[e] = 0

    def sb(self, name, shape, dtype=F32, stack=None):
        self.nsb = getattr(self, "nsb", 0) + 1
        return (stack or self.es).enter_context(self.nc.sbuf_tensor(f"sb{self.nsb}_{name}", list(shape), dtype))

    def mkbanks(self):
        for i in range(8):
            t = self.es.enter_context(self.nc.psum_tensor(f"bank{i}", [128, 512], F32))
            self.banks.append((t, Buf(f"bank{i}")))

    def bank(self):
        b = self.banks[self.bi]
        self.bi = (self.bi + 1) % 8
        return b

    def _wait(self, e, ev):
        sem, val, src = ev
        if src == e and (not SAME_ENGINE_SYNC or e == "pe"):
            return
        key = id(sem)
        if self.seen[e].get(key, 0) >= val:
            return
        self.E[e].wait_ge(sem, val)
        self.seen[e][key] = val

    def _deps(self, e, reads, writes, adds, isdma=False):
        for b in reads:
            for ev in b.w.values():
                self._wait(e, ev)
        for b in writes:
            for ev in b.w.values():
                self._wait(e, ev)
            for ev in b.r.values():
                self._wait(e, ev)
        for b in adds:
            for ev in b.r.values():
                self._wait(e, ev)
            for ev in b.w.values():
                if not (isdma and ev[2] == "dma"):
                    self._wait(e, ev)

    def _commit(self, key, ev, reads, writes, adds):
        for b in reads:
            b.r[key] = ev
        for b in writes:
            b.w = {key: ev}
            b.r = {}
        for b in adds:
            b.w[key] = ev

    def op(self, e, fn, reads=(), writes=(), adds=()):
        self._deps(e, reads, writes, adds)
        ins = fn(self.E[e])
        if self.cnt[e] >= EPOCH:
            self._new_sem(e)
        self.cnt[e] += 1
        ins.then_inc(self.sem[e], 1)
        ev = (self.sem[e], self.cnt[e], e)
        self._commit(e, ev, reads, writes, adds)
        self.ninstr += 1
        return ev

    def dma(self, out, in_, reads=(), writes=(), adds=(), q="sp"):
        self._deps(q, reads, writes, adds, True)
        j = self.drr[q]
        self.drr[q] = (j + 1) % NDMASEM
        if self.dcnt[q][j] > 0:
            self._wait(q, (self.dsem[q][j], self.dcnt[q][j] * 16, "dma"))
        if self.dcnt[q][j] >= DMA_EPOCH:
            self.dsem[q][j] = self._alloc_sem(f"d{q}{j}")
            self.dcnt[q][j] = 0
        sem = self.dsem[q][j]
        self.E[q].dma_start(out=out, in_=in_).then_inc(sem, 16)
        self.dcnt[q][j] += 1
        ev = (sem, self.dcnt[q][j] * 16, "dma")
        self._commit(id(sem), ev, reads, writes, adds)
        self.ninstr += 1
        return ev

    def barrier(self):
        if getattr(self, "ptick", None) is not None:
            self.op("pool", lambda e: e.memset(self.ptick[:], 0.0), writes=[self.ptickb])
        evs = []
        for e in ("pe", "act", "dve", "pool", "sp"):
            if self.cnt[e] > 0:
                evs.append((self.sem[e], self.cnt[e], e))
        for q in self.dsem:
            for j in range(NDMASEM):
                if self.dcnt[q][j] > 0:
                    evs.append((self.dsem[q][j], self.dcnt[q][j] * 16, "dma"))
        for e in self.E:
            for ev in evs:
                if ev[2] != e:
                    self._wait(e, ev)


PHASES = None


CURL = [0]
PHASES_L = None


def i_phase_on(i):
    if PHASES_L is not None:
        return i in PHASES_L[CURL[0]]
    return PHASES is None or i in PHASES


def build(nlayers=L, debug=False):
    nc = bass.Bass("TRN2", target_bir_lowering=False)
    k = KB(nc)
    k.mkbanks()

    def din(name, shape):
        return nc.dram_tensor(name, list(shape), F32, kind="ExternalInput")

    def dscr(name, shape, out=False):
        return nc.dram_tensor(name, list(shape), F32, kind="ExternalOutput" if (out and debug) else "Internal")

    xin = din("xin", [8, 128, TT])
    cin = din("cin", [128, 8, 2])
    w_mod = din("w_mod", [L, D, 6 * D]); w_in = din("w_in", [L, D, 2432]); w_out = din("w_out", [L, D, D])
    w_gu = din("w_gu", [L, D, 5632]); w_down = din("w_down", [L, 2816, D])
    ppd = din("pp", [L, 128, NP])
    sgwT = din("sgwT", [L, 128, 4, 128]); sgb = din("sgb", [L, 1, 512]); sink = din("sink", [L, 1, 8])
    w2d = din("w2", [L, 128, 256]); a2d = din("a2", [L, 128, 256]); g2d = din("g2", [L, 128, 256])
    cc = din("cc", [128, 384]); smL = din("smL", [128, 9, 4]); smC = din("smC", [128, 9, 2])
    cmd = din("cm", [64, 4, 512]); amd = din("am", [128, 2, 512]); rsd = din("rs", [128, 512])
    cosd = din("cosT", [128, T]); sind = din("sinT", [128, T])
    outT = nc.dram_tensor("outT", [8, 128, T], F32, kind="ExternalOutput")

    xs = dscr("xs", [8, 128, TT], True); bxs = Buf("xs")
    pxT = dscr("pxT", [19, 128, TT], True); bpx = Buf("pxT")
    mixT = dscr("mixT", [8, 128, TT], True); bmix = Buf("mixT")
    rmT = dscr("rmT", [9, 128, TT], True); brm = Buf("rmT")
    scin = nc.dram_tensor("scin", [4, 128, 5, TT], BF16, kind="Internal"); bscin = Buf("scin")
    gcd = dscr("gcd", [4, 128, NCHK], True); bgcd = Buf("gcd")
    rksT = dscr("rksT", [2, 128, TT], True); brks = Buf("rksT")
    yT = dscr("yT", [2, 256, TT], True); byT = Buf("yT")
    bout = Buf("outT")

    cst = k.sb("cst", [128, 384]); bc = Buf("cst")
    k.dma(cst[:], cc.ap(), writes=[bc])
    ident = cst[:, 0:128]; bones = cst[:, 128:256]; Rm = cst[:, 256:384]
    k.ptick = k.sb("ptick", [128, 8]); k.ptickb = Buf("ptick")
    ones = k.sb("ones", [128, 128]); bo = Buf("ones")
    k.op("dve", lambda e: e.memset(ones[:], 1.0), writes=[bo])
    onesb128 = k.sb("onesb128", [128, 128], BF16); bob128 = Buf("onesb128")
    k.op("dve", lambda e: e.memset(onesb128[:], 1.0), writes=[bob128])
    cact = k.sb("cact", [128, 8, 2]); bca = Buf("cact")
    k.dma(cact[:], cin.ap(), writes=[bca])
    k.op("act", lambda e: e.activation(out=cact[:], in_=cact[:], func=AF.Silu), writes=[bca])
    pp = k.sb("ppt", [128, NP]); bpp = Buf("pp")
    modt = k.sb("modt", [128, 48, 2]); bmod = Buf("modt")
    coef = k.sb("coef", [128, 6, 8, 2]); bcoef = Buf("coef")
    cfs = k.sb("cfs", [128, 9, 7]); bcfs = Buf("cfs")
    smLs = k.sb("smLs", [128, 9, 4]); smCs = k.sb("smCs", [128, 9, 2]); bsm = Buf("sm")
    k.dma(smLs[:], smL.ap(), adds=[bsm]); k.dma(smCs[:], smC.ap(), adds=[bsm])
    omka = k.sb("omka", [128, 2]); bomka = Buf("omka")
    k.barrier()

    def xsrc(l):
        return xin if l == 0 else xs

    def rmsn(st, X, bX, W, SQ, bSQ, rs, brs):
        k.op("act", lambda e: e.activation(out=SQ[:, :, :W], in_=X[:, :, :W], func=AF.Square), reads=[bX], writes=[bSQ])
        bk, bb = k.bank()
        for c in range(8):
            k.op("pe", lambda e, c=c: e.matmul(bk[:, :W], lhsT=onesb128[:], rhs=SQ[:, c, :W], start=(c == 0), stop=(c == 7)),
                 reads=[bSQ, bob128], writes=[bb])
        k.op("dve", lambda e: e.tensor_scalar(out=rs[:, :W], in0=bk[:, :W], scalar1=1.0 / D, scalar2=1e-6, op0=ALU.mult, op1=ALU.add),
             reads=[bb], writes=[brs])
        k.op("act", lambda e: e.activation(out=rs[:, :W], in_=rs[:, :W], func=AF.Sqrt), writes=[brs])
        k.op("dve", lambda e: e.reciprocal(out=rs[:, :W], in_=rs[:, :W]), writes=[brs])

    t32 = [k.sb(f"t32_{i}", [128, 512]) for i in range(2)]; bt32 = [Buf("t32a"), Buf("t32b")]

    def modulate(H, bH, X, bX, rs, brs, W, ia, ib, who):
        for c in range(8):
            t_ = t32[c % 2]; bt_ = bt32[c % 2]
            k.op("dve", lambda e, c=c, t_=t_: e.scalar_tensor_tensor(out=t_[:, :W], in0=X[:, c, :W], scalar=coef[:, ia, c, who:who + 1],
                                                                   in1=rs[:, :W], op0=ALU.mult, op1=ALU.mult),
                 reads=[bX, brs, bcoef], writes=[bt_])
            k.op("act", lambda e, c=c, t_=t_: e.activation(out=H[:, c, :W], in_=t_[:, :W], func=AF.Identity, bias=coef[:, ib, c, who:who + 1]),
                 reads=[bcoef, bt_], writes=[bH] if c == 0 else (), adds=[bH] if c else ())

    castrr = [0]

    def cast(out, in_, reads, writes=(), adds=()):
        castrr[0] = (castrr[0] + 1) % 2
        if castrr[0] == 0:
            k.op("act", lambda e: e.copy(out=out, in_=in_), reads=reads, writes=writes, adds=adds)
        else:
            k.op("dve", lambda e: e.tensor_copy(out=out, in_=in_), reads=reads, writes=writes, adds=adds)

    def resid_update(X, bX, Y, bY, rs, brs, W, ig, who):
        for j in range(8):
            k.op("dve", lambda e, j=j: e.scalar_tensor_tensor(out=Y[:, j, :W], in0=Y[:, j, :W], scalar=coef[:, ig, j, who:who + 1],
                                                             in1=rs[:, :W], op0=ALU.mult, op1=ALU.mult),
                 reads=[brs, bcoef], writes=[bY])
            k.op("dve", lambda e, j=j: e.tensor_tensor(out=X[:, j, :W], in0=X[:, j, :W], in1=Y[:, j, :W], op=ALU.add),
                 reads=[bY], writes=[bX])

    evflip = [0]

    def evac(out, in_, reads, writes=(), adds=()):
        evflip[0] ^= 1
        if evflip[0]:
            k.op("act", lambda e: e.copy(out=out, in_=in_), reads=reads, writes=writes, adds=adds)
        else:
            k.op("dve", lambda e: e.tensor_copy(out=out, in_=in_), reads=reads, writes=writes, adds=adds)

    for l in range(nlayers):
        last = (l == L - 1)
        CURL[0] = l
        with ExitStack() as st:
          if i_phase_on(0):
            k.dma(pp[:], ppd.ap()[l], writes=[bpp])
            wm = [k.sb(f"wm{i}", [128, 8, 1024], stack=st) for i in range(2)]; bwm = [Buf("wm0"), Buf("wm1")]
            for m in range(6):
                k.dma(wm[m % 2][:], w_mod.ap()[l][:, m * 1024:(m + 1) * 1024].rearrange("(kc p) n -> p kc n", p=128), writes=[bwm[m % 2]],
                      q="sp" if m % 2 == 0 else "pool")
                for c in range(8):
                    bk, bb = k.bank()
                    for kc in range(8):
                        k.op("pe", lambda e, kc=kc, c=c, m=m: e.matmul(bk[:, 0:2], lhsT=wm[m % 2][:, kc, c * 128:(c + 1) * 128], rhs=cact[:, kc, :],
                                                                      start=(kc == 0), stop=(kc == 7)), reads=[bwm[m % 2], bca], writes=[bb])
                    j = m * 8 + c
                    k.op("dve", lambda e, j=j: e.tensor_scalar(out=modt[:, j, :], in0=bk[:, 0:2], scalar1=pp[:, BM + j:BM + j + 1], scalar2=None, op0=ALU.add),
                         reads=[bb, bpp], adds=[bmod])
            for who in range(2):
                for (ic, isc, ig) in ((0, 1, 0), (3, 4, 2)):
                    k.op("dve", lambda e, who=who, ic=ic, isc=isc, ig=ig: e.tensor_tensor(out=coef[:, ic, :, who], in0=modt[:, isc * 8:isc * 8 + 8, who],
                                                                                        in1=pp[:, NG + ig * 8:NG + ig * 8 + 8], op=ALU.mult),
                         reads=[bmod, bpp], writes=[bcoef])
                    k.op("dve", lambda e, who=who, ic=ic, ig=ig: e.tensor_tensor(out=coef[:, ic, :, who], in0=coef[:, ic, :, who],
                                                                               in1=pp[:, NG + ig * 8:NG + ig * 8 + 8], op=ALU.add), writes=[bcoef])
                for (ic, ish) in ((1, 0), (4, 3)):
                    k.op("dve", lambda e, who=who, ic=ic, ish=ish: e.tensor_copy(out=coef[:, ic, :, who], in_=modt[:, ish * 8:ish * 8 + 8, who]),
                         reads=[bmod], writes=[bcoef])
                for (ic, igt, ig) in ((2, 2, 1), (5, 5, 3)):
                    k.op("dve", lambda e, who=who, ic=ic, igt=igt, ig=ig: e.tensor_tensor(out=coef[:, ic, :, who], in0=modt[:, igt * 8:igt * 8 + 8, who],
                                                                                        in1=pp[:, NG + ig * 8:NG + ig * 8 + 8], op=ALU.mult),
                         reads=[bmod, bpp], writes=[bcoef])
            k.op("dve", lambda e: e.tensor_scalar(out=cfs[:, :, 0], in0=pp[:, MU:MU + 9], scalar1=-1.0, scalar2=1.0, op0=ALU.mult, op1=ALU.add),
                 reads=[bpp], writes=[bcfs])
            for dd in range(4):
                k.op("dve", lambda e, dd=dd: e.tensor_tensor(out=cfs[:, :, 1 + dd], in0=pp[:, MU:MU + 9], in1=smLs[:, :, dd], op=ALU.mult),
                     reads=[bpp, bsm], writes=[bcfs])
            for dd in range(2):
                k.op("dve", lambda e, dd=dd: e.tensor_tensor(out=cfs[:, :, 5 + dd], in0=pp[:, MU:MU + 9], in1=smCs[:, :, dd], op=ALU.mult),
                     reads=[bpp, bsm], writes=[bcfs])
            k.op("dve", lambda e: e.tensor_scalar(out=omka[:], in0=pp[:, KA:KA + 2], scalar1=-1.0, scalar2=1.0, op0=ALU.mult, op1=ALU.add),
                 reads=[bpp], writes=[bomka])
            k.barrier()

        with ExitStack() as st:
          if i_phase_on(1):
            win = k.sb("win", [128, 8, 2432], BF16, stack=st); bwin = Buf("win")
            wst = [k.sb(f"wst{i}", [128, 2432], stack=st) for i in range(2)]; bwst = [Buf("wst0"), Buf("wst1")]
            wv = w_in.ap()[l].rearrange("(kc p) n -> p kc n", p=128)
            for kc in range(8):
                k.dma(wst[kc % 2][:], wv[:, kc, :], writes=[bwst[kc % 2]], q="sp" if kc % 2 == 0 else "pool")
                cast(win[:, kc, :], wst[kc % 2][:], [bwst[kc % 2]], adds=[bwin])
            X = k.sb("X", [128, 8, 512], stack=st); bX = Buf("X")
            SQ = k.sb("SQ", [128, 8, 512], BF16, stack=st); bSQ = Buf("SQ")
            H = k.sb("H", [128, 8, 512], BF16, stack=st); bH = Buf("H")
            rs = k.sb("rs", [128, 512], stack=st); brs = Buf("rs")
            stg = [k.sb(f"stg{i}", [128, 512], stack=st) for i in range(2)]; bstg = [Buf("s0"), Buf("s1")]
            xv = xsrc(l).ap().rearrange("c p t -> p c t")
            for (t0, W, who) in TILES:
                k.dma(X[:, :, :W], xv[:, :, t0:t0 + W], reads=[bxs], writes=[bX])
                rmsn(st, X, bX, W, SQ, bSQ, rs, brs)
                modulate(H, bH, X, bX, rs, brs, W, 0, 1, who)
                for j in range(19):
                    bk, bb = k.bank()
                    for kc in range(8):
                        k.op("pe", lambda e, kc=kc, j=j: e.matmul(bk[:, :W], lhsT=win[:, kc, j * 128:(j + 1) * 128], rhs=H[:, kc, :W],
                                                                 start=(kc == 0), stop=(kc == 7)), reads=[bwin, bH], writes=[bb])
                    s = stg[j % 2]
                    evac(s[:, :W], bk[:, :W], [bb], writes=[bstg[j % 2]])
                    k.dma(pxT.ap()[j][:, t0:t0 + W], s[:, :W], reads=[bstg[j % 2]], adds=[bpx])
            k.barrier()

        with ExitStack() as st:
          if i_phase_on(2):
            fb = [k.sb(f"fb{i}", [128, TT], stack=st) for i in range(2)]; bfb = [Buf("fb0"), Buf("fb1")]
            ob = [k.sb(f"ob{i}", [128, TT], stack=st) for i in range(2)]; bob = [Buf("ob0"), Buf("ob1")]
            for ci in range(9):
                f = fb[ci % 2]; o = ob[ci % 2]; bf = bfb[ci % 2]; bo_ = bob[ci % 2]
                k.dma(f[:], pxT.ap()[SRC_CH[ci]], reads=[bpx], writes=[bf], q="sp" if ci % 2 == 0 else "pool")
                k.op("act", lambda e, f=f, o=o, ci=ci: e.activation(out=o[:], in_=f[:], func=AF.Identity, scale=cfs[:, ci, 0:1]), reads=[bf, bcfs], writes=[bo_])
                fl = f[:, CT:].rearrange("p (r w) -> p r w", w=64)
                ol = o[:, CT:].rearrange("p (r w) -> p r w", w=64)
                views = [(ol[:, :, 1:64], fl[:, :, 0:63]), (ol[:, :, 0:63], fl[:, :, 1:64]), (ol[:, 1:64, :], fl[:, 0:63, :]), (ol[:, 0:63, :], fl[:, 1:64, :]),
                         (o[:, 1:CT], f[:, 0:CT - 1]), (o[:, 0:CT - 1], f[:, 1:CT])]
                for di, (ov, fv) in enumerate(views):
                    eng = "dve"
                    k.op(eng, lambda e, ov=ov, fv=fv, ci=ci, di=di: e.scalar_tensor_tensor(out=ov, in0=fv, scalar=cfs[:, ci, 1 + di:2 + di], in1=ov,
                                                                                          op0=ALU.mult, op1=ALU.add), reads=[bf, bcfs], writes=[bo_])
                k.dma(rmT.ap()[ci], o[:], reads=[bo_], adds=[brm])
            k.barrier()

        with ExitStack() as st:
          if i_phase_on(3):
            w2 = k.sb("w2s", [128, 256], stack=st); a2 = k.sb("a2s", [128, 256], stack=st); bw2 = Buf("w2")
            k.dma(w2[:], w2d.ap()[l], adds=[bw2]); k.dma(a2[:], a2d.ap()[l], adds=[bw2])
            rsm = k.sb("rsm", [128, 512], stack=st); brsm = Buf("rsm")
            k.dma(rsm[:], rsd.ap(), writes=[brsm])
            rm = k.sb("rm", [128, 9, 512], stack=st); brmt = Buf("rm")
            tw = k.sb("tw", [128, 512], stack=st); btw = Buf("tw")
            kkr = k.sb("kkr", [128, 2, 512], stack=st); bkkr = Buf("kkr")
            sqk = k.sb("sqk", [128, 2, 512], stack=st); bsqk = Buf("sqk")
            kk = k.sb("kk", [128, 2, 512], stack=st); bkk = Buf("kk")
            nr = k.sb("nr", [128, 512], stack=st); bnr = Buf("nr")
            sg = k.sb("sg", [128, 512], stack=st); bsg = Buf("sg")
            aa = k.sb("aa", [128, 512], stack=st); baa = Buf("aa")
            cs = k.sb("cs", [128, 512], stack=st); bcs = Buf("cs")
            Ei = k.sb("Ei", [128, 512], stack=st); bEi = Buf("Ei")
            Ex = k.sb("Ex", [128, 512], stack=st); bEx = Buf("Ex")
            gi = k.sb("gi", [128, 512], stack=st); bgi = Buf("gi")
            gx = k.sb("gx", [128, 512], stack=st); bgx = Buf("gx")
            gv = k.sb("gv", [128, 512], stack=st); bgv = Buf("gv")
            gc = k.sb("gc", [128, 8], stack=st); bgc = Buf("gc")
            kd = k.sb("kd", [128, 512], stack=st); bkd = Buf("kd")
            bt = k.sb("bt", [128, 512], stack=st); bbt = Buf("bt")
            kds = k.sb("kds", [128, 2, 512], stack=st); bkds = Buf("kds")
            so = [k.sb(f"so{i}", [128, 5, 512], BF16, stack=st) for i in range(2)]; bso = [Buf("so0"), Buf("so1")]
            it = 0
            for (t0, W, who) in TILES:
                nck = W // 64
                k.dma(rm[:, :, :W], rmT.ap().rearrange("c p t -> p c t")[:, :, t0:t0 + W], reads=[brm], writes=[brmt])
                k.op("act", lambda e: e.activation(out=tw[:, :W], in_=rm[:, 7, :W], func=AF.Tanh), reads=[brmt], writes=[btw])
                for hp in range(2):
                    k.op("dve", lambda e, hp=hp: e.tensor_scalar(out=kkr[:, hp, :W], in0=rm[:, 3 + hp, :W], scalar1=pp[:, KK + hp:KK + hp + 1], scalar2=None, op0=ALU.mult),
                         reads=[brmt, bpp], adds=[bkkr])
                k.op("act", lambda e: e.activation(out=sqk[:, :, :W], in_=kkr[:, :, :W], func=AF.Square), reads=[bkkr], writes=[bsqk])
                for hp in range(2):
                    bk, bb = k.bank()
                    k.op("pe", lambda e, hp=hp: e.matmul(bk[:, :W], lhsT=bones, rhs=sqk[:, hp, :W], start=True, stop=True), reads=[bsqk, bc], writes=[bb])
                    k.op("act", lambda e: e.activation(out=nr[:, :W], in_=bk[:, :W], func=AF.Sqrt), reads=[bb], writes=[bnr])
                    k.op("dve", lambda e: e.tensor_scalar(out=nr[:, :W], in0=nr[:, :W], scalar1=1e-12, scalar2=None, op0=ALU.max), writes=[bnr])
                    k.op("dve", lambda e: e.reciprocal(out=nr[:, :W], in_=nr[:, :W]), writes=[bnr])
                    k.op("dve", lambda e, hp=hp: e.tensor_tensor(out=kk[:, hp, :W], in0=kkr[:, hp, :W], in1=nr[:, :W], op=ALU.mult), reads=[bkkr, bnr], adds=[bkk])
                for hp in range(2):
                    for d in range(2):
                        bk, bb = k.bank()
                        k.op("pe", lambda e, d=d, hp=hp: e.matmul(bk[:, :W], lhsT=w2[d * 64:(d + 1) * 64, hp * 128:(hp + 1) * 128], rhs=tw[d * 64:(d + 1) * 64, :W],
                                                                 start=True, stop=True), reads=[bw2, btw], writes=[bb])
                        k.op("act", lambda e, d=d, hp=hp: e.activation(out=sg[:, :W], in_=bk[:, :W], func=AF.Sigmoid, bias=pp[:, W0 + d * 2 + hp:W0 + d * 2 + hp + 1]),
                             reads=[bb, bpp], writes=[bsg])
                        bk2, bb2 = k.bank()
                        k.op("pe", lambda e, d=d, hp=hp: e.matmul(bk2[:, :W], lhsT=a2[d * 64:(d + 1) * 64, hp * 128:(hp + 1) * 128], rhs=rm[d * 64:(d + 1) * 64, 8, :W],
                                                                 start=True, stop=True), reads=[bw2, brmt], writes=[bb2])
                        k.op("act", lambda e, d=d, hp=hp: e.activation(out=aa[:, :W], in_=bk2[:, :W], func=AF.Sigmoid, bias=pp[:, A0 + d * 2 + hp:A0 + d * 2 + hp + 1]),
                             reads=[bb2, bpp], writes=[baa])
                        k.op("dve", lambda e: e.tensor_tensor_scan(out=cs[:, :W], data0=rsm[:, :W], data1=sg[:, :W], initial=0.0, op0=ALU.mult, op1=ALU.add),
                             reads=[brsm, bsg], writes=[bcs])
                        cs3 = cs[:, :W].rearrange("p (c t) -> p c t", t=64)
                        if d == 0:
                            k.op("dve", lambda e: e.tensor_tensor(out=Ex[:, :W], in0=cs[:, :W], in1=sg[:, :W], op=ALU.subtract), reads=[bcs, bsg], writes=[bEx])
                            Eiv = cs; bEiv = bcs
                        else:
                            Ex3 = Ex[:, :W].rearrange("p (c t) -> p c t", t=64)
                            k.op("dve", lambda e, cs3=cs3, Ex3=Ex3: e.tensor_tensor(out=Ex3, in0=cs3[:, :, 63:64].broadcast_to([128, nck, 64]), in1=cs3, op=ALU.subtract),
                                 reads=[bcs], writes=[bEx])
                            k.op("dve", lambda e: e.tensor_tensor(out=Ei[:, :W], in0=Ex[:, :W], in1=sg[:, :W], op=ALU.add), reads=[bEx, bsg], writes=[bEi])
                            Eiv = Ei; bEiv = bEi
                        k.op("act", lambda e, cs3=cs3: e.activation(out=gc[:, :nck], in_=cs3[:, :, 63], func=AF.Exp, scale=-C0), reads=[bcs], writes=[bgc])
                        k.op("act", lambda e, Eiv=Eiv: e.activation(out=gi[:, :W], in_=Eiv[:, :W], func=AF.Exp, scale=-C0), reads=[bEiv], writes=[bgi])
                        k.op("act", lambda e: e.activation(out=gx[:, :W], in_=Ex[:, :W], func=AF.Exp, scale=-C0), reads=[bEx], writes=[bgx])
                        k.op("act", lambda e, Eiv=Eiv: e.activation(out=gv[:, :W], in_=Eiv[:, :W], func=AF.Exp, scale=C0), reads=[bEiv], writes=[bgv])
                        s_ = so[it % 2]; bs_ = bso[it % 2]; it += 1
                        k.op("dve", lambda e, s_=s_, hp=hp: e.tensor_tensor(out=s_[:, 0, :W], in0=kk[:, hp, :W], in1=gx[:, :W], op=ALU.mult), reads=[bkk, bgx], writes=[bs_])
                        k.op("dve", lambda e, s_=s_, hp=hp: e.tensor_tensor(out=s_[:, 1, :W], in0=rm[:, hp, :W], in1=gi[:, :W], op=ALU.mult), reads=[brmt, bgi], adds=[bs_])
                        k.op("dve", lambda e, hp=hp: e.tensor_scalar(out=kd[:, :W], in0=aa[:, :W], scalar1=pp[:, KA + hp:KA + hp + 1], scalar2=omka[:, hp:hp + 1],
                                                                    op0=ALU.mult, op1=ALU.add), reads=[baa, bpp, bomka], writes=[bkd])
                        k.op("dve", lambda e, hp=hp: e.tensor_tensor(out=kd[:, :W], in0=kd[:, :W], in1=rm[:, 3 + hp, :W], op=ALU.mult), reads=[brmt], writes=[bkd])
                        k.op("dve", lambda e, s_=s_: e.tensor_tensor(out=s_[:, 2, :W], in0=kd[:, :W], in1=gv[:, :W], op=ALU.mult), reads=[bkd, bgv], adds=[bs_])
                        k.op("dve", lambda e, hp=hp: e.tensor_tensor(out=bt[:, :W], in0=aa[:, :W], in1=kk[:, hp, :W], op=ALU.mult), reads=[baa, bkk], writes=[bbt])
                        k.op("dve", lambda e, s_=s_: e.tensor_tensor(out=s_[:, 3, :W], in0=bt[:, :W], in1=gv[:, :W], op=ALU.mult), reads=[bbt, bgv], adds=[bs_])
                        if d == 0:
                            k.op("dve", lambda e, hp=hp: e.tensor_copy(out=kds[:, hp, :W], in_=kd[:, :W]), reads=[bkd], writes=[bkds])
                        else:
                            k.op("dve", lambda e, hp=hp: e.tensor_tensor(out=kds[:, hp, :W], in0=kds[:, hp, :W], in1=kd[:, :W], op=ALU.add), reads=[bkd], writes=[bkds])
                        k.op("act", lambda e, s_=s_, hp=hp: e.copy(out=s_[:, 4, :W], in_=rm[:, 5 + hp, :W]), reads=[brmt], adds=[bs_])
                        k.dma(scin.ap()[d * 2 + hp][:, :, t0:t0 + W], s_[:, :, :W], reads=[bs_], adds=[bscin])
                        k.dma(gcd.ap()[d * 2 + hp][:, t0 // 64:t0 // 64 + nck], gc[:, :nck], reads=[bgc], adds=[bgcd])
                    k.op("dve", lambda e, hp=hp: e.scalar_tensor_tensor(out=kds[:, hp, :W], in0=kds[:, hp, :W], scalar=pp[:, RK + hp:RK + hp + 1], in1=rm[:, hp, :W],
                                                                       op0=ALU.mult, op1=ALU.mult), reads=[brmt, bpp], writes=[bkds])
                k.dma(rksT.ap().rearrange("c p t -> p c t")[:, :, t0:t0 + W], kds[:, :, :W], reads=[bkds], adds=[brks])
            k.barrier()

        with ExitStack() as st:
          if i_phase_on(4):
            cm = k.sb("cm", [64, 4, 512], stack=st); bcm = Buf("cm")
            k.dma(cm[:], cmd.ap(), writes=[bcm])
            idb = k.sb("idb", [64, 64], BF16, stack=st); bidb = Buf("idb")
            k.op("dve", lambda e: e.tensor_copy(out=idb[:], in_=ident[0:64, 0:64]), reads=[bc], writes=[bidb])
            inT = [k.sb(f"inT{d}", [64, 4, 5, 512], BF16, stack=st) for d in range(2)]; binT = [Buf("inT0"), Buf("inT1")]
            gct = [k.sb(f"gct{d}", [64, 4, 8], stack=st) for d in range(2)]; bgct = [Buf("g0"), Buf("g1")]
            yst = [k.sb(f"yst{d}", [64, 4, 512], stack=st) for d in range(2)]; byst = [Buf("y0"), Buf("y1")]
            Hs = k.sb("Hs", [64, 8, 64], stack=st); bHs = Buf("Hs")
            Hb = k.sb("Hb", [64, 8, 64], BF16, stack=st); bHb = Buf("Hb")
            k.op("dve", lambda e: e.memset(Hs[:], 0.0), writes=[bHs])
            k.op("dve", lambda e: e.memset(Hb[:], 0.0), writes=[bHb])

            def b16(name):
                return k.sb(name, [64, 512], BF16, stack=st), Buf(name)
            Ktm, bKtm = b16("Ktm"); NBtm, bNB = b16("NBtm"); Vtm, bVtm = b16("Vtm")
            Pm = [None, None]; bPm = [None, None]; Nm = [None, None]; bNm = [None, None]; Ab = [None, None]; bAb = [None, None]
            for i in range(2):
                Pm[i], bPm[i] = b16(f"Pm{i}"); Nm[i], bNm[i] = b16(f"Nm{i}"); Ab[i], bAb[i] = b16(f"Ab{i}")
            A32 = [k.sb(f"A32{i}", [64, 512], stack=st) for i in range(2)]; bA32 = [Buf("A32a"), Buf("A32b")]
            AKT, bAKT = b16("AKT"); RKT, bRKT = b16("RKT"); RBT, bRBT = b16("RBT"); RHSs, bRHS = b16("RHSs"); Us, bUs = b16("Us")
            Ht = k.sb("Ht", [64, 8, 64], stack=st); bHt = Buf("Ht")
            Fc = [(0, ci) for ci in range(4)] + [(ti, ci) for ti in range(1, 9) for ci in range(8)]
            Bc = [(0, ci) for ci in range(3, -1, -1)] + [(ti, ci) for ti in range(8, 0, -1) for ci in range(7, -1, -1)]
            cur = [None, None]

            def flush(d):
                ti = cur[d]
                t0, W, _ = TILES[ti]
                yv = yT.ap()[d].rearrange("(h v) t -> v h t", v=64)
                k.dma(yv[:, :, t0:t0 + W], yst[d][:, :, :W], reads=[byst[d]], adds=[byT])

            def load(d, ti):
                t0, W, _ = TILES[ti]
                for h in range(4):
                    hp = h // 2; e_ = h % 2
                    k.dma(inT[d][:, h, :, :W], scin.ap()[d * 2 + hp][e_ * 64:(e_ + 1) * 64, :, t0:t0 + W], reads=[bscin],
                          adds=[binT[d]], q="sp" if h % 2 == 0 else "pool")
                    k.dma(gct[d][:, h, :W // 64], gcd.ap()[d * 2 + hp][e_ * 64:(e_ + 1) * 64, t0 // 64:t0 // 64 + W // 64], reads=[bgcd],
                          writes=[bgct[d]] if h == 0 else (), adds=[bgct[d]] if h else ())

            def opd(d, h, kind, ci):
                return inT[d][:, h, kind, ci * 64:(ci + 1) * 64]

            def blk(t, q):
                return t[:, q * 64:(q + 1) * 64]

            import os as _os
            R3N = int(_os.environ.get('R3N', str(NCHK)))
            for step in range(R3N):
                tcs = [Fc[step], Bc[step]]
                for d in range(2):
                    if cur[d] != tcs[d][0]:
                        if cur[d] is not None:
                            flush(d)
                        load(d, tcs[d][0])
                        cur[d] = tcs[d][0]
                cis = [tcs[0][1], tcs[1][1]]
                for kind, dst, bdst, neg in ((2, Ktm, bKtm, False), (3, NBtm, bNB, True), (4, Vtm, bVtm, False)):
                    bk, bb = k.bank()
                    for d in range(2):
                        for h in range(4):
                            q = d * 4 + h
                            k.op("pe", lambda e, d=d, h=h, q=q, kind=kind, bk=bk: e.matmul(blk(bk[0:64, :], q), lhsT=opd(d, h, kind, cis[d]), rhs=idb[:], start=True, stop=True),
                                 reads=[binT[d], bidb], writes=[bb])
                    if neg:
                        k.op("act", lambda e, dst=dst, bk=bk: e.activation(out=dst[:], in_=bk[0:64, :], func=AF.Copy, scale=-1.0), reads=[bb], writes=[bdst])
                    elif kind == 2:
                        k.op("dve", lambda e, dst=dst, bk=bk: e.tensor_copy(out=dst[:], in_=bk[0:64, :]), reads=[bb], writes=[bdst])
                    else:
                        k.op("act", lambda e, dst=dst, bk=bk: e.copy(out=dst[:], in_=bk[0:64, :]), reads=[bb], writes=[bdst])

                def prod(la, ra, dst, bdst, mki, neg):
                    bk, bb = k.bank()
                    for d in range(2):
                        for h in range(4):
                            q = d * 4 + h
                            k.op("pe", lambda e, d=d, h=h, q=q: e.matmul(blk(bk[0:64, :], q), lhsT=opd(d, h, la, cis[d]), rhs=opd(d, h, ra, cis[d]),
                                                                        start=True, stop=True), reads=[binT[d]], writes=[bb])
                    k.op("dve", lambda e: e.scalar_tensor_tensor(out=dst[:], in0=bk[0:64, :], scalar=(-1.0 if neg else 1.0), in1=cm[:, mki, :], op0=ALU.mult, op1=ALU.mult),
                         reads=[bb, bcm], writes=[bdst])
                prod(3, 0, Pm[0], bPm[0], 0, True)
                prod(0, 3, Nm[0], bNm[0], 1, True)
                prod(2, 0, AKT, bAKT, 0, False)
                prod(2, 1, RKT, bRKT, 2, False)
                prod(3, 1, RBT, bRBT, 2, True)
                k.op("dve", lambda e: e.tensor_tensor(out=A32[0][:], in0=Pm[0][:], in1=cm[:, 3, :], op=ALU.add), reads=[bPm[0], bcm], writes=[bA32[0]])
                k.op("act", lambda e: e.copy(out=Ab[0][:], in_=A32[0][:]), reads=[bA32[0]], writes=[bAb[0]])
                pc = 0; ac = 0
                for j in range(1, 6):
                    nn = 1 - pc
                    bk, bb = k.bank()
                    for q in range(8):
                        k.op("pe", lambda e, q=q, pc=pc: e.matmul(blk(bk[0:64, :], q), lhsT=blk(Pm[pc], q), rhs=blk(Nm[pc], q), start=True, stop=True),
                             reads=[bPm[pc], bNm[pc]], writes=[bb])
                    if j < 5:
                        bk2, bb2 = k.bank()
                        for q in range(8):
                            k.op("pe", lambda e, q=q, pc=pc: e.matmul(blk(bk2[0:64, :], q), lhsT=blk(Nm[pc], q), rhs=blk(Pm[pc], q), start=True, stop=True),
                                 reads=[bPm[pc], bNm[pc]], writes=[bb2])
                    k.op("act", lambda e, nn=nn, bk=bk: e.copy(out=Nm[nn][:], in_=bk[0:64, :]), reads=[bb], writes=[bNm[nn]])
                    if j < 5:
                        k.op("dve", lambda e, nn=nn, bk2=bk2: e.tensor_copy(out=Pm[nn][:], in_=bk2[0:64, :]), reads=[bb2], writes=[bPm[nn]])
                    bk3, bb3 = k.bank()
                    for q in range(8):
                        k.op("pe", lambda e, q=q, nn=nn, ac=ac: e.matmul(blk(bk3[0:64, :], q), lhsT=blk(Nm[nn], q), rhs=blk(Ab[ac], q), start=True, stop=True),
                             reads=[bNm[nn], bAb[ac]], writes=[bb3])
                    k.op("dve", lambda e, ac=ac, bk3=bk3: e.tensor_tensor(out=A32[1 - ac][:], in0=bk3[0:64, :], in1=A32[ac][:], op=ALU.add), reads=[bb3, bA32[ac]], writes=[bA32[1 - ac]])
                    k.op("act", lambda e, ac=ac: e.copy(out=Ab[1 - ac][:], in_=A32[1 - ac][:]), reads=[bA32[1 - ac]], writes=[bAb[1 - ac]])
                    pc = nn; ac = 1 - ac
                A = Ab[ac]; bA = bAb[ac]
                bk, bb = k.bank()
                for d in range(2):
                    for h in range(4):
                        q = d * 4 + h
                        k.op("pe", lambda e, d=d, h=h, q=q: e.matmul(blk(bk[0:64, :], q), lhsT=opd(d, h, 0, cis[d]), rhs=Hb[:, q, :], start=True, stop=False),
                             reads=[binT[d], bHb], writes=[bb])
                        k.op("pe", lambda e, q=q: e.matmul(blk(bk[0:64, :], q), lhsT=blk(AKT, q), rhs=blk(Vtm, q), start=False, stop=True),
                             reads=[bAKT, bVtm], writes=[bb])
                k.op("act", lambda e, bk=bk: e.copy(out=RHSs[:], in_=bk[0:64, :]), reads=[bb], writes=[bRHS])
                bk, bb = k.bank()
                for q in range(8):
                    k.op("pe", lambda e, q=q: e.matmul(blk(bk[0:64, :], q), lhsT=blk(A, q), rhs=blk(RHSs, q), start=True, stop=True), reads=[bA, bRHS], writes=[bb])
                k.op("dve", lambda e, bk=bk: e.tensor_copy(out=Us[:], in_=bk[0:64, :]), reads=[bb], writes=[bUs])
                for d in range(2):
                    bk, bb = k.bank()
                    for h in range(4):
                        q = d * 4 + h
                        k.op("pe", lambda e, d=d, h=h, q=q, bk=bk: e.matmul(blk(bk[0:64, :], h), lhsT=Hb[:, q, :], rhs=opd(d, h, 1, cis[d]), start=True, stop=False),
                             reads=[binT[d], bHb], writes=[bb])
                        k.op("pe", lambda e, q=q, h=h, bk=bk: e.matmul(blk(bk[0:64, :], h), lhsT=blk(Vtm, q), rhs=blk(RKT, q), start=False, stop=False), reads=[bVtm, bRKT], writes=[bb])
                        k.op("pe", lambda e, q=q, h=h, bk=bk: e.matmul(blk(bk[0:64, :], h), lhsT=blk(Us, q), rhs=blk(RBT, q), start=False, stop=True), reads=[bUs, bRBT], writes=[bb])
                    if d == 0:
                        k.op("act", lambda e, d=d, bk=bk: e.copy(out=yst[d][:, 0, cis[d] * 64:(cis[d] + 1) * 64], in_=bk[0:64, 0:64]), reads=[bb], adds=[byst[d]])
                        for h in range(1, 4):
                            k.op("act", lambda e, d=d, h=h, bk=bk: e.copy(out=yst[d][:, h, cis[d] * 64:(cis[d] + 1) * 64], in_=bk[0:64, h * 64:(h + 1) * 64]), reads=[bb], adds=[byst[d]])
                    else:
                        for h in range(4):
                            k.op("dve", lambda e, d=d, h=h, bk=bk: e.tensor_copy(out=yst[d][:, h, cis[d] * 64:(cis[d] + 1) * 64], in_=bk[0:64, h * 64:(h + 1) * 64]), reads=[bb], adds=[byst[d]])
                bk, bb = k.bank()
                for q in range(8):
                    k.op("pe", lambda e, q=q: e.matmul(blk(bk[0:64, :], q), lhsT=blk(Ktm, q), rhs=blk(Vtm, q), start=True, stop=False), reads=[bKtm, bVtm], writes=[bb])
                    k.op("pe", lambda e, q=q: e.matmul(blk(bk[0:64, :], q), lhsT=blk(NBtm, q), rhs=blk(Us, q), start=False, stop=True), reads=[bNB, bUs], writes=[bb])
                k.op("dve", lambda e, bk=bk: e.tensor_tensor(out=Ht[:].rearrange("p a b -> p (a b)"), in0=bk[0:64, :], in1=Hs[:].rearrange("p a b -> p (a b)"), op=ALU.add),
                     reads=[bb, bHs], writes=[bHt])
                for d in range(2):
                    for h in range(4):
                        q = d * 4 + h
                        k.op("dve", lambda e, d=d, h=h, q=q: e.tensor_scalar(out=Hs[:, q, :], in0=Ht[:, q, :], scalar1=gct[d][:, h, cis[d]:cis[d] + 1], scalar2=None, op0=ALU.mult),
                             reads=[bHt, bgct[d]], writes=[bHs] if q == 0 else (), adds=[bHs] if q else ())
                k.op("act", lambda e: e.copy(out=Hb[:].rearrange("p a b -> p (a b)"), in_=Hs[:].rearrange("p a b -> p (a b)")), reads=[bHs], writes=[bHb])
            flush(0); flush(1)
            k.barrier()

        with ExitStack() as st:
          if i_phase_on(5):
            g2 = k.sb("g2s", [128, 256], stack=st); bg2 = Buf("g2")
            k.dma(g2[:], g2d.ap()[l], writes=[bg2])
            yf = k.sb("yf", [128, 2, 512], stack=st); byf = Buf("yf")
            yb = k.sb("yb", [128, 2, 512], stack=st); byb = Buf("yb")
            rmo = k.sb("rmo", [128, 7, 512], stack=st); brmo = Buf("rmo")
            rk_ = k.sb("rks", [128, 2, 512], stack=st); brk_ = Buf("rks")
            sq = k.sb("sq4", [128, 512], stack=st); bsq = Buf("sq4")
            mu = k.sb("mu4", [128, 512], stack=st); bmu = Buf("mu4")
            va = k.sb("va4", [128, 512], stack=st); bva = Buf("va4")
            sgl = k.sb("sgl", [128, 512], stack=st); bsgl = Buf("sgl")
            orw = k.sb("orw", [128, 2, 512], stack=st); borw = Buf("orw")
            bn = k.sb("bn4", [128, 512], stack=st); bbn = Buf("bn4")
            for (t0, W, who) in TILES:
                if last and who == 1:
                    continue
                yv = yT.ap().rearrange("d (hp p) t -> d p hp t", p=128)
                k.dma(yf[:, :, :W], yv[0][:, :, t0:t0 + W], reads=[byT], writes=[byf])
                k.dma(yb[:, :, :W], yv[1][:, :, t0:t0 + W], reads=[byT], writes=[byb])
                k.dma(rmo[:, :, :W], rmT.ap().rearrange("c p t -> p c t")[:, 0:7, t0:t0 + W], reads=[brm], writes=[brmo])
                k.dma(rk_[:, :, :W], rksT.ap().rearrange("c p t -> p c t")[:, :, t0:t0 + W], reads=[brks], writes=[brk_])
                k.op("dve", lambda e: e.tensor_tensor(out=yf[:, :, :W], in0=yf[:, :, :W], in1=yb[:, :, :W], op=ALU.add), reads=[byb], writes=[byf])
                k.op("act", lambda e: e.activation(out=sgl[:, :W], in_=rmo[:, 2, :W], func=AF.Sigmoid), reads=[brmo], writes=[bsgl])
                for hp in range(2):
                    bk1, bb1 = k.bank()
                    k.op("pe", lambda e, hp=hp: e.matmul(bk1[:, :W], lhsT=bones, rhs=yf[:, hp, :W], start=True, stop=True), reads=[byf, bc], writes=[bb1])
                    k.op("act", lambda e, hp=hp: e.activation(out=sq[:, :W], in_=yf[:, hp, :W], func=AF.Square), reads=[byf], writes=[bsq])
                    bk2, bb2 = k.bank()
                    k.op("pe", lambda e: e.matmul(bk2[:, :W], lhsT=bones, rhs=sq[:, :W], start=True, stop=True), reads=[bsq, bc], writes=[bb2])
                    k.op("dve", lambda e: e.tensor_scalar(out=mu[:, :W], in0=bk1[:, :W], scalar1=1.0 / 64, scalar2=None, op0=ALU.mult), reads=[bb1], writes=[bmu])
                    k.op("dve", lambda e: e.tensor_tensor(out=va[:, :W], in0=mu[:, :W], in1=mu[:, :W], op=ALU.mult), reads=[bmu], writes=[bva])
                    k.op("dve", lambda e: e.scalar_tensor_tensor(out=va[:, :W], in0=bk2[:, :W], scalar=1.0 / 64, in1=va[:, :W], op0=ALU.mult, op1=ALU.subtract),
                         reads=[bb2], writes=[bva])
                    k.op("dve", lambda e: e.tensor_scalar(out=va[:, :W], in0=va[:, :W], scalar1=64e-5, scalar2=None, op0=ALU.add), writes=[bva])
                    k.op("act", lambda e: e.activation(out=va[:, :W], in_=va[:, :W], func=AF.Sqrt), writes=[bva])
                    k.op("dve", lambda e: e.reciprocal(out=va[:, :W], in_=va[:, :W]), writes=[bva])
                    k.op("dve", lambda e, hp=hp: e.tensor_tensor(out=yf[:, hp, :W], in0=yf[:, hp, :W], in1=mu[:, :W], op=ALU.subtract), reads=[bmu], writes=[byf])
                    k.op("dve", lambda e, hp=hp: e.tensor_tensor(out=yf[:, hp, :W], in0=yf[:, hp, :W], in1=va[:, :W], op=ALU.mult), reads=[bva], writes=[byf])
                    k.op("dve", lambda e, hp=hp: e.tensor_scalar(out=yf[:, hp, :W], in0=yf[:, hp, :W], scalar1=pp[:, LXG + hp:LXG + hp + 1], scalar2=pp[:, LXB + hp:LXB + hp + 1],
                                                                op0=ALU.mult, op1=ALU.add), reads=[bpp], writes=[byf])
                    bk3, bb3 = k.bank()
                    k.op("pe", lambda e, hp=hp: e.matmul(bk3[:, :W], lhsT=bones, rhs=rk_[:, hp, :W], start=True, stop=True), reads=[brk_, bc], writes=[bb3])
                    k.op("dve", lambda e, hp=hp: e.tensor_tensor(out=bn[:, :W], in0=bk3[:, :W], in1=rmo[:, 5 + hp, :W], op=ALU.mult), reads=[bb3, brmo], writes=[bbn])
                    k.op("dve", lambda e, hp=hp: e.tensor_tensor(out=bn[:, :W], in0=bn[:, :W], in1=yf[:, hp, :W], op=ALU.add), reads=[byf], writes=[bbn])
                    bk4, bb4 = k.bank()
                    k.op("pe", lambda e, hp=hp: e.matmul(bk4[:, :W], lhsT=g2[:, hp * 128:(hp + 1) * 128], rhs=sgl[:, :W], start=True, stop=True), reads=[bg2, bsgl], writes=[bb4])
                    k.op("dve", lambda e, hp=hp: e.tensor_tensor(out=orw[:, hp, :W], in0=bk4[:, :W], in1=bn[:, :W], op=ALU.mult), reads=[bb4, bbn], writes=[borw])
                k.dma(mixT.ap().rearrange("c p t -> p c t")[:, 6:8, t0:t0 + W], orw[:, :, :W], reads=[borw], adds=[bmix])
            k.barrier()

        with ExitStack() as st:
          if i_phase_on(6):
            wsT = k.sb("wsT", [128, 4, 128], stack=st); bws = Buf("wsT")
            k.dma(wsT[:], sgwT.ap()[l], writes=[bws])
            bsb = k.sb("bsb", [128, 512], stack=st); bbsb = Buf("bsb")
            k.dma(bsb[:], sgb.ap()[l].partition_broadcast(128), writes=[bbsb])
            uv = k.sb("uv", [128, 4, 512], stack=st); buv = Buf("uv")
            sq = k.sb("sqg", [128, 2, 512], stack=st); bsq = Buf("sqg")
            mn = k.sb("mng", [128, 512], stack=st); bmn = Buf("mng")
            va = k.sb("vag", [128, 512], stack=st); bva = Buf("vag")
            vtm = k.sb("vtmg", [128, 256], stack=st); bvtm = Buf("vtmg")
            tmp = k.sb("tmpg", [128, 128], stack=st); btmp = Buf("tmpg")
            osg = k.sb("osg", [128, 2, 512], stack=st); bosg = Buf("osg")
            for (t0, W, who) in TILES:
                if last and who == 1:
                    continue
                k.dma(uv[:, :, :W], pxT.ap().rearrange("c p t -> p c t")[:, 0:4, t0:t0 + W], reads=[bpx], writes=[buv])
                k.op("act", lambda e: e.activation(out=uv[:, :, :W], in_=uv[:, :, :W], func=AF.Gelu), writes=[buv])
                k.op("act", lambda e: e.activation(out=sq[:, :, :W], in_=uv[:, 2:4, :W], func=AF.Square), reads=[buv], writes=[bsq])
                bk1, bb1 = k.bank(); bk2, bb2 = k.bank()
                for c in range(2):
                    k.op("pe", lambda e, c=c: e.matmul(bk1[:, :W], lhsT=ones[:], rhs=uv[:, 2 + c, :W], start=(c == 0), stop=(c == 1)), reads=[buv, bo], writes=[bb1])
                for c in range(2):
                    k.op("pe", lambda e, c=c: e.matmul(bk2[:, :W], lhsT=ones[:], rhs=sq[:, c, :W], start=(c == 0), stop=(c == 1)), reads=[bsq, bo], writes=[bb2])
                k.op("dve", lambda e: e.tensor_scalar(out=mn[:, :W], in0=bk1[:, :W], scalar1=1.0 / 256, scalar2=None, op0=ALU.mult), reads=[bb1], writes=[bmn])
                k.op("dve", lambda e: e.tensor_tensor(out=va[:, :W], in0=mn[:, :W], in1=mn[:, :W], op=ALU.mult), reads=[bmn], writes=[bva])
                k.op("dve", lambda e: e.scalar_tensor_tensor(out=va[:, :W], in0=bk2[:, :W], scalar=1.0 / 256, in1=va[:, :W], op0=ALU.mult, op1=ALU.subtract), reads=[bb2], writes=[bva])
                k.op("dve", lambda e: e.tensor_scalar(out=va[:, :W], in0=va[:, :W], scalar1=1e-5, scalar2=None, op0=ALU.add), writes=[bva])
                k.op("act", lambda e: e.activation(out=va[:, :W], in_=va[:, :W], func=AF.Sqrt), writes=[bva])
                k.op("dve", lambda e: e.reciprocal(out=va[:, :W], in_=va[:, :W]), writes=[bva])
                for c in range(2):
                    k.op("dve", lambda e, c=c: e.tensor_tensor(out=uv[:, 2 + c, :W], in0=uv[:, 2 + c, :W], in1=mn[:, :W], op=ALU.subtract), reads=[bmn], writes=[buv])
                    k.op("dve", lambda e, c=c: e.tensor_tensor(out=uv[:, 2 + c, :W], in0=uv[:, 2 + c, :W], in1=va[:, :W], op=ALU.mult), reads=[bva], writes=[buv])
                    k.op("dve", lambda e, c=c: e.tensor_scalar(out=uv[:, 2 + c, :W], in0=uv[:, 2 + c, :W], scalar1=pp[:, SLG + c:SLG + c + 1], scalar2=pp[:, SLB + c:SLB + c + 1],
                                                              op0=ALU.mult, op1=ALU.add), reads=[bpp], writes=[buv])
                for nb in range(W // 128):
                    cols = slice(nb * 128, (nb + 1) * 128)
                    bkT, bbT = k.bank()
                    for c in range(2):
                        k.op("pe", lambda e, c=c, cols=cols: e.transpose(bkT[:, c * 128:(c + 1) * 128], uv[:, 2 + c, cols], ident), reads=[buv, bc], writes=[bbT])
                    k.op("act", lambda e: e.copy(out=vtm[:], in_=bkT[:, 0:256]), reads=[bbT], writes=[bvtm])
                    bkM, bbM = k.bank()
                    for g in range(4):
                        k.op("pe", lambda e, g=g: e.matmul(bkM[:, g * 128:(g + 1) * 128], lhsT=vtm[:, (g // 2) * 128:(g // 2 + 1) * 128], rhs=wsT[:, g, :], start=True, stop=True),
                             reads=[bvtm, bws], writes=[bbM])
                    for g in range(4):
                        ps_ = slice((g % 2) * 64, (g % 2 + 1) * 64)
                        k.op("dve", lambda e, g=g, ps_=ps_: e.tensor_tensor(out=tmp[ps_, :], in0=bkM[ps_, g * 128:(g + 1) * 128], in1=bsb[ps_, g * 128:(g + 1) * 128], op=ALU.add),
                             reads=[bbM, bbsb], writes=[btmp])
                        k.op("dve", lambda e, g=g, ps_=ps_, cols=cols: e.tensor_tensor(out=osg[ps_, g // 2, cols], in0=tmp[ps_, :], in1=uv[ps_, g // 2, cols], op=ALU.mult),
                             reads=[btmp, buv], writes=[bosg])
                k.dma(mixT.ap().rearrange("c p t -> p c t")[:, 0:2, t0:t0 + W], osg[:, :, :W], reads=[bosg], adds=[bmix])
            k.barrier()

        with ExitStack() as st:
          if i_phase_on(7):
            cosT = k.sb("cosT", [128, T], stack=st); sinT = k.sb("sinT", [128, T], stack=st); btab = Buf("tab")
            k.dma(cosT[:], cosd.ap(), adds=[btab]); k.dma(sinT[:], sind.ap(), adds=[btab], q="pool")
            am = k.sb("am", [128, 2, 512], stack=st); bam = Buf("am")
            k.dma(am[:], amd.ap(), writes=[bam])
            esb = k.sb("esb", [64, 8], stack=st); besb = Buf("esb")
            k.dma(esb[:], sink.ap()[l].partition_broadcast(64), writes=[besb])
            k.op("act", lambda e: e.activation(out=esb[:], in_=esb[:], func=AF.Exp), writes=[besb])
            KT2 = k.sb("KT2", [128, 2, TT], BF16, stack=st); bKT = Buf("KT2")
            Vt = k.sb("Vt", [128, NTOKB, 128], BF16, stack=st); bVt = Buf("Vt")
            onesb = k.sb("onesb", [128, 64], BF16, stack=st); bonb = Buf("onesb")
            k.op("dve", lambda e: e.memset(onesb[:], 1.0), writes=[bonb])
            t2 = k.sb("t2r", [128, 512], stack=st); bt2 = Buf("t2r")
            kraw = k.sb("kraw", [128, TT], stack=st); bkr = Buf("kraw")
            t1 = k.sb("t1", [128, 512], stack=st); bt1 = Buf("t1")
            k.dma(kraw[:], pxT.ap()[11], reads=[bpx], writes=[bkr])

            def rope(dst, src, bsrc, tl, W, wr_reads, bdst):
                bk, bb = k.bank()
                k.op("pe", lambda e: e.matmul(bk[:, :W], lhsT=Rm, rhs=src, start=True, stop=True), reads=[bsrc, bc], writes=[bb])
                k.op("dve", lambda e: e.tensor_tensor(out=t1[:, :W], in0=bk[:, :W], in1=sinT[:, tl:tl + W], op=ALU.mult), reads=[bb, btab], writes=[bt1])
                k.op("dve", lambda e: e.tensor_tensor(out=t2[:, :W], in0=src, in1=cosT[:, tl:tl + W], op=ALU.mult), reads=[bsrc, btab], writes=[bt2])
                k.op("dve", lambda e: e.tensor_tensor(out=dst, in0=t2[:, :W], in1=t1[:, :W], op=ALU.add), reads=[bt1, bt2], writes=[bdst])

            for i in range(8):
                sl = slice(CT + i * 512, CT + (i + 1) * 512)
                rope(kraw[:, sl], kraw[:, sl], bkr, i * 512, 512, [], bkr)
            for hk in range(2):
                ps_ = slice(hk * 64, (hk + 1) * 64)
                for e_ in range(2):
                    po = slice(e_ * 64, (e_ + 1) * 64)
                    k.op("dve" if e_ == 0 else "act", (lambda e, hk=hk, ps_=ps_, po=po: e.tensor_copy(out=KT2[po, hk, :], in_=kraw[ps_, :])) if e_ == 0 else (lambda e, hk=hk, ps_=ps_, po=po: e.copy(out=KT2[po, hk, :], in_=kraw[ps_, :])), reads=[bkr], adds=[bKT])
            vraw = kraw
            k.dma(vraw[:], pxT.ap()[12], reads=[bpx], writes=[bkr])
            for b4 in range(0, NTOKB, 4):
                nb_ = min(4, NTOKB - b4)
                bk, bb = k.bank()
                for i in range(nb_):
                    k.op("pe", lambda e, i=i, b4=b4: e.transpose(bk[:, i * 128:(i + 1) * 128], vraw[:, (b4 + i) * 128:(b4 + i + 1) * 128], ident), reads=[bkr, bc], writes=[bb])
                evac(Vt[:, b4:b4 + nb_, :].rearrange("p a b -> p (a b)"), bk[:, :nb_ * 128], [bb], adds=[bVt])
            qraw = k.sb("qraw", [128, 4, 512], stack=st); bq = Buf("qraw")
            qr = k.sb("qr", [128, 4, 512], BF16, stack=st); bqr = Buf("qr")
            PT = [k.sb(f"PT{i}", [128, 512], BF16, stack=st) for i in range(2)]; bPT = [Buf("PT0"), Buf("PT1")]
            rc = k.sb("rc", [64, 512], stack=st); brc = Buf("rc")
            obt = k.sb("obt", [64, 2, 4, 512], stack=st); bobt = Buf("obt")
            pti = 0; sbi = [0]
            for (t0, W, who) in TILES:
                if last and who == 1:
                    continue
                k.dma(qraw[:, :, :W], pxT.ap().rearrange("c p t -> p c t")[:, 4:8, t0:t0 + W], reads=[bpx], writes=[bq])
                if who == 0:
                    for c in range(4):
                        rope(qr[:, c, :W], qraw[:, c, :W], bq, t0 - CT, W, [], bqr)
                    Q = qr; bQ = bqr
                else:
                    for c in range(4):
                        cast(qr[:, c, :W], qraw[:, c, :W], [bq], writes=[bqr] if c == 0 else (), adds=[bqr] if c else ())
                    Q = qr; bQ = bqr
                for nb in range(W // 128):
                    n = t0 // 128 + nb
                    cols = slice(nb * 128, (nb + 1) * 128)
                    if who == 1:
                        kbs = [(0, None), (1, None)]
                    else:
                        kbs = [(0, None), (1, None)]
                        if n - 1 >= 2:
                            kbs.append((n - 1, 0))
                        kbs.append((n, None))
                        if n + 1 < NTOKB:
                            kbs.append((n + 1, 1))
                    GO = [0, 2, 1, 3]
                    for hk in range(2):
                        bkO, bbO = k.banks[(hk % 2) * 2]; bkD, bbD = k.banks[(hk % 2) * 2 + 1]
                        for idx, (kb, mk) in enumerate(kbs):
                            bkS = [None, None]; bbS = [None, None]
                            bkS[0], bbS[0] = k.banks[4 + (sbi[0] % 2) * 2]; bkS[1], bbS[1] = k.banks[5 + (sbi[0] % 2) * 2]; sbi[0] += 1
                            for pos in range(4):
                                g = GO[pos]; h = hk * 4 + g; e_ = h % 2; chn = h // 2
                                assert e_ == pos // 2
                                k.op("pe", lambda e, pos=pos, e_=e_, chn=chn, kb=kb, hk=hk: e.matmul(bkS[e_][:, (pos % 2) * 128:(pos % 2 + 1) * 128],
                                                                                                    lhsT=KT2[e_ * 64:(e_ + 1) * 64, hk, kb * 128:(kb + 1) * 128],
                                                                                                    rhs=Q[e_ * 64:(e_ + 1) * 64, chn, cols], start=True, stop=True),
                                     reads=[bKT, bQ], writes=[bbS[e_]])
                            P_ = PT[pti % 2]; bP_ = bPT[pti % 2]; pti += 1
                            k.op("act", lambda e, P_=P_, b0=bkS[0]: e.activation(out=P_[:, 0:256], in_=b0[:, 0:256], func=AF.Exp, scale=0.125), reads=[bbS[0]], writes=[bP_])
                            k.op("act", lambda e, P_=P_, b1=bkS[1]: e.activation(out=P_[:, 256:512], in_=b1[:, 0:256], func=AF.Exp, scale=0.125), reads=[bbS[1]], adds=[bP_])
                            if mk is not None:
                                k.op("dve", lambda e, P_=P_, mk=mk: e.tensor_tensor(out=P_[:], in0=P_[:], in1=am[:, mk, :], op=ALU.mult), reads=[bam], writes=[bP_])
                            k.op("pe", lambda e, P_=P_, kb=kb, hk=hk, idx=idx: e.matmul(bkO[0:64, :], lhsT=Vt[:, kb, hk * 64:(hk + 1) * 64], rhs=P_[:], start=(idx == 0), stop=(idx == len(kbs) - 1)),
                                 reads=[bVt, bP_], writes=[bbO])
                            k.op("pe", lambda e, P_=P_, idx=idx: e.matmul(bkD[0:64, :], lhsT=onesb[:], rhs=P_[:], start=(idx == 0), stop=(idx == len(kbs) - 1)), reads=[bonb, bP_], writes=[bbD])
                        for pos in range(4):
                            h = hk * 4 + GO[pos]
                            k.op("dve", lambda e, pos=pos, h=h, bkD=bkD: e.tensor_scalar(out=rc[:, pos * 128:(pos + 1) * 128], in0=bkD[0:64, pos * 128:(pos + 1) * 128],
                                                                                       scalar1=esb[:, h:h + 1], scalar2=None, op0=ALU.add),
                                 reads=[bbD, besb], writes=[brc] if pos == 0 else (), adds=[brc] if pos else ())
                        k.op("dve", lambda e: e.reciprocal(out=rc[:], in_=rc[:]), writes=[brc])
                        for pos in range(4):
                            g = GO[pos]
                            k.op("dve", lambda e, pos=pos, g=g, bkO=bkO, hk=hk, cols=cols: e.tensor_tensor(out=obt[:, hk, g, cols], in0=bkO[0:64, pos * 128:(pos + 1) * 128],
                                                                                                      in1=rc[:, pos * 128:(pos + 1) * 128], op=ALU.mult), reads=[bbO, brc], adds=[bobt])
                for hk in range(2):
                    for g in range(4):
                        h = hk * 4 + g
                        k.dma(mixT.ap()[2 + h // 2][(h % 2) * 64:(h % 2 + 1) * 64, t0:t0 + W], obt[:, hk, g, :W], reads=[bobt], adds=[bmix], q="sp" if g % 2 == 0 else "pool")
                k.op("dve", lambda e: e.memset(rc[:, 0:1], 0.0), reads=[], writes=[bobt, brc])
            k.barrier()

        with ExitStack() as st:
          if i_phase_on(8):
            wo = k.sb("wo", [128, 8, 1024], BF16, stack=st); bwo = Buf("wo")
            wst = [k.sb(f"wst5{i}", [128, 1024], stack=st) for i in range(2)]; bwst = [Buf("wst0"), Buf("wst1")]
            wv = w_out.ap()[l].rearrange("(kc p) n -> p kc n", p=128)
            for kc in range(8):
                k.dma(wst[kc % 2][:], wv[:, kc, :], writes=[bwst[kc % 2]], q="sp" if kc % 2 == 0 else "pool")
                cast(wo[:, kc, :], wst[kc % 2][:], [bwst[kc % 2]], adds=[bwo])
            Mb = k.sb("Mb5", [128, 8, 512], BF16, stack=st); bMb = Buf("Mb5")
            X = k.sb("X5", [128, 8, 512], stack=st); bX = Buf("X5")
            M = k.sb("M5", [128, 8, 512], stack=st); bM = Buf("M5")
            Y = k.sb("Y5", [128, 8, 512], stack=st); bY = Buf("Y5")
            SQ = k.sb("SQ5", [128, 8, 512], BF16, stack=st); bSQ = Buf("SQ5")
            rs = k.sb("rs5", [128, 512], stack=st); brs = Buf("rs5")
            xv = xsrc(l).ap().rearrange("c p t -> p c t")
            xo = xs.ap().rearrange("c p t -> p c t")
            for (t0, W, who) in TILES:
                if last and who == 1:
                    continue
                k.dma(X[:, :, :W], xv[:, :, t0:t0 + W], reads=[bxs], writes=[bX])
                k.dma(M[:, :, :W], mixT.ap().rearrange("c p t -> p c t")[:, :, t0:t0 + W], reads=[bmix], writes=[bM], q="pool")
                for kc in range(8):
                    cast(Mb[:, kc, :W], M[:, kc, :W], [bM], writes=[bMb] if kc == 0 else (), adds=[bMb] if kc else ())
                for j in range(8):
                    bk, bb = k.bank()
                    for kc in range(8):
                        k.op("pe", lambda e, kc=kc, j=j: e.matmul(bk[:, :W], lhsT=wo[:, kc, j * 128:(j + 1) * 128], rhs=Mb[:, kc, :W], start=(kc == 0), stop=(kc == 7)),
                             reads=[bwo, bMb], writes=[bb])
                    evac(Y[:, j, :W], bk[:, :W], [bb], adds=[bY])
                rmsn(st, Y, bY, W, SQ, bSQ, rs, brs)
                resid_update(X, bX, Y, bY, rs, brs, W, 2, who)
                k.dma(xo[:, :, t0:t0 + W], X[:, :, :W], reads=[bX], adds=[bxs])
            k.barrier()

        with ExitStack() as st:
          if i_phase_on(9):
            wgb = k.sb("wgb", [128, 8, 5632], BF16, stack=st); bwgb = Buf("wgb")
            NST = 3
            wst = [k.sb(f"wst6{i}", [128, 1024], stack=st) for i in range(NST)]; bwst = [Buf(f"wst{i}") for i in range(NST)]
            wdj = [k.sb(f"wdj{i}", [128, 1024], BF16, stack=st) for i in range(2)]; bwdj = [Buf("wdj0"), Buf("wdj1")]
            gv_ = w_gu.ap()[l].rearrange("(kc p) n -> p kc n", p=128)
            dv_ = w_down.ap()[l].rearrange("(j p) n -> p j n", p=128)
            wi = 0
            for kc in range(8):
                for cb in range(6):
                    c0_ = cb * 1024; cw = min(1024, 5632 - c0_)
                    k.dma(wst[wi % NST][:, :cw], gv_[:, kc, c0_:c0_ + cw], writes=[bwst[wi % NST]], q="sp" if wi % 2 == 0 else "pool")
                    cast(wgb[:, kc, c0_:c0_ + cw], wst[wi % NST][:, :cw], [bwst[wi % NST]], adds=[bwgb])
                    wi += 1
            X = k.sb("X6", [128, 8, 512], stack=st); bX = Buf("X6")
            H = k.sb("H6", [128, 8, 512], BF16, stack=st); bH = Buf("H6")
            SQ = k.sb("SQ6", [128, 8, 512], BF16, stack=st); bSQ = Buf("SQ6")
            Y = k.sb("Y6", [128, 8, 512], stack=st); bY = Buf("Y6")
            rs = k.sb("rs6", [128, 512], stack=st); brs = Buf("rs6")
            actb = k.sb("actb", [128, 22, 512], BF16, stack=st); bact = Buf("actb")
            sl = [k.sb(f"sl6{i}", [128, 512], stack=st) for i in range(2)]; bsl = [Buf("sl0"), Buf("sl1")]
            xo = xs.ap().rearrange("c p t -> p c t")
            dj = 0
            for (t0, W, who) in TILES:
                if last and who == 1:
                    continue
                k.dma(X[:, :, :W], xo[:, :, t0:t0 + W], reads=[bxs], writes=[bX])
                rmsn(st, X, bX, W, SQ, bSQ, rs, brs)
                modulate(H, bH, X, bX, rs, brs, W, 3, 4, who)
                for j in range(22):
                    bkg, bbg = k.bank(); bku, bbu = k.bank()
                    for kc in range(8):
                        k.op("pe", lambda e, kc=kc, j=j: e.matmul(bkg[:, :W], lhsT=wgb[:, kc, j * 128:(j + 1) * 128], rhs=H[:, kc, :W], start=(kc == 0), stop=(kc == 7)),
                             reads=[bwgb, bH], writes=[bbg])
                    for kc in range(8):
                        k.op("pe", lambda e, kc=kc, j=j: e.matmul(bku[:, :W], lhsT=wgb[:, kc, 2816 + j * 128:2816 + (j + 1) * 128], rhs=H[:, kc, :W], start=(kc == 0), stop=(kc == 7)),
                             reads=[bwgb, bH], writes=[bbu])
                    s_ = sl[j % 2]; bs_ = bsl[j % 2]
                    k.op("act", lambda e, bkg=bkg, s_=s_: e.activation(out=s_[:, :W], in_=bkg[:, :W], func=AF.Silu), reads=[bbg], writes=[bs_])
                    k.op("dve", lambda e, j=j, bku=bku, s_=s_: e.tensor_tensor(out=actb[:, j, :W], in0=bku[:, :W], in1=s_[:, :W], op=ALU.mult), reads=[bbu, bs_],
                         writes=[bact] if j == 0 else (), adds=[bact] if j else ())
                for j in range(22):
                    w_ = wst[wi % NST]; bw_ = bwst[wi % NST]
                    k.dma(w_[:], dv_[:, j, :], writes=[bw_], q="sp" if wi % 2 == 0 else "pool")
                    wi += 1
                    d_ = wdj[dj % 2]; bd_ = bwdj[dj % 2]; dj += 1
                    cast(d_[:], w_[:], [bw_], writes=[bd_])
                    for jo in range(8):
                        bk, bb = k.banks[jo]
                        k.op("pe", lambda e, j=j, jo=jo, d_=d_, bk=bk: e.matmul(bk[:, :W], lhsT=d_[:, jo * 128:(jo + 1) * 128], rhs=actb[:, j, :W], start=(j == 0), stop=(j == 21)),
                             reads=[bd_, bact], writes=[bb])
                for jo in range(8):
                    bk, bb = k.banks[jo]
                    evac(Y[:, jo, :W], bk[:, :W], [bb], writes=[bY] if jo == 0 else (), adds=[bY] if jo else ())
                rmsn(st, Y, bY, W, SQ, bSQ, rs, brs)
                resid_update(X, bX, Y, bY, rs, brs, W, 5, who)
                if l == L - 1:
                    k.dma(outT.ap().rearrange("c p t -> p c t")[:, :, t0 - CT:t0 - CT + W], X[:, :, :W], reads=[bX], adds=[bout])
                else:
                    k.dma(xo[:, :, t0:t0 + W], X[:, :, :W], reads=[bX], adds=[bxs])
            k.barrier()
    k.barrier()
    return nc, k


def host_consts():
    ident = np.eye(128, dtype=np.float32)
    bones = np.zeros((128, 128), np.float32); bones[:64, :64] = 1; bones[64:, 64:] = 1
    Rm = np.zeros((128, 128), np.float32)
    for m in range(128):
        i = m % 32
        if i < 16:
            Rm[m + 16, m] = -1.0
        else:
            Rm[m - 16, m] = 1.0
    cc = np.concatenate([ident, bones, Rm], axis=1)
    smL = np.zeros((128, 9, 4), np.float32); smC = np.zeros((128, 9, 2), np.float32)
    for ci in range(9):
        for p in range(128):
            if ci < 3:
                ch = ci * 128 + p; qd = ch // 96; half = 0 if ch < 192 else 1
            else:
                ch = (ci - 3) * 128 + p; qd = ch // 192; half = 0 if ch < 384 else 1
            smL[p, ci, qd] = 1; smC[p, ci, half] = 1
    s = np.arange(64)[:, None]; t = np.arange(64)[None, :]
    mS = [(s < t), (s > t)]; mST = [(s > t), (s < t)]; mI = [(s <= t), (s >= t)]
    cm = np.zeros((64, 4, 512), np.float32)
    for d in range(2):
        for h in range(4):
            q = d * 4 + h
            cm[:, 0, q * 64:(q + 1) * 64] = mS[d]
            cm[:, 1, q * 64:(q + 1) * 64] = mST[d]
            cm[:, 2, q * 64:(q + 1) * 64] = mI[d]
            cm[:, 3, q * 64:(q + 1) * 64] = np.eye(64)
    j = np.arange(128)[:, None]; i = np.arange(128)[None, :]
    am = np.zeros((128, 2, 512), np.float32)
    for g in range(4):
        am[:, 0, g * 128:(g + 1) * 128] = (j >= i)
        am[:, 1, g * 128:(g + 1) * 128] = (j <= i)
    rs = np.ones((128, 512), np.float32); rs[:, ::64] = 0
    tt = np.arange(T)
    pos = np.stack([tt // 64, tt % 64], -1).astype(np.float32)
    inv = (10000.0 ** (-np.arange(16, dtype=np.float32) / 16)).astype(np.float32)
    ang = pos[:, :, None] * inv
    cosT = np.zeros((128, T), np.float32); sinT = np.zeros((128, T), np.float32)
    for p in range(128):
        dd = p % 64; ax = dd // 32; fr = dd % 16
        cosT[p] = np.cos(ang[:, ax, fr]); sinT[p] = np.sin(ang[:, ax, fr])
    return dict(cc=cc, smL=smL, smC=smC, cm=cm, am=am, rs=rs, cosT=cosT, sinT=sinT)


def cmaj(v, n):
    return np.ascontiguousarray(np.asarray(v, np.float32).reshape(n, 128).T)


def host_inputs(inp, b):
    f = lambda a: np.ascontiguousarray(np.asarray(a, np.float32))
    xall = np.concatenate([inp["ctx"][b], inp["x"][b]], axis=0)
    xin = np.ascontiguousarray(xall.T.reshape(8, 128, TT))
    cin = np.stack([cmaj(inp["c"][b], 8), cmaj(inp["c_ctx"], 8)], axis=-1)
    pp = np.zeros((L, 128, NP), np.float32)
    for l in range(L):
        for i in range(4):
            pp[l, :, NG + i * 8:NG + i * 8 + 8] = cmaj(inp["norm_g"][l, i], 8)
        pp[l, :, BM:BM + 48] = cmaj(inp["b_mod"][l], 48)
        pp[l, :, MU:MU + 9] = cmaj(inp["rw_mu"][l], 9)
        pp[l, :, KK:KK + 2] = cmaj(inp["rw_kk"][l], 2)
        pp[l, :, KA:KA + 2] = cmaj(inp["rw_ka"][l], 2)
        pp[l, :, RK:RK + 2] = cmaj(inp["rw_rk"][l].reshape(-1), 2)
        pp[l, :, LXG:LXG + 2] = cmaj(inp["rw_lnx_g"][l], 2)
        pp[l, :, LXB:LXB + 2] = cmaj(inp["rw_lnx_b"][l], 2)
        pp[l, :, W0:W0 + 4] = cmaj(inp["rw_w0"][l].reshape(-1), 4)
        pp[l, :, A0:A0 + 4] = cmaj(inp["rw_a0"][l].reshape(-1), 4)
        pp[l, :, SLG:SLG + 2] = cmaj(inp["sg_ln_g"][l], 2)
        pp[l, :, SLB:SLB + 2] = cmaj(inp["sg_ln_b"][l], 2)
    m = dict(xin=xin, cin=f(cin), w_mod=f(inp["w_mod"]), w_in=f(inp["w_in"]), w_out=f(inp["w_out"]), w_gu=f(inp["w_gu"]), w_down=f(inp["w_down"]),
             pp=pp, sgwT=f(np.transpose(inp["sg_w"], (0, 3, 1, 2))), sgb=f(inp["sg_b"]).reshape(L, 1, 512), sink=f(inp["attn_sink"]).reshape(L, 1, 8),
             w2=f(inp["rw_w2"]).reshape(L, 128, 256), a2=f(inp["rw_a2"]).reshape(L, 128, 256), g2=f(inp["rw_g2"]))
    m.update(host_consts())
    return m


REAL_CORES = [0, 1, 4, 5]


def kernel(**inp):
    inp = {k_: np.asarray(v) for k_, v in inp.items()}
    nc, _ = build()
    maps = [host_inputs(inp, b) for b in range(4)]
    zmap = {k_: np.zeros_like(v) for k_, v in maps[0].items()}
    in_maps = [zmap] * 8
    in_maps = list(in_maps)
    for b, c in enumerate(REAL_CORES):
        in_maps[c] = maps[b]
    res = run_bass_kernel_spmd(nc, in_maps, core_ids=list(range(8)))
    out = np.zeros((4, T, D), np.float32)
    for b, c in enumerate(REAL_CORES):
        o = res.results[c]["outT"]
        out[b] = o.reshape(D, T).T
    return out
```
